# Optimizing a Trainium2 kernel written in Bass

```python
import math
import jax, jax.numpy as jnp
from jax import lax
import numpy as np

D_MODEL = 1024
BATCH = 4
SEQ = 8192
DEPTH = 2

N_MIXERS = 2
N_HEADS = 8
HEAD_DIM = D_MODEL // N_HEADS
ATTN_WIDTH = N_HEADS * HEAD_DIM
BLOCK = 256
TOPK_BLOCKS = 3
QUERY_CHUNK = 32
CONV_WIDTH = 3
D_FF = int(math.ceil((8 * D_MODEL / 3) / 256) * 256)
N_ATTN_LAYERS = (DEPTH + 1) // 2
N_CONV_LAYERS = DEPTH // 2
EPS = 1e-6

kernel_name = "hybrid_moba_shortconv_adaln"


def rms_norm(x, gain):
    xf = x.astype(jnp.float32)
    y = xf * lax.rsqrt(jnp.mean(xf * xf, axis=-1, keepdims=True) + EPS)
    return (y * gain.astype(jnp.float32)).astype(x.dtype)


def modulate(x, gain, shift, scale):
    return rms_norm(x, gain) * (1 + scale[:, None, :]) + shift[:, None, :]


def moba_attention(h, w_qkv, w_o, q_gain, k_gain):
    B, S, _ = h.shape
    qkv = (h @ w_qkv).reshape(B, S, 3, N_HEADS, HEAD_DIM).transpose(2, 0, 3, 1, 4)
    q = rms_norm(qkv[0], q_gain) * (HEAD_DIM ** -0.5)
    k = rms_norm(qkv[1], k_gain)
    v = qkv[2]
    n_blocks = -(-S // BLOCK)
    pad = n_blocks * BLOCK - S
    kp = jnp.pad(k, ((0, 0), (0, 0), (0, pad), (0, 0)))
    vp = jnp.pad(v, ((0, 0), (0, 0), (0, pad), (0, 0)))
    k_blocks = kp.reshape(B, N_HEADS, n_blocks, BLOCK, HEAD_DIM)
    v_blocks = vp.reshape(B, N_HEADS, n_blocks, BLOCK, HEAD_DIM)
    k_mean = jnp.mean(k_blocks.astype(jnp.float32), axis=3).astype(k.dtype)
    n_sel = min(TOPK_BLOCKS, n_blocks)
    bi = jnp.arange(B)[:, None, None, None]
    hi = jnp.arange(N_HEADS)[None, :, None, None]
    blk_ids = jnp.arange(n_blocks)

    def chunk(ci):
        q0 = ci * QUERY_CHUNK
        nb = q0 // BLOCK
        qc = lax.dynamic_slice_in_dim(q, q0, QUERY_CHUNK, axis=2)
        k_own = lax.dynamic_slice_in_dim(kp, nb * BLOCK, BLOCK, axis=2)
        v_own = lax.dynamic_slice_in_dim(vp, nb * BLOCK, BLOCK, axis=2)
        route = jnp.einsum('bhqd,bhnd->bhqn', qc, k_mean).astype(jnp.float32)
        route = jnp.where(blk_ids < nb, route, -jnp.inf)
        _, idx = lax.top_k(route, n_sel)
        k_sel = k_blocks[bi, hi, idx]
        v_sel = v_blocks[bi, hi, idx]
        qpos = q0 + jnp.arange(QUERY_CHUNK)
        kpos = nb * BLOCK + jnp.arange(BLOCK)
        s_own = jnp.einsum('bhqd,bhkd->bhqk', qc, k_own).astype(jnp.float32)
        s_own = jnp.where(kpos[None, :] <= qpos[:, None], s_own, -jnp.inf)
        s_sel = jnp.einsum('bhqd,bhqjkd->bhqjk', qc, k_sel).astype(jnp.float32)
        slot_ok = jnp.arange(n_sel) < nb
        s_sel = jnp.where(slot_ok[:, None], s_sel, -jnp.inf)
        s_sel = s_sel.reshape(B, N_HEADS, QUERY_CHUNK, n_sel * BLOCK)
        p = jax.nn.softmax(jnp.concatenate([s_own, s_sel], axis=-1), axis=-1).astype(v.dtype)
        p_own = p[..., :BLOCK]
        p_sel = p[..., BLOCK:].reshape(B, N_HEADS, QUERY_CHUNK, n_sel, BLOCK)
        return (jnp.einsum('bhqk,bhkd->bhqd', p_own, v_own)
                + jnp.einsum('bhqjk,bhqjkd->bhqd', p_sel, v_sel))

    o = lax.map(chunk, jnp.arange(S // QUERY_CHUNK))
    o = o.transpose(1, 0, 3, 2, 4).reshape(B, S, ATTN_WIDTH)
    return o @ w_o


def short_conv(h, w_in, conv_w, w_out):
    S = h.shape[1]
    b_gate, c_gate, u = jnp.split(h @ w_in, 3, axis=-1)
    u = c_gate * u
    up = jnp.pad(u, ((0, 0), (CONV_WIDTH - 1, 0), (0, 0)))
    y = sum(conv_w[j] * up[:, j:j + S] for j in range(CONV_WIDTH))
    return (b_gate * y) @ w_out


def swiglu(h, w_gate_up, w_down):
    g, u = jnp.split(h @ w_gate_up, 2, axis=-1)
    return (jax.nn.silu(g) * u) @ w_down


def setup_inputs(seed: int = 0) -> dict:
    key = jax.random.key(seed)
    ks = jax.random.split(key, 16)
    f32 = jnp.float32
    nrm = lambda k, shape, s: (jax.random.normal(k, shape, f32) * s)
    return {
        "x": nrm(ks[0], (BATCH, SEQ, D_MODEL), 1.0),
        "c": nrm(ks[1], (BATCH, D_MODEL), 1.0),
        "w_ada": nrm(ks[2], (DEPTH, D_MODEL, 6 * D_MODEL), 0.5 * D_MODEL ** -0.5),
        "b_ada": nrm(ks[3], (DEPTH, 6 * D_MODEL), 0.02),
        "norm_mix": 1.0 + nrm(ks[4], (DEPTH, D_MODEL), 0.02),
        "norm_ffn": 1.0 + nrm(ks[5], (DEPTH, D_MODEL), 0.02),
        "w_qkv": nrm(ks[6], (N_ATTN_LAYERS, D_MODEL, 3 * ATTN_WIDTH), D_MODEL ** -0.5),
        "w_o": nrm(ks[7], (N_ATTN_LAYERS, ATTN_WIDTH, D_MODEL), ATTN_WIDTH ** -0.5),
        "q_gain": 1.0 + nrm(ks[8], (N_ATTN_LAYERS, HEAD_DIM), 0.02),
        "k_gain": 1.0 + nrm(ks[9], (N_ATTN_LAYERS, HEAD_DIM), 0.02),
        "w_in": nrm(ks[10], (N_CONV_LAYERS, D_MODEL, 3 * D_MODEL), D_MODEL ** -0.5),
        "conv_w": nrm(ks[11], (N_CONV_LAYERS, CONV_WIDTH, D_MODEL), CONV_WIDTH ** -0.5),
        "w_out": nrm(ks[12], (N_CONV_LAYERS, D_MODEL, D_MODEL), D_MODEL ** -0.5),
        "w_gate_up": nrm(ks[13], (DEPTH, D_MODEL, 2 * D_FF), D_MODEL ** -0.5),
        "w_down": nrm(ks[14], (DEPTH, D_FF, D_MODEL), D_FF ** -0.5),
    }


def reference(x, c, w_ada, b_ada, norm_mix, norm_ffn, w_qkv, w_o, q_gain, k_gain,
              w_in, conv_w, w_out, w_gate_up, w_down):
    sc = jax.nn.silu(c)
    for i in range(DEPTH):
        mod = sc @ w_ada[i] + b_ada[i]
        sh_m, sc_m, g_m, sh_f, sc_f, g_f = jnp.split(mod, 6, axis=-1)
        h = modulate(x, norm_mix[i], sh_m, sc_m)
        j = i // N_MIXERS
        if i % N_MIXERS == 0:
            y = moba_attention(h, w_qkv[j], w_o[j], q_gain[j], k_gain[j])
        else:
            y = short_conv(h, w_in[j], conv_w[j], w_out[j])
        x = x + g_m[:, None, :] * y
        h = modulate(x, norm_ffn[i], sh_f, sc_f)
        x = x + g_f[:, None, :] * swiglu(h, w_gate_up[i], w_down[i])
    return x
```

```python
import contextlib
import numpy as np
import concourse.bass as bass
import concourse.mybir as mybir
from concourse.bass_utils import run_bass_kernel_spmd

F32 = mybir.dt.float32
BF16 = mybir.dt.bfloat16
ALU = mybir.AluOpType
AF = mybir.ActivationFunctionType

D = 1024
KC = 8
G = 512
H = 8
DH = 128
DFF = 2816
NJ = 22
NPOS = 16
NOWN = 8
NBLK = 32
SEQ = 8192
NEG = -30000.0
EPS = 1e-6
NW = 4
ENGS = ["pe", "act", "dve", "pool", "sp"]
INORDER = ("pe", "act", "dve")


class Sched:
    def __init__(self, nc, n_dma_sems=6):
        self.nc = nc
        self.ops = []
        self.last_write = {}
        self.readers = {}
        self.n_dma_sems = n_dma_sems

    def add(self, eng, fn, reads=(), writes=(), dma=False):
        oid = len(self.ops)
        deps = set()
        for b in reads:
            if b in self.last_write:
                deps.add(self.last_write[b])
        for b in writes:
            if b in self.last_write:
                deps.add(self.last_write[b])
            for r in self.readers.get(b, {}).values():
                deps.update(r)
        deps.discard(oid)
        self.ops.append(dict(id=oid, eng=eng, fn=fn, deps=deps, dma=dma, signal=False))
        for b in reads:
            rd = self.readers.setdefault(b, {})
            if dma or eng not in INORDER:
                rd.setdefault((eng, "dma"), []).append(oid)
            else:
                rd[eng] = [oid]
        for b in writes:
            self.last_write[b] = oid
            self.readers[b] = {}
        return oid

    def emit(self, final_wait_eng="sp"):
        nc = self.nc
        ops = self.ops

        def needs_sync(p, ceng):
            if p["dma"]:
                return True
            if p["eng"] != ceng:
                return True
            return p["eng"] in ("act", "dve", "pool")

        for op in ops:
            for d in op["deps"]:
                p = ops[d]
                if needs_sync(p, op["eng"]):
                    p["signal"] = True
            if op["dma"]:
                op["signal"] = True
        eng_count = {e: 0 for e in ENGS}
        dma_count = {}
        dma_rr = {e: 0 for e in ENGS}
        sem_keys = {}
        for op in ops:
            if not op["signal"]:
                continue
            if op["dma"]:
                k = dma_rr[op["eng"]] % self.n_dma_sems
                dma_rr[op["eng"]] += 1
                key = ("dma", op["eng"], k)
                prev = dma_count.get(key, 0)
                op["prev_on_sem"] = (key, prev)
                dma_count[key] = prev + 16
                op["sig"] = (key, prev + 16)
            else:
                key = ("eng", op["eng"])
                eng_count[op["eng"]] += 1
                op["sig"] = (key, eng_count[op["eng"]])
            sem_keys[key] = None
        with contextlib.ExitStack() as st:
            sems = {}
            for key in sem_keys:
                sems[key] = st.enter_context(nc.semaphore("s_" + "_".join(str(x) for x in key)))
            block = st.enter_context(nc.Block())
            per_eng = {e: [o for o in ops if o["eng"] == e] for e in ENGS}

            def body(ename):
                def run(eng):
                    waited = {}

                    def wait(key, val):
                        if waited.get(key, 0) >= val:
                            return
                        eng.wait_ge(sems[key], val)
                        waited[key] = val

                    for op in per_eng[ename]:
                        need = {}
                        for d in op["deps"]:
                            p = ops[d]
                            if not needs_sync(p, ename):
                                continue
                            key, val = p["sig"]
                            need[key] = max(need.get(key, 0), val)
                        if op["dma"]:
                            key, prev = op["prev_on_sem"]
                            if prev > 0:
                                need[key] = max(need.get(key, 0), prev)
                        for key, val in need.items():
                            wait(key, val)
                        ins = op["fn"](eng)
                        if op["signal"]:
                            key, val = op["sig"]
                            ins.then_inc(sems[key], 16 if op["dma"] else 1)
                    if ename == final_wait_eng:
                        for key, val in dma_count.items():
                            wait(key, val)
                        for e in ENGS:
                            if eng_count[e] > 0 and e != ename:
                                wait(("eng", e), eng_count[e])
                return run

            block.tensor(body("pe"))
            block.scalar(body("act"))
            block.vector(body("dve"))
            block.gpsimd(body("pool"))
            block.sync(body("sp"))


SM_C = 0
SM_BADA = 8
SM_NMIX = 104
SM_NFFN = 120
SM_QG = 136
SM_KG = 137
SM_CONV = 138
SM_HVALID = 162
SM_IDENT = 178
NSM = 306


class Ctx:
    def __init__(self, nc):
        self.nc = nc
        self.S = Sched(nc)
        self.wslot = 0
        self.bankrr = 0
        a = lambda n_, sh_, d_: nc.alloc_sbuf_tensor('sb_' + n_, sh_, d_)
        self.small = a("small", [128, NSM], F32)
        self.ident_bf = a("ident_bf", [128, 128], BF16)
        self.ones_bf = a("ones_bf", [128, 128], BF16)
        self.eps_t = a("eps_t", [128, 1], F32)
        self.wring = [a(f"wring{i}", [128, 4096], BF16) for i in range(NW)]
        self.xin = a("xin", [128, 4, D], F32)
        self.xT = a("xT", [128, KC, G], F32)
        self.sq = a("sq", [128, KC, G], BF16)
        self.hT = a("hT", [128, KC, G], BF16)
        self.tmp = [a(f"tmp{i}", [128, G], F32) for i in range(2)]
        self.std = a("std", [128, G], F32)
        self.rstd = a("rstd", [128, G], F32)
        self.big = a("big", [128, 24, G], BF16)
        self.mod = a("mod", [128, 2, 48], F32)
        self.modG = a("modG", [128, 2, 2, KC], F32)
        self.scbf = a("scbf", [128, KC], BF16)
        self.qgs = a("qgs", [128, 1], F32)
        self.ps = [nc.alloc_psum_tensor(f"ps{i}", [128, 512], F32) for i in range(7)]
        self.psb = nc.alloc_psum_tensor("psb", [128, 1024], BF16)

    @property
    def ident(self):
        return self.small[:, SM_IDENT:SM_IDENT + 128]

    def bank(self, lo=0, hi=7):
        b = lo + self.bankrr % (hi - lo)
        self.bankrr += 1
        return b

    def slab(self, src3d, kparts, ncols):
        slot = self.wslot % NW
        self.wslot += 1
        view = self.wring[slot][:, 0:kparts * ncols].rearrange("p (k n) -> p k n", k=kparts)
        self.S.add("pool", lambda e: e.dma_start(out=view, in_=src3d), writes=[("w", slot)], dma=True)
        return view, ("w", slot)

    def wcols(self, w2d, c0, ncols, r0=0, kparts=KC):
        return w2d[r0:r0 + kparts * 128, c0:c0 + ncols].rearrange("(k p) n -> p k n", p=128)

    def setup(self, small_ap):
        S = self.S
        S.add("sp", lambda e: e.dma_start(out=self.small[:], in_=small_ap), writes=["small"], dma=True)
        S.add("dve", lambda e: e.memset(self.ones_bf[:], 1.0), writes=["ones"])
        S.add("dve", lambda e: e.memset(self.eps_t[:], EPS), writes=["eps"])
        S.add("dve", lambda e: e.tensor_copy(self.ident_bf[:], self.ident), reads=["small"], writes=["identbf"])
        S.add("act", lambda e: e.activation(out=self.scbf[:], in_=self.small[:, SM_C:SM_C + 8], func=AF.Silu),
              reads=["small"], writes=["scbf"])
        S.add("act", lambda e: e.mul(self.qgs[:], self.small[:, SM_QG:SM_QG + 1], DH ** -0.5),
              reads=["small"], writes=["qgs"])

    def adaln(self, w_ada_l, l):
        S = self.S
        bank = self.bank()
        psk = ("ps", bank)
        for s in range(12):
            view, wk = self.slab(self.wcols(w_ada_l, s * 512, 512), KC, 512)
            for o in range(4):
                oc = s * 4 + o
                for kc in range(KC):
                    S.add("pe", lambda e, oc=oc, kc=kc, o=o, view=view: e.matmul(
                        self.ps[bank][:, oc:oc + 1], lhsT=view[:, kc, o * 128:(o + 1) * 128],
                        rhs=self.scbf[:, kc:kc + 1], start=(kc == 0), stop=(kc == KC - 1)),
                        reads=[wk, "scbf"], writes=[psk])
        mod = self.mod
        S.add("dve", lambda e: e.tensor_tensor(out=mod[:, l, :], in0=self.ps[bank][:, 0:48],
                                               in1=self.small[:, SM_BADA + 48 * l:SM_BADA + 48 * (l + 1)], op=ALU.add),
              reads=[psk, "small"], writes=[("mod", l)])
        S.add("dve", lambda e: e.scalar_tensor_tensor(out=self.modG[:, l, 0, :], in0=mod[:, l, 8:16], scalar=1.0,
                                                      in1=self.small[:, SM_NMIX + 8 * l:SM_NMIX + 8 * (l + 1)],
                                                      op0=ALU.add, op1=ALU.mult),
              reads=[("mod", l), "small"], writes=[("modG", l, 0)])
        S.add("dve", lambda e: e.scalar_tensor_tensor(out=self.modG[:, l, 1, :], in0=mod[:, l, 32:40], scalar=1.0,
                                                      in1=self.small[:, SM_NFFN + 8 * l:SM_NFFN + 8 * (l + 1)],
                                                      op0=ALU.add, op1=ALU.mult),
              reads=[("mod", l), "small"], writes=[("modG", l, 1)])

    def modcols(self, l, which):
        base = 0 if which == 0 else 24
        return (self.modG[:, l, which, :], self.mod[:, l, base:base + 8], self.mod[:, l, base + 16:base + 24],
                [("mod", l), ("modG", l, which)])

    def load_xT(self, x_rows, N=G, ntile=4):
        S = self.S
        xin = self.xin
        S.add("sp", lambda e: e.dma_start(out=xin[:, 0:ntile, :], in_=x_rows.rearrange("(t p) f -> p t f", p=128)),
              writes=["xin"], dma=True)
        for kc in range(KC):
            bank = self.bank()
            for t in range(ntile):
                S.add("pe", lambda e, kc=kc, t=t, bank=bank: e.transpose(
                    self.ps[bank][:, t * 128:(t + 1) * 128], xin[:, t, kc * 128:(kc + 1) * 128], self.ident),
                    reads=["xin", "small"], writes=[("ps", bank)])
            eng = "dve" if kc % 2 == 0 else "act"
            if eng == "dve":
                S.add("dve", lambda e, kc=kc, bank=bank: e.tensor_copy(self.xT[:, kc, 0:ntile * 128],
                                                                        self.ps[bank][:, 0:ntile * 128]),
                      reads=[("ps", bank)], writes=[("xT", kc)])
            else:
                S.add("act", lambda e, kc=kc, bank=bank: e.copy(self.xT[:, kc, 0:ntile * 128],
                                                                 self.ps[bank][:, 0:ntile * 128]),
                      reads=[("ps", bank)], writes=[("xT", kc)])

    def store_xT(self, out_rows, ntile=4):
        S = self.S
        xin = self.xin
        for t in range(ntile):
            for hf in range(2):
                bank = self.bank()
                for k4 in range(4):
                    kc = hf * 4 + k4
                    S.add("pe", lambda e, kc=kc, t=t, bank=bank, k4=k4: e.transpose(
                        self.ps[bank][:, k4 * 128:(k4 + 1) * 128], self.xT[:, kc, t * 128:(t + 1) * 128], self.ident),
                        reads=[("xT", kc), "small"], writes=[("ps", bank)])
                if hf == 0:
                    S.add("dve", lambda e, t=t, bank=bank, hf=hf: e.tensor_copy(xin[:, t, hf * 512:(hf + 1) * 512],
                                                                                 self.ps[bank][:]),
                          reads=[("ps", bank)], writes=["xin"])
                else:
                    S.add("act", lambda e, t=t, bank=bank, hf=hf: e.copy(xin[:, t, hf * 512:(hf + 1) * 512],
                                                                          self.ps[bank][:]),
                          reads=[("ps", bank)], writes=["xin"])
        S.add("sp", lambda e: e.dma_start(out=out_rows.rearrange("(t p) f -> p t f", p=128), in_=xin[:, 0:ntile, :]),
              reads=["xin"], writes=[("out", id(out_rows))], dma=True)

    def layernorm(self, l, which, N=G):
        S = self.S
        Gc, Sc, _, mkeys = self.modcols(l, which)
        for kc in range(KC):
            S.add("act", lambda e, kc=kc: e.activation(out=self.sq[:, kc, :N], in_=self.xT[:, kc, :N], func=AF.Square),
                  reads=[("xT", kc)], writes=[("sq", kc)])
        bank = self.bank()
        for kc in range(KC):
            S.add("pe", lambda e, kc=kc: e.matmul(self.ps[bank][:, :N], lhsT=self.ones_bf[:], rhs=self.sq[:, kc, :N],
                                                  start=(kc == 0), stop=(kc == KC - 1)),
                  reads=[("sq", kc), "ones"], writes=[("ps", bank)])
        S.add("act", lambda e: e.activation(out=self.std[:, :N], in_=self.ps[bank][:, :N], func=AF.Sqrt,
                                            bias=self.eps_t[:, 0:1], scale=1.0 / D),
              reads=[("ps", bank), "eps"], writes=["std"])
        S.add("dve", lambda e: e.reciprocal(out=self.rstd[:, :N], in_=self.std[:, :N]), reads=["std"], writes=["rstd"])
        for kc in range(KC):
            tb = kc % 2
            S.add("dve", lambda e, kc=kc, tb=tb: e.tensor_tensor(out=self.tmp[tb][:, :N], in0=self.xT[:, kc, :N],
                                                                 in1=self.rstd[:, :N], op=ALU.mult),
                  reads=[("xT", kc), "rstd"], writes=[("tmp", tb)])
            S.add("act", lambda e, kc=kc, tb=tb: e.activation(out=self.hT[:, kc, :N], in_=self.tmp[tb][:, :N],
                                                              func=AF.Identity, scale=Gc[:, kc:kc + 1],
                                                              bias=Sc[:, kc:kc + 1]),
                  reads=[("tmp", tb)] + mkeys, writes=[("hT", kc)])

    def proj_fm(self, w2d, c0, nchunks, consume, rhs_of=None, rhs_keys=None, N=G, kparts=KC, r0=0):
        S = self.S
        if rhs_of is None:
            rhs_of = lambda kc: self.hT[:, kc, :N]
            rhs_keys = lambda kc: ("hT", kc)
        oc = 0
        while oc < nchunks:
            nch = min(4, nchunks - oc)
            view, wk = self.slab(self.wcols(w2d, c0 + oc * 128, nch * 128, r0=r0, kparts=kparts), kparts, nch * 128)
            for o in range(nch):
                bank = self.bank()
                for kc in range(kparts):
                    S.add("pe", lambda e, kc=kc, o=o, bank=bank, view=view: e.matmul(
                        self.ps[bank][:, :N], lhsT=view[:, kc, o * 128:(o + 1) * 128], rhs=rhs_of(kc),
                        start=(kc == 0), stop=(kc == kparts - 1)),
                        reads=[wk, rhs_keys(kc)], writes=[("ps", bank)])
                consume(oc + o, bank)
            oc += nch

    def residual_add(self, l, which, fc, bank, N=G):
        _, _, gate, mkeys = self.modcols(l, which)
        self.S.add("dve", lambda e: e.scalar_tensor_tensor(out=self.xT[:, fc, :N], in0=self.ps[bank][:, :N],
                                                           scalar=gate[:, fc:fc + 1], in1=self.xT[:, fc, :N],
                                                           op0=ALU.mult, op1=ALU.add),
                   reads=[("ps", bank), ("xT", fc)] + mkeys, writes=[("xT", fc)])

    def ffn(self, l, w_gu, w_dn, N=G):
        S = self.S
        big = self.big
        j = 0
        while j < NJ:
            nch = min(4, NJ - j)
            gview, gk = self.slab(self.wcols(w_gu, j * 128, nch * 128), KC, nch * 128)
            uview, uk = self.slab(self.wcols(w_gu, DFF + j * 128, nch * 128), KC, nch * 128)
            for o in range(nch):
                jj = j + o
                bg = self.bank()
                bu = self.bank()
                for kc in range(KC):
                    S.add("pe", lambda e, kc=kc, o=o, bg=bg, gview=gview: e.matmul(
                        self.ps[bg][:, :N], lhsT=gview[:, kc, o * 128:(o + 1) * 128], rhs=self.hT[:, kc, :N],
                        start=(kc == 0), stop=(kc == KC - 1)), reads=[gk, ("hT", kc)], writes=[("ps", bg)])
                for kc in range(KC):
                    S.add("pe", lambda e, kc=kc, o=o, bu=bu, uview=uview: e.matmul(
                        self.ps[bu][:, :N], lhsT=uview[:, kc, o * 128:(o + 1) * 128], rhs=self.hT[:, kc, :N],
                        start=(kc == 0), stop=(kc == KC - 1)), reads=[uk, ("hT", kc)], writes=[("ps", bu)])
                tb = jj % 2
                S.add("act", lambda e, bg=bg, tb=tb: e.activation(out=self.tmp[tb][:, :N], in_=self.ps[bg][:, :N],
                                                                  func=AF.Silu),
                      reads=[("ps", bg)], writes=[("tmp", tb)])
                S.add("dve", lambda e, bu=bu, tb=tb, jj=jj: e.tensor_tensor(out=big[:, jj, :N], in0=self.ps[bu][:, :N],
                                                                             in1=self.tmp[tb][:, :N], op=ALU.mult),
                      reads=[("ps", bu), ("tmp", tb)], writes=[("big", jj)])
            j += nch
        for fp in range(4):
            v0, k0 = self.slab(self.wcols(w_dn, fp * 256, 256, r0=0, kparts=11), 11, 256)
            v1, k1 = self.slab(self.wcols(w_dn, fp * 256, 256, r0=11 * 128, kparts=11), 11, 256)
            banks = [self.bank(), self.bank()]
            for jj in range(NJ):
                view, wk = (v0, k0) if jj < 11 else (v1, k1)
                for f2 in range(2):
                    S.add("pe", lambda e, jj=jj, f2=f2, view=view, b=banks[f2]: e.matmul(
                        self.ps[b][:, :N], lhsT=view[:, jj % 11, f2 * 128:(f2 + 1) * 128], rhs=big[:, jj, :N],
                        start=(jj == 0), stop=(jj == NJ - 1)), reads=[wk, ("big", jj)], writes=[("ps", banks[f2])])
            for f2 in range(2):
                self.residual_add(l, 1, fp * 2 + f2, banks[f2], N)


def build_layer0():
    nc = bass.Bass("TRN2", target_bir_lowering=False)
    dt = nc.dram_tensor
    xseq = dt("xseq", [SEQ, D], F32, kind="ExternalInput").ap()
    small_ap = dt("small", [128, NSM], F32, kind="ExternalInput").ap()
    tabs_ap = dt("tabs", [128, 2048], F32, kind="ExternalInput").ap()
    pat_ap = dt("pat", [128, 8 * G], BF16, kind="ExternalInput").ap()
    sel_ap = dt("sel", [32, 32 * 128], BF16, kind="ExternalInput").ap()
    w_ada = dt("w_ada", [1, D, 6 * D], F32, kind="ExternalInput").ap()
    w_qkv = dt("w_qkv", [D, 3 * D], F32, kind="ExternalInput").ap()
    w_o = dt("w_o", [D, D], F32, kind="ExternalInput").ap()
    w_gu = dt("w_gu", [D, 2 * DFF], F32, kind="ExternalInput").ap()
    w_dn = dt("w_dn", [DFF, D], F32, kind="ExternalInput").ap()
    out = dt("out", [NOWN * G, D], F32, kind="ExternalOutput").ap()
    Kt = dt("Kt", [H, DH, SEQ], BF16).ap()
    Vs = dt("Vs", [SEQ, D], BF16).ap()
    Qt = dt("Qt", [H, DH, NOWN * G], BF16).ap()
    Ot = dt("Ot", [NOWN, DH, H, G], BF16).ap()

    C = Ctx(nc)
    S = C.S
    a = lambda n_, sh_, d_: nc.alloc_sbuf_tensor('sb_' + n_, sh_, d_)
    tabs = a("tabs", [128, 2048], F32)
    pat = a("pat", [128, 8, G], BF16)
    sel = a("sel", [32, 32, 128], BF16)
    kmean_f = a("kmean_f", [128, H, NBLK], F32)
    kmean_bf = a("kmean_bf", [128, H, NBLK], BF16)
    Kh = a("Kh", [128, SEQ], BF16)
    Vh = a("Vh", [128, 64, DH], BF16)
    Qh = a("Qh", [128, NOWN * G], BF16)
    pT = [a(f"pT{i}", [128, G], BF16) for i in range(2)]
    Rsb = a("Rsb", [128, 4, NBLK], F32)
    max8 = a("max8", [128, 4, 8], F32)
    mb = a("mb", [128, 4, NBLK], BF16)
    mbT = [a(f"mbT{i}", [32, G], BF16) for i in range(2)]
    recip = a("recip", [128, G], F32)
    oTsb = [a(f"oTsb{i}", [128, G], BF16) for i in range(2)]
    oTs = a("oTs", [128, H, G], BF16)
    big = C.big

    C.setup(small_ap)
    S.add("sp", lambda e: e.dma_start(out=tabs[:], in_=tabs_ap), writes=["tabs"], dma=True)
    S.add("sp", lambda e: e.dma_start(out=pat[:].rearrange("p a b -> p (a b)"), in_=pat_ap), writes=["pat"], dma=True)
    S.add("sp", lambda e: e.dma_start(out=sel[:].rearrange("p a b -> p (a b)"), in_=sel_ap), writes=["sel"], dma=True)
    S.add("dve", lambda e: e.memset(kmean_f[:], 0.0), writes=["kmean_f"])
    C.adaln(w_ada[0], 0)
    kg = C.small[:, SM_KG:SM_KG + 1]

    def qk_head(src_c0, hh, dst_slot, gain_ap, gain_keys, wview, wk, o, kmean_pos=None):
        bank = C.bank()
        for kc in range(KC):
            S.add("pe", lambda e, kc=kc: e.matmul(C.ps[bank][:], lhsT=wview[:, kc, o * 128:(o + 1) * 128],
                                                  rhs=C.hT[:, kc, :], start=(kc == 0), stop=(kc == KC - 1)),
                  reads=[wk, ("hT", kc)], writes=[("ps", bank)])
        sqk = C.sq[:, hh % 2, :]
        S.add("act", lambda e: e.activation(out=sqk, in_=C.ps[bank][:], func=AF.Square),
              reads=[("ps", bank)], writes=[("sq", hh % 2)])
        b2 = C.bank()
        S.add("pe", lambda e: e.matmul(C.ps[b2][:], lhsT=C.ones_bf[:], rhs=sqk, start=True, stop=True),
              reads=["ones", ("sq", hh % 2)], writes=[("ps", b2)])
        S.add("act", lambda e: e.activation(out=C.std[:], in_=C.ps[b2][:], func=AF.Sqrt, bias=C.eps_t[:, 0:1],
                                            scale=1.0 / DH), reads=[("ps", b2), "eps"], writes=["std"])
        S.add("dve", lambda e: e.reciprocal(out=C.rstd[:], in_=C.std[:]), reads=["std"], writes=["rstd"])
        if kmean_pos is None:
            S.add("dve", lambda e: e.scalar_tensor_tensor(out=big[:, dst_slot, :], in0=C.ps[bank][:], scalar=gain_ap,
                                                          in1=C.rstd[:], op0=ALU.mult, op1=ALU.mult),
                  reads=[("ps", bank), "rstd"] + gain_keys, writes=[("big", dst_slot)])
        else:
            for bb in range(2):
                pb = kmean_pos * 2 + bb
                S.add("dve", lambda e, bb=bb, pb=pb: e.scalar_tensor_tensor(
                    out=big[:, dst_slot, bb * 256:(bb + 1) * 256], in0=C.ps[bank][:, bb * 256:(bb + 1) * 256],
                    scalar=gain_ap, in1=C.rstd[:, bb * 256:(bb + 1) * 256], op0=ALU.mult, op1=ALU.mult,
                    accum_out=kmean_f[:, hh, pb:pb + 1]),
                    reads=[("ps", bank), "rstd", "kmean_f"] + gain_keys, writes=[("big", dst_slot), "kmean_f"])

    for p in range(NPOS):
        own = (p % 2 == 1)
        C.load_xT(xseq[p * G:(p + 1) * G, :])
        C.layernorm(0, 0)
        for s in range(2):
            wview, wk = C.slab(C.wcols(w_qkv, D + s * 512, 512), KC, 512)
            for o in range(4):
                qk_head(None, s * 4 + o, s * 4 + o, kg, ["small"], wview, wk, o, kmean_pos=p)
        S.add("sp", lambda e, p=p: e.dma_start(out=Kt.rearrange("h d t -> d h t")[:, :, p * G:(p + 1) * G],
                                               in_=big[:, 0:8, :]),
              reads=[("big", s_) for s_ in range(8)], writes=[("Kt", p)], dma=True)
        for hf in range(2):
            wview, wk = C.slab(C.wcols(w_qkv, 2 * D + hf * 512, 512), KC, 512)
            for t in range(4):
                bank = C.bank()
                for kc in range(KC):
                    S.add("pe", lambda e, kc=kc, t=t, bank=bank, wview=wview: e.matmul(
                        C.ps[bank][:], lhsT=C.hT[:, kc, t * 128:(t + 1) * 128], rhs=wview[:, kc, :],
                        start=(kc == 0), stop=(kc == KC - 1)), reads=[wk, ("hT", kc)], writes=[("ps", bank)])
                slot = 16 + 2 * t + hf
                if t % 2 == 0:
                    S.add("act", lambda e, bank=bank, slot=slot: e.copy(big[:, slot, :], C.ps[bank][:]),
                          reads=[("ps", bank)], writes=[("big", slot)])
                else:
                    S.add("dve", lambda e, bank=bank, slot=slot: e.tensor_copy(big[:, slot, :], C.ps[bank][:]),
                          reads=[("ps", bank)], writes=[("big", slot)])
        S.add("sp", lambda e, p=p: e.dma_start(
            out=Vs[p * G:(p + 1) * G, :].rearrange("(t k) (hf c) -> k t hf c", k=128, hf=2),
            in_=big[:, 16:24, :].rearrange("p (t hf) c -> p t hf c", hf=2)),
            reads=[("big", s_) for s_ in range(16, 24)], writes=[("Vs", p)], dma=True)
        if own:
            i = p // 2
            for s in range(2):
                wview, wk = C.slab(C.wcols(w_qkv, s * 512, 512), KC, 512)
                for o in range(4):
                    qk_head(None, s * 4 + o, 8 + s * 4 + o, C.qgs[:, 0:1], ["qgs"], wview, wk, o)
            S.add("sp", lambda e, i=i: e.dma_start(out=Qt.rearrange("h d t -> d h t")[:, :, i * G:(i + 1) * G],
                                                   in_=big[:, 8:16, :]),
                  reads=[("big", s_) for s_ in range(8, 16)], writes=[("Qt", i)], dma=True)
    S.add("dve", lambda e: e.tensor_copy(kmean_bf[:], kmean_f[:]), reads=["kmean_f"], writes=["kmean_bf"])

    rb = tabs[:, 0:1024].rearrange("p (i q n) -> p i q n", i=NOWN, q=4)
    npast = tabs[:, 1024:2048].rearrange("p (i q n) -> p i q n", i=NOWN, q=4)
    RB = 2

    def route(hh, i, par):
        for qt in range(4):
            S.add("pe", lambda e, qt=qt: e.matmul(C.ps[RB][:, qt * NBLK:(qt + 1) * NBLK],
                                                  lhsT=Qh[:, i * G + qt * 128:i * G + (qt + 1) * 128],
                                                  rhs=kmean_bf[:, hh, :], start=True, stop=True),
                  reads=[("Qh",), "kmean_bf"], writes=[("ps", RB)])
        S.add("dve", lambda e: e.tensor_tensor(out=Rsb[:], in0=C.ps[RB][:, 0:4 * NBLK].rearrange("p (q n) -> p q n", q=4),
                                               in1=rb[:, i, :, :], op=ALU.add),
              reads=[("ps", RB), "tabs"], writes=["Rsb"])
        for qt in range(4):
            S.add("dve", lambda e, qt=qt: e.max(out=max8[:, qt, :], in_=Rsb[:, qt, :]), reads=["Rsb"], writes=[("max8", qt)])
            S.add("dve", lambda e, qt=qt: e.scalar_tensor_tensor(out=mb[:, qt, :], in0=Rsb[:, qt, :],
                                                                 scalar=max8[:, qt, 2:3], in1=npast[:, i, qt, :],
                                                                 op0=ALU.is_lt, op1=ALU.mult),
                  reads=["Rsb", ("max8", qt), "tabs"], writes=[("mb", qt)])
        for qt in range(4):
            S.add("pe", lambda e, qt=qt: e.transpose(C.psb[0:32, qt * 128:(qt + 1) * 128], mb[:, qt, :], C.ident_bf[:]),
                  reads=[("mb", qt), "identbf"], writes=["psb"])
        S.add("act", lambda e: e.copy(mbT[par][:], C.psb[0:32, 0:G]), reads=["psb"], writes=[("mbT", par)])

    def attn(hh, i, par):
        nkt = 8 * i + 8
        ob = 3 + par
        db = 5 + par

        def qk(kt):
            sb = kt % 2
            win = kt >= 8 * i
            S.add("pe", lambda e: e.matmul(C.ps[sb][:], lhsT=Kh[:, kt * 128:(kt + 1) * 128], rhs=Qh[:, i * G:(i + 1) * G],
                                           start=True, stop=False), reads=[("Kh",), ("Qh",)], writes=[("ps", sb)])
            S.add("pe", lambda e: e.matmul(C.ps[sb][:], lhsT=sel[:, kt // 2, :], rhs=mbT[par][:], start=False,
                                           stop=(not win)), reads=["sel", ("mbT", par)], writes=[("ps", sb)])
            if win:
                S.add("pe", lambda e: e.matmul(C.ps[sb][:], lhsT=C.ident_bf[:], rhs=pat[:, kt - 8 * i, :], start=False,
                                               stop=True), reads=["identbf", "pat"], writes=[("ps", sb)])
            S.add("act", lambda e: e.activation(out=pT[sb][:], in_=C.ps[sb][:], func=AF.Exp),
                  reads=[("ps", sb)], writes=[("pT", sb)])

        def pv(kt):
            sb = kt % 2
            S.add("pe", lambda e: e.matmul(C.ps[ob][:], lhsT=Vh[:, kt, :], rhs=pT[sb][:], start=(kt == 0),
                                           stop=(kt == nkt - 1)), reads=[("Vh",), ("pT", sb)], writes=[("ps", ob)])
            S.add("pe", lambda e: e.matmul(C.ps[db][:], lhsT=C.ones_bf[:], rhs=pT[sb][:], start=(kt == 0),
                                           stop=(kt == nkt - 1)), reads=["ones", ("pT", sb)], writes=[("ps", db)])

        qk(0)
        for kt in range(nkt):
            if kt + 1 < nkt:
                qk(kt + 1)
            pv(kt)
        S.add("dve", lambda e: e.reciprocal(out=recip[:], in_=C.ps[db][:]), reads=[("ps", db)], writes=["recip"])
        S.add("dve", lambda e: e.tensor_tensor(out=oTsb[par][:], in0=C.ps[ob][:], in1=recip[:], op=ALU.mult),
              reads=[("ps", ob), "recip"], writes=[("oTsb", par)])
        S.add("sp", lambda e: e.dma_start(out=Ot[i, :, hh, :], in_=oTsb[par][:]), reads=[("oTsb", par)],
              writes=[("Ot", i, hh)], dma=True)

    cnt = 0
    for hh in range(H):
        S.add("sp", lambda e, hh=hh: e.dma_start(out=Kh[:], in_=Kt[hh]), reads=[("Kt", p) for p in range(NPOS)],
              writes=[("Kh",)], dma=True)
        S.add("sp", lambda e, hh=hh: e.dma_start(out=Qh[:], in_=Qt[hh]), reads=[("Qt", i) for i in range(NOWN)],
              writes=[("Qh",)], dma=True)
        for q4 in range(4):
            S.add("sp", lambda e, hh=hh, q4=q4: e.dma_start(
                out=Vh[:, q4 * 16:(q4 + 1) * 16, :],
                in_=Vs[q4 * 2048:(q4 + 1) * 2048, hh * DH:(hh + 1) * DH].rearrange("(kt k) c -> k kt c", k=128)),
                reads=[("Vs", p) for p in range(NPOS)], writes=[("Vh",)], dma=True)
        route(hh, 0, cnt % 2)
        for i in range(NOWN):
            par = cnt % 2
            if i + 1 < NOWN:
                route(hh, i + 1, (cnt + 1) % 2)
            attn(hh, i, par)
            cnt += 1

    for i in range(NOWN):
        p = 2 * i + 1
        C.load_xT(xseq[p * G:(p + 1) * G, :])
        S.add("sp", lambda e, i=i: e.dma_start(out=oTs[:], in_=Ot[i]), reads=[("Ot", i, hh) for hh in range(H)],
              writes=["oTs"], dma=True)
        C.proj_fm(w_o, 0, 8, lambda fc, bank: C.residual_add(0, 0, fc, bank),
                  rhs_of=lambda kc: oTs[:, kc, :], rhs_keys=lambda kc: "oTs")
        C.layernorm(0, 1)
        C.ffn(0, w_gu, w_dn)
        C.store_xT(out[i * G:(i + 1) * G, :])
    S.emit()
    return nc


def layer1_group(C, l, i, x_rows, uh, ubuf, cg, cv, w_in, w_out, w_gu, w_dn, out_rows):
    S = C.S
    big = C.big
    cw = C.small[:, SM_CONV:SM_CONV + 24].rearrange("p (j k) -> p j k", j=3)
    C.load_xT(x_rows)
    C.layernorm(l, 0)
    for fc in range(KC):
        S.add("dve", lambda e, fc=fc: e.tensor_copy(ubuf[:, fc, 0:2], uh[:, fc, 2 * i:2 * i + 2]),
              reads=[("uh", fc)], writes=[("ubuf", fc)])
    for s in range(2):
        views = []
        for part in range(3):
            views.append(C.slab(C.wcols(w_in, part * D + s * 512, 512), KC, 512))
        for o in range(4):
            fc = s * 4 + o
            banks = [C.bank(), C.bank(), C.bank()]
            for part in (1, 2, 0):
                view, wk = views[part]
                bk = banks[part]
                for kc in range(KC):
                    S.add("pe", lambda e, kc=kc, o=o, bk=bk, view=view: e.matmul(
                        C.ps[bk][:], lhsT=view[:, kc, o * 128:(o + 1) * 128], rhs=C.hT[:, kc, :],
                        start=(kc == 0), stop=(kc == KC - 1)), reads=[wk, ("hT", kc)], writes=[("ps", bk)])
            bb, bc, bu = banks
            S.add("act", lambda e, bc=bc: e.copy(cg[:], C.ps[bc][:]), reads=[("ps", bc)], writes=["cg"])
            S.add("dve", lambda e, bu=bu, fc=fc: e.tensor_tensor(out=ubuf[:, fc, 2:2 + G], in0=C.ps[bu][:], in1=cg[:],
                                                                 op=ALU.mult),
                  reads=[("ps", bu), "cg"], writes=[("ubuf", fc)])
            cb = fc % 2
            S.add("dve", lambda e, fc=fc, cb=cb: e.tensor_scalar(out=cv[cb][:], in0=ubuf[:, fc, 0:G],
                                                                  scalar1=cw[:, 0, fc:fc + 1], scalar2=None,
                                                                  op0=ALU.mult),
                  reads=[("ubuf", fc), "small"], writes=[("cv", cb)])
            S.add("dve", lambda e, fc=fc, cb=cb: e.scalar_tensor_tensor(out=cv[cb][:], in0=ubuf[:, fc, 1:1 + G],
                                                                         scalar=cw[:, 1, fc:fc + 1], in1=cv[cb][:],
                                                                         op0=ALU.mult, op1=ALU.add),
                  reads=[("ubuf", fc), "small", ("cv", cb)], writes=[("cv", cb)])
            S.add("dve", lambda e, fc=fc, cb=cb: e.scalar_tensor_tensor(out=cv[cb][:], in0=ubuf[:, fc, 2:2 + G],
                                                                         scalar=cw[:, 2, fc:fc + 1], in1=cv[cb][:],
                                                                         op0=ALU.mult, op1=ALU.add),
                  reads=[("ubuf", fc), "small", ("cv", cb)], writes=[("cv", cb)])
            S.add("dve", lambda e, fc=fc, cb=cb, bb=bb: e.tensor_tensor(out=big[:, fc, :], in0=C.ps[bb][:], in1=cv[cb][:],
                                                                         op=ALU.mult),
                  reads=[("ps", bb), ("cv", cb)], writes=[("big", fc)])
    C.proj_fm(w_out, 0, 8, lambda fc, bank: C.residual_add(l, 0, fc, bank),
              rhs_of=lambda kc: big[:, kc, :], rhs_keys=lambda kc: ("big", kc))
    C.layernorm(l, 1)
    C.ffn(l, w_gu, w_dn)
    C.store_xT(out_rows)


def layer1_halo_u(C, l, uh, ctmp, w_in):
    S = C.S
    hval = C.small[:, SM_HVALID:SM_HVALID + 16]
    C.layernorm(l, 0, N=16)
    for part in range(2):
        def consume(oc, bank, part=part):
            if part == 0:
                S.add("act", lambda e: e.copy(uh[:, oc, :], C.ps[bank][:, 0:16]), reads=[("ps", bank)],
                      writes=[("uh", oc)])
            else:
                S.add("dve", lambda e: e.tensor_tensor(out=ctmp[:], in0=C.ps[bank][:, 0:16], in1=uh[:, oc, :],
                                                       op=ALU.mult), reads=[("ps", bank), ("uh", oc)], writes=["ctmp"])
                S.add("dve", lambda e: e.tensor_tensor(out=uh[:, oc, :], in0=ctmp[:], in1=hval, op=ALU.mult),
                      reads=["ctmp", "small"], writes=[("uh", oc)])
        C.proj_fm(w_in, D + part * D, 8, consume, N=16)


def build_layer1():
    nc = bass.Bass("TRN2", target_bir_lowering=False)
    dt = nc.dram_tensor
    x1 = dt("x1", [NOWN * G, D], F32, kind="ExternalInput").ap()
    xhalo = dt("xhalo", [128, D], F32, kind="ExternalInput").ap()
    small_ap = dt("small", [128, NSM], F32, kind="ExternalInput").ap()
    w_ada = dt("w_ada", [1, D, 6 * D], F32, kind="ExternalInput").ap()
    w_in = dt("w_in", [D, 3 * D], F32, kind="ExternalInput").ap()
    w_out = dt("w_out", [D, D], F32, kind="ExternalInput").ap()
    w_gu = dt("w_gu", [D, 2 * DFF], F32, kind="ExternalInput").ap()
    w_dn = dt("w_dn", [DFF, D], F32, kind="ExternalInput").ap()
    out = dt("out", [NOWN * G, D], F32, kind="ExternalOutput").ap()

    C = Ctx(nc)
    a = lambda n_, sh_, d_: nc.alloc_sbuf_tensor('sb_' + n_, sh_, d_)
    uh = a("uh", [128, KC, 16], F32)
    ctmp = a("ctmp", [128, 16], F32)
    cg = a("cg", [128, G], F32)
    ubuf = a("ubuf", [128, KC, 2 + G], F32)
    cv = [a(f"cv{i}", [128, G], F32) for i in range(2)]
    C.setup(small_ap)
    C.adaln(w_ada[0], 0)
    C.load_xT(xhalo, ntile=1)
    layer1_halo_u(C, 0, uh, ctmp, w_in)
    for i in range(NOWN):
        layer1_group(C, 0, i, x1[i * G:(i + 1) * G, :], uh, ubuf, cg, cv, w_in, w_out, w_gu, w_dn,
                     out[i * G:(i + 1) * G, :])
    C.S.emit()
    return nc


def _g_of_pos(p, half):
    return p if half == 1 else (p ^ 1)


def _tables(half):
    rb = np.zeros((NOWN, 4, NBLK), np.float32)
    npst = np.zeros((NOWN, 4, NBLK), np.float32)
    for i in range(NOWN):
        for qt in range(4):
            nbq = 2 * (2 * i + half) + qt // 2
            for pb in range(NBLK):
                gb = 2 * _g_of_pos(pb // 2, half) + pb % 2
                if gb < nbq:
                    npst[i, qt, pb] = NEG
                else:
                    rb[i, qt, pb] = -1e30
    tabs = np.concatenate([rb.reshape(-1), npst.reshape(-1)])[None, :].repeat(128, 0).astype(np.float32)
    pat = np.zeros((128, 8, G), np.float32)
    k = np.arange(128)[:, None]
    q = np.arange(G)[None, :]
    qt = q // 128
    for ktw in range(8):
        if ktw < 4:
            pat[:, ktw, :] = 0.0 if half == 1 else NEG
        else:
            kt_ = ktw - 4
            kb = kt_ // 2
            qb = qt // 2
            kpos = (kt_ % 2) * 128 + k
            qpos = (qt % 2) * 128 + (q % 128)
            m = np.where(kb < qb, 0.0, np.where(kb > qb, NEG, np.where(kpos <= qpos, 0.0, NEG)))
            pat[:, ktw, :] = m
    sel = np.zeros((32, 32, 128), np.float32)
    for pb in range(32):
        sel[pb, pb, :] = 1.0
    import ml_dtypes
    bf = ml_dtypes.bfloat16
    return tabs, pat.reshape(128, 8 * G).astype(bf), sel.reshape(32, 32 * 128).astype(bf)


def _small(c_b, b_ada_ls, nmix_ls, nffn_ls, q_gain, k_gain, conv_w, half):
    sm = np.zeros((128, NSM), np.float32)
    sm[:, SM_C:SM_C + 8] = c_b.reshape(8, 128).T
    for l, ba in enumerate(b_ada_ls):
        sm[:, SM_BADA + 48 * l:SM_BADA + 48 * (l + 1)] = ba.reshape(48, 128).T
    for l, v in enumerate(nmix_ls):
        sm[:, SM_NMIX + 8 * l:SM_NMIX + 8 * (l + 1)] = v.reshape(8, 128).T
    for l, v in enumerate(nffn_ls):
        sm[:, SM_NFFN + 8 * l:SM_NFFN + 8 * (l + 1)] = v.reshape(8, 128).T
    sm[:, SM_QG] = q_gain
    sm[:, SM_KG] = k_gain
    sm[:, SM_CONV:SM_CONV + 24] = conv_w.reshape(3, 8, 128).transpose(2, 0, 1).reshape(128, 24)
    hv = np.ones(16, np.float32)
    if half == 0:
        hv[0:2] = 0.0
    sm[:, SM_HVALID:SM_HVALID + 16] = hv[None, :]
    sm[:, SM_IDENT:SM_IDENT + 128] = np.eye(128, dtype=np.float32)
    return sm


_NC_CACHE = {}


def _get(name, fn):
    if name not in _NC_CACHE:
        _NC_CACHE[name] = fn()
    return _NC_CACHE[name]


def kernel(x, c, w_ada, b_ada, norm_mix, norm_ffn, w_qkv, w_o, q_gain, k_gain, w_in, conv_w, w_out,
           w_gate_up, w_down):
    f = lambda a: np.ascontiguousarray(np.asarray(a, dtype=np.float32))
    x, c, w_ada, b_ada, norm_mix, norm_ffn = map(f, (x, c, w_ada, b_ada, norm_mix, norm_ffn))
    w_qkv, w_o, q_gain, k_gain, w_in, conv_w, w_out, w_gate_up, w_down = map(
        f, (w_qkv, w_o, q_gain, k_gain, w_in, conv_w, w_out, w_gate_up, w_down))
    B = x.shape[0]
    ncores = 8
    in_maps = []
    for core in range(ncores):
        b, half = core // 2, core % 2
        perm = [_g_of_pos(p, half) for p in range(NPOS)]
        xseq = np.ascontiguousarray(x[b].reshape(NPOS, G, D)[perm].reshape(SEQ, D))
        tabs, pat, sel = _tables(half)
        sm = _small(c[b], [b_ada[0]], [norm_mix[0]], [norm_ffn[0]], q_gain[0], k_gain[0], conv_w[0], half)
        in_maps.append(dict(xseq=xseq, small=sm, tabs=tabs, pat=pat, sel=sel, w_ada=w_ada[0:1], w_qkv=w_qkv[0],
                            w_o=w_o[0], w_gu=w_gate_up[0], w_dn=w_down[0]))
    ncA = _get("A", build_layer0)
    resA = run_bass_kernel_spmd(ncA, in_maps, core_ids=list(range(ncores)))
    x1 = np.zeros_like(x)
    for core in range(ncores):
        b, half = core // 2, core % 2
        o = np.asarray(resA.results[core]["out"]).reshape(NOWN, G, D)
        for i in range(NOWN):
            g = 2 * i + half
            x1[b, g * G:(g + 1) * G] = o[i]
    in_maps = []
    for core in range(ncores):
        b, half = core // 2, core % 2
        xo = np.ascontiguousarray(x1[b].reshape(NPOS, G, D)[[2 * i + half for i in range(NOWN)]].reshape(NOWN * G, D))
        xh = np.zeros((128, D), np.float32)
        for i in range(NOWN):
            g = 2 * i + half
            if g > 0:
                xh[2 * i:2 * i + 2] = x1[b, g * G - 2:g * G]
        sm = _small(c[b], [b_ada[1]], [norm_mix[1]], [norm_ffn[1]], q_gain[0], k_gain[0], conv_w[0], half)
        in_maps.append(dict(x1=xo, xhalo=xh, small=sm, w_ada=w_ada[1:2], w_in=w_in[0], w_out=w_out[0],
                            w_gu=w_gate_up[1], w_dn=w_down[1]))
    ncB = _get("B", build_layer1)
    resB = run_bass_kernel_spmd(ncB, in_maps, core_ids=list(range(ncores)))
    out = np.zeros_like(x)
    for core in range(ncores):
        b, half = core // 2, core % 2
        o = np.asarray(resB.results[core]["out"]).reshape(NOWN, G, D)
        for i in range(NOWN):
            g = 2 * i + half
            out[b, g * G:(g + 1) * G] = o[i]
    return out
```

```python
import contextlib
import numpy as np
import concourse.bass as bass
import concourse.mybir as mybir
from concourse.bass_utils import run_bass_kernel_spmd

F32 = mybir.dt.float32
BF16 = mybir.dt.bfloat16
ALU = mybir.AluOpType
AF = mybir.ActivationFunctionType

D = 1024
KC = 8
G = 512
H = 8
DH = 128
DFF = 2816
NJ = 22
NPOS = 16
NOWN = 8
NBLK = 32
SEQ = 8192
NEG = -30000.0
EPS = 1e-6
NW = 4
ENGS = ["pe", "act", "dve", "pool", "sp"]
INORDER = ("pe", "act", "dve")


class Sched:
    def __init__(self, nc, n_dma_sems=6):
        self.nc = nc
        self.ops = []
        self.last_write = {}
        self.readers = {}
        self.n_dma_sems = n_dma_sems

    def add(self, eng, fn, reads=(), writes=(), dma=False):
        oid = len(self.ops)
        deps = set()
        for b in reads:
            if b in self.last_write:
                deps.add(self.last_write[b])
        for b in writes:
            if b in self.last_write:
                deps.add(self.last_write[b])
            for r in self.readers.get(b, {}).values():
                deps.update(r)
        deps.discard(oid)
        self.ops.append(dict(id=oid, eng=eng, fn=fn, deps=deps, dma=dma, signal=False))
        for b in reads:
            rd = self.readers.setdefault(b, {})
            if dma or eng not in INORDER:
                rd.setdefault((eng, "dma"), []).append(oid)
            else:
                rd[eng] = [oid]
        for b in writes:
            self.last_write[b] = oid
            self.readers[b] = {}
        return oid

    def emit(self, final_wait_eng="sp"):
        nc = self.nc
        ops = self.ops

        def needs_sync(p, ceng):
            if p["dma"]:
                return True
            if p["eng"] != ceng:
                return True
            return p["eng"] in ("act", "dve", "pool")

        for op in ops:
            for d in op["deps"]:
                p = ops[d]
                if needs_sync(p, op["eng"]):
                    p["signal"] = True
            if op["dma"]:
                op["signal"] = True
        eng_count = {e: 0 for e in ENGS}
        dma_count = {}
        dma_rr = {e: 0 for e in ENGS}
        sem_keys = {}
        for op in ops:
            if not op["signal"]:
                continue
            if op["dma"]:
                k = dma_rr[op["eng"]] % self.n_dma_sems
                dma_rr[op["eng"]] += 1
                key = ("dma", op["eng"], k)
                prev = dma_count.get(key, 0)
                op["prev_on_sem"] = (key, prev)
                dma_count[key] = prev + 16
                op["sig"] = (key, prev + 16)
            else:
                key = ("eng", op["eng"])
                eng_count[op["eng"]] += 1
                op["sig"] = (key, eng_count[op["eng"]])
            sem_keys[key] = None
        with contextlib.ExitStack() as st:
            sems = {}
            for key in sem_keys:
                sems[key] = st.enter_context(nc.semaphore("s_" + "_".join(str(x) for x in key)))
            block = st.enter_context(nc.Block())
            per_eng = {e: [o for o in ops if o["eng"] == e] for e in ENGS}

            def body(ename):
                def run(eng):
                    waited = {}

                    def wait(key, val):
                        if waited.get(key, 0) >= val:
                            return
                        eng.wait_ge(sems[key], val)
                        waited[key] = val

                    for op in per_eng[ename]:
                        need = {}
                        for d in op["deps"]:
                            p = ops[d]
                            if not needs_sync(p, ename):
                                continue
                            key, val = p["sig"]
                            need[key] = max(need.get(key, 0), val)
                        if op["dma"]:
                            key, prev = op["prev_on_sem"]
                            if prev > 0:
                                need[key] = max(need.get(key, 0), prev)
                        for key, val in need.items():
                            wait(key, val)
                        ins = op["fn"](eng)
                        if op["signal"]:
                            key, val = op["sig"]
                            ins.then_inc(sems[key], 16 if op["dma"] else 1)
                    if ename == final_wait_eng:
                        for key, val in dma_count.items():
                            wait(key, val)
                        for e in ENGS:
                            if eng_count[e] > 0 and e != ename:
                                wait(("eng", e), eng_count[e])
                return run

            block.tensor(body("pe"))
            block.scalar(body("act"))
            block.vector(body("dve"))
            block.gpsimd(body("pool"))
            block.sync(body("sp"))


SM_C = 0
SM_BADA = 8
SM_NMIX = 104
SM_NFFN = 120
SM_QG = 136
SM_KG = 137
SM_CONV = 138
SM_HVALID = 162
SM_IDENT = 178
NSM = 306


class Ctx:
    def __init__(self, nc):
        self.nc = nc
        self.S = Sched(nc)
        self.wslot = 0
        self.bankrr = 0
        a = lambda n_, sh_, d_: nc.alloc_sbuf_tensor('sb_' + n_, sh_, d_)
        self.small = a("small", [128, NSM], F32)
        self.ident_bf = a("ident_bf", [128, 128], BF16)
        self.ones_bf = a("ones_bf", [128, 128], BF16)
        self.eps_t = a("eps_t", [128, 1], F32)
        self.wring = [a(f"wring{i}", [128, 4096], BF16) for i in range(NW)]
        self.xin = a("xin", [128, 4, D], F32)
        self.xT = a("xT", [128, KC, G], F32)
        self.sq = a("sq", [128, KC, G], BF16)
        self.hT = a("hT", [128, KC, G], BF16)
        self.tmp = [a(f"tmp{i}", [128, G], F32) for i in range(2)]
        self.std = a("std", [128, G], F32)
        self.rstd = a("rstd", [128, G], F32)
        self.big = a("big", [128, 24, G], BF16)
        self.mod = a("mod", [128, 2, 48], F32)
        self.modG = a("modG", [128, 2, 2, KC], F32)
        self.scbf = a("scbf", [128, KC], BF16)
        self.qgs = a("qgs", [128, 1], F32)
        self.ps = [nc.alloc_psum_tensor(f"ps{i}", [128, 512], F32) for i in range(7)]
        self.psb = nc.alloc_psum_tensor("psb", [128, 1024], BF16)

    @property
    def ident(self):
        return self.small[:, SM_IDENT:SM_IDENT + 128]

    def bank(self, lo=0, hi=7):
        b = lo + self.bankrr % (hi - lo)
        self.bankrr += 1
        return b

    def slab(self, src3d, kparts, ncols):
        slot = self.wslot % NW
        self.wslot += 1
        view = self.wring[slot][:, 0:kparts * ncols].rearrange("p (k n) -> p k n", k=kparts)
        self.S.add("pool", lambda e: e.dma_start(out=view, in_=src3d), writes=[("w", slot)], dma=True)
        return view, ("w", slot)

    def wcols(self, w2d, c0, ncols, r0=0, kparts=KC):
        return w2d[r0:r0 + kparts * 128, c0:c0 + ncols].rearrange("(k p) n -> p k n", p=128)

    def setup(self, small_ap):
        S = self.S
        S.add("sp", lambda e: e.dma_start(out=self.small[:], in_=small_ap), writes=["small"], dma=True)
        S.add("dve", lambda e: e.memset(self.ones_bf[:], 1.0), writes=["ones"])
        S.add("dve", lambda e: e.memset(self.eps_t[:], EPS), writes=["eps"])
        S.add("dve", lambda e: e.tensor_copy(self.ident_bf[:], self.ident), reads=["small"], writes=["identbf"])
        S.add("act", lambda e: e.activation(out=self.scbf[:], in_=self.small[:, SM_C:SM_C + 8], func=AF.Silu),
              reads=["small"], writes=["scbf"])
        S.add("act", lambda e: e.mul(self.qgs[:], self.small[:, SM_QG:SM_QG + 1], DH ** -0.5),
              reads=["small"], writes=["qgs"])

    def adaln(self, w_ada_l, l):
        S = self.S
        bank = self.bank()
        psk = ("ps", bank)
        for s in range(12):
            view, wk = self.slab(self.wcols(w_ada_l, s * 512, 512), KC, 512)
            for o in range(4):
                oc = s * 4 + o
                for kc in range(KC):
                    S.add("pe", lambda e, oc=oc, kc=kc, o=o, view=view: e.matmul(
                        self.ps[bank][:, oc:oc + 1], lhsT=view[:, kc, o * 128:(o + 1) * 128],
                        rhs=self.scbf[:, kc:kc + 1], start=(kc == 0), stop=(kc == KC - 1)),
                        reads=[wk, "scbf"], writes=[psk])
        mod = self.mod
        S.add("dve", lambda e: e.tensor_tensor(out=mod[:, l, :], in0=self.ps[bank][:, 0:48],
                                               in1=self.small[:, SM_BADA + 48 * l:SM_BADA + 48 * (l + 1)], op=ALU.add),
              reads=[psk, "small"], writes=[("mod", l)])
        S.add("dve", lambda e: e.scalar_tensor_tensor(out=self.modG[:, l, 0, :], in0=mod[:, l, 8:16], scalar=1.0,
                                                      in1=self.small[:, SM_NMIX + 8 * l:SM_NMIX + 8 * (l + 1)],
                                                      op0=ALU.add, op1=ALU.mult),
              reads=[("mod", l), "small"], writes=[("modG", l, 0)])
        S.add("dve", lambda e: e.scalar_tensor_tensor(out=self.modG[:, l, 1, :], in0=mod[:, l, 32:40], scalar=1.0,
                                                      in1=self.small[:, SM_NFFN + 8 * l:SM_NFFN + 8 * (l + 1)],
                                                      op0=ALU.add, op1=ALU.mult),
              reads=[("mod", l), "small"], writes=[("modG", l, 1)])

    def modcols(self, l, which):
        base = 0 if which == 0 else 24
        return (self.modG[:, l, which, :], self.mod[:, l, base:base + 8], self.mod[:, l, base + 16:base + 24],
                [("mod", l), ("modG", l, which)])

    def load_xT(self, x_rows, N=G, ntile=4):
        S = self.S
        xin = self.xin
        S.add("sp", lambda e: e.dma_start(out=xin[:, 0:ntile, :], in_=x_rows.rearrange("(t p) f -> p t f", p=128)),
              writes=["xin"], dma=True)
        for kc in range(KC):
            bank = self.bank()
            for t in range(ntile):
                S.add("pe", lambda e, kc=kc, t=t, bank=bank: e.transpose(
                    self.ps[bank][:, t * 128:(t + 1) * 128], xin[:, t, kc * 128:(kc + 1) * 128], self.ident),
                    reads=["xin", "small"], writes=[("ps", bank)])
            eng = "dve" if kc % 2 == 0 else "act"
            if eng == "dve":
                S.add("dve", lambda e, kc=kc, bank=bank: e.tensor_copy(self.xT[:, kc, 0:ntile * 128],
                                                                        self.ps[bank][:, 0:ntile * 128]),
                      reads=[("ps", bank)], writes=[("xT", kc)])
            else:
                S.add("act", lambda e, kc=kc, bank=bank: e.copy(self.xT[:, kc, 0:ntile * 128],
                                                                 self.ps[bank][:, 0:ntile * 128]),
                      reads=[("ps", bank)], writes=[("xT", kc)])

    def store_xT(self, out_rows, ntile=4):
        S = self.S
        xin = self.xin
        for t in range(ntile):
            for hf in range(2):
                bank = self.bank()
                for k4 in range(4):
                    kc = hf * 4 + k4
                    S.add("pe", lambda e, kc=kc, t=t, bank=bank, k4=k4: e.transpose(
                        self.ps[bank][:, k4 * 128:(k4 + 1) * 128], self.xT[:, kc, t * 128:(t + 1) * 128], self.ident),
                        reads=[("xT", kc), "small"], writes=[("ps", bank)])
                if hf == 0:
                    S.add("dve", lambda e, t=t, bank=bank, hf=hf: e.tensor_copy(xin[:, t, hf * 512:(hf + 1) * 512],
                                                                                 self.ps[bank][:]),
                          reads=[("ps", bank)], writes=["xin"])
                else:
                    S.add("act", lambda e, t=t, bank=bank, hf=hf: e.copy(xin[:, t, hf * 512:(hf + 1) * 512],
                                                                          self.ps[bank][:]),
                          reads=[("ps", bank)], writes=["xin"])
        S.add("sp", lambda e: e.dma_start(out=out_rows.rearrange("(t p) f -> p t f", p=128), in_=xin[:, 0:ntile, :]),
              reads=["xin"], writes=[("out", id(out_rows))], dma=True)

    def layernorm(self, l, which, N=G):
        S = self.S
        Gc, Sc, _, mkeys = self.modcols(l, which)
        for kc in range(KC):
            S.add("act", lambda e, kc=kc: e.activation(out=self.sq[:, kc, :N], in_=self.xT[:, kc, :N], func=AF.Square),
                  reads=[("xT", kc)], writes=[("sq", kc)])
        bank = self.bank()
        for kc in range(KC):
            S.add("pe", lambda e, kc=kc: e.matmul(self.ps[bank][:, :N], lhsT=self.ones_bf[:], rhs=self.sq[:, kc, :N],
                                                  start=(kc == 0), stop=(kc == KC - 1)),
                  reads=[("sq", kc), "ones"], writes=[("ps", bank)])
        S.add("act", lambda e: e.activation(out=self.std[:, :N], in_=self.ps[bank][:, :N], func=AF.Sqrt,
                                            bias=self.eps_t[:, 0:1], scale=1.0 / D),
              reads=[("ps", bank), "eps"], writes=["std"])
        S.add("dve", lambda e: e.reciprocal(out=self.rstd[:, :N], in_=self.std[:, :N]), reads=["std"], writes=["rstd"])
        for kc in range(KC):
            tb = kc % 2
            S.add("dve", lambda e, kc=kc, tb=tb: e.tensor_tensor(out=self.tmp[tb][:, :N], in0=self.xT[:, kc, :N],
                                                                 in1=self.rstd[:, :N], op=ALU.mult),
                  reads=[("xT", kc), "rstd"], writes=[("tmp", tb)])
            S.add("act", lambda e, kc=kc, tb=tb: e.activation(out=self.hT[:, kc, :N], in_=self.tmp[tb][:, :N],
                                                              func=AF.Identity, scale=Gc[:, kc:kc + 1],
                                                              bias=Sc[:, kc:kc + 1]),
                  reads=[("tmp", tb)] + mkeys, writes=[("hT", kc)])

    def proj_fm(self, w2d, c0, nchunks, consume, rhs_of=None, rhs_keys=None, N=G, kparts=KC, r0=0):
        S = self.S
        if rhs_of is None:
            rhs_of = lambda kc: self.hT[:, kc, :N]
            rhs_keys = lambda kc: ("hT", kc)
        oc = 0
        while oc < nchunks:
            nch = min(4, nchunks - oc)
            view, wk = self.slab(self.wcols(w2d, c0 + oc * 128, nch * 128, r0=r0, kparts=kparts), kparts, nch * 128)
            for o in range(nch):
                bank = self.bank()
                for kc in range(kparts):
                    S.add("pe", lambda e, kc=kc, o=o, bank=bank, view=view: e.matmul(
                        self.ps[bank][:, :N], lhsT=view[:, kc, o * 128:(o + 1) * 128], rhs=rhs_of(kc),
                        start=(kc == 0), stop=(kc == kparts - 1)),
                        reads=[wk, rhs_keys(kc)], writes=[("ps", bank)])
                consume(oc + o, bank)
            oc += nch

    def residual_add(self, l, which, fc, bank, N=G):
        _, _, gate, mkeys = self.modcols(l, which)
        self.S.add("dve", lambda e: e.scalar_tensor_tensor(out=self.xT[:, fc, :N], in0=self.ps[bank][:, :N],
                                                           scalar=gate[:, fc:fc + 1], in1=self.xT[:, fc, :N],
                                                           op0=ALU.mult, op1=ALU.add),
                   reads=[("ps", bank), ("xT", fc)] + mkeys, writes=[("xT", fc)])

    def ffn(self, l, w_gu, w_dn, N=G):
        S = self.S
        big = self.big
        j = 0
        while j < NJ:
            nch = min(4, NJ - j)
            gview, gk = self.slab(self.wcols(w_gu, j * 128, nch * 128), KC, nch * 128)
            uview, uk = self.slab(self.wcols(w_gu, DFF + j * 128, nch * 128), KC, nch * 128)
            for o in range(nch):
                jj = j + o
                bg = self.bank()
                bu = self.bank()
                for kc in range(KC):
                    S.add("pe", lambda e, kc=kc, o=o, bg=bg, gview=gview: e.matmul(
                        self.ps[bg][:, :N], lhsT=gview[:, kc, o * 128:(o + 1) * 128], rhs=self.hT[:, kc, :N],
                        start=(kc == 0), stop=(kc == KC - 1)), reads=[gk, ("hT", kc)], writes=[("ps", bg)])
                for kc in range(KC):
                    S.add("pe", lambda e, kc=kc, o=o, bu=bu, uview=uview: e.matmul(
                        self.ps[bu][:, :N], lhsT=uview[:, kc, o * 128:(o + 1) * 128], rhs=self.hT[:, kc, :N],
                        start=(kc == 0), stop=(kc == KC - 1)), reads=[uk, ("hT", kc)], writes=[("ps", bu)])
                tb = jj % 2
                S.add("act", lambda e, bg=bg, tb=tb: e.activation(out=self.tmp[tb][:, :N], in_=self.ps[bg][:, :N],
                                                                  func=AF.Silu),
                      reads=[("ps", bg)], writes=[("tmp", tb)])
                S.add("dve", lambda e, bu=bu, tb=tb, jj=jj: e.tensor_tensor(out=big[:, jj, :N], in0=self.ps[bu][:, :N],
                                                                             in1=self.tmp[tb][:, :N], op=ALU.mult),
                      reads=[("ps", bu), ("tmp", tb)], writes=[("big", jj)])
            j += nch
        for fp in range(4):
            v0, k0 = self.slab(self.wcols(w_dn, fp * 256, 256, r0=0, kparts=11), 11, 256)
            v1, k1 = self.slab(self.wcols(w_dn, fp * 256, 256, r0=11 * 128, kparts=11), 11, 256)
            banks = [self.bank(), self.bank()]
            for jj in range(NJ):
                view, wk = (v0, k0) if jj < 11 else (v1, k1)
                for f2 in range(2):
                    S.add("pe", lambda e, jj=jj, f2=f2, view=view, b=banks[f2]: e.matmul(
                        self.ps[b][:, :N], lhsT=view[:, jj % 11, f2 * 128:(f2 + 1) * 128], rhs=big[:, jj, :N],
                        start=(jj == 0), stop=(jj == NJ - 1)), reads=[wk, ("big", jj)], writes=[("ps", banks[f2])])
            for f2 in range(2):
                self.residual_add(l, 1, fp * 2 + f2, banks[f2], N)


def build_fused():
    nc = bass.Bass("TRN2", target_bir_lowering=False)
    dt = nc.dram_tensor
    xseq = dt("xseq", [SEQ, D], F32, kind="ExternalInput").ap()
    xhalo = dt("xhalo", [128, D], F32, kind="ExternalInput").ap()
    small_ap = dt("small", [128, NSM], F32, kind="ExternalInput").ap()
    tabs_ap = dt("tabs", [128, 2048 + 64], F32, kind="ExternalInput").ap()
    pat_ap = dt("pat", [128, 8 * G + 64 * 16], BF16, kind="ExternalInput").ap()
    sel_ap = dt("sel", [32, 32 * 128], BF16, kind="ExternalInput").ap()
    w_ada = dt("w_ada", [2, D, 6 * D], F32, kind="ExternalInput").ap()
    w_qkv = dt("w_qkv", [D, 3 * D], F32, kind="ExternalInput").ap()
    w_o = dt("w_o", [D, D], F32, kind="ExternalInput").ap()
    w_in = dt("w_in", [D, 3 * D], F32, kind="ExternalInput").ap()
    w_out = dt("w_out", [D, D], F32, kind="ExternalInput").ap()
    w_gu = dt("w_gu", [2, D, 2 * DFF], F32, kind="ExternalInput").ap()
    w_dn = dt("w_dn", [2, DFF, D], F32, kind="ExternalInput").ap()
    out = dt("out", [NOWN * G, D], F32, kind="ExternalOutput").ap()
    Kt = dt("Kt", [H, DH, SEQ], BF16).ap()
    Vs = dt("Vs", [SEQ, D], BF16).ap()
    Qt = dt("Qt", [H, DH, NOWN * G], BF16).ap()
    Ot = dt("Ot", [NOWN, DH, H, G], BF16).ap()

    C = Ctx(nc)
    S = C.S
    a = lambda n_, sh_, d_: nc.alloc_sbuf_tensor('sb_' + n_, sh_, d_)
    tabs = a("tabs", [128, 2048 + 64], F32)
    pat = a("pat", [128, 8 * G + 64 * 16], BF16)
    sel = a("sel", [32, 32, 128], BF16)
    kmean_f = a("kmean_f", [128, H, NBLK], F32)
    kmean_bf = a("kmean_bf", [128, H, NBLK], BF16)
    Kh = a("Kh", [128, SEQ], BF16)
    Vh = a("Vh", [128, 64, DH], BF16)
    Qh = a("Qh", [128, NOWN * G], BF16)
    QhaloT = a("QhaloT", [128, H, 16], BF16)
    oTh = a("oTh", [128, H, 16], BF16)
    pT = [a(f"pT{i}", [128, G], BF16) for i in range(2)]
    Rsb = a("Rsb", [128, 4, NBLK], F32)
    max8 = a("max8", [128, 4, 8], F32)
    mb = a("mb", [128, 4, NBLK], BF16)
    mbT = [a(f"mbT{i}", [32, G], BF16) for i in range(2)]
    oTsb = [a(f"oTsb{i}", [128, G], BF16) for i in range(2)]
    uh = a("uh", [128, KC, 16], F32)
    ctmp = a("ctmp", [128, 16], F32)
    cg = a("cg", [128, G], F32)
    ubuf = a("ubuf", [128, 2, 2 + G], F32)
    cv = [a(f"cv{i}", [128, G], F32) for i in range(2)]
    big = C.big
    recip = C.std

    C.setup(small_ap)
    S.add("sp", lambda e: e.dma_start(out=tabs[:], in_=tabs_ap), writes=["tabs"], dma=True)
    S.add("sp", lambda e: e.dma_start(out=pat[:], in_=pat_ap), writes=["pat"], dma=True)
    S.add("sp", lambda e: e.dma_start(out=sel[:].rearrange("p a b -> p (a b)"), in_=sel_ap), writes=["sel"], dma=True)
    S.add("dve", lambda e: e.memset(kmean_f[:], 0.0), writes=["kmean_f"])
    C.adaln(w_ada[0], 0)
    C.adaln(w_ada[1], 1)
    kg = C.small[:, SM_KG:SM_KG + 1]

    def qk_head(hh, dst_ap, dst_key, gain_ap, gain_keys, wview, wk, o, kmean_pos=None, N=G):
        bank = C.bank()
        for kc in range(KC):
            S.add("pe", lambda e, kc=kc: e.matmul(C.ps[bank][:, :N], lhsT=wview[:, kc, o * 128:(o + 1) * 128],
                                                  rhs=C.hT[:, kc, :N], start=(kc == 0), stop=(kc == KC - 1)),
                  reads=[wk, ("hT", kc)], writes=[("ps", bank)])
        sqk = C.sq[:, hh % 2, :N]
        S.add("act", lambda e: e.activation(out=sqk, in_=C.ps[bank][:, :N], func=AF.Square),
              reads=[("ps", bank)], writes=[("sq", hh % 2)])
        b2 = C.bank()
        S.add("pe", lambda e: e.matmul(C.ps[b2][:, :N], lhsT=C.ones_bf[:], rhs=sqk, start=True, stop=True),
              reads=["ones", ("sq", hh % 2)], writes=[("ps", b2)])
        S.add("act", lambda e: e.activation(out=C.std[:, :N], in_=C.ps[b2][:, :N], func=AF.Sqrt, bias=C.eps_t[:, 0:1],
                                            scale=1.0 / DH), reads=[("ps", b2), "eps"], writes=["std"])
        S.add("dve", lambda e: e.reciprocal(out=C.rstd[:, :N], in_=C.std[:, :N]), reads=["std"], writes=["rstd"])
        if kmean_pos is None:
            S.add("dve", lambda e: e.scalar_tensor_tensor(out=dst_ap, in0=C.ps[bank][:, :N], scalar=gain_ap,
                                                          in1=C.rstd[:, :N], op0=ALU.mult, op1=ALU.mult),
                  reads=[("ps", bank), "rstd"] + gain_keys, writes=[dst_key])
        else:
            for bb in range(2):
                pb = kmean_pos * 2 + bb
                S.add("dve", lambda e, bb=bb, pb=pb: e.scalar_tensor_tensor(
                    out=dst_ap[:, bb * 256:(bb + 1) * 256], in0=C.ps[bank][:, bb * 256:(bb + 1) * 256],
                    scalar=gain_ap, in1=C.rstd[:, bb * 256:(bb + 1) * 256], op0=ALU.mult, op1=ALU.mult,
                    accum_out=kmean_f[:, hh, pb:pb + 1]),
                    reads=[("ps", bank), "rstd", "kmean_f"] + gain_keys, writes=[dst_key, "kmean_f"])

    for p in range(NPOS):
        own = (p % 2 == 1)
        C.load_xT(xseq[p * G:(p + 1) * G, :])
        C.layernorm(0, 0)
        for s in range(2):
            wview, wk = C.slab(C.wcols(w_qkv, D + s * 512, 512), KC, 512)
            for o in range(4):
                hh = s * 4 + o
                qk_head(hh, big[:, hh, :], ("big", hh), kg, ["small"], wview, wk, o, kmean_pos=p)
        S.add("sp", lambda e, p=p: e.dma_start(out=Kt.rearrange("h d t -> d h t")[:, :, p * G:(p + 1) * G],
                                               in_=big[:, 0:8, :]),
              reads=[("big", s_) for s_ in range(8)], writes=[("Kt", p)], dma=True)
        for hf in range(2):
            wview, wk = C.slab(C.wcols(w_qkv, 2 * D + hf * 512, 512), KC, 512)
            for t in range(4):
                bank = C.bank()
                for kc in range(KC):
                    S.add("pe", lambda e, kc=kc, t=t, bank=bank, wview=wview: e.matmul(
                        C.ps[bank][:], lhsT=C.hT[:, kc, t * 128:(t + 1) * 128], rhs=wview[:, kc, :],
                        start=(kc == 0), stop=(kc == KC - 1)), reads=[wk, ("hT", kc)], writes=[("ps", bank)])
                slot = 16 + 2 * t + hf
                if t % 2 == 0:
                    S.add("act", lambda e, bank=bank, slot=slot: e.copy(big[:, slot, :], C.ps[bank][:]),
                          reads=[("ps", bank)], writes=[("big", slot)])
                else:
                    S.add("dve", lambda e, bank=bank, slot=slot: e.tensor_copy(big[:, slot, :], C.ps[bank][:]),
                          reads=[("ps", bank)], writes=[("big", slot)])
        S.add("sp", lambda e, p=p: e.dma_start(
            out=Vs[p * G:(p + 1) * G, :].rearrange("(t k) (hf c) -> k t hf c", k=128, hf=2),
            in_=big[:, 16:24, :].rearrange("p (t hf) c -> p t hf c", hf=2)),
            reads=[("big", s_) for s_ in range(16, 24)], writes=[("Vs", p)], dma=True)
        if own:
            i = p // 2
            for s in range(2):
                wview, wk = C.slab(C.wcols(w_qkv, s * 512, 512), KC, 512)
                for o in range(4):
                    hh = s * 4 + o
                    qk_head(hh, big[:, 8 + hh, :], ("big", 8 + hh), C.qgs[:, 0:1], ["qgs"], wview, wk, o)
            S.add("sp", lambda e, i=i: e.dma_start(out=Qt.rearrange("h d t -> d h t")[:, :, i * G:(i + 1) * G],
                                                   in_=big[:, 8:16, :]),
                  reads=[("big", s_) for s_ in range(8, 16)], writes=[("Qt", i)], dma=True)
    C.load_xT(xhalo, ntile=1)
    C.layernorm(0, 0, N=16)
    for s in range(2):
        wview, wk = C.slab(C.wcols(w_qkv, s * 512, 512), KC, 512)
        for o in range(4):
            hh = s * 4 + o
            qk_head(hh, QhaloT[:, hh, :], ("QhaloT", hh), C.qgs[:, 0:1], ["qgs"], wview, wk, o, N=16)
    S.add("dve", lambda e: e.tensor_copy(kmean_bf[:], kmean_f[:]), reads=["kmean_f"], writes=["kmean_bf"])

    rb = tabs[:, 0:1024].rearrange("p (i q n) -> p i q n", i=NOWN, q=4)
    npast = tabs[:, 1024:2048].rearrange("p (i q n) -> p i q n", i=NOWN, q=4)
    rb_h = tabs[:, 2048:2080]
    np_h = tabs[:, 2080:2112]
    patg = pat[:, 0:8 * G].rearrange("p (a b) -> p a b", a=8)
    hpat = pat[:, 8 * G:8 * G + 1024].rearrange("p (a b) -> p a b", a=64)
    RB = 2

    def desc_group(hh, i):
        return dict(N=G, nq=4, qrows=128, q_ap=Qh[:, i * G:(i + 1) * G], q_keys=[("Qh",)],
                    rb=rb[:, i, :, :], npst=npast[:, i, :, :], nkt=8 * i + 8,
                    pat_of=(lambda kt: patg[:, kt - 8 * i, :] if kt >= 8 * i else None), i=i, hh=hh)

    def desc_halo(hh):
        return dict(N=16, nq=1, qrows=16, q_ap=QhaloT[:, hh, :], q_keys=[("QhaloT", hh)],
                    rb=rb_h[0:16, :].rearrange("p (q n) -> p q n", q=1), npst=np_h[0:16, :].rearrange("p (q n) -> p q n", q=1),
                    nkt=64, pat_of=(lambda kt: hpat[:, kt, :]), i=None, hh=hh)

    def route(dsc, par):
        hh, N, nq, qr = dsc["hh"], dsc["N"], dsc["nq"], dsc["qrows"]
        qw = min(N, 128)
        for qt in range(nq):
            S.add("pe", lambda e, qt=qt: e.matmul(C.ps[RB][0:qr, qt * NBLK:(qt + 1) * NBLK],
                                                  lhsT=dsc["q_ap"][:, qt * qw:(qt + 1) * qw],
                                                  rhs=kmean_bf[:, hh, :], start=True, stop=True),
                  reads=dsc["q_keys"] + ["kmean_bf"], writes=[("ps", RB)])
        S.add("dve", lambda e: e.tensor_tensor(out=Rsb[0:qr, 0:nq, :],
                                               in0=C.ps[RB][0:qr, 0:nq * NBLK].rearrange("p (q n) -> p q n", q=nq),
                                               in1=dsc["rb"], op=ALU.add),
              reads=[("ps", RB), "tabs"], writes=["Rsb"])
        for qt in range(nq):
            S.add("dve", lambda e, qt=qt: e.max(out=max8[0:qr, qt, :], in_=Rsb[0:qr, qt, :]), reads=["Rsb"],
                  writes=[("max8", qt)])
            S.add("dve", lambda e, qt=qt: e.scalar_tensor_tensor(out=mb[0:qr, qt, :], in0=Rsb[0:qr, qt, :],
                                                                 scalar=max8[0:qr, qt, 2:3], in1=dsc["npst"][:, qt, :],
                                                                 op0=ALU.is_lt, op1=ALU.mult),
                  reads=["Rsb", ("max8", qt), "tabs"], writes=[("mb", qt)])
        for qt in range(nq):
            S.add("pe", lambda e, qt=qt: e.transpose(C.psb[0:32, qt * 128:qt * 128 + qr], mb[0:qr, qt, :],
                                                     C.ident_bf[0:qr, 0:qr]),
                  reads=[("mb", qt), "identbf"], writes=["psb"])
        S.add("act", lambda e: e.copy(mbT[par][:, 0:N], C.psb[0:32, 0:N]), reads=["psb"], writes=[("mbT", par)])

    def attn(dsc, par):
        hh, N, nkt = dsc["hh"], dsc["N"], dsc["nkt"]
        ob = 3 + par
        db = 5 + par

        def qk(kt):
            sb = kt % 2
            pt = dsc["pat_of"](kt)
            S.add("pe", lambda e: e.matmul(C.ps[sb][:, :N], lhsT=Kh[:, kt * 128:(kt + 1) * 128], rhs=dsc["q_ap"],
                                           start=True, stop=False), reads=[("Kh",)] + dsc["q_keys"], writes=[("ps", sb)])
            S.add("pe", lambda e: e.matmul(C.ps[sb][:, :N], lhsT=sel[:, kt // 2, :], rhs=mbT[par][:, 0:N], start=False,
                                           stop=(pt is None)), reads=["sel", ("mbT", par)], writes=[("ps", sb)])
            if pt is not None:
                S.add("pe", lambda e: e.matmul(C.ps[sb][:, :N], lhsT=C.ident_bf[:], rhs=pt, start=False, stop=True),
                      reads=["identbf", "pat"], writes=[("ps", sb)])
            S.add("act", lambda e: e.activation(out=pT[sb][:, :N], in_=C.ps[sb][:, :N], func=AF.Exp),
                  reads=[("ps", sb)], writes=[("pT", sb)])

        def pv(kt):
            sb = kt % 2
            S.add("pe", lambda e: e.matmul(C.ps[ob][:, :N], lhsT=Vh[:, kt, :], rhs=pT[sb][:, :N], start=(kt == 0),
                                           stop=(kt == nkt - 1)), reads=[("Vh",), ("pT", sb)], writes=[("ps", ob)])
            S.add("pe", lambda e: e.matmul(C.ps[db][:, :N], lhsT=C.ones_bf[:], rhs=pT[sb][:, :N], start=(kt == 0),
                                           stop=(kt == nkt - 1)), reads=["ones", ("pT", sb)], writes=[("ps", db)])

        qk(0)
        for kt in range(nkt):
            if kt + 1 < nkt:
                qk(kt + 1)
            pv(kt)
        S.add("dve", lambda e: e.reciprocal(out=recip[:, :N], in_=C.ps[db][:, :N]), reads=[("ps", db)], writes=["std"])
        if dsc["i"] is None:
            S.add("dve", lambda e: e.tensor_tensor(out=oTh[:, hh, :], in0=C.ps[ob][:, :N], in1=recip[:, :N], op=ALU.mult),
                  reads=[("ps", ob), "std"], writes=[("oTh", hh)])
        else:
            i = dsc["i"]
            S.add("dve", lambda e: e.tensor_tensor(out=oTsb[par][:], in0=C.ps[ob][:], in1=recip[:], op=ALU.mult),
                  reads=[("ps", ob), "std"], writes=[("oTsb", par)])
            S.add("sp", lambda e: e.dma_start(out=Ot[i, :, hh, :], in_=oTsb[par][:]), reads=[("oTsb", par)],
                  writes=[("Ot", i, hh)], dma=True)

    cnt = 0
    for hh in range(H):
        S.add("sp", lambda e, hh=hh: e.dma_start(out=Kh[:], in_=Kt[hh]), reads=[("Kt", p) for p in range(NPOS)],
              writes=[("Kh",)], dma=True)
        S.add("sp", lambda e, hh=hh: e.dma_start(out=Qh[:], in_=Qt[hh]), reads=[("Qt", i) for i in range(NOWN)],
              writes=[("Qh",)], dma=True)
        for q4 in range(4):
            S.add("sp", lambda e, hh=hh, q4=q4: e.dma_start(
                out=Vh[:, q4 * 16:(q4 + 1) * 16, :],
                in_=Vs[q4 * 2048:(q4 + 1) * 2048, hh * DH:(hh + 1) * DH].rearrange("(kt k) c -> k kt c", k=128)),
                reads=[("Vs", p) for p in range(NPOS)], writes=[("Vh",)], dma=True)
        seq = [desc_halo(hh)] + [desc_group(hh, i) for i in range(NOWN)]
        route(seq[0], cnt % 2)
        for n_, dsc in enumerate(seq):
            par = cnt % 2
            if n_ + 1 < len(seq):
                route(seq[n_ + 1], (cnt + 1) % 2)
            attn(dsc, par)
            cnt += 1

    C.load_xT(xhalo, ntile=1)
    C.proj_fm(w_o, 0, 8, lambda fc, bank: C.residual_add(0, 0, fc, bank, N=16),
              rhs_of=lambda kc: oTh[:, kc, :], rhs_keys=lambda kc: ("oTh", kc), N=16)
    C.layernorm(0, 1, N=16)
    C.ffn(0, w_gu[0], w_dn[0], N=16)
    layer1_halo_u(C, 1, uh, ctmp, w_in)
    oTs = Qh[:, :].rearrange("p (h t) -> p h t", h=H)
    for i in range(NOWN):
        p = 2 * i + 1
        C.load_xT(xseq[p * G:(p + 1) * G, :])
        S.add("sp", lambda e, i=i: e.dma_start(out=oTs, in_=Ot[i]), reads=[("Ot", i, hh) for hh in range(H)],
              writes=[("Qh",)], dma=True)
        C.proj_fm(w_o, 0, 8, lambda fc, bank: C.residual_add(0, 0, fc, bank),
                  rhs_of=lambda kc: oTs[:, kc, :], rhs_keys=lambda kc: ("Qh",))
        C.layernorm(0, 1)
        C.ffn(0, w_gu[0], w_dn[0])
        layer1_group(C, 1, i, None, uh, ubuf, cg, cv, w_in, w_out, w_gu[1], w_dn[1], out[i * G:(i + 1) * G, :])
    S.emit()
    return nc


def layer1_group(C, l, i, x_rows, uh, ubuf, cg, cv, w_in, w_out, w_gu, w_dn, out_rows):
    S = C.S
    big = C.big
    cw = C.small[:, SM_CONV:SM_CONV + 24].rearrange("p (j k) -> p j k", j=3)
    if x_rows is not None:
        C.load_xT(x_rows)
    C.layernorm(l, 0)
    for fc in range(0):
        S.add("dve", lambda e, fc=fc: e.tensor_copy(ubuf[:, fc, 0:2], uh[:, fc, 2 * i:2 * i + 2]),
              reads=[("uh", fc)], writes=[("ubuf", ub)])
    for s in range(2):
        views = []
        for part in range(3):
            views.append(C.slab(C.wcols(w_in, part * D + s * 512, 512), KC, 512))
        for o in range(4):
            fc = s * 4 + o
            banks = [C.bank(), C.bank(), C.bank()]
            for part in (1, 2, 0):
                view, wk = views[part]
                bk = banks[part]
                for kc in range(KC):
                    S.add("pe", lambda e, kc=kc, o=o, bk=bk, view=view: e.matmul(
                        C.ps[bk][:], lhsT=view[:, kc, o * 128:(o + 1) * 128], rhs=C.hT[:, kc, :],
                        start=(kc == 0), stop=(kc == KC - 1)), reads=[wk, ("hT", kc)], writes=[("ps", bk)])
            bb, bc, bu = banks
            ub = fc % 2
            S.add("act", lambda e, fc=fc, ub=ub: e.copy(ubuf[:, ub, 0:2], uh[:, fc, 2 * i:2 * i + 2]),
                  reads=[("uh", fc)], writes=[("ubuf", ub)])
            S.add("act", lambda e, bc=bc: e.copy(cg[:], C.ps[bc][:]), reads=[("ps", bc)], writes=["cg"])
            S.add("dve", lambda e, bu=bu, ub=ub: e.tensor_tensor(out=ubuf[:, ub, 2:2 + G], in0=C.ps[bu][:], in1=cg[:],
                                                                 op=ALU.mult),
                  reads=[("ps", bu), "cg"], writes=[("ubuf", ub)])
            cb = fc % 2
            S.add("dve", lambda e, fc=fc, cb=cb, ub=ub: e.tensor_scalar(out=cv[cb][:], in0=ubuf[:, ub, 0:G],
                                                                  scalar1=cw[:, 0, fc:fc + 1], scalar2=None,
                                                                  op0=ALU.mult),
                  reads=[("ubuf", ub), "small"], writes=[("cv", cb)])
            S.add("dve", lambda e, fc=fc, cb=cb, ub=ub: e.scalar_tensor_tensor(out=cv[cb][:], in0=ubuf[:, ub, 1:1 + G],
                                                                         scalar=cw[:, 1, fc:fc + 1], in1=cv[cb][:],
                                                                         op0=ALU.mult, op1=ALU.add),
                  reads=[("ubuf", ub), "small", ("cv", cb)], writes=[("cv", cb)])
            S.add("dve", lambda e, fc=fc, cb=cb, ub=ub: e.scalar_tensor_tensor(out=cv[cb][:], in0=ubuf[:, ub, 2:2 + G],
                                                                         scalar=cw[:, 2, fc:fc + 1], in1=cv[cb][:],
                                                                         op0=ALU.mult, op1=ALU.add),
                  reads=[("ubuf", ub), "small", ("cv", cb)], writes=[("cv", cb)])
            S.add("dve", lambda e, fc=fc, cb=cb, bb=bb: e.tensor_tensor(out=big[:, fc, :], in0=C.ps[bb][:], in1=cv[cb][:],
                                                                         op=ALU.mult),
                  reads=[("ps", bb), ("cv", cb)], writes=[("big", fc)])
    C.proj_fm(w_out, 0, 8, lambda fc, bank: C.residual_add(l, 0, fc, bank),
              rhs_of=lambda kc: big[:, kc, :], rhs_keys=lambda kc: ("big", kc))
    C.layernorm(l, 1)
    C.ffn(l, w_gu, w_dn)
    C.store_xT(out_rows)


def layer1_halo_u(C, l, uh, ctmp, w_in):
    S = C.S
    hval = C.small[:, SM_HVALID:SM_HVALID + 16]
    C.layernorm(l, 0, N=16)
    for part in range(2):
        def consume(oc, bank, part=part):
            if part == 0:
                S.add("act", lambda e: e.copy(uh[:, oc, :], C.ps[bank][:, 0:16]), reads=[("ps", bank)],
                      writes=[("uh", oc)])
            else:
                S.add("dve", lambda e: e.tensor_tensor(out=ctmp[:], in0=C.ps[bank][:, 0:16], in1=uh[:, oc, :],
                                                       op=ALU.mult), reads=[("ps", bank), ("uh", oc)], writes=["ctmp"])
                S.add("dve", lambda e: e.tensor_tensor(out=uh[:, oc, :], in0=ctmp[:], in1=hval, op=ALU.mult),
                      reads=["ctmp", "small"], writes=[("uh", oc)])
        C.proj_fm(w_in, D + part * D, 8, consume, N=16)


def _g_of_pos(p, half):
    return p if half == 1 else (p ^ 1)


def _tables(half):
    rb = np.zeros((NOWN, 4, NBLK), np.float32)
    npst = np.zeros((NOWN, 4, NBLK), np.float32)
    for i in range(NOWN):
        for qt in range(4):
            nbq = 2 * (2 * i + half) + qt // 2
            for pb in range(NBLK):
                gb = 2 * _g_of_pos(pb // 2, half) + pb % 2
                if gb < nbq:
                    npst[i, qt, pb] = NEG
                else:
                    rb[i, qt, pb] = -1e30
    tabs = np.concatenate([rb.reshape(-1), npst.reshape(-1)])[None, :].repeat(128, 0).astype(np.float32)
    rbh = np.zeros((128, NBLK), np.float32)
    nph = np.zeros((128, NBLK), np.float32)
    hpat = np.zeros((128, 64, 16), np.float32)
    kk = np.arange(128)
    for i in range(NOWN):
        gh = 2 * i + half - 1
        for t in range(2):
            col = 2 * i + t
            if gh < 0:
                continue
            gbq = 2 * gh + 1
            for pb in range(NBLK):
                gb = 2 * _g_of_pos(pb // 2, half) + pb % 2
                if gb < gbq:
                    nph[col, pb] = NEG
                else:
                    rbh[col, pb] = -1e30
                for sub in range(2):
                    kt = pb * 2 + sub
                    if gb < gbq:
                        hpat[:, kt, col] = 0.0
                    elif gb > gbq:
                        hpat[:, kt, col] = NEG
                    else:
                        hpat[:, kt, col] = np.where(sub * 128 + kk <= 254 + t, 0.0, NEG)
    tabs = np.concatenate([tabs, rbh, nph], axis=1).astype(np.float32)
    pat = np.zeros((128, 8, G), np.float32)
    k = np.arange(128)[:, None]
    q = np.arange(G)[None, :]
    qt = q // 128
    for ktw in range(8):
        if ktw < 4:
            pat[:, ktw, :] = 0.0 if half == 1 else NEG
        else:
            kt_ = ktw - 4
            kb = kt_ // 2
            qb = qt // 2
            kpos = (kt_ % 2) * 128 + k
            qpos = (qt % 2) * 128 + (q % 128)
            m = np.where(kb < qb, 0.0, np.where(kb > qb, NEG, np.where(kpos <= qpos, 0.0, NEG)))
            pat[:, ktw, :] = m
    sel = np.zeros((32, 32, 128), np.float32)
    for pb in range(32):
        sel[pb, pb, :] = 1.0
    import ml_dtypes
    bf = ml_dtypes.bfloat16
    patall = np.concatenate([pat.reshape(128, 8 * G), hpat.reshape(128, 1024)], axis=1)
    return tabs, patall.astype(bf), sel.reshape(32, 32 * 128).astype(bf)


def _small(c_b, b_ada_ls, nmix_ls, nffn_ls, q_gain, k_gain, conv_w, half):
    sm = np.zeros((128, NSM), np.float32)
    sm[:, SM_C:SM_C + 8] = c_b.reshape(8, 128).T
    for l, ba in enumerate(b_ada_ls):
        sm[:, SM_BADA + 48 * l:SM_BADA + 48 * (l + 1)] = ba.reshape(48, 128).T
    for l, v in enumerate(nmix_ls):
        sm[:, SM_NMIX + 8 * l:SM_NMIX + 8 * (l + 1)] = v.reshape(8, 128).T
    for l, v in enumerate(nffn_ls):
        sm[:, SM_NFFN + 8 * l:SM_NFFN + 8 * (l + 1)] = v.reshape(8, 128).T
    sm[:, SM_QG] = q_gain
    sm[:, SM_KG] = k_gain
    sm[:, SM_CONV:SM_CONV + 24] = conv_w.reshape(3, 8, 128).transpose(2, 0, 1).reshape(128, 24)
    hv = np.ones(16, np.float32)
    if half == 0:
        hv[0:2] = 0.0
    sm[:, SM_HVALID:SM_HVALID + 16] = hv[None, :]
    sm[:, SM_IDENT:SM_IDENT + 128] = np.eye(128, dtype=np.float32)
    return sm


_NC_CACHE = {}


def _get(name, fn):
    if name not in _NC_CACHE:
        _NC_CACHE[name] = fn()
    return _NC_CACHE[name]


def make_in_maps(x, c, w_ada, b_ada, norm_mix, norm_ffn, w_qkv, w_o, q_gain, k_gain, w_in, conv_w, w_out,
                 w_gate_up, w_down):
    in_maps = []
    for core in range(8):
        b, half = core // 2, core % 2
        perm = [_g_of_pos(p, half) for p in range(NPOS)]
        xseq = np.ascontiguousarray(x[b].reshape(NPOS, G, D)[perm].reshape(SEQ, D))
        xh = np.zeros((128, D), np.float32)
        for i in range(NOWN):
            g = 2 * i + half
            if g > 0:
                xh[2 * i:2 * i + 2] = x[b, g * G - 2:g * G]
            else:
                xh[2 * i:2 * i + 2] = x[b, 0:2]
        tabs, pat, sel = _tables(half)
        sm = _small(c[b], [b_ada[0], b_ada[1]], [norm_mix[0], norm_mix[1]], [norm_ffn[0], norm_ffn[1]],
                    q_gain[0], k_gain[0], conv_w[0], half)
        in_maps.append(dict(xseq=xseq, xhalo=xh, small=sm, tabs=tabs, pat=pat, sel=sel, w_ada=w_ada, w_qkv=w_qkv[0],
                            w_o=w_o[0], w_in=w_in[0], w_out=w_out[0], w_gu=w_gate_up, w_dn=w_down))
    return in_maps


def kernel(x, c, w_ada, b_ada, norm_mix, norm_ffn, w_qkv, w_o, q_gain, k_gain, w_in, conv_w, w_out,
           w_gate_up, w_down):
    f = lambda a: np.ascontiguousarray(np.asarray(a, dtype=np.float32))
    args = list(map(f, (x, c, w_ada, b_ada, norm_mix, norm_ffn, w_qkv, w_o, q_gain, k_gain, w_in, conv_w, w_out,
                        w_gate_up, w_down)))
    x = args[0]
    in_maps = make_in_maps(*args)
    nc = _get("F", build_fused)
    res = run_bass_kernel_spmd(nc, in_maps, core_ids=list(range(8)))
    out = np.zeros_like(x)
    for core in range(8):
        b, half = core // 2, core % 2
        o = np.asarray(res.results[core]["out"]).reshape(NOWN, G, D)
        for i in range(NOWN):
            g = 2 * i + half
            out[b, g * G:(g + 1) * G] = o[i]
    return out
```

```python
import contextlib
import numpy as np
import concourse.bass as bass
import concourse.mybir as mybir
from concourse.bass_utils import run_bass_kernel_spmd

F32 = mybir.dt.float32
BF16 = mybir.dt.bfloat16
ALU = mybir.AluOpType
AF = mybir.ActivationFunctionType

D = 1024
KC = 8
G = 512
H = 8
DH = 128
DFF = 2816
NJ = 22
NPOS = 16
NOWN = 8
NBLK = 32
SEQ = 8192
NEG = -30000.0
EPS = 1e-6
NW = 4
ENGS = ["pe", "act", "dve", "pool", "sp"]
INORDER = ("pe", "act", "dve")


class Sched:
    def __init__(self, nc, n_dma_sems=6):
        self.nc = nc
        self.ops = []
        self.last_write = {}
        self.readers = {}
        self.n_dma_sems = n_dma_sems

    def add(self, eng, fn, reads=(), writes=(), dma=False):
        oid = len(self.ops)
        deps = set()
        for b in reads:
            if b in self.last_write:
                deps.add(self.last_write[b])
        for b in writes:
            if b in self.last_write:
                deps.add(self.last_write[b])
            for r in self.readers.get(b, {}).values():
                deps.update(r)
        deps.discard(oid)
        self.ops.append(dict(id=oid, eng=eng, fn=fn, deps=deps, dma=dma, signal=False))
        for b in reads:
            rd = self.readers.setdefault(b, {})
            if dma or eng not in INORDER:
                rd.setdefault((eng, "dma"), []).append(oid)
            else:
                rd[eng] = [oid]
        for b in writes:
            self.last_write[b] = oid
            self.readers[b] = {}
        return oid

    def emit(self, final_wait_eng="sp"):
        nc = self.nc
        ops = self.ops

        def needs_sync(p, ceng):
            if p["dma"]:
                return True
            if p["eng"] != ceng:
                return True
            return p["eng"] in ("act", "dve", "pool")

        for op in ops:
            for d in op["deps"]:
                p = ops[d]
                if needs_sync(p, op["eng"]):
                    p["signal"] = True
            if op["dma"]:
                op["signal"] = True
        eng_count = {e: 0 for e in ENGS}
        dma_count = {}
        dma_rr = {e: 0 for e in ENGS}
        sem_keys = {}
        for op in ops:
            if not op["signal"]:
                continue
            if op["dma"]:
                k = dma_rr[op["eng"]] % self.n_dma_sems
                dma_rr[op["eng"]] += 1
                key = ("dma", op["eng"], k)
                prev = dma_count.get(key, 0)
                op["prev_on_sem"] = (key, prev)
                dma_count[key] = prev + 16
                op["sig"] = (key, prev + 16)
            else:
                key = ("eng", op["eng"])
                eng_count[op["eng"]] += 1
                op["sig"] = (key, eng_count[op["eng"]])
            sem_keys[key] = None
        with contextlib.ExitStack() as st:
            sems = {}
            for key in sem_keys:
                sems[key] = st.enter_context(nc.semaphore("s_" + "_".join(str(x) for x in key)))
            block = st.enter_context(nc.Block())
            per_eng = {e: [o for o in ops if o["eng"] == e] for e in ENGS}

            def body(ename):
                def run(eng):
                    waited = {}

                    def wait(key, val):
                        if waited.get(key, 0) >= val:
                            return
                        eng.wait_ge(sems[key], val)
                        waited[key] = val

                    for op in per_eng[ename]:
                        need = {}
                        for d in op["deps"]:
                            p = ops[d]
                            if not needs_sync(p, ename):
                                continue
                            key, val = p["sig"]
                            need[key] = max(need.get(key, 0), val)
                        if op["dma"]:
                            key, prev = op["prev_on_sem"]
                            if prev > 0:
                                need[key] = max(need.get(key, 0), prev)
                        for key, val in need.items():
                            wait(key, val)
                        ins = op["fn"](eng)
                        if op["signal"]:
                            key, val = op["sig"]
                            ins.then_inc(sems[key], 16 if op["dma"] else 1)
                    if ename == final_wait_eng:
                        for key, val in dma_count.items():
                            wait(key, val)
                        for e in ENGS:
                            if eng_count[e] > 0 and e != ename:
                                wait(("eng", e), eng_count[e])
                return run

            block.tensor(body("pe"))
            block.scalar(body("act"))
            block.vector(body("dve"))
            block.gpsimd(body("pool"))
            block.sync(body("sp"))


SM_C = 0
SM_BADA = 8
SM_NMIX = 104
SM_NFFN = 120
SM_QG = 136
SM_KG = 137
SM_CONV = 138
SM_HVALID = 162
SM_IDENT = 178
NSM = 306


class Ctx:
    def __init__(self, nc):
        self.nc = nc
        self.S = Sched(nc)
        self.wslot = 0
        self.bankrr = 0
        a = lambda n_, sh_, d_: nc.alloc_sbuf_tensor('sb_' + n_, sh_, d_)
        self.small = a("small", [128, NSM], F32)
        self.ident_bf = a("ident_bf", [128, 128], BF16)
        self.ones_bf = a("ones_bf", [128, 128], BF16)
        self.eps_t = a("eps_t", [128, 1], F32)
        self.wring = [a(f"wring{i}", [128, 4096], BF16) for i in range(NW)]
        self.xin = a("xin", [128, 4, D], F32)
        self.xT = a("xT", [128, KC, G], F32)
        self.sq = a("sq", [128, KC, G], BF16)
        self.hT = a("hT", [128, KC, G], BF16)
        self.tmp = [a(f"tmp{i}", [128, G], F32) for i in range(2)]
        self.std = a("std", [128, G], F32)
        self.rstd = a("rstd", [128, G], F32)
        self.big = a("big", [128, 24, G], BF16)
        self.mod = a("mod", [128, 2, 48], F32)
        self.modG = a("modG", [128, 2, 2, KC], F32)
        self.scbf = a("scbf", [128, KC], BF16)
        self.qgs = a("qgs", [128, 1], F32)
        self.ps = [nc.alloc_psum_tensor(f"ps{i}", [128, 512], F32) for i in range(7)]
        self.psb = nc.alloc_psum_tensor("psb", [128, 1024], BF16)

    @property
    def ident(self):
        return self.small[:, SM_IDENT:SM_IDENT + 128]

    def bank(self, lo=0, hi=7):
        b = lo + self.bankrr % (hi - lo)
        self.bankrr += 1
        return b

    def slab(self, src3d, kparts, ncols):
        rk = []
        if isinstance(src3d, tuple):
            src3d, rk = src3d
        slot = self.wslot % NW
        self.wslot += 1
        view = self.wring[slot][:, 0:kparts * ncols].rearrange("p (k n) -> p k n", k=kparts)
        self.S.add("pool", lambda e: e.dma_start(out=view, in_=src3d), reads=rk, writes=[("w", slot)], dma=True)
        return view, ("w", slot)

    def wcols(self, w2d, c0, ncols, r0=0, kparts=KC):
        rk = []
        if isinstance(w2d, tuple):
            w2d, rk = w2d
        ap = w2d[r0:r0 + kparts * 128, c0:c0 + ncols].rearrange("(k p) n -> p k n", p=128)
        return (ap, rk) if rk else ap

    def precast(self, name, w2d, rows_per):
        R, Ncol = w2d.shape
        wb = self.nc.dram_tensor("Wb_" + name, [R, Ncol], BF16).ap()
        keys = []
        r = 0
        while r < R:
            r1 = min(R, r + rows_per)
            key = ("Wb", name, r)
            self.S.add("pool", lambda e, r=r, r1=r1: e.dma_start(out=wb[r:r1, :], in_=w2d[r:r1, :]),
                       writes=[key], dma=True)
            keys.append(key)
            r = r1
        return (wb, keys)

    def setup(self, small_ap):
        S = self.S
        S.add("sp", lambda e: e.dma_start(out=self.small[:], in_=small_ap), writes=["small"], dma=True)
        S.add("dve", lambda e: e.memset(self.ones_bf[:], 1.0), writes=["ones"])
        S.add("dve", lambda e: e.memset(self.eps_t[:], EPS), writes=["eps"])
        S.add("dve", lambda e: e.tensor_copy(self.ident_bf[:], self.ident), reads=["small"], writes=["identbf"])
        S.add("act", lambda e: e.activation(out=self.scbf[:], in_=self.small[:, SM_C:SM_C + 8], func=AF.Silu),
              reads=["small"], writes=["scbf"])
        S.add("act", lambda e: e.mul(self.qgs[:], self.small[:, SM_QG:SM_QG + 1], DH ** -0.5),
              reads=["small"], writes=["qgs"])

    def adaln(self, w_ada_l, l):
        for _ in self.adaln_gen(w_ada_l, l):
            pass

    def adaln_gen(self, w_ada_l, l):
        S = self.S
        bank = 2
        c0 = 256
        first = True
        for s in range(12):
            if s > 0:
                yield
            view, wk = self.slab(self.wcols(w_ada_l, s * 512, 512), KC, 512)
            for o in range(4):
                oc = s * 4 + o
                for kc in range(KC):
                    wr = ["ps2mod"] + ([("ps", bank)] if first else [])
                    first = False
                    S.add("pe", lambda e, oc=oc, kc=kc, o=o, view=view: e.matmul(
                        self.ps[bank][:, c0 + oc:c0 + oc + 1], lhsT=view[:, kc, o * 128:(o + 1) * 128],
                        rhs=self.scbf[:, kc:kc + 1], start=(kc == 0), stop=(kc == KC - 1)),
                        reads=[wk, "scbf"], writes=wr)
        mod = self.mod
        S.add("dve", lambda e: e.tensor_tensor(out=mod[:, l, :], in0=self.ps[bank][:, c0:c0 + 48],
                                               in1=self.small[:, SM_BADA + 48 * l:SM_BADA + 48 * (l + 1)], op=ALU.add),
              reads=["ps2mod", ("ps", bank), "small"], writes=[("mod", l)])
        S.add("dve", lambda e: e.scalar_tensor_tensor(out=self.modG[:, l, 0, :], in0=mod[:, l, 8:16], scalar=1.0,
                                                      in1=self.small[:, SM_NMIX + 8 * l:SM_NMIX + 8 * (l + 1)],
                                                      op0=ALU.add, op1=ALU.mult),
              reads=[("mod", l), "small"], writes=[("modG", l, 0)])
        S.add("dve", lambda e: e.scalar_tensor_tensor(out=self.modG[:, l, 1, :], in0=mod[:, l, 32:40], scalar=1.0,
                                                      in1=self.small[:, SM_NFFN + 8 * l:SM_NFFN + 8 * (l + 1)],
                                                      op0=ALU.add, op1=ALU.mult),
              reads=[("mod", l), "small"], writes=[("modG", l, 1)])

    def modcols(self, l, which):
        base = 0 if which == 0 else 24
        return (self.modG[:, l, which, :], self.mod[:, l, base:base + 8], self.mod[:, l, base + 16:base + 24],
                [("mod", l), ("modG", l, which)])

    def load_xT(self, x_rows, N=G, ntile=4):
        S = self.S
        xin = self.xin
        S.add("sp", lambda e: e.dma_start(out=xin[:, 0:ntile, :], in_=x_rows.rearrange("(t p) f -> p t f", p=128)),
              writes=["xin"], dma=True)
        for kc in range(KC):
            bank = self.bank()
            for t in range(ntile):
                S.add("pe", lambda e, kc=kc, t=t, bank=bank: e.transpose(
                    self.ps[bank][:, t * 128:(t + 1) * 128], xin[:, t, kc * 128:(kc + 1) * 128], self.ident),
                    reads=["xin", "small"], writes=[("ps", bank)])
            eng = "dve" if kc % 2 == 0 else "act"
            if eng == "dve":
                S.add("dve", lambda e, kc=kc, bank=bank: e.tensor_copy(self.xT[:, kc, 0:ntile * 128],
                                                                        self.ps[bank][:, 0:ntile * 128]),
                      reads=[("ps", bank)], writes=[("xT", kc)])
            else:
                S.add("act", lambda e, kc=kc, bank=bank: e.copy(self.xT[:, kc, 0:ntile * 128],
                                                                 self.ps[bank][:, 0:ntile * 128]),
                      reads=[("ps", bank)], writes=[("xT", kc)])

    def store_xT(self, out_rows, ntile=4):
        S = self.S
        xin = self.xin
        for t in range(ntile):
            for hf in range(2):
                bank = self.bank()
                for k4 in range(4):
                    kc = hf * 4 + k4
                    S.add("pe", lambda e, kc=kc, t=t, bank=bank, k4=k4: e.transpose(
                        self.ps[bank][:, k4 * 128:(k4 + 1) * 128], self.xT[:, kc, t * 128:(t + 1) * 128], self.ident),
                        reads=[("xT", kc), "small"], writes=[("ps", bank)])
                if hf == 0:
                    S.add("dve", lambda e, t=t, bank=bank, hf=hf: e.tensor_copy(xin[:, t, hf * 512:(hf + 1) * 512],
                                                                                 self.ps[bank][:]),
                          reads=[("ps", bank)], writes=["xin"])
                else:
                    S.add("act", lambda e, t=t, bank=bank, hf=hf: e.copy(xin[:, t, hf * 512:(hf + 1) * 512],
                                                                          self.ps[bank][:]),
                          reads=[("ps", bank)], writes=["xin"])
        S.add("sp", lambda e: e.dma_start(out=out_rows.rearrange("(t p) f -> p t f", p=128), in_=xin[:, 0:ntile, :]),
              reads=["xin"], writes=[("out", id(out_rows))], dma=True)

    def layernorm(self, l, which, N=G):
        S = self.S
        Gc, Sc, _, mkeys = self.modcols(l, which)
        for kc in range(KC):
            S.add("act", lambda e, kc=kc: e.activation(out=self.sq[:, kc, :N], in_=self.xT[:, kc, :N], func=AF.Square),
                  reads=[("xT", kc)], writes=[("sq", kc)])
        bank = self.bank()
        for kc in range(KC):
            S.add("pe", lambda e, kc=kc: e.matmul(self.ps[bank][:, :N], lhsT=self.ones_bf[:], rhs=self.sq[:, kc, :N],
                                                  start=(kc == 0), stop=(kc == KC - 1)),
                  reads=[("sq", kc), "ones"], writes=[("ps", bank)])
        S.add("act", lambda e: e.activation(out=self.std[:, :N], in_=self.ps[bank][:, :N], func=AF.Ln,
                                            bias=self.eps_t[:, 0:1], scale=1.0 / D),
              reads=[("ps", bank), "eps"], writes=["std"])
        S.add("act", lambda e: e.activation(out=self.rstd[:, :N], in_=self.std[:, :N], func=AF.Exp, scale=-0.5),
              reads=["std"], writes=["rstd"])
        for kc in range(KC):
            tb = kc % 2
            S.add("dve", lambda e, kc=kc, tb=tb: e.tensor_tensor(out=self.tmp[tb][:, :N], in0=self.xT[:, kc, :N],
                                                                 in1=self.rstd[:, :N], op=ALU.mult),
                  reads=[("xT", kc), "rstd"], writes=[("tmp", tb)])
            S.add("act", lambda e, kc=kc, tb=tb: e.activation(out=self.hT[:, kc, :N], in_=self.tmp[tb][:, :N],
                                                              func=AF.Identity, scale=Gc[:, kc:kc + 1],
                                                              bias=Sc[:, kc:kc + 1]),
                  reads=[("tmp", tb)] + mkeys, writes=[("hT", kc)])

    def proj_fm(self, w2d, c0, nchunks, consume, rhs_of=None, rhs_keys=None, N=G, kparts=KC, r0=0):
        S = self.S
        if rhs_of is None:
            rhs_of = lambda kc: self.hT[:, kc, :N]
            rhs_keys = lambda kc: ("hT", kc)
        oc = 0
        while oc < nchunks:
            nch = min(4, nchunks - oc)
            view, wk = self.slab(self.wcols(w2d, c0 + oc * 128, nch * 128, r0=r0, kparts=kparts), kparts, nch * 128)
            for o in range(nch):
                bank = self.bank()
                for kc in range(kparts):
                    S.add("pe", lambda e, kc=kc, o=o, bank=bank, view=view: e.matmul(
                        self.ps[bank][:, :N], lhsT=view[:, kc, o * 128:(o + 1) * 128], rhs=rhs_of(kc),
                        start=(kc == 0), stop=(kc == kparts - 1)),
                        reads=[wk, rhs_keys(kc)], writes=[("ps", bank)])
                consume(oc + o, bank)
            oc += nch

    def residual_add(self, l, which, fc, bank, N=G):
        _, _, gate, mkeys = self.modcols(l, which)
        self.S.add("dve", lambda e: e.scalar_tensor_tensor(out=self.xT[:, fc, :N], in0=self.ps[bank][:, :N],
                                                           scalar=gate[:, fc:fc + 1], in1=self.xT[:, fc, :N],
                                                           op0=ALU.mult, op1=ALU.add),
                   reads=[("ps", bank), ("xT", fc)] + mkeys, writes=[("xT", fc)])

    def ffn(self, l, w_gu, w_dn, N=G):
        S = self.S
        big = self.big
        j = 0
        while j < NJ:
            nch = min(4, NJ - j)
            gview, gk = self.slab(self.wcols(w_gu, j * 128, nch * 128), KC, nch * 128)
            uview, uk = self.slab(self.wcols(w_gu, DFF + j * 128, nch * 128), KC, nch * 128)
            for o in range(nch):
                jj = j + o
                bg = self.bank()
                bu = self.bank()
                for kc in range(KC):
                    S.add("pe", lambda e, kc=kc, o=o, bg=bg, gview=gview: e.matmul(
                        self.ps[bg][:, :N], lhsT=gview[:, kc, o * 128:(o + 1) * 128], rhs=self.hT[:, kc, :N],
                        start=(kc == 0), stop=(kc == KC - 1)), reads=[gk, ("hT", kc)], writes=[("ps", bg)])
                for kc in range(KC):
                    S.add("pe", lambda e, kc=kc, o=o, bu=bu, uview=uview: e.matmul(
                        self.ps[bu][:, :N], lhsT=uview[:, kc, o * 128:(o + 1) * 128], rhs=self.hT[:, kc, :N],
                        start=(kc == 0), stop=(kc == KC - 1)), reads=[uk, ("hT", kc)], writes=[("ps", bu)])
                tb = jj % 2
                S.add("act", lambda e, bg=bg, tb=tb: e.activation(out=self.tmp[tb][:, :N], in_=self.ps[bg][:, :N],
                                                                  func=AF.Silu),
                      reads=[("ps", bg)], writes=[("tmp", tb)])
                S.add("dve", lambda e, bu=bu, tb=tb, jj=jj: e.tensor_tensor(out=big[:, jj, :N], in0=self.ps[bu][:, :N],
                                                                             in1=self.tmp[tb][:, :N], op=ALU.mult),
                      reads=[("ps", bu), ("tmp", tb)], writes=[("big", jj)])
            j += nch
        for fp in range(4):
            v0, k0 = self.slab(self.wcols(w_dn, fp * 256, 256, r0=0, kparts=11), 11, 256)
            v1, k1 = self.slab(self.wcols(w_dn, fp * 256, 256, r0=11 * 128, kparts=11), 11, 256)
            banks = [self.bank(), self.bank()]
            for jj in range(NJ):
                view, wk = (v0, k0) if jj < 11 else (v1, k1)
                for f2 in range(2):
                    S.add("pe", lambda e, jj=jj, f2=f2, view=view, b=banks[f2]: e.matmul(
                        self.ps[b][:, :N], lhsT=view[:, jj % 11, f2 * 128:(f2 + 1) * 128], rhs=big[:, jj, :N],
                        start=(jj == 0), stop=(jj == NJ - 1)), reads=[wk, ("big", jj)], writes=[("ps", banks[f2])])
            for f2 in range(2):
                self.residual_add(l, 1, fp * 2 + f2, banks[f2], N)


def build_fused():
    nc = bass.Bass("TRN2", target_bir_lowering=False)
    dt = nc.dram_tensor
    xseq = dt("xseq", [SEQ, D], F32, kind="ExternalInput").ap()
    xhalo = dt("xhalo", [128, D], F32, kind="ExternalInput").ap()
    small_ap = dt("small", [128, NSM], F32, kind="ExternalInput").ap()
    tabs_ap = dt("tabs", [128, 2048 + 64], F32, kind="ExternalInput").ap()
    pat_ap = dt("pat", [128, 8 * G + 64 * 16], BF16, kind="ExternalInput").ap()
    sel_ap = dt("sel", [32, 32 * 128], BF16, kind="ExternalInput").ap()
    w_ada = dt("w_ada", [2, D, 6 * D], F32, kind="ExternalInput").ap()
    w_qkv = dt("w_qkv", [D, 3 * D], F32, kind="ExternalInput").ap()
    w_o = dt("w_o", [D, D], F32, kind="ExternalInput").ap()
    w_in = dt("w_in", [D, 3 * D], F32, kind="ExternalInput").ap()
    w_out = dt("w_out", [D, D], F32, kind="ExternalInput").ap()
    w_gu = dt("w_gu", [2, D, 2 * DFF], F32, kind="ExternalInput").ap()
    w_dn = dt("w_dn", [2, DFF, D], F32, kind="ExternalInput").ap()
    out = dt("out", [NOWN * G, D], F32, kind="ExternalOutput").ap()
    Kt = dt("Kt", [H, DH, SEQ], BF16).ap()
    Vs = dt("Vs", [SEQ, D], BF16).ap()
    Qt = dt("Qt", [H, DH, NOWN * G], BF16).ap()
    Ot = dt("Ot", [NOWN, DH, H, G], BF16).ap()

    C = Ctx(nc)
    S = C.S
    a = lambda n_, sh_, d_: nc.alloc_sbuf_tensor('sb_' + n_, sh_, d_)
    tabs = a("tabs", [128, 2048 + 64], F32)
    pat = a("pat", [128, 8 * G + 64 * 16], BF16)
    sel = a("sel", [32, 32, 128], BF16)
    kmean_f = a("kmean_f", [128, H, NBLK], F32)
    kmean_bf = a("kmean_bf", [128, H, NBLK], BF16)
    Kh = a("Kh", [128, SEQ], BF16)
    Vh = a("Vh", [128, 64, DH], BF16)
    Qh = a("Qh", [128, NOWN * G], BF16)
    QhaloT = a("QhaloT", [128, H, 16], BF16)
    oTh = a("oTh", [128, H, 16], BF16)
    pT = [a(f"pT{i}", [128, G], BF16) for i in range(2)]
    Rsb = a("Rsb", [128, 4, NBLK], F32)
    max8 = a("max8", [128, 4, 8], F32)
    mb = a("mb", [128, 4, NBLK], BF16)
    mbT = [a(f"mbT{i}", [32, G], BF16) for i in range(2)]
    oTsb = [a(f"oTsb{i}", [128, G], BF16) for i in range(2)]
    uh = a("uh", [128, KC, 16], F32)
    ctmp = a("ctmp", [128, 16], F32)
    cg = a("cg", [128, G], F32)
    ubuf = a("ubuf", [128, 2, 2 + G], F32)
    cv = [a(f"cv{i}", [128, G], F32) for i in range(2)]
    big = C.big
    recip = C.std

    C.setup(small_ap)
    S.add("sp", lambda e: e.dma_start(out=tabs[:], in_=tabs_ap), writes=["tabs"], dma=True)
    S.add("sp", lambda e: e.dma_start(out=pat[:], in_=pat_ap), writes=["pat"], dma=True)
    S.add("sp", lambda e: e.dma_start(out=sel[:].rearrange("p a b -> p (a b)"), in_=sel_ap), writes=["sel"], dma=True)
    S.add("dve", lambda e: e.memset(kmean_f[:], 0.0), writes=["kmean_f"])
    wb_qkv = C.precast("qkv", w_qkv, 256)
    C.adaln(w_ada[0], 0)
    kg = C.small[:, SM_KG:SM_KG + 1]
    acc = [a(f"acc{i}", [128, G], F32) for i in range(2)]
    ones_f = a("ones_f", [128, 128], F32)
    S.add("dve", lambda e: e.memset(ones_f[:], 1.0), writes=["ones_f"])

    def qk_head(hh, dst_ap, dst_key, gain_ap, gain_keys, wview, wk, o, kmean_pos=None, N=G):
        bank = C.bank()
        for kc in range(KC):
            S.add("pe", lambda e, kc=kc: e.matmul(C.ps[bank][:, :N], lhsT=wview[:, kc, o * 128:(o + 1) * 128],
                                                  rhs=C.hT[:, kc, :N], start=(kc == 0), stop=(kc == KC - 1)),
                  reads=[wk, ("hT", kc)], writes=[("ps", bank)])
        sqk = C.sq[:, hh % 2, :N]
        S.add("act", lambda e: e.activation(out=sqk, in_=C.ps[bank][:, :N], func=AF.Square),
              reads=[("ps", bank)], writes=[("sq", hh % 2)])
        b2 = C.bank()
        S.add("pe", lambda e: e.matmul(C.ps[b2][:, :N], lhsT=C.ones_bf[:], rhs=sqk, start=True, stop=True),
              reads=["ones", ("sq", hh % 2)], writes=[("ps", b2)])
        S.add("act", lambda e: e.activation(out=C.std[:, :N], in_=C.ps[b2][:, :N], func=AF.Ln, bias=C.eps_t[:, 0:1],
                                            scale=1.0 / DH), reads=[("ps", b2), "eps"], writes=["std"])
        S.add("act", lambda e: e.activation(out=C.rstd[:, :N], in_=C.std[:, :N], func=AF.Exp, scale=-0.5),
              reads=["std"], writes=["rstd"])
        if kmean_pos is None:
            S.add("dve", lambda e: e.scalar_tensor_tensor(out=dst_ap, in0=C.ps[bank][:, :N], scalar=gain_ap,
                                                          in1=C.rstd[:, :N], op0=ALU.mult, op1=ALU.mult),
                  reads=[("ps", bank), "rstd"] + gain_keys, writes=[dst_key])
        else:
            for bb in range(2):
                pb = kmean_pos * 2 + bb
                S.add("dve", lambda e, bb=bb, pb=pb: e.scalar_tensor_tensor(
                    out=dst_ap[:, bb * 256:(bb + 1) * 256], in0=C.ps[bank][:, bb * 256:(bb + 1) * 256],
                    scalar=gain_ap, in1=C.rstd[:, bb * 256:(bb + 1) * 256], op0=ALU.mult, op1=ALU.mult,
                    accum_out=kmean_f[:, hh, pb:pb + 1]),
                    reads=[("ps", bank), "rstd", "kmean_f"] + gain_keys, writes=[dst_key, "kmean_f"])

    for p in range(NPOS):
        own = (p % 2 == 1)
        C.load_xT(xseq[p * G:(p + 1) * G, :])
        C.layernorm(0, 0)
        for s in range(2):
            wview, wk = C.slab(C.wcols(wb_qkv, D + s * 512, 512), KC, 512)
            for o in range(4):
                hh = s * 4 + o
                qk_head(hh, big[:, hh, :], ("big", hh), kg, ["small"], wview, wk, o, kmean_pos=p)
        S.add("sp", lambda e, p=p: e.dma_start(out=Kt.rearrange("h d t -> d h t")[:, :, p * G:(p + 1) * G],
                                               in_=big[:, 0:8, :]),
              reads=[("big", s_) for s_ in range(8)], writes=[("Kt", p)], dma=True)
        for hf in range(2):
            wview, wk = C.slab(C.wcols(wb_qkv, 2 * D + hf * 512, 512), KC, 512)
            for t in range(4):
                bank = C.bank()
                for kc in range(KC):
                    S.add("pe", lambda e, kc=kc, t=t, bank=bank, wview=wview: e.matmul(
                        C.ps[bank][:], lhsT=C.hT[:, kc, t * 128:(t + 1) * 128], rhs=wview[:, kc, :],
                        start=(kc == 0), stop=(kc == KC - 1)), reads=[wk, ("hT", kc)], writes=[("ps", bank)])
                slot = 16 + 2 * t + hf
                if t % 2 == 0:
                    S.add("act", lambda e, bank=bank, slot=slot: e.copy(big[:, slot, :], C.ps[bank][:]),
                          reads=[("ps", bank)], writes=[("big", slot)])
                else:
                    S.add("dve", lambda e, bank=bank, slot=slot: e.tensor_copy(big[:, slot, :], C.ps[bank][:]),
                          reads=[("ps", bank)], writes=[("big", slot)])
        S.add("sp", lambda e, p=p: e.dma_start(
            out=Vs[p * G:(p + 1) * G, :].rearrange("(t k) (hf c) -> k t hf c", k=128, hf=2),
            in_=big[:, 16:24, :].rearrange("p (t hf) c -> p t hf c", hf=2)),
            reads=[("big", s_) for s_ in range(16, 24)], writes=[("Vs", p)], dma=True)
        if own:
            i = p // 2
            for s in range(2):
                wview, wk = C.slab(C.wcols(wb_qkv, s * 512, 512), KC, 512)
                for o in range(4):
                    hh = s * 4 + o
                    qk_head(hh, big[:, 8 + hh, :], ("big", 8 + hh), C.qgs[:, 0:1], ["qgs"], wview, wk, o)
            S.add("sp", lambda e, i=i: e.dma_start(out=Qt.rearrange("h d t -> d h t")[:, :, i * G:(i + 1) * G],
                                                   in_=big[:, 8:16, :]),
                  reads=[("big", s_) for s_ in range(8, 16)], writes=[("Qt", i)], dma=True)
    C.load_xT(xhalo, ntile=1)
    C.layernorm(0, 0, N=16)
    for s in range(2):
        wview, wk = C.slab(C.wcols(wb_qkv, s * 512, 512), KC, 512)
        for o in range(4):
            hh = s * 4 + o
            qk_head(hh, QhaloT[:, hh, :], ("QhaloT", hh), C.qgs[:, 0:1], ["qgs"], wview, wk, o, N=16)
    S.add("dve", lambda e: e.tensor_copy(kmean_bf[:], kmean_f[:]), reads=["kmean_f"], writes=["kmean_bf"])

    rb = tabs[:, 0:1024].rearrange("p (i q n) -> p i q n", i=NOWN, q=4)
    npast = tabs[:, 1024:2048].rearrange("p (i q n) -> p i q n", i=NOWN, q=4)
    rb_h = tabs[:, 2048:2080]
    np_h = tabs[:, 2080:2112]
    patg = pat[:, 0:8 * G].rearrange("p (a b) -> p a b", a=8)
    hpat = pat[:, 8 * G:8 * G + 1024].rearrange("p (a b) -> p a b", a=64)
    RB = 2

    def desc_group(hh, i):
        return dict(N=G, nq=4, qrows=128, q_ap=Qh[:, i * G:(i + 1) * G], q_keys=[("Qh",)],
                    rb=rb[:, i, :, :], npst=npast[:, i, :, :], nkt=8 * i + 8,
                    pat_of=(lambda kt: patg[:, kt - 8 * i, :] if kt >= 8 * i else None), i=i, hh=hh)

    def desc_halo(hh):
        return dict(N=16, nq=1, qrows=16, q_ap=QhaloT[:, hh, :], q_keys=[("QhaloT", hh)],
                    rb=rb_h[0:16, :].rearrange("p (q n) -> p q n", q=1), npst=np_h[0:16, :].rearrange("p (q n) -> p q n", q=1),
                    nkt=64, pat_of=(lambda kt: hpat[:, kt, :]), i=None, hh=hh)

    def route(dsc, par):
        hh, N, nq, qr = dsc["hh"], dsc["N"], dsc["nq"], dsc["qrows"]
        qw = min(N, 128)
        for qt in range(nq):
            S.add("pe", lambda e, qt=qt: e.matmul(C.ps[RB][0:qr, qt * NBLK:(qt + 1) * NBLK],
                                                  lhsT=dsc["q_ap"][:, qt * qw:(qt + 1) * qw],
                                                  rhs=kmean_bf[:, hh, :], start=True, stop=True),
                  reads=dsc["q_keys"] + ["kmean_bf"], writes=[("ps", RB)])
        S.add("dve", lambda e: e.tensor_tensor(out=Rsb[0:qr, 0:nq, :],
                                               in0=C.ps[RB][0:qr, 0:nq * NBLK].rearrange("p (q n) -> p q n", q=nq),
                                               in1=dsc["rb"], op=ALU.add),
              reads=[("ps", RB), "tabs"], writes=["Rsb"])
        for qt in range(nq):
            S.add("dve", lambda e, qt=qt: e.max(out=max8[0:qr, qt, :], in_=Rsb[0:qr, qt, :]), reads=["Rsb"],
                  writes=[("max8", qt)])
            S.add("dve", lambda e, qt=qt: e.scalar_tensor_tensor(out=mb[0:qr, qt, :], in0=Rsb[0:qr, qt, :],
                                                                 scalar=max8[0:qr, qt, 2:3], in1=dsc["npst"][:, qt, :],
                                                                 op0=ALU.is_lt, op1=ALU.mult),
                  reads=["Rsb", ("max8", qt), "tabs"], writes=[("mb", qt)])
        for qt in range(nq):
            S.add("pe", lambda e, qt=qt: e.transpose(C.psb[0:32, qt * 128:qt * 128 + qr], mb[0:qr, qt, :],
                                                     C.ident_bf[0:qr, 0:qr]),
                  reads=[("mb", qt), "identbf"], writes=["psb"])
        S.add("act", lambda e: e.copy(mbT[par][:, 0:N], C.psb[0:32, 0:N]), reads=["psb"], writes=[("mbT", par)])

    def attn(dsc, par):
        hh, N, nkt = dsc["hh"], dsc["N"], dsc["nkt"]
        ob = 3 + par
        db = 5 + par

        def qk(kt):
            sb = kt % 2
            pt = dsc["pat_of"](kt)
            S.add("pe", lambda e: e.matmul(C.ps[sb][:, :N], lhsT=Kh[:, kt * 128:(kt + 1) * 128], rhs=dsc["q_ap"],
                                           start=True, stop=False), reads=[("Kh",)] + dsc["q_keys"], writes=[("ps", sb)])
            S.add("pe", lambda e: e.matmul(C.ps[sb][:, :N], lhsT=sel[:, kt // 2, :], rhs=mbT[par][:, 0:N], start=False,
                                           stop=(pt is None)), reads=["sel", ("mbT", par)], writes=[("ps", sb)])
            if pt is not None:
                S.add("pe", lambda e: e.matmul(C.ps[sb][:, :N], lhsT=C.ident_bf[:], rhs=pt, start=False, stop=True),
                      reads=["identbf", "pat"], writes=[("ps", sb)])
            S.add("act", lambda e: e.activation(out=pT[sb][:, :N], in_=C.ps[sb][:, :N], func=AF.Exp),
                  reads=[("ps", sb)], writes=[("pT", sb)])

        def pv(kt):
            sb = kt % 2
            S.add("pe", lambda e: e.matmul(C.ps[ob][:, :N], lhsT=Vh[:, kt, :], rhs=pT[sb][:, :N], start=(kt == 0),
                                           stop=(kt == nkt - 1)), reads=[("Vh",), ("pT", sb)], writes=[("ps", ob)])
            ai = kt % 2
            if kt < 2:
                S.add("dve", lambda e: e.tensor_copy(acc[ai][:, :N], pT[sb][:, :N]), reads=[("pT", sb)],
                      writes=[("acc", ai)])
            else:
                S.add("dve", lambda e: e.tensor_tensor(out=acc[ai][:, :N], in0=acc[ai][:, :N], in1=pT[sb][:, :N],
                                                       op=ALU.add), reads=[("pT", sb), ("acc", ai)], writes=[("acc", ai)])

        qk(0)
        for kt in range(nkt):
            if kt + 1 < nkt:
                qk(kt + 1)
            pv(kt)
        for ai in range(2):
            S.add("pe", lambda e, ai=ai: e.matmul(C.ps[db][:, :N], lhsT=ones_f[:], rhs=acc[ai][:, :N], start=(ai == 0),
                                                  stop=(ai == 1)), reads=["ones_f", ("acc", ai)], writes=[("ps", db)])
        S.add("act", lambda e: e.activation(out=C.rstd[:, :N], in_=C.ps[db][:, :N], func=AF.Ln),
              reads=[("ps", db)], writes=["rstd"])
        S.add("act", lambda e: e.activation(out=recip[:, :N], in_=C.rstd[:, :N], func=AF.Exp, scale=-1.0),
              reads=["rstd"], writes=["std"])
        if dsc["i"] is None:
            S.add("dve", lambda e: e.tensor_tensor(out=oTh[:, hh, :], in0=C.ps[ob][:, :N], in1=recip[:, :N], op=ALU.mult),
                  reads=[("ps", ob), "std"], writes=[("oTh", hh)])
        else:
            i = dsc["i"]
            S.add("dve", lambda e: e.tensor_tensor(out=oTsb[par][:], in0=C.ps[ob][:], in1=recip[:], op=ALU.mult),
                  reads=[("ps", ob), "std"], writes=[("oTsb", par)])
            S.add("sp", lambda e: e.dma_start(out=Ot[i, :, hh, :], in_=oTsb[par][:]), reads=[("oTsb", par)],
                  writes=[("Ot", i, hh)], dma=True)

    cnt = 0
    bg_adaln = C.adaln_gen(w_ada[1], 1)
    wb = {}
    for hh in range(H):
        S.add("sp", lambda e, hh=hh: e.dma_start(out=Kh[:], in_=Kt[hh]), reads=[("Kt", p) for p in range(NPOS)],
              writes=[("Kh",)], dma=True)
        S.add("sp", lambda e, hh=hh: e.dma_start(out=Qh[:], in_=Qt[hh]), reads=[("Qt", i) for i in range(NOWN)],
              writes=[("Qh",)], dma=True)
        for q4 in range(4):
            S.add("sp", lambda e, hh=hh, q4=q4: e.dma_start(
                out=Vh[:, q4 * 16:(q4 + 1) * 16, :],
                in_=Vs[q4 * 2048:(q4 + 1) * 2048, hh * DH:(hh + 1) * DH].rearrange("(kt k) c -> k kt c", k=128)),
                reads=[("Vs", p) for p in range(NPOS)], writes=[("Vh",)], dma=True)
        seq = [desc_halo(hh)] + [desc_group(hh, i) for i in range(NOWN)]
        route(seq[0], cnt % 2)
        for n_, dsc in enumerate(seq):
            par = cnt % 2
            if n_ + 1 < len(seq):
                route(seq[n_ + 1], (cnt + 1) % 2)
            if hh == 0:
                for _ in range(2):
                    next(bg_adaln, None)
            attn(dsc, par)
            cnt += 1
        if hh == 0:
            for _ in bg_adaln:
                pass
            wb["o"] = C.precast("o", w_o, 512)
            wb["gu0"] = C.precast("gu0", w_gu[0], 256)
            wb["dn0"] = C.precast("dn0", w_dn[0], 1408)
            wb["in"] = C.precast("in", w_in, 512)
            wb["out"] = C.precast("out", w_out, 512)
            wb["gu1"] = C.precast("gu1", w_gu[1], 256)
            wb["dn1"] = C.precast("dn1", w_dn[1], 1408)

    C.load_xT(xhalo, ntile=1)
    C.proj_fm(wb["o"], 0, 8, lambda fc, bank: C.residual_add(0, 0, fc, bank, N=16),
              rhs_of=lambda kc: oTh[:, kc, :], rhs_keys=lambda kc: ("oTh", kc), N=16)
    C.layernorm(0, 1, N=16)
    C.ffn(0, wb["gu0"], wb["dn0"], N=16)
    layer1_halo_u(C, 1, uh, ctmp, wb["in"])
    oTs = Qh[:, :].rearrange("p (h t) -> p h t", h=H)
    for i in range(NOWN):
        p = 2 * i + 1
        C.load_xT(xseq[p * G:(p + 1) * G, :])
        S.add("sp", lambda e, i=i: e.dma_start(out=oTs, in_=Ot[i]), reads=[("Ot", i, hh) for hh in range(H)],
              writes=[("Qh",)], dma=True)
        C.proj_fm(wb["o"], 0, 8, lambda fc, bank: C.residual_add(0, 0, fc, bank),
                  rhs_of=lambda kc: oTs[:, kc, :], rhs_keys=lambda kc: ("Qh",))
        C.layernorm(0, 1)
        C.ffn(0, wb["gu0"], wb["dn0"])
        layer1_group(C, 1, i, None, uh, ubuf, cg, cv, wb["in"], wb["out"], wb["gu1"], wb["dn1"], out[i * G:(i + 1) * G, :])
    S.emit()
    return nc


def layer1_group(C, l, i, x_rows, uh, ubuf, cg, cv, w_in, w_out, w_gu, w_dn, out_rows):
    S = C.S
    big = C.big
    cw = C.small[:, SM_CONV:SM_CONV + 24].rearrange("p (j k) -> p j k", j=3)
    if x_rows is not None:
        C.load_xT(x_rows)
    C.layernorm(l, 0)
    for fc in range(0):
        S.add("dve", lambda e, fc=fc: e.tensor_copy(ubuf[:, fc, 0:2], uh[:, fc, 2 * i:2 * i + 2]),
              reads=[("uh", fc)], writes=[("ubuf", ub)])
    for s in range(2):
        views = []
        for part in range(3):
            views.append(C.slab(C.wcols(w_in, part * D + s * 512, 512), KC, 512))
        for o in range(4):
            fc = s * 4 + o
            banks = [C.bank(), C.bank(), C.bank()]
            for part in (1, 2, 0):
                view, wk = views[part]
                bk = banks[part]
                for kc in range(KC):
                    S.add("pe", lambda e, kc=kc, o=o, bk=bk, view=view: e.matmul(
                        C.ps[bk][:], lhsT=view[:, kc, o * 128:(o + 1) * 128], rhs=C.hT[:, kc, :],
                        start=(kc == 0), stop=(kc == KC - 1)), reads=[wk, ("hT", kc)], writes=[("ps", bk)])
            bb, bc, bu = banks
            ub = fc % 2
            S.add("act", lambda e, fc=fc, ub=ub: e.copy(ubuf[:, ub, 0:2], uh[:, fc, 2 * i:2 * i + 2]),
                  reads=[("uh", fc)], writes=[("ubuf", ub)])
            S.add("act", lambda e, bc=bc: e.copy(cg[:], C.ps[bc][:]), reads=[("ps", bc)], writes=["cg"])
            S.add("dve", lambda e, bu=bu, ub=ub: e.tensor_tensor(out=ubuf[:, ub, 2:2 + G], in0=C.ps[bu][:], in1=cg[:],
                                                                 op=ALU.mult),
                  reads=[("ps", bu), "cg"], writes=[("ubuf", ub)])
            cb = fc % 2
            S.add("dve", lambda e, fc=fc, cb=cb, ub=ub: e.tensor_scalar(out=cv[cb][:], in0=ubuf[:, ub, 0:G],
                                                                  scalar1=cw[:, 0, fc:fc + 1], scalar2=None,
                                                                  op0=ALU.mult),
                  reads=[("ubuf", ub), "small"], writes=[("cv", cb)])
            S.add("dve", lambda e, fc=fc, cb=cb, ub=ub: e.scalar_tensor_tensor(out=cv[cb][:], in0=ubuf[:, ub, 1:1 + G],
                                                                         scalar=cw[:, 1, fc:fc + 1], in1=cv[cb][:],
                                                                         op0=ALU.mult, op1=ALU.add),
                  reads=[("ubuf", ub), "small", ("cv", cb)], writes=[("cv", cb)])
            S.add("dve", lambda e, fc=fc, cb=cb, ub=ub: e.scalar_tensor_tensor(out=cv[cb][:], in0=ubuf[:, ub, 2:2 + G],
                                                                         scalar=cw[:, 2, fc:fc + 1], in1=cv[cb][:],
                                                                         op0=ALU.mult, op1=ALU.add),
                  reads=[("ubuf", ub), "small", ("cv", cb)], writes=[("cv", cb)])
            S.add("dve", lambda e, fc=fc, cb=cb, bb=bb: e.tensor_tensor(out=big[:, fc, :], in0=C.ps[bb][:], in1=cv[cb][:],
                                                                         op=ALU.mult),
                  reads=[("ps", bb), ("cv", cb)], writes=[("big", fc)])
    C.proj_fm(w_out, 0, 8, lambda fc, bank: C.residual_add(l, 0, fc, bank),
              rhs_of=lambda kc: big[:, kc, :], rhs_keys=lambda kc: ("big", kc))
    C.layernorm(l, 1)
    C.ffn(l, w_gu, w_dn)
    C.store_xT(out_rows)


def layer1_halo_u(C, l, uh, ctmp, w_in):
    S = C.S
    hval = C.small[:, SM_HVALID:SM_HVALID + 16]
    C.layernorm(l, 0, N=16)
    for part in range(2):
        def consume(oc, bank, part=part):
            if part == 0:
                S.add("act", lambda e: e.copy(uh[:, oc, :], C.ps[bank][:, 0:16]), reads=[("ps", bank)],
                      writes=[("uh", oc)])
            else:
                S.add("dve", lambda e: e.tensor_tensor(out=ctmp[:], in0=C.ps[bank][:, 0:16], in1=uh[:, oc, :],
                                                       op=ALU.mult), reads=[("ps", bank), ("uh", oc)], writes=["ctmp"])
                S.add("dve", lambda e: e.tensor_tensor(out=uh[:, oc, :], in0=ctmp[:], in1=hval, op=ALU.mult),
                      reads=["ctmp", "small"], writes=[("uh", oc)])
        C.proj_fm(w_in, D + part * D, 8, consume, N=16)


def _g_of_pos(p, half):
    return p if half == 1 else (p ^ 1)


def _tables(half):
    rb = np.zeros((NOWN, 4, NBLK), np.float32)
    npst = np.zeros((NOWN, 4, NBLK), np.float32)
    for i in range(NOWN):
        for qt in range(4):
            nbq = 2 * (2 * i + half) + qt // 2
            for pb in range(NBLK):
                gb = 2 * _g_of_pos(pb // 2, half) + pb % 2
                if gb < nbq:
                    npst[i, qt, pb] = NEG
                else:
                    rb[i, qt, pb] = -1e30
    tabs = np.concatenate([rb.reshape(-1), npst.reshape(-1)])[None, :].repeat(128, 0).astype(np.float32)
    rbh = np.zeros((128, NBLK), np.float32)
    nph = np.zeros((128, NBLK), np.float32)
    hpat = np.zeros((128, 64, 16), np.float32)
    kk = np.arange(128)
    for i in range(NOWN):
        gh = 2 * i + half - 1
        for t in range(2):
            col = 2 * i + t
            if gh < 0:
                continue
            gbq = 2 * gh + 1
            for pb in range(NBLK):
                gb = 2 * _g_of_pos(pb // 2, half) + pb % 2
                if gb < gbq:
                    nph[col, pb] = NEG
                else:
                    rbh[col, pb] = -1e30
                for sub in range(2):
                    kt = pb * 2 + sub
                    if gb < gbq:
                        hpat[:, kt, col] = 0.0
                    elif gb > gbq:
                        hpat[:, kt, col] = NEG
                    else:
                        hpat[:, kt, col] = np.where(sub * 128 + kk <= 254 + t, 0.0, NEG)
    tabs = np.concatenate([tabs, rbh, nph], axis=1).astype(np.float32)
    pat = np.zeros((128, 8, G), np.float32)
    k = np.arange(128)[:, None]
    q = np.arange(G)[None, :]
    qt = q // 128
    for ktw in range(8):
        if ktw < 4:
            pat[:, ktw, :] = 0.0 if half == 1 else NEG
        else:
            kt_ = ktw - 4
            kb = kt_ // 2
            qb = qt // 2
            kpos = (kt_ % 2) * 128 + k
            qpos = (qt % 2) * 128 + (q % 128)
            m = np.where(kb < qb, 0.0, np.where(kb > qb, NEG, np.where(kpos <= qpos, 0.0, NEG)))
            pat[:, ktw, :] = m
    sel = np.zeros((32, 32, 128), np.float32)
    for pb in range(32):
        sel[pb, pb, :] = 1.0
    import ml_dtypes
    bf = ml_dtypes.bfloat16
    patall = np.concatenate([pat.reshape(128, 8 * G), hpat.reshape(128, 1024)], axis=1)
    return tabs, patall.astype(bf), sel.reshape(32, 32 * 128).astype(bf)


def _small(c_b, b_ada_ls, nmix_ls, nffn_ls, q_gain, k_gain, conv_w, half):
    sm = np.zeros((128, NSM), np.float32)
    sm[:, SM_C:SM_C + 8] = c_b.reshape(8, 128).T
    for l, ba in enumerate(b_ada_ls):
        sm[:, SM_BADA + 48 * l:SM_BADA + 48 * (l + 1)] = ba.reshape(48, 128).T
    for l, v in enumerate(nmix_ls):
        sm[:, SM_NMIX + 8 * l:SM_NMIX + 8 * (l + 1)] = v.reshape(8, 128).T
    for l, v in enumerate(nffn_ls):
        sm[:, SM_NFFN + 8 * l:SM_NFFN + 8 * (l + 1)] = v.reshape(8, 128).T
    sm[:, SM_QG] = q_gain
    sm[:, SM_KG] = k_gain
    sm[:, SM_CONV:SM_CONV + 24] = conv_w.reshape(3, 8, 128).transpose(2, 0, 1).reshape(128, 24)
    hv = np.ones(16, np.float32)
    if half == 0:
        hv[0:2] = 0.0
    sm[:, SM_HVALID:SM_HVALID + 16] = hv[None, :]
    sm[:, SM_IDENT:SM_IDENT + 128] = np.eye(128, dtype=np.float32)
    return sm


_NC_CACHE = {}


def _get(name, fn):
    if name not in _NC_CACHE:
        _NC_CACHE[name] = fn()
    return _NC_CACHE[name]


def make_in_maps(x, c, w_ada, b_ada, norm_mix, norm_ffn, w_qkv, w_o, q_gain, k_gain, w_in, conv_w, w_out,
                 w_gate_up, w_down):
    in_maps = []
    for core in range(8):
        b, half = core // 2, core % 2
        perm = [_g_of_pos(p, half) for p in range(NPOS)]
        xseq = np.ascontiguousarray(x[b].reshape(NPOS, G, D)[perm].reshape(SEQ, D))
        xh = np.zeros((128, D), np.float32)
        for i in range(NOWN):
            g = 2 * i + half
            if g > 0:
                xh[2 * i:2 * i + 2] = x[b, g * G - 2:g * G]
            else:
                xh[2 * i:2 * i + 2] = x[b, 0:2]
        tabs, pat, sel = _tables(half)
        sm = _small(c[b], [b_ada[0], b_ada[1]], [norm_mix[0], norm_mix[1]], [norm_ffn[0], norm_ffn[1]],
                    q_gain[0], k_gain[0], conv_w[0], half)
        in_maps.append(dict(xseq=xseq, xhalo=xh, small=sm, tabs=tabs, pat=pat, sel=sel, w_ada=w_ada, w_qkv=w_qkv[0],
                            w_o=w_o[0], w_in=w_in[0], w_out=w_out[0], w_gu=w_gate_up, w_dn=w_down))
    return in_maps


def kernel(x, c, w_ada, b_ada, norm_mix, norm_ffn, w_qkv, w_o, q_gain, k_gain, w_in, conv_w, w_out,
           w_gate_up, w_down):
    f = lambda a: np.ascontiguousarray(np.asarray(a, dtype=np.float32))
    args = list(map(f, (x, c, w_ada, b_ada, norm_mix, norm_ffn, w_qkv, w_o, q_gain, k_gain, w_in, conv_w, w_out,
                        w_gate_up, w_down)))
    x = args[0]
    in_maps = make_in_maps(*args)
    nc = _get("F", build_fused)
    res = run_bass_kernel_spmd(nc, in_maps, core_ids=list(range(8)))
    out = np.zeros_like(x)
    for core in range(8):
        b, half = core // 2, core % 2
        o = np.asarray(res.results[core]["out"]).reshape(NOWN, G, D)
        for i in range(NOWN):
            g = 2 * i + half
            out[b, g * G:(g + 1) * G] = o[i]
    return out
```

```python
import contextlib
import numpy as np
import concourse.bass as bass
import concourse.mybir as mybir
from concourse.bass_utils import run_bass_kernel_spmd

F32 = mybir.dt.float32
BF16 = mybir.dt.bfloat16
ALU = mybir.AluOpType
AF = mybir.ActivationFunctionType

D = 1024
KC = 8
G = 512
H = 8
DH = 128
DFF = 2816
NJ = 22
NPOS = 16
NOWN = 8
NBLK = 32
SEQ = 8192
NEG = -30000.0
EPS = 1e-6
NW = 4
ENGS = ["pe", "act", "dve", "pool", "sp"]
INORDER = ("pe", "act", "dve")


class Sched:
    def __init__(self, nc, n_dma_sems=6):
        self.nc = nc
        self.ops = []
        self.last_write = {}
        self.readers = {}
        self.n_dma_sems = n_dma_sems

    def add(self, eng, fn, reads=(), writes=(), dma=False):
        oid = len(self.ops)
        deps = set()
        for b in reads:
            if b in self.last_write:
                deps.add(self.last_write[b])
        for b in writes:
            if b in self.last_write:
                deps.add(self.last_write[b])
            for r in self.readers.get(b, {}).values():
                deps.update(r)
        deps.discard(oid)
        self.ops.append(dict(id=oid, eng=eng, fn=fn, deps=deps, dma=dma, signal=False))
        for b in reads:
            rd = self.readers.setdefault(b, {})
            if dma or eng not in INORDER:
                rd.setdefault((eng, "dma"), []).append(oid)
            else:
                rd[eng] = [oid]
        for b in writes:
            self.last_write[b] = oid
            self.readers[b] = {}
        return oid

    def emit(self, final_wait_eng="sp"):
        nc = self.nc
        ops = self.ops

        def needs_sync(p, ceng):
            if p["dma"]:
                return True
            if p["eng"] != ceng:
                return True
            return p["eng"] in ("act", "dve", "pool")

        for op in ops:
            for d in op["deps"]:
                p = ops[d]
                if needs_sync(p, op["eng"]):
                    p["signal"] = True
            if op["dma"]:
                op["signal"] = True
        eng_count = {e: 0 for e in ENGS}
        dma_count = {}
        dma_rr = {e: 0 for e in ENGS}
        sem_keys = {}
        for op in ops:
            if not op["signal"]:
                continue
            if op["dma"]:
                k = dma_rr[op["eng"]] % self.n_dma_sems
                dma_rr[op["eng"]] += 1
                key = ("dma", op["eng"], k)
                prev = dma_count.get(key, 0)
                op["prev_on_sem"] = (key, prev)
                dma_count[key] = prev + 16
                op["sig"] = (key, prev + 16)
            else:
                key = ("eng", op["eng"])
                eng_count[op["eng"]] += 1
                op["sig"] = (key, eng_count[op["eng"]])
            sem_keys[key] = None
        with contextlib.ExitStack() as st:
            sems = {}
            for key in sem_keys:
                sems[key] = st.enter_context(nc.semaphore("s_" + "_".join(str(x) for x in key)))
            block = st.enter_context(nc.Block())
            per_eng = {e: [o for o in ops if o["eng"] == e] for e in ENGS}

            def body(ename):
                def run(eng):
                    waited = {}

                    def wait(key, val):
                        if waited.get(key, 0) >= val:
                            return
                        eng.wait_ge(sems[key], val)
                        waited[key] = val

                    for op in per_eng[ename]:
                        need = {}
                        for d in op["deps"]:
                            p = ops[d]
                            if not needs_sync(p, ename):
                                continue
                            key, val = p["sig"]
                            need[key] = max(need.get(key, 0), val)
                        if op["dma"]:
                            key, prev = op["prev_on_sem"]
                            if prev > 0:
                                need[key] = max(need.get(key, 0), prev)
                        for key, val in need.items():
                            wait(key, val)
                        ins = op["fn"](eng)
                        if op["signal"]:
                            key, val = op["sig"]
                            ins.then_inc(sems[key], 16 if op["dma"] else 1)
                    if ename == final_wait_eng:
                        for key, val in dma_count.items():
                            wait(key, val)
                        for e in ENGS:
                            if eng_count[e] > 0 and e != ename:
                                wait(("eng", e), eng_count[e])
                return run

            block.tensor(body("pe"))
            block.scalar(body("act"))
            block.vector(body("dve"))
            block.gpsimd(body("pool"))
            block.sync(body("sp"))


SM_C = 0
SM_BADA = 8
SM_NMIX = 104
SM_NFFN = 120
SM_QG = 136
SM_KG = 137
SM_CONV = 138
SM_HVALID = 162
SM_IDENT = 178
NSM = 306


class Ctx:
    def __init__(self, nc):
        self.nc = nc
        self.S = Sched(nc)
        self.wslot = 0
        self.bankrr = 0
        a = lambda n_, sh_, d_: nc.alloc_sbuf_tensor('sb_' + n_, sh_, d_)
        self.small = a("small", [128, NSM], F32)
        self.ident_bf = a("ident_bf", [128, 128], BF16)
        self.ones_bf = a("ones_bf", [128, 128], BF16)
        self.eps_t = a("eps_t", [128, 1], F32)
        self.wring = [a(f"wring{i}", [128, 4096], BF16) for i in range(NW)]
        self.xin = a("xin", [128, 4, D], F32)
        self.xT = a("xT", [128, KC, G], F32)
        self.sq = a("sq", [128, KC, G], BF16)
        self.hT = a("hT", [128, KC, G], BF16)
        self.tmp = [a(f"tmp{i}", [128, G], F32) for i in range(2)]
        self.std = a("std", [128, G], F32)
        self.rstd = a("rstd", [128, G], F32)
        self.big = a("big", [128, 24, G], BF16)
        self.mod = a("mod", [128, 2, 48], F32)
        self.modG = a("modG", [128, 2, 2, KC], F32)
        self.scbf = a("scbf", [128, KC], BF16)
        self.qgs = a("qgs", [128, 1], F32)
        self.pspair = [nc.alloc_psum_tensor(f"psp{i}", [128, 1024], F32) for i in range(3)]
        self.ps = []
        for i in range(3):
            self.ps.append(self.pspair[i][:, 0:512])
            self.ps.append(self.pspair[i][:, 512:1024])
        self.ps.append(nc.alloc_psum_tensor("ps6", [128, 512], F32)[:, :])
        self.psb = nc.alloc_psum_tensor("psb", [128, 1024], BF16)

    @property
    def ident(self):
        return self.small[:, SM_IDENT:SM_IDENT + 128]

    def bank(self, lo=0, hi=7):
        b = lo + self.bankrr % (hi - lo)
        self.bankrr += 1
        return b

    def slab(self, src3d, kparts, ncols):
        rk = []
        if isinstance(src3d, tuple):
            src3d, rk = src3d
        slot = self.wslot % NW
        self.wslot += 1
        view = self.wring[slot][:, 0:kparts * ncols].rearrange("p (k n) -> p k n", k=kparts)
        self.S.add("pool", lambda e: e.dma_start(out=view, in_=src3d), reads=rk, writes=[("w", slot)], dma=True)
        return view, ("w", slot)

    def wcols(self, w2d, c0, ncols, r0=0, kparts=KC):
        rk = []
        if isinstance(w2d, tuple):
            w2d, rk = w2d
        ap = w2d[r0:r0 + kparts * 128, c0:c0 + ncols].rearrange("(k p) n -> p k n", p=128)
        return (ap, rk) if rk else ap

    def precast(self, name, w2d, rows_per):
        R, Ncol = w2d.shape
        wb = self.nc.dram_tensor("Wb_" + name, [R, Ncol], BF16).ap()
        keys = []
        r = 0
        while r < R:
            r1 = min(R, r + rows_per)
            key = ("Wb", name, r)
            self.S.add("pool", lambda e, r=r, r1=r1: e.dma_start(out=wb[r:r1, :], in_=w2d[r:r1, :]),
                       writes=[key], dma=True)
            keys.append(key)
            r = r1
        return (wb, keys)

    def setup(self, small_ap):
        S = self.S
        S.add("sp", lambda e: e.dma_start(out=self.small[:], in_=small_ap), writes=["small"], dma=True)
        S.add("dve", lambda e: e.memset(self.ones_bf[:], 1.0), writes=["ones"])
        S.add("dve", lambda e: e.memset(self.eps_t[:], EPS), writes=["eps"])
        S.add("dve", lambda e: e.tensor_copy(self.ident_bf[:], self.ident), reads=["small"], writes=["identbf"])
        S.add("act", lambda e: e.activation(out=self.scbf[:], in_=self.small[:, SM_C:SM_C + 8], func=AF.Silu),
              reads=["small"], writes=["scbf"])
        S.add("act", lambda e: e.mul(self.qgs[:], self.small[:, SM_QG:SM_QG + 1], DH ** -0.5),
              reads=["small"], writes=["qgs"])

    def adaln(self, w_ada_l, l):
        for _ in self.adaln_gen(w_ada_l, l):
            pass

    def adaln_gen(self, w_ada_l, l):
        S = self.S
        bank = 6
        c0 = 256
        first = True
        for s in range(12):
            if s > 0:
                yield
            view, wk = self.slab(self.wcols(w_ada_l, s * 512, 512), KC, 512)
            for o in range(4):
                oc = s * 4 + o
                for kc in range(KC):
                    wr = ["ps2mod"] + ([("ps", bank)] if first else [])
                    first = False
                    S.add("pe", lambda e, oc=oc, kc=kc, o=o, view=view: e.matmul(
                        self.ps[bank][:, c0 + oc:c0 + oc + 1], lhsT=view[:, kc, o * 128:(o + 1) * 128],
                        rhs=self.scbf[:, kc:kc + 1], start=(kc == 0), stop=(kc == KC - 1)),
                        reads=[wk, "scbf"], writes=wr)
        mod = self.mod
        S.add("dve", lambda e: e.tensor_tensor(out=mod[:, l, :], in0=self.ps[bank][:, c0:c0 + 48],
                                               in1=self.small[:, SM_BADA + 48 * l:SM_BADA + 48 * (l + 1)], op=ALU.add),
              reads=["ps2mod", ("ps", bank), "small"], writes=[("mod", l)])
        S.add("dve", lambda e: e.scalar_tensor_tensor(out=self.modG[:, l, 0, :], in0=mod[:, l, 8:16], scalar=1.0,
                                                      in1=self.small[:, SM_NMIX + 8 * l:SM_NMIX + 8 * (l + 1)],
                                                      op0=ALU.add, op1=ALU.mult),
              reads=[("mod", l), "small"], writes=[("modG", l, 0)])
        S.add("dve", lambda e: e.scalar_tensor_tensor(out=self.modG[:, l, 1, :], in0=mod[:, l, 32:40], scalar=1.0,
                                                      in1=self.small[:, SM_NFFN + 8 * l:SM_NFFN + 8 * (l + 1)],
                                                      op0=ALU.add, op1=ALU.mult),
              reads=[("mod", l), "small"], writes=[("modG", l, 1)])

    def modcols(self, l, which):
        base = 0 if which == 0 else 24
        return (self.modG[:, l, which, :], self.mod[:, l, base:base + 8], self.mod[:, l, base + 16:base + 24],
                [("mod", l), ("modG", l, which)])

    def load_xT(self, x_rows, N=G, ntile=4):
        S = self.S
        xin = self.xin
        S.add("sp", lambda e: e.dma_start(out=xin[:, 0:ntile, :], in_=x_rows.rearrange("(t p) f -> p t f", p=128)),
              writes=["xin"], dma=True)
        for kc in range(KC):
            bank = self.bank()
            for t in range(ntile):
                S.add("pe", lambda e, kc=kc, t=t, bank=bank: e.transpose(
                    self.ps[bank][:, t * 128:(t + 1) * 128], xin[:, t, kc * 128:(kc + 1) * 128], self.ident),
                    reads=["xin", "small"], writes=[("ps", bank)])
            eng = "dve" if kc % 2 == 0 else "act"
            if eng == "dve":
                S.add("dve", lambda e, kc=kc, bank=bank: e.tensor_copy(self.xT[:, kc, 0:ntile * 128],
                                                                        self.ps[bank][:, 0:ntile * 128]),
                      reads=[("ps", bank)], writes=[("xT", kc)])
            else:
                S.add("act", lambda e, kc=kc, bank=bank: e.copy(self.xT[:, kc, 0:ntile * 128],
                                                                 self.ps[bank][:, 0:ntile * 128]),
                      reads=[("ps", bank)], writes=[("xT", kc)])

    def store_xT(self, out_rows, ntile=4):
        S = self.S
        xin = self.xin
        for t in range(ntile):
            for hf in range(2):
                bank = self.bank()
                for k4 in range(4):
                    kc = hf * 4 + k4
                    S.add("pe", lambda e, kc=kc, t=t, bank=bank, k4=k4: e.transpose(
                        self.ps[bank][:, k4 * 128:(k4 + 1) * 128], self.xT[:, kc, t * 128:(t + 1) * 128], self.ident),
                        reads=[("xT", kc), "small"], writes=[("ps", bank)])
                if hf == 0:
                    S.add("dve", lambda e, t=t, bank=bank, hf=hf: e.tensor_copy(xin[:, t, hf * 512:(hf + 1) * 512],
                                                                                 self.ps[bank][:]),
                          reads=[("ps", bank)], writes=["xin"])
                else:
                    S.add("act", lambda e, t=t, bank=bank, hf=hf: e.copy(xin[:, t, hf * 512:(hf + 1) * 512],
                                                                          self.ps[bank][:]),
                          reads=[("ps", bank)], writes=["xin"])
        S.add("sp", lambda e: e.dma_start(out=out_rows.rearrange("(t p) f -> p t f", p=128), in_=xin[:, 0:ntile, :]),
              reads=["xin"], writes=[("out", id(out_rows))], dma=True)

    def layernorm(self, l, which, N=G):
        S = self.S
        Gc, Sc, _, mkeys = self.modcols(l, which)
        for kc in range(KC):
            S.add("act", lambda e, kc=kc: e.activation(out=self.sq[:, kc, :N], in_=self.xT[:, kc, :N], func=AF.Square),
                  reads=[("xT", kc)], writes=[("sq", kc)])
        bank = self.bank()
        for kc in range(KC):
            S.add("pe", lambda e, kc=kc: e.matmul(self.ps[bank][:, :N], lhsT=self.ones_bf[:], rhs=self.sq[:, kc, :N],
                                                  start=(kc == 0), stop=(kc == KC - 1)),
                  reads=[("sq", kc), "ones"], writes=[("ps", bank)])
        S.add("act", lambda e: e.activation(out=self.std[:, :N], in_=self.ps[bank][:, :N], func=AF.Ln,
                                            bias=self.eps_t[:, 0:1], scale=1.0 / D),
              reads=[("ps", bank), "eps"], writes=["std"])
        S.add("act", lambda e: e.activation(out=self.rstd[:, :N], in_=self.std[:, :N], func=AF.Exp, scale=-0.5),
              reads=["std"], writes=["rstd"])
        for kc in range(KC):
            tb = kc % 2
            S.add("dve", lambda e, kc=kc, tb=tb: e.tensor_tensor(out=self.tmp[tb][:, :N], in0=self.xT[:, kc, :N],
                                                                 in1=self.rstd[:, :N], op=ALU.mult),
                  reads=[("xT", kc), "rstd"], writes=[("tmp", tb)])
            S.add("act", lambda e, kc=kc, tb=tb: e.activation(out=self.hT[:, kc, :N], in_=self.tmp[tb][:, :N],
                                                              func=AF.Identity, scale=Gc[:, kc:kc + 1],
                                                              bias=Sc[:, kc:kc + 1]),
                  reads=[("tmp", tb)] + mkeys, writes=[("hT", kc)])

    def proj_fm(self, w2d, c0, nchunks, consume, rhs_of=None, rhs_keys=None, N=G, kparts=KC, r0=0):
        S = self.S
        if rhs_of is None:
            rhs_of = lambda kc: self.hT[:, kc, :N]
            rhs_keys = lambda kc: ("hT", kc)
        oc = 0
        while oc < nchunks:
            nch = min(4, nchunks - oc)
            view, wk = self.slab(self.wcols(w2d, c0 + oc * 128, nch * 128, r0=r0, kparts=kparts), kparts, nch * 128)
            for o in range(nch):
                bank = self.bank()
                for kc in range(kparts):
                    S.add("pe", lambda e, kc=kc, o=o, bank=bank, view=view: e.matmul(
                        self.ps[bank][:, :N], lhsT=view[:, kc, o * 128:(o + 1) * 128], rhs=rhs_of(kc),
                        start=(kc == 0), stop=(kc == kparts - 1)),
                        reads=[wk, rhs_keys(kc)], writes=[("ps", bank)])
                consume(oc + o, bank)
            oc += nch

    def residual_add(self, l, which, fc, bank, N=G):
        _, _, gate, mkeys = self.modcols(l, which)
        self.S.add("dve", lambda e: e.scalar_tensor_tensor(out=self.xT[:, fc, :N], in0=self.ps[bank][:, :N],
                                                           scalar=gate[:, fc:fc + 1], in1=self.xT[:, fc, :N],
                                                           op0=ALU.mult, op1=ALU.add),
                   reads=[("ps", bank), ("xT", fc)] + mkeys, writes=[("xT", fc)])

    def ffn(self, l, w_gu, w_dn, N=G):
        S = self.S
        big = self.big
        j = 0
        while j < NJ:
            nch = min(4, NJ - j)
            gview, gk = self.slab(self.wcols(w_gu, j * 128, nch * 128), KC, nch * 128)
            uview, uk = self.slab(self.wcols(w_gu, DFF + j * 128, nch * 128), KC, nch * 128)
            for o in range(nch):
                jj = j + o
                bg = self.bank()
                bu = self.bank()
                for kc in range(KC):
                    S.add("pe", lambda e, kc=kc, o=o, bg=bg, gview=gview: e.matmul(
                        self.ps[bg][:, :N], lhsT=gview[:, kc, o * 128:(o + 1) * 128], rhs=self.hT[:, kc, :N],
                        start=(kc == 0), stop=(kc == KC - 1)), reads=[gk, ("hT", kc)], writes=[("ps", bg)])
                for kc in range(KC):
                    S.add("pe", lambda e, kc=kc, o=o, bu=bu, uview=uview: e.matmul(
                        self.ps[bu][:, :N], lhsT=uview[:, kc, o * 128:(o + 1) * 128], rhs=self.hT[:, kc, :N],
                        start=(kc == 0), stop=(kc == KC - 1)), reads=[uk, ("hT", kc)], writes=[("ps", bu)])
                tb = jj % 2
                S.add("act", lambda e, bg=bg, tb=tb: e.activation(out=self.tmp[tb][:, :N], in_=self.ps[bg][:, :N],
                                                                  func=AF.Silu),
                      reads=[("ps", bg)], writes=[("tmp", tb)])
                S.add("dve", lambda e, bu=bu, tb=tb, jj=jj: e.tensor_tensor(out=big[:, jj, :N], in0=self.ps[bu][:, :N],
                                                                             in1=self.tmp[tb][:, :N], op=ALU.mult),
                      reads=[("ps", bu), ("tmp", tb)], writes=[("big", jj)])
            j += nch
        for fp in range(4):
            v0, k0 = self.slab(self.wcols(w_dn, fp * 256, 256, r0=0, kparts=11), 11, 256)
            v1, k1 = self.slab(self.wcols(w_dn, fp * 256, 256, r0=11 * 128, kparts=11), 11, 256)
            banks = [self.bank(), self.bank()]
            for jj in range(NJ):
                view, wk = (v0, k0) if jj < 11 else (v1, k1)
                for f2 in range(2):
                    S.add("pe", lambda e, jj=jj, f2=f2, view=view, b=banks[f2]: e.matmul(
                        self.ps[b][:, :N], lhsT=view[:, jj % 11, f2 * 128:(f2 + 1) * 128], rhs=big[:, jj, :N],
                        start=(jj == 0), stop=(jj == NJ - 1)), reads=[wk, ("big", jj)], writes=[("ps", banks[f2])])
            for f2 in range(2):
                self.residual_add(l, 1, fp * 2 + f2, banks[f2], N)


def build_fused():
    nc = bass.Bass("TRN2", target_bir_lowering=False)
    dt = nc.dram_tensor
    xseq = dt("xseq", [SEQ, D], F32, kind="ExternalInput").ap()
    xhalo = dt("xhalo", [128, D], F32, kind="ExternalInput").ap()
    small_ap = dt("small", [128, NSM], F32, kind="ExternalInput").ap()
    tabs_ap = dt("tabs", [128, 2048 + 64], F32, kind="ExternalInput").ap()
    pat_ap = dt("pat", [128, 8 * G + 64 * 16], BF16, kind="ExternalInput").ap()
    sel_ap = dt("sel", [128, 32 * 128], BF16, kind="ExternalInput").ap()
    w_ada = dt("w_ada", [2, D, 6 * D], F32, kind="ExternalInput").ap()
    w_qkv = dt("w_qkv", [D, 3 * D], F32, kind="ExternalInput").ap()
    w_o = dt("w_o", [D, D], F32, kind="ExternalInput").ap()
    w_in = dt("w_in", [D, 3 * D], F32, kind="ExternalInput").ap()
    w_out = dt("w_out", [D, D], F32, kind="ExternalInput").ap()
    w_gu = dt("w_gu", [2, D, 2 * DFF], F32, kind="ExternalInput").ap()
    w_dn = dt("w_dn", [2, DFF, D], F32, kind="ExternalInput").ap()
    out = dt("out", [NOWN * G, D], F32, kind="ExternalOutput").ap()
    Kt = dt("Kt", [H, DH, SEQ], BF16).ap()
    Vs = dt("Vs", [SEQ, D], BF16).ap()
    Qt = dt("Qt", [H, DH, NOWN * G], BF16).ap()
    Ot = dt("Ot", [NOWN, DH, H, G], BF16).ap()

    C = Ctx(nc)
    S = C.S
    a = lambda n_, sh_, d_: nc.alloc_sbuf_tensor('sb_' + n_, sh_, d_)
    tabs = a("tabs", [128, 2048 + 64], F32)
    pat = a("pat", [128, 8 * G + 64 * 16], BF16)
    sel = a("sel", [128, 32, 128], BF16)
    kmean_f = a("kmean_f", [128, H, NBLK], F32)
    kmean_bf = a("kmean_bf", [128, H, NBLK], BF16)
    Kh = a("Kh", [128, SEQ], BF16)
    Vh = a("Vh", [128, 64, DH], BF16)
    Qh = a("Qh", [128, NOWN * G], BF16)
    QhaloT = a("QhaloT", [128, H, 16], BF16)
    oTh = a("oTh", [128, H, 16], BF16)
    pTp = [a(f"pTp{i}", [128, 2, G], BF16) for i in range(2)]
    Rsb = a("Rsb", [128, 4, NBLK], F32)
    max8 = a("max8", [128, 4, 8], F32)
    mb = a("mb", [128, 4, NBLK], BF16)
    mbT = [a(f"mbT{i}", [128, G], BF16) for i in range(2)]
    oTsb = [a(f"oTsb{i}", [128, G], BF16) for i in range(2)]
    uh = a("uh", [128, KC, 16], F32)
    ctmp = a("ctmp", [128, 16], F32)
    cg = a("cg", [128, G], F32)
    ubuf = a("ubuf", [128, 2, 2 + G], F32)
    cv = [a(f"cv{i}", [128, G], F32) for i in range(2)]
    big = C.big
    recip = C.std

    C.setup(small_ap)
    S.add("sp", lambda e: e.dma_start(out=tabs[:], in_=tabs_ap), writes=["tabs"], dma=True)
    S.add("sp", lambda e: e.dma_start(out=pat[:], in_=pat_ap), writes=["pat"], dma=True)
    S.add("sp", lambda e: e.dma_start(out=sel[:].rearrange("p a b -> p (a b)"), in_=sel_ap), writes=["sel"], dma=True)
    S.add("dve", lambda e: e.memset(kmean_f[:], 0.0), writes=["kmean_f"])
    wb_qkv = C.precast("qkv", w_qkv, 256)
    C.adaln(w_ada[0], 0)
    kg = C.small[:, SM_KG:SM_KG + 1]
    acc = a("acc", [128, 2, G], F32)
    for i_ in range(2):
        S.add("dve", lambda e, i_=i_: e.memset(mbT[i_][:], 0.0), writes=[("mbT", i_)])
    ones_f = a("ones_f", [128, 128], F32)
    S.add("dve", lambda e: e.memset(ones_f[:], 1.0), writes=["ones_f"])

    def qk_head(hh, dst_ap, dst_key, gain_ap, gain_keys, wview, wk, o, kmean_pos=None, N=G):
        bank = C.bank()
        for kc in range(KC):
            S.add("pe", lambda e, kc=kc: e.matmul(C.ps[bank][:, :N], lhsT=wview[:, kc, o * 128:(o + 1) * 128],
                                                  rhs=C.hT[:, kc, :N], start=(kc == 0), stop=(kc == KC - 1)),
                  reads=[wk, ("hT", kc)], writes=[("ps", bank)])
        sqk = C.sq[:, hh, :N]
        S.add("act", lambda e: e.activation(out=sqk, in_=C.ps[bank][:, :N], func=AF.Square),
              reads=[("ps", bank)], writes=[("sq", hh)])
        b2 = C.bank()
        S.add("pe", lambda e: e.matmul(C.ps[b2][:, :N], lhsT=C.ones_bf[:], rhs=sqk, start=True, stop=True),
              reads=["ones", ("sq", hh)], writes=[("ps", b2)])
        if hh % 2 == 0:
            stdb, stdk, rstdb, rstdk = C.std, "std", C.rstd, "rstd"
        else:
            stdb, stdk, rstdb, rstdk = C.tmp[0], ("tmp", 0), C.tmp[1], ("tmp", 1)
        S.add("act", lambda e: e.activation(out=stdb[:, :N], in_=C.ps[b2][:, :N], func=AF.Ln, bias=C.eps_t[:, 0:1],
                                            scale=1.0 / DH), reads=[("ps", b2), "eps"], writes=[stdk])
        S.add("act", lambda e: e.activation(out=rstdb[:, :N], in_=stdb[:, :N], func=AF.Exp, scale=-0.5),
              reads=[stdk], writes=[rstdk])
        if kmean_pos is None:
            S.add("dve", lambda e: e.scalar_tensor_tensor(out=dst_ap, in0=C.ps[bank][:, :N], scalar=gain_ap,
                                                          in1=rstdb[:, :N], op0=ALU.mult, op1=ALU.mult),
                  reads=[("ps", bank), rstdk] + gain_keys, writes=[dst_key])
        else:
            for bb in range(2):
                pb = kmean_pos * 2 + bb
                S.add("dve", lambda e, bb=bb, pb=pb: e.scalar_tensor_tensor(
                    out=dst_ap[:, bb * 256:(bb + 1) * 256], in0=C.ps[bank][:, bb * 256:(bb + 1) * 256],
                    scalar=gain_ap, in1=rstdb[:, bb * 256:(bb + 1) * 256], op0=ALU.mult, op1=ALU.mult,
                    accum_out=kmean_f[:, hh, pb:pb + 1]),
                    reads=[("ps", bank), rstdk, "kmean_f"] + gain_keys, writes=[dst_key, "kmean_f"])

    for p in range(NPOS):
        own = (p % 2 == 1)
        C.load_xT(xseq[p * G:(p + 1) * G, :])
        C.layernorm(0, 0)
        for s in range(2):
            wview, wk = C.slab(C.wcols(wb_qkv, D + s * 512, 512), KC, 512)
            for o in range(4):
                hh = s * 4 + o
                qk_head(hh, big[:, hh, :], ("big", hh), kg, ["small"], wview, wk, o, kmean_pos=p)
        S.add("sp", lambda e, p=p: e.dma_start(out=Kt.rearrange("h d t -> d h t")[:, :, p * G:(p + 1) * G],
                                               in_=big[:, 0:8, :]),
              reads=[("big", s_) for s_ in range(8)], writes=[("Kt", p)], dma=True)
        for hf in range(2):
            wview, wk = C.slab(C.wcols(wb_qkv, 2 * D + hf * 512, 512), KC, 512)
            for t in range(4):
                bank = C.bank()
                for kc in range(KC):
                    S.add("pe", lambda e, kc=kc, t=t, bank=bank, wview=wview: e.matmul(
                        C.ps[bank][:], lhsT=C.hT[:, kc, t * 128:(t + 1) * 128], rhs=wview[:, kc, :],
                        start=(kc == 0), stop=(kc == KC - 1)), reads=[wk, ("hT", kc)], writes=[("ps", bank)])
                slot = 16 + 2 * t + hf
                if t % 2 == 0:
                    S.add("act", lambda e, bank=bank, slot=slot: e.copy(big[:, slot, :], C.ps[bank][:]),
                          reads=[("ps", bank)], writes=[("big", slot)])
                else:
                    S.add("dve", lambda e, bank=bank, slot=slot: e.tensor_copy(big[:, slot, :], C.ps[bank][:]),
                          reads=[("ps", bank)], writes=[("big", slot)])
        S.add("sp", lambda e, p=p: e.dma_start(
            out=Vs[p * G:(p + 1) * G, :].rearrange("(t k) (hf c) -> k t hf c", k=128, hf=2),
            in_=big[:, 16:24, :].rearrange("p (t hf) c -> p t hf c", hf=2)),
            reads=[("big", s_) for s_ in range(16, 24)], writes=[("Vs", p)], dma=True)
        if own:
            i = p // 2
            for s in range(2):
                wview, wk = C.slab(C.wcols(wb_qkv, s * 512, 512), KC, 512)
                for o in range(4):
                    hh = s * 4 + o
                    qk_head(hh, big[:, 8 + hh, :], ("big", 8 + hh), C.qgs[:, 0:1], ["qgs"], wview, wk, o)
            S.add("sp", lambda e, i=i: e.dma_start(out=Qt.rearrange("h d t -> d h t")[:, :, i * G:(i + 1) * G],
                                                   in_=big[:, 8:16, :]),
                  reads=[("big", s_) for s_ in range(8, 16)], writes=[("Qt", i)], dma=True)
    C.load_xT(xhalo, ntile=1)
    C.layernorm(0, 0, N=16)
    for s in range(2):
        wview, wk = C.slab(C.wcols(wb_qkv, s * 512, 512), KC, 512)
        for o in range(4):
            hh = s * 4 + o
            qk_head(hh, QhaloT[:, hh, :], ("QhaloT", hh), C.qgs[:, 0:1], ["qgs"], wview, wk, o, N=16)
    S.add("dve", lambda e: e.tensor_copy(kmean_bf[:], kmean_f[:]), reads=["kmean_f"], writes=["kmean_bf"])

    rb = tabs[:, 0:1024].rearrange("p (i q n) -> p i q n", i=NOWN, q=4)
    npast = tabs[:, 1024:2048].rearrange("p (i q n) -> p i q n", i=NOWN, q=4)
    rb_h = tabs[:, 2048:2080]
    np_h = tabs[:, 2080:2112]
    patg = pat[:, 0:8 * G].rearrange("p (a b) -> p a b", a=8)
    hpat = pat[:, 8 * G:8 * G + 1024].rearrange("p (a b) -> p a b", a=64)
    RB = 6

    def desc_group(hh, i):
        return dict(N=G, nq=4, qrows=128, q_ap=Qh[:, i * G:(i + 1) * G], q_keys=[("Qh",)],
                    rb=rb[:, i, :, :], npst=npast[:, i, :, :], nkt=8 * i + 8,
                    pat_of=(lambda kt: patg[:, kt - 8 * i, :] if kt >= 8 * i else None), i=i, hh=hh)

    def desc_halo(hh):
        return dict(N=16, nq=1, qrows=16, q_ap=QhaloT[:, hh, :], q_keys=[("QhaloT", hh)],
                    rb=rb_h[0:16, :].rearrange("p (q n) -> p q n", q=1), npst=np_h[0:16, :].rearrange("p (q n) -> p q n", q=1),
                    nkt=64, pat_of=(lambda kt: hpat[:, kt, :]), i=None, hh=hh)

    def route(dsc, par):
        hh, N, nq, qr = dsc["hh"], dsc["N"], dsc["nq"], dsc["qrows"]
        qw = min(N, 128)
        for qt in range(nq):
            S.add("pe", lambda e, qt=qt: e.matmul(C.ps[RB][0:qr, qt * NBLK:(qt + 1) * NBLK],
                                                  lhsT=dsc["q_ap"][:, qt * qw:(qt + 1) * qw],
                                                  rhs=kmean_bf[:, hh, :], start=True, stop=True),
                  reads=dsc["q_keys"] + ["kmean_bf"], writes=[("ps", RB)])
        S.add("dve", lambda e: e.tensor_tensor(out=Rsb[0:qr, 0:nq, :],
                                               in0=C.ps[RB][0:qr, 0:nq * NBLK].rearrange("p (q n) -> p q n", q=nq),
                                               in1=dsc["rb"], op=ALU.add),
              reads=[("ps", RB), "tabs"], writes=["Rsb"])
        for qt in range(nq):
            S.add("dve", lambda e, qt=qt: e.max(out=max8[0:qr, qt, :], in_=Rsb[0:qr, qt, :]), reads=["Rsb"],
                  writes=[("max8", qt)])
            S.add("dve", lambda e, qt=qt: e.scalar_tensor_tensor(out=mb[0:qr, qt, :], in0=Rsb[0:qr, qt, :],
                                                                 scalar=max8[0:qr, qt, 2:3], in1=dsc["npst"][:, qt, :],
                                                                 op0=ALU.is_lt, op1=ALU.mult),
                  reads=["Rsb", ("max8", qt), "tabs"], writes=[("mb", qt)])
        for qt in range(nq):
            S.add("pe", lambda e, qt=qt: e.transpose(C.psb[0:32, qt * 128:qt * 128 + qr], mb[0:qr, qt, :],
                                                     C.ident_bf[0:qr, 0:qr]),
                  reads=[("mb", qt), "identbf"], writes=["psb"])
        S.add("act", lambda e: e.copy(mbT[par][0:32, 0:N], C.psb[0:32, 0:N]), reads=["psb"], writes=[("mbT", par)])

    def attn(dsc, par):
        hh, N, nkt = dsc["hh"], dsc["N"], dsc["nkt"]
        ob = 4 + par
        db = 4 + (1 - par)
        npair = nkt // 2

        def qkpair(pr):
            X = pr % 2
            for sub in range(2):
                kt = 2 * pr + sub
                sb = 2 * X + sub
                pt = dsc["pat_of"](kt)
                S.add("pe", lambda e, kt=kt, sb=sb: e.matmul(C.ps[sb][:, :N], lhsT=Kh[:, kt * 128:(kt + 1) * 128],
                                                             rhs=dsc["q_ap"], start=True, stop=False),
                      reads=[("Kh",)] + dsc["q_keys"], writes=[("ps", sb)])
                S.add("pe", lambda e, sb=sb, pt=pt: e.matmul(C.ps[sb][:, :N], lhsT=sel[:, pr, :], rhs=mbT[par][:, 0:N],
                                                             start=False, stop=(pt is None)),
                      reads=["sel", ("mbT", par)], writes=[("ps", sb)])
                if pt is not None:
                    S.add("pe", lambda e, sb=sb, pt=pt: e.matmul(C.ps[sb][:, :N], lhsT=C.ident_bf[:], rhs=pt, start=False,
                                                                 stop=True),
                          reads=["identbf", "pat"], writes=[("ps", sb)])
            scv = C.pspair[X][:, :].rearrange("p (b n) -> p b n", b=2)[:, :, 0:N]
            S.add("act", lambda e: e.activation(out=pTp[X][:, :, 0:N], in_=scv, func=AF.Exp),
                  reads=[("ps", 2 * X), ("ps", 2 * X + 1)], writes=[("pTp", X)])

        def pvpair(pr):
            X = pr % 2
            for sub in range(2):
                kt = 2 * pr + sub
                S.add("pe", lambda e, kt=kt, sub=sub: e.matmul(C.ps[ob][:, :N], lhsT=Vh[:, kt, :], rhs=pTp[X][:, sub, 0:N],
                                                               start=(kt == 0), stop=(kt == nkt - 1)),
                      reads=[("Vh",), ("pTp", X)], writes=[("ps", ob)])
            if pr == 0:
                S.add("dve", lambda e: e.tensor_copy(acc[:, :, 0:N], pTp[X][:, :, 0:N]), reads=[("pTp", X)], writes=["acc"])
            else:
                S.add("dve", lambda e: e.tensor_tensor(out=acc[:, :, 0:N], in0=acc[:, :, 0:N], in1=pTp[X][:, :, 0:N],
                                                       op=ALU.add), reads=[("pTp", X), "acc"], writes=["acc"])

        qkpair(0)
        for pr in range(npair):
            if pr + 1 < npair:
                qkpair(pr + 1)
            pvpair(pr)
        for ai in range(2):
            S.add("pe", lambda e, ai=ai: e.matmul(C.ps[db][:, :N], lhsT=ones_f[:], rhs=acc[:, ai, 0:N], start=(ai == 0),
                                                  stop=(ai == 1)), reads=["ones_f", "acc"], writes=[("ps", db)])
        S.add("act", lambda e: e.activation(out=C.rstd[:, :N], in_=C.ps[db][:, :N], func=AF.Ln),
              reads=[("ps", db)], writes=["rstd"])
        S.add("act", lambda e: e.activation(out=recip[:, :N], in_=C.rstd[:, :N], func=AF.Exp, scale=-1.0),
              reads=["rstd"], writes=["std"])
        if dsc["i"] is None:
            S.add("dve", lambda e: e.tensor_tensor(out=oTh[:, hh, :], in0=C.ps[ob][:, :N], in1=recip[:, :N], op=ALU.mult),
                  reads=[("ps", ob), "std"], writes=[("oTh", hh)])
        else:
            i = dsc["i"]
            S.add("dve", lambda e: e.tensor_tensor(out=oTsb[par][:], in0=C.ps[ob][:], in1=recip[:], op=ALU.mult),
                  reads=[("ps", ob), "std"], writes=[("oTsb", par)])
            S.add("sp", lambda e: e.dma_start(out=Ot[i, :, hh, :], in_=oTsb[par][:]), reads=[("oTsb", par)],
                  writes=[("Ot", i, hh)], dma=True)

    cnt = 0
    bg_adaln = C.adaln_gen(w_ada[1], 1)
    wb = {}
    for hh in range(H):
        S.add("sp", lambda e, hh=hh: e.dma_start(out=Kh[:], in_=Kt[hh]), reads=[("Kt", p) for p in range(NPOS)],
              writes=[("Kh",)], dma=True)
        S.add("sp", lambda e, hh=hh: e.dma_start(out=Qh[:], in_=Qt[hh]), reads=[("Qt", i) for i in range(NOWN)],
              writes=[("Qh",)], dma=True)
        for q4 in range(4):
            S.add("sp", lambda e, hh=hh, q4=q4: e.dma_start(
                out=Vh[:, q4 * 16:(q4 + 1) * 16, :],
                in_=Vs[q4 * 2048:(q4 + 1) * 2048, hh * DH:(hh + 1) * DH].rearrange("(kt k) c -> k kt c", k=128)),
                reads=[("Vs", p) for p in range(NPOS)], writes=[("Vh",)], dma=True)
        seq = [desc_halo(hh)] + [desc_group(hh, i) for i in range(NOWN)]
        route(seq[0], cnt % 2)
        for n_, dsc in enumerate(seq):
            par = cnt % 2
            if n_ + 1 < len(seq):
                route(seq[n_ + 1], (cnt + 1) % 2)
            if hh == 0:
                for _ in range(2):
                    next(bg_adaln, None)
            attn(dsc, par)
            cnt += 1
        if hh == 0:
            for _ in bg_adaln:
                pass
            wb["o"] = C.precast("o", w_o, 512)
            wb["gu0"] = C.precast("gu0", w_gu[0], 256)
            wb["dn0"] = C.precast("dn0", w_dn[0], 1408)
            wb["in"] = C.precast("in", w_in, 512)
            wb["out"] = C.precast("out", w_out, 512)
            wb["gu1"] = C.precast("gu1", w_gu[1], 256)
            wb["dn1"] = C.precast("dn1", w_dn[1], 1408)

    C.load_xT(xhalo, ntile=1)
    C.proj_fm(wb["o"], 0, 8, lambda fc, bank: C.residual_add(0, 0, fc, bank, N=16),
              rhs_of=lambda kc: oTh[:, kc, :], rhs_keys=lambda kc: ("oTh", kc), N=16)
    C.layernorm(0, 1, N=16)
    C.ffn(0, wb["gu0"], wb["dn0"], N=16)
    layer1_halo_u(C, 1, uh, ctmp, wb["in"])
    oTs = Qh[:, :].rearrange("p (h t) -> p h t", h=H)
    for i in range(NOWN):
        p = 2 * i + 1
        C.load_xT(xseq[p * G:(p + 1) * G, :])
        S.add("sp", lambda e, i=i: e.dma_start(out=oTs, in_=Ot[i]), reads=[("Ot", i, hh) for hh in range(H)],
              writes=[("Qh",)], dma=True)
        C.proj_fm(wb["o"], 0, 8, lambda fc, bank: C.residual_add(0, 0, fc, bank),
                  rhs_of=lambda kc: oTs[:, kc, :], rhs_keys=lambda kc: ("Qh",))
        C.layernorm(0, 1)
        C.ffn(0, wb["gu0"], wb["dn0"])
        layer1_group(C, 1, i, None, uh, ubuf, cg, cv, wb["in"], wb["out"], wb["gu1"], wb["dn1"], out[i * G:(i + 1) * G, :])
    S.emit()
    return nc


def layer1_group(C, l, i, x_rows, uh, ubuf, cg, cv, w_in, w_out, w_gu, w_dn, out_rows):
    S = C.S
    big = C.big
    cw = C.small[:, SM_CONV:SM_CONV + 24].rearrange("p (j k) -> p j k", j=3)
    if x_rows is not None:
        C.load_xT(x_rows)
    C.layernorm(l, 0)
    for fc in range(0):
        S.add("dve", lambda e, fc=fc: e.tensor_copy(ubuf[:, fc, 0:2], uh[:, fc, 2 * i:2 * i + 2]),
              reads=[("uh", fc)], writes=[("ubuf", ub)])
    for s in range(2):
        views = []
        for part in range(3):
            views.append(C.slab(C.wcols(w_in, part * D + s * 512, 512), KC, 512))
        for o in range(4):
            fc = s * 4 + o
            banks = [C.bank(), C.bank(), C.bank()]
            for part in (1, 2, 0):
                view, wk = views[part]
                bk = banks[part]
                for kc in range(KC):
                    S.add("pe", lambda e, kc=kc, o=o, bk=bk, view=view: e.matmul(
                        C.ps[bk][:], lhsT=view[:, kc, o * 128:(o + 1) * 128], rhs=C.hT[:, kc, :],
                        start=(kc == 0), stop=(kc == KC - 1)), reads=[wk, ("hT", kc)], writes=[("ps", bk)])
            bb, bc, bu = banks
            ub = fc % 2
            S.add("act", lambda e, fc=fc, ub=ub: e.copy(ubuf[:, ub, 0:2], uh[:, fc, 2 * i:2 * i + 2]),
                  reads=[("uh", fc)], writes=[("ubuf", ub)])
            S.add("act", lambda e, bc=bc: e.copy(cg[:], C.ps[bc][:]), reads=[("ps", bc)], writes=["cg"])
            S.add("dve", lambda e, bu=bu, ub=ub: e.tensor_tensor(out=ubuf[:, ub, 2:2 + G], in0=C.ps[bu][:], in1=cg[:],
                                                                 op=ALU.mult),
                  reads=[("ps", bu), "cg"], writes=[("ubuf", ub)])
            cb = fc % 2
            S.add("dve", lambda e, fc=fc, cb=cb, ub=ub: e.tensor_scalar(out=cv[cb][:], in0=ubuf[:, ub, 0:G],
                                                                  scalar1=cw[:, 0, fc:fc + 1], scalar2=None,
                                                                  op0=ALU.mult),
                  reads=[("ubuf", ub), "small"], writes=[("cv", cb)])
            S.add("dve", lambda e, fc=fc, cb=cb, ub=ub: e.scalar_tensor_tensor(out=cv[cb][:], in0=ubuf[:, ub, 1:1 + G],
                                                                         scalar=cw[:, 1, fc:fc + 1], in1=cv[cb][:],
                                                                         op0=ALU.mult, op1=ALU.add),
                  reads=[("ubuf", ub), "small", ("cv", cb)], writes=[("cv", cb)])
            S.add("dve", lambda e, fc=fc, cb=cb, ub=ub: e.scalar_tensor_tensor(out=cv[cb][:], in0=ubuf[:, ub, 2:2 + G],
                                                                         scalar=cw[:, 2, fc:fc + 1], in1=cv[cb][:],
                                                                         op0=ALU.mult, op1=ALU.add),
                  reads=[("ubuf", ub), "small", ("cv", cb)], writes=[("cv", cb)])
            S.add("dve", lambda e, fc=fc, cb=cb, bb=bb: e.tensor_tensor(out=big[:, fc, :], in0=C.ps[bb][:], in1=cv[cb][:],
                                                                         op=ALU.mult),
                  reads=[("ps", bb), ("cv", cb)], writes=[("big", fc)])
    C.proj_fm(w_out, 0, 8, lambda fc, bank: C.residual_add(l, 0, fc, bank),
              rhs_of=lambda kc: big[:, kc, :], rhs_keys=lambda kc: ("big", kc))
    C.layernorm(l, 1)
    C.ffn(l, w_gu, w_dn)
    C.store_xT(out_rows)


def layer1_halo_u(C, l, uh, ctmp, w_in):
    S = C.S
    hval = C.small[:, SM_HVALID:SM_HVALID + 16]
    C.layernorm(l, 0, N=16)
    for part in range(2):
        def consume(oc, bank, part=part):
            if part == 0:
                S.add("act", lambda e: e.copy(uh[:, oc, :], C.ps[bank][:, 0:16]), reads=[("ps", bank)],
                      writes=[("uh", oc)])
            else:
                S.add("dve", lambda e: e.tensor_tensor(out=ctmp[:], in0=C.ps[bank][:, 0:16], in1=uh[:, oc, :],
                                                       op=ALU.mult), reads=[("ps", bank), ("uh", oc)], writes=["ctmp"])
                S.add("dve", lambda e: e.tensor_tensor(out=uh[:, oc, :], in0=ctmp[:], in1=hval, op=ALU.mult),
                      reads=["ctmp", "small"], writes=[("uh", oc)])
        C.proj_fm(w_in, D + part * D, 8, consume, N=16)


def _g_of_pos(p, half):
    return p if half == 1 else (p ^ 1)


def _tables(half):
    rb = np.zeros((NOWN, 4, NBLK), np.float32)
    npst = np.zeros((NOWN, 4, NBLK), np.float32)
    for i in range(NOWN):
        for qt in range(4):
            nbq = 2 * (2 * i + half) + qt // 2
            for pb in range(NBLK):
                gb = 2 * _g_of_pos(pb // 2, half) + pb % 2
                if gb < nbq:
                    npst[i, qt, pb] = NEG
                else:
                    rb[i, qt, pb] = -1e30
    tabs = np.concatenate([rb.reshape(-1), npst.reshape(-1)])[None, :].repeat(128, 0).astype(np.float32)
    rbh = np.zeros((128, NBLK), np.float32)
    nph = np.zeros((128, NBLK), np.float32)
    hpat = np.zeros((128, 64, 16), np.float32)
    kk = np.arange(128)
    for i in range(NOWN):
        gh = 2 * i + half - 1
        for t in range(2):
            col = 2 * i + t
            if gh < 0:
                continue
            gbq = 2 * gh + 1
            for pb in range(NBLK):
                gb = 2 * _g_of_pos(pb // 2, half) + pb % 2
                if gb < gbq:
                    nph[col, pb] = NEG
                else:
                    rbh[col, pb] = -1e30
                for sub in range(2):
                    kt = pb * 2 + sub
                    if gb < gbq:
                        hpat[:, kt, col] = 0.0
                    elif gb > gbq:
                        hpat[:, kt, col] = NEG
                    else:
                        hpat[:, kt, col] = np.where(sub * 128 + kk <= 254 + t, 0.0, NEG)
    tabs = np.concatenate([tabs, rbh, nph], axis=1).astype(np.float32)
    pat = np.zeros((128, 8, G), np.float32)
    k = np.arange(128)[:, None]
    q = np.arange(G)[None, :]
    qt = q // 128
    for ktw in range(8):
        if ktw < 4:
            pat[:, ktw, :] = 0.0 if half == 1 else NEG
        else:
            kt_ = ktw - 4
            kb = kt_ // 2
            qb = qt // 2
            kpos = (kt_ % 2) * 128 + k
            qpos = (qt % 2) * 128 + (q % 128)
            m = np.where(kb < qb, 0.0, np.where(kb > qb, NEG, np.where(kpos <= qpos, 0.0, NEG)))
            pat[:, ktw, :] = m
    sel = np.zeros((128, 32, 128), np.float32)
    for pb in range(32):
        sel[pb, pb, :] = 1.0
    import ml_dtypes
    bf = ml_dtypes.bfloat16
    patall = np.concatenate([pat.reshape(128, 8 * G), hpat.reshape(128, 1024)], axis=1)
    return tabs, patall.astype(bf), sel.reshape(128, 32 * 128).astype(bf)


def _small(c_b, b_ada_ls, nmix_ls, nffn_ls, q_gain, k_gain, conv_w, half):
    sm = np.zeros((128, NSM), np.float32)
    sm[:, SM_C:SM_C + 8] = c_b.reshape(8, 128).T
    for l, ba in enumerate(b_ada_ls):
        sm[:, SM_BADA + 48 * l:SM_BADA + 48 * (l + 1)] = ba.reshape(48, 128).T
    for l, v in enumerate(nmix_ls):
        sm[:, SM_NMIX + 8 * l:SM_NMIX + 8 * (l + 1)] = v.reshape(8, 128).T
    for l, v in enumerate(nffn_ls):
        sm[:, SM_NFFN + 8 * l:SM_NFFN + 8 * (l + 1)] = v.reshape(8, 128).T
    sm[:, SM_QG] = q_gain
    sm[:, SM_KG] = k_gain
    sm[:, SM_CONV:SM_CONV + 24] = conv_w.reshape(3, 8, 128).transpose(2, 0, 1).reshape(128, 24)
    hv = np.ones(16, np.float32)
    if half == 0:
        hv[0:2] = 0.0
    sm[:, SM_HVALID:SM_HVALID + 16] = hv[None, :]
    sm[:, SM_IDENT:SM_IDENT + 128] = np.eye(128, dtype=np.float32)
    return sm


_NC_CACHE = {}


def _get(name, fn):
    if name not in _NC_CACHE:
        _NC_CACHE[name] = fn()
    return _NC_CACHE[name]


def make_in_maps(x, c, w_ada, b_ada, norm_mix, norm_ffn, w_qkv, w_o, q_gain, k_gain, w_in, conv_w, w_out,
                 w_gate_up, w_down):
    in_maps = []
    for core in range(8):
        b, half = core // 2, core % 2
        perm = [_g_of_pos(p, half) for p in range(NPOS)]
        xseq = np.ascontiguousarray(x[b].reshape(NPOS, G, D)[perm].reshape(SEQ, D))
        xh = np.zeros((128, D), np.float32)
        for i in range(NOWN):
            g = 2 * i + half
            if g > 0:
                xh[2 * i:2 * i + 2] = x[b, g * G - 2:g * G]
            else:
                xh[2 * i:2 * i + 2] = x[b, 0:2]
        tabs, pat, sel = _tables(half)
        sm = _small(c[b], [b_ada[0], b_ada[1]], [norm_mix[0], norm_mix[1]], [norm_ffn[0], norm_ffn[1]],
                    q_gain[0], k_gain[0], conv_w[0], half)
        in_maps.append(dict(xseq=xseq, xhalo=xh, small=sm, tabs=tabs, pat=pat, sel=sel, w_ada=w_ada, w_qkv=w_qkv[0],
                            w_o=w_o[0], w_in=w_in[0], w_out=w_out[0], w_gu=w_gate_up, w_dn=w_down))
    return in_maps


def kernel(x, c, w_ada, b_ada, norm_mix, norm_ffn, w_qkv, w_o, q_gain, k_gain, w_in, conv_w, w_out,
           w_gate_up, w_down):
    f = lambda a: np.ascontiguousarray(np.asarray(a, dtype=np.float32))
    args = list(map(f, (x, c, w_ada, b_ada, norm_mix, norm_ffn, w_qkv, w_o, q_gain, k_gain, w_in, conv_w, w_out,
                        w_gate_up, w_down)))
    x = args[0]
    in_maps = make_in_maps(*args)
    nc = _get("F", build_fused)
    res = run_bass_kernel_spmd(nc, in_maps, core_ids=list(range(8)))
    out = np.zeros_like(x)
    for core in range(8):
        b, half = core // 2, core % 2
        o = np.asarray(res.results[core]["out"]).reshape(NOWN, G, D)
        for i in range(NOWN):
            g = 2 * i + half
            out[b, g * G:(g + 1) * G] = o[i]
    return out
```

```python
import contextlib
import numpy as np
import concourse.bass as bass
import concourse.mybir as mybir
from concourse.bass_utils import run_bass_kernel_spmd

F32 = mybir.dt.float32
BF16 = mybir.dt.bfloat16
ALU = mybir.AluOpType
AF = mybir.ActivationFunctionType

D = 1024
KC = 8
G = 512
H = 8
DH = 128
DFF = 2816
NJ = 22
NPOS = 16
NOWN = 8
NBLK = 32
SEQ = 8192
NEG = -30000.0
EPS = 1e-6
NW = 4
ENGS = ["pe", "act", "dve", "pool", "sp"]
INORDER = ("pe", "act", "dve")


class Sched:
    def __init__(self, nc, n_dma_sems=6):
        self.nc = nc
        self.ops = []
        self.last_write = {}
        self.readers = {}
        self.n_dma_sems = n_dma_sems

    def add(self, eng, fn, reads=(), writes=(), dma=False):
        oid = len(self.ops)
        deps = set()
        for b in reads:
            if b in self.last_write:
                deps.add(self.last_write[b])
        for b in writes:
            if b in self.last_write:
                deps.add(self.last_write[b])
            for r in self.readers.get(b, {}).values():
                deps.update(r)
        deps.discard(oid)
        self.ops.append(dict(id=oid, eng=eng, fn=fn, deps=deps, dma=dma, signal=False))
        for b in reads:
            rd = self.readers.setdefault(b, {})
            if dma or eng not in INORDER:
                rd.setdefault((eng, "dma"), []).append(oid)
            else:
                rd[eng] = [oid]
        for b in writes:
            self.last_write[b] = oid
            self.readers[b] = {}
        return oid

    def emit(self, final_wait_eng="sp"):
        nc = self.nc
        ops = self.ops

        def needs_sync(p, ceng):
            if p["dma"]:
                return True
            if p["eng"] != ceng:
                return True
            return p["eng"] in ("act", "dve", "pool")

        for op in ops:
            for d in op["deps"]:
                p = ops[d]
                if needs_sync(p, op["eng"]):
                    p["signal"] = True
            if op["dma"]:
                op["signal"] = True
        eng_count = {e: 0 for e in ENGS}
        dma_count = {}
        dma_rr = {e: 0 for e in ENGS}
        sem_keys = {}
        for op in ops:
            if not op["signal"]:
                continue
            if op["dma"]:
                k = dma_rr[op["eng"]] % self.n_dma_sems
                dma_rr[op["eng"]] += 1
                key = ("dma", op["eng"], k)
                prev = dma_count.get(key, 0)
                op["prev_on_sem"] = (key, prev)
                dma_count[key] = prev + 16
                op["sig"] = (key, prev + 16)
            else:
                key = ("eng", op["eng"])
                eng_count[op["eng"]] += 1
                op["sig"] = (key, eng_count[op["eng"]])
            sem_keys[key] = None
        with contextlib.ExitStack() as st:
            sems = {}
            for key in sem_keys:
                sems[key] = st.enter_context(nc.semaphore("s_" + "_".join(str(x) for x in key)))
            block = st.enter_context(nc.Block())
            per_eng = {e: [o for o in ops if o["eng"] == e] for e in ENGS}

            def body(ename):
                def run(eng):
                    waited = {}

                    def wait(key, val):
                        if waited.get(key, 0) >= val:
                            return
                        eng.wait_ge(sems[key], val)
                        waited[key] = val

                    for op in per_eng[ename]:
                        need = {}
                        for d in op["deps"]:
                            p = ops[d]
                            if not needs_sync(p, ename):
                                continue
                            key, val = p["sig"]
                            need[key] = max(need.get(key, 0), val)
                        if op["dma"]:
                            key, prev = op["prev_on_sem"]
                            if prev > 0:
                                need[key] = max(need.get(key, 0), prev)
                        for key, val in need.items():
                            wait(key, val)
                        ins = op["fn"](eng)
                        if op["signal"]:
                            key, val = op["sig"]
                            ins.then_inc(sems[key], 16 if op["dma"] else 1)
                    if ename == final_wait_eng:
                        for key, val in dma_count.items():
                            wait(key, val)
                        for e in ENGS:
                            if eng_count[e] > 0 and e != ename:
                                wait(("eng", e), eng_count[e])
                return run

            block.tensor(body("pe"))
            block.scalar(body("act"))
            block.vector(body("dve"))
            block.gpsimd(body("pool"))
            block.sync(body("sp"))


SM_C = 0
SM_BADA = 8
SM_NMIX = 104
SM_NFFN = 120
SM_QG = 136
SM_KG = 137
SM_CONV = 138
SM_HVALID = 162
SM_IDENT = 178
NSM = 306


class Ctx:
    def __init__(self, nc):
        self.nc = nc
        self.S = Sched(nc)
        self.wslot = 0
        self.bankrr = 0
        a = lambda n_, sh_, d_: nc.alloc_sbuf_tensor('sb_' + n_, sh_, d_)
        self.small = a("small", [128, NSM], F32)
        self.ident_bf = a("ident_bf", [128, 128], BF16)
        self.ones_bf = a("ones_bf", [128, 128], BF16)
        self.eps_t = a("eps_t", [128, 1], F32)
        self.wring = [a(f"wring{i}", [128, 4096], BF16) for i in range(NW)]
        self.xin = a("xin", [128, 4, D], F32)
        self.xT = a("xT", [128, KC, G], F32)
        self.sq = a("sq", [128, KC, G], BF16)
        self.hT = a("hT", [128, KC, G], BF16)
        self.tmp = [a(f"tmp{i}", [128, G], F32) for i in range(2)]
        self.std = a("std", [128, G], F32)
        self.rstd = a("rstd", [128, G], F32)
        self.big = a("big", [128, 24, G], BF16)
        self.mod = a("mod", [128, 2, 48], F32)
        self.modG = a("modG", [128, 2, 2, KC], F32)
        self.scbf = a("scbf", [128, KC], BF16)
        self.qgs = a("qgs", [128, 1], F32)
        self.pspair = [nc.alloc_psum_tensor(f"psp{i}", [128, 1024], F32) for i in range(3)]
        self.ps = []
        for i in range(3):
            self.ps.append(self.pspair[i][:, 0:512])
            self.ps.append(self.pspair[i][:, 512:1024])
        self.ps.append(nc.alloc_psum_tensor("ps6", [128, 512], F32)[:, :])
        self.psb = nc.alloc_psum_tensor("psb", [128, 1024], BF16)

    @property
    def ident(self):
        return self.small[:, SM_IDENT:SM_IDENT + 128]

    def bank(self, lo=0, hi=7):
        b = lo + self.bankrr % (hi - lo)
        self.bankrr += 1
        return b

    def slab(self, src3d, kparts, ncols):
        rk = []
        if isinstance(src3d, tuple):
            src3d, rk = src3d
        slot = self.wslot % NW
        self.wslot += 1
        view = self.wring[slot][:, 0:kparts * ncols].rearrange("p (k n) -> p k n", k=kparts)
        self.S.add("pool", lambda e: e.dma_start(out=view, in_=src3d), reads=rk, writes=[("w", slot)], dma=True)
        return view, ("w", slot)

    def wcols(self, w2d, c0, ncols, r0=0, kparts=KC):
        rk = []
        if isinstance(w2d, tuple):
            w2d, rk = w2d
        ap = w2d[r0:r0 + kparts * 128, c0:c0 + ncols].rearrange("(k p) n -> p k n", p=128)
        return (ap, rk) if rk else ap

    def precast(self, name, w2d, rows_per):
        R, Ncol = w2d.shape
        wb = self.nc.dram_tensor("Wb_" + name, [R, Ncol], BF16).ap()
        keys = []
        r = 0
        while r < R:
            r1 = min(R, r + rows_per)
            key = ("Wb", name, r)
            self.S.add("pool", lambda e, r=r, r1=r1: e.dma_start(out=wb[r:r1, :], in_=w2d[r:r1, :]),
                       writes=[key], dma=True)
            keys.append(key)
            r = r1
        return (wb, keys)

    def setup(self, small_ap):
        S = self.S
        S.add("sp", lambda e: e.dma_start(out=self.small[:], in_=small_ap), writes=["small"], dma=True)
        S.add("dve", lambda e: e.memset(self.ones_bf[:], 1.0), writes=["ones"])
        S.add("dve", lambda e: e.memset(self.eps_t[:], EPS), writes=["eps"])
        S.add("dve", lambda e: e.tensor_copy(self.ident_bf[:], self.ident), reads=["small"], writes=["identbf"])
        S.add("act", lambda e: e.activation(out=self.scbf[:], in_=self.small[:, SM_C:SM_C + 8], func=AF.Silu),
              reads=["small"], writes=["scbf"])
        S.add("act", lambda e: e.mul(self.qgs[:], self.small[:, SM_QG:SM_QG + 1], DH ** -0.5),
              reads=["small"], writes=["qgs"])

    def adaln(self, w_ada_l, l):
        for _ in self.adaln_gen(w_ada_l, l):
            pass

    def adaln_gen(self, w_ada_l, l):
        S = self.S
        bank = 6
        c0 = 256
        first = True
        for s in range(12):
            if s > 0:
                yield
            view, wk = self.slab(self.wcols(w_ada_l, s * 512, 512), KC, 512)
            for o in range(4):
                oc = s * 4 + o
                for kc in range(KC):
                    wr = ["ps2mod"] + ([("ps", bank)] if first else [])
                    first = False
                    S.add("pe", lambda e, oc=oc, kc=kc, o=o, view=view: e.matmul(
                        self.ps[bank][:, c0 + oc:c0 + oc + 1], lhsT=view[:, kc, o * 128:(o + 1) * 128],
                        rhs=self.scbf[:, kc:kc + 1], start=(kc == 0), stop=(kc == KC - 1)),
                        reads=[wk, "scbf"], writes=wr)
        mod = self.mod
        S.add("dve", lambda e: e.tensor_tensor(out=mod[:, l, :], in0=self.ps[bank][:, c0:c0 + 48],
                                               in1=self.small[:, SM_BADA + 48 * l:SM_BADA + 48 * (l + 1)], op=ALU.add),
              reads=["ps2mod", ("ps", bank), "small"], writes=[("mod", l)])
        S.add("dve", lambda e: e.scalar_tensor_tensor(out=self.modG[:, l, 0, :], in0=mod[:, l, 8:16], scalar=1.0,
                                                      in1=self.small[:, SM_NMIX + 8 * l:SM_NMIX + 8 * (l + 1)],
                                                      op0=ALU.add, op1=ALU.mult),
              reads=[("mod", l), "small"], writes=[("modG", l, 0)])
        S.add("dve", lambda e: e.scalar_tensor_tensor(out=self.modG[:, l, 1, :], in0=mod[:, l, 32:40], scalar=1.0,
                                                      in1=self.small[:, SM_NFFN + 8 * l:SM_NFFN + 8 * (l + 1)],
                                                      op0=ALU.add, op1=ALU.mult),
              reads=[("mod", l), "small"], writes=[("modG", l, 1)])

    def modcols(self, l, which):
        base = 0 if which == 0 else 24
        return (self.modG[:, l, which, :], self.mod[:, l, base:base + 8], self.mod[:, l, base + 16:base + 24],
                [("mod", l), ("modG", l, which)])

    def load_xT(self, x_rows, N=G, ntile=4):
        S = self.S
        xin = self.xin
        if x_rows is not None:
            self.load_x_dma(x_rows, ntile)
        for kc in range(KC):
            bank = self.bank()
            for t in range(ntile):
                S.add("pe", lambda e, kc=kc, t=t, bank=bank: e.transpose(
                    self.ps[bank][:, t * 128:(t + 1) * 128], xin[:, t, kc * 128:(kc + 1) * 128], self.ident),
                    reads=["xin", "small"], writes=[("ps", bank)])
            eng = "dve" if kc % 2 == 0 else "act"
            if eng == "dve":
                S.add("dve", lambda e, kc=kc, bank=bank: e.tensor_copy(self.xT[:, kc, 0:ntile * 128],
                                                                        self.ps[bank][:, 0:ntile * 128]),
                      reads=[("ps", bank)], writes=[("xT", kc)])
            else:
                S.add("act", lambda e, kc=kc, bank=bank: e.copy(self.xT[:, kc, 0:ntile * 128],
                                                                 self.ps[bank][:, 0:ntile * 128]),
                      reads=[("ps", bank)], writes=[("xT", kc)])

    def load_x_dma(self, x_rows, ntile=4):
        xin = self.xin
        self.S.add("sp", lambda e: e.dma_start(out=xin[:, 0:ntile, :], in_=x_rows.rearrange("(t p) f -> p t f", p=128)),
                   writes=["xin"], dma=True)

    def store_xT(self, out_rows, ntile=4):
        S = self.S
        xin = self.xin
        for t in range(ntile):
            for hf in range(2):
                bank = self.bank()
                for k4 in range(4):
                    kc = hf * 4 + k4
                    S.add("pe", lambda e, kc=kc, t=t, bank=bank, k4=k4: e.transpose(
                        self.ps[bank][:, k4 * 128:(k4 + 1) * 128], self.xT[:, kc, t * 128:(t + 1) * 128], self.ident),
                        reads=[("xT", kc), "small"], writes=[("ps", bank)])
                if hf == 0:
                    S.add("dve", lambda e, t=t, bank=bank, hf=hf: e.tensor_copy(xin[:, t, hf * 512:(hf + 1) * 512],
                                                                                 self.ps[bank][:]),
                          reads=[("ps", bank)], writes=["xin"])
                else:
                    S.add("act", lambda e, t=t, bank=bank, hf=hf: e.copy(xin[:, t, hf * 512:(hf + 1) * 512],
                                                                          self.ps[bank][:]),
                          reads=[("ps", bank)], writes=["xin"])
        S.add("sp", lambda e: e.dma_start(out=out_rows.rearrange("(t p) f -> p t f", p=128), in_=xin[:, 0:ntile, :]),
              reads=["xin"], writes=[("out", id(out_rows))], dma=True)

    def layernorm(self, l, which, N=G):
        S = self.S
        Gc, Sc, _, mkeys = self.modcols(l, which)
        for kc in range(KC):
            S.add("act", lambda e, kc=kc: e.activation(out=self.sq[:, kc, :N], in_=self.xT[:, kc, :N], func=AF.Square),
                  reads=[("xT", kc)], writes=[("sq", kc)])
        bank = self.bank()
        for kc in range(KC):
            S.add("pe", lambda e, kc=kc: e.matmul(self.ps[bank][:, :N], lhsT=self.ones_bf[:], rhs=self.sq[:, kc, :N],
                                                  start=(kc == 0), stop=(kc == KC - 1)),
                  reads=[("sq", kc), "ones"], writes=[("ps", bank)])
        S.add("act", lambda e: e.activation(out=self.std[:, :N], in_=self.ps[bank][:, :N], func=AF.Ln,
                                            bias=self.eps_t[:, 0:1], scale=1.0 / D),
              reads=[("ps", bank), "eps"], writes=["std"])
        S.add("act", lambda e: e.activation(out=self.rstd[:, :N], in_=self.std[:, :N], func=AF.Exp, scale=-0.5),
              reads=["std"], writes=["rstd"])
        for kc in range(KC):
            tb = kc % 2
            S.add("dve", lambda e, kc=kc, tb=tb: e.tensor_tensor(out=self.tmp[tb][:, :N], in0=self.xT[:, kc, :N],
                                                                 in1=self.rstd[:, :N], op=ALU.mult),
                  reads=[("xT", kc), "rstd"], writes=[("tmp", tb)])
            S.add("act", lambda e, kc=kc, tb=tb: e.activation(out=self.hT[:, kc, :N], in_=self.tmp[tb][:, :N],
                                                              func=AF.Identity, scale=Gc[:, kc:kc + 1],
                                                              bias=Sc[:, kc:kc + 1]),
                  reads=[("tmp", tb)] + mkeys, writes=[("hT", kc)])

    def proj_fm(self, w2d, c0, nchunks, consume, rhs_of=None, rhs_keys=None, N=G, kparts=KC, r0=0):
        S = self.S
        if rhs_of is None:
            rhs_of = lambda kc: self.hT[:, kc, :N]
            rhs_keys = lambda kc: ("hT", kc)
        oc = 0
        while oc < nchunks:
            nch = min(4, nchunks - oc)
            view, wk = self.slab(self.wcols(w2d, c0 + oc * 128, nch * 128, r0=r0, kparts=kparts), kparts, nch * 128)
            for o in range(nch):
                bank = self.bank()
                for kc in range(kparts):
                    S.add("pe", lambda e, kc=kc, o=o, bank=bank, view=view: e.matmul(
                        self.ps[bank][:, :N], lhsT=view[:, kc, o * 128:(o + 1) * 128], rhs=rhs_of(kc),
                        start=(kc == 0), stop=(kc == kparts - 1)),
                        reads=[wk, rhs_keys(kc)], writes=[("ps", bank)])
                consume(oc + o, bank)
            oc += nch

    def residual_add(self, l, which, fc, bank, N=G):
        _, _, gate, mkeys = self.modcols(l, which)
        self.S.add("dve", lambda e: e.scalar_tensor_tensor(out=self.xT[:, fc, :N], in0=self.ps[bank][:, :N],
                                                           scalar=gate[:, fc:fc + 1], in1=self.xT[:, fc, :N],
                                                           op0=ALU.mult, op1=ALU.add),
                   reads=[("ps", bank), ("xT", fc)] + mkeys, writes=[("xT", fc)])

    def ffn(self, l, w_gu, w_dn, N=G):
        S = self.S
        big = self.big
        j = 0
        while j < NJ:
            nch = min(4, NJ - j)
            gview, gk = self.slab(self.wcols(w_gu, j * 128, nch * 128), KC, nch * 128)
            uview, uk = self.slab(self.wcols(w_gu, DFF + j * 128, nch * 128), KC, nch * 128)
            for o in range(nch):
                jj = j + o
                bg = self.bank()
                bu = self.bank()
                for kc in range(KC):
                    S.add("pe", lambda e, kc=kc, o=o, bg=bg, gview=gview: e.matmul(
                        self.ps[bg][:, :N], lhsT=gview[:, kc, o * 128:(o + 1) * 128], rhs=self.hT[:, kc, :N],
                        start=(kc == 0), stop=(kc == KC - 1)), reads=[gk, ("hT", kc)], writes=[("ps", bg)])
                for kc in range(KC):
                    S.add("pe", lambda e, kc=kc, o=o, bu=bu, uview=uview: e.matmul(
                        self.ps[bu][:, :N], lhsT=uview[:, kc, o * 128:(o + 1) * 128], rhs=self.hT[:, kc, :N],
                        start=(kc == 0), stop=(kc == KC - 1)), reads=[uk, ("hT", kc)], writes=[("ps", bu)])
                tb = jj % 2
                S.add("act", lambda e, bg=bg, tb=tb: e.activation(out=self.tmp[tb][:, :N], in_=self.ps[bg][:, :N],
                                                                  func=AF.Silu),
                      reads=[("ps", bg)], writes=[("tmp", tb)])
                S.add("dve", lambda e, bu=bu, tb=tb, jj=jj: e.tensor_tensor(out=big[:, jj, :N], in0=self.ps[bu][:, :N],
                                                                             in1=self.tmp[tb][:, :N], op=ALU.mult),
                      reads=[("ps", bu), ("tmp", tb)], writes=[("big", jj)])
            j += nch
        for fp in range(4):
            v0, k0 = self.slab(self.wcols(w_dn, fp * 256, 256, r0=0, kparts=11), 11, 256)
            v1, k1 = self.slab(self.wcols(w_dn, fp * 256, 256, r0=11 * 128, kparts=11), 11, 256)
            banks = [self.bank(), self.bank()]
            for jj in range(NJ):
                view, wk = (v0, k0) if jj < 11 else (v1, k1)
                for f2 in range(2):
                    S.add("pe", lambda e, jj=jj, f2=f2, view=view, b=banks[f2]: e.matmul(
                        self.ps[b][:, :N], lhsT=view[:, jj % 11, f2 * 128:(f2 + 1) * 128], rhs=big[:, jj, :N],
                        start=(jj == 0), stop=(jj == NJ - 1)), reads=[wk, ("big", jj)], writes=[("ps", banks[f2])])
            for f2 in range(2):
                self.residual_add(l, 1, fp * 2 + f2, banks[f2], N)


def build_fused():
    nc = bass.Bass("TRN2", target_bir_lowering=False)
    dt = nc.dram_tensor
    xseq = dt("xseq", [SEQ, D], F32, kind="ExternalInput").ap()
    xhalo = dt("xhalo", [128, D], F32, kind="ExternalInput").ap()
    small_ap = dt("small", [128, NSM], F32, kind="ExternalInput").ap()
    tabs_ap = dt("tabs", [128, 2048 + 64], F32, kind="ExternalInput").ap()
    pat_ap = dt("pat", [128, 8 * G + 64 * 16], BF16, kind="ExternalInput").ap()
    sel_ap = dt("sel", [128, 32 * 128], BF16, kind="ExternalInput").ap()
    w_ada = dt("w_ada", [2, D, 6 * D], F32, kind="ExternalInput").ap()
    w_qkv = dt("w_qkv", [D, 3 * D], F32, kind="ExternalInput").ap()
    w_o = dt("w_o", [D, D], F32, kind="ExternalInput").ap()
    w_in = dt("w_in", [D, 3 * D], F32, kind="ExternalInput").ap()
    w_out = dt("w_out", [D, D], F32, kind="ExternalInput").ap()
    w_gu = dt("w_gu", [2, D, 2 * DFF], F32, kind="ExternalInput").ap()
    w_dn = dt("w_dn", [2, DFF, D], F32, kind="ExternalInput").ap()
    out = dt("out", [NOWN * G, D], F32, kind="ExternalOutput").ap()
    Kt = dt("Kt", [H, DH, SEQ], BF16).ap()
    Vs = dt("Vs", [H, 128, 64, DH], BF16).ap()
    Qt = dt("Qt", [H, DH, NOWN * G], BF16).ap()
    Ot = dt("Ot", [NOWN, DH, H, G], BF16).ap()

    C = Ctx(nc)
    S = C.S
    a = lambda n_, sh_, d_: nc.alloc_sbuf_tensor('sb_' + n_, sh_, d_)
    tabs = a("tabs", [128, 2048 + 64], F32)
    pat = a("pat", [128, 8 * G + 64 * 16], BF16)
    sel = a("sel", [128, 32, 128], BF16)
    kmean_f = a("kmean_f", [128, H, NBLK], F32)
    kmean_bf = a("kmean_bf", [128, H, NBLK], BF16)
    Kh = a("Kh", [128, SEQ], BF16)
    Vh = a("Vh", [128, 64, DH], BF16)
    Qh = a("Qh", [128, NOWN * G], BF16)
    QhaloT = a("QhaloT", [128, H, 16], BF16)
    oTh = a("oTh", [128, H, 16], BF16)
    pTp = [a(f"pTp{i}", [128, 2, G], BF16) for i in range(2)]
    Rsb = a("Rsb", [128, 4, NBLK], F32)
    max8 = a("max8", [128, 4, 8], F32)
    mb = a("mb", [128, 4, NBLK], BF16)
    mbT = [a(f"mbT{i}", [128, G], BF16) for i in range(2)]
    oTsb = [a(f"oTsb{i}", [128, G], BF16) for i in range(2)]
    uh = a("uh", [128, KC, 16], F32)
    ctmp = a("ctmp", [128, 16], F32)
    cg = a("cg", [128, G], F32)
    ubuf = a("ubuf", [128, 2, 2 + G], F32)
    cv = [a(f"cv{i}", [128, G], F32) for i in range(2)]
    big = C.big
    recip = C.std

    C.setup(small_ap)
    S.add("sp", lambda e: e.dma_start(out=tabs[:], in_=tabs_ap), writes=["tabs"], dma=True)
    S.add("sp", lambda e: e.dma_start(out=pat[:], in_=pat_ap), writes=["pat"], dma=True)
    S.add("sp", lambda e: e.dma_start(out=sel[:].rearrange("p a b -> p (a b)"), in_=sel_ap), writes=["sel"], dma=True)
    S.add("dve", lambda e: e.memset(kmean_f[:], 0.0), writes=["kmean_f"])
    wb_qkv = C.precast("qkv", w_qkv, 256)
    C.adaln(w_ada[0], 0)
    kg = C.small[:, SM_KG:SM_KG + 1]
    acc = a("acc", [128, 2, G], F32)
    for i_ in range(2):
        S.add("dve", lambda e, i_=i_: e.memset(mbT[i_][:], 0.0), writes=[("mbT", i_)])
    ones_f = a("ones_f", [128, 128], F32)
    S.add("dve", lambda e: e.memset(ones_f[:], 1.0), writes=["ones_f"])

    def qk_head(hh, dst_ap, dst_key, gain_ap, gain_keys, wview, wk, o, kmean_pos=None, N=G):
        bank = C.bank()
        for kc in range(KC):
            S.add("pe", lambda e, kc=kc: e.matmul(C.ps[bank][:, :N], lhsT=wview[:, kc, o * 128:(o + 1) * 128],
                                                  rhs=C.hT[:, kc, :N], start=(kc == 0), stop=(kc == KC - 1)),
                  reads=[wk, ("hT", kc)], writes=[("ps", bank)])
        sqk = C.sq[:, hh, :N]
        S.add("act", lambda e: e.activation(out=sqk, in_=C.ps[bank][:, :N], func=AF.Square),
              reads=[("ps", bank)], writes=[("sq", hh)])
        b2 = C.bank()
        S.add("pe", lambda e: e.matmul(C.ps[b2][:, :N], lhsT=C.ones_bf[:], rhs=sqk, start=True, stop=True),
              reads=["ones", ("sq", hh)], writes=[("ps", b2)])
        if hh % 2 == 0:
            stdb, stdk, rstdb, rstdk = C.std, "std", C.rstd, "rstd"
        else:
            stdb, stdk, rstdb, rstdk = C.tmp[0], ("tmp", 0), C.tmp[1], ("tmp", 1)
        S.add("act", lambda e: e.activation(out=stdb[:, :N], in_=C.ps[b2][:, :N], func=AF.Ln, bias=C.eps_t[:, 0:1],
                                            scale=1.0 / DH), reads=[("ps", b2), "eps"], writes=[stdk])
        S.add("act", lambda e: e.activation(out=rstdb[:, :N], in_=stdb[:, :N], func=AF.Exp, scale=-0.5),
              reads=[stdk], writes=[rstdk])
        if kmean_pos is None:
            S.add("dve", lambda e: e.scalar_tensor_tensor(out=dst_ap, in0=C.ps[bank][:, :N], scalar=gain_ap,
                                                          in1=rstdb[:, :N], op0=ALU.mult, op1=ALU.mult),
                  reads=[("ps", bank), rstdk] + gain_keys, writes=[dst_key])
        else:
            for bb in range(2):
                pb = kmean_pos * 2 + bb
                S.add("dve", lambda e, bb=bb, pb=pb: e.scalar_tensor_tensor(
                    out=dst_ap[:, bb * 256:(bb + 1) * 256], in0=C.ps[bank][:, bb * 256:(bb + 1) * 256],
                    scalar=gain_ap, in1=rstdb[:, bb * 256:(bb + 1) * 256], op0=ALU.mult, op1=ALU.mult,
                    accum_out=kmean_f[:, hh, pb:pb + 1]),
                    reads=[("ps", bank), rstdk, "kmean_f"] + gain_keys, writes=[dst_key, "kmean_f"])

    C.load_x_dma(xseq[0:G, :])
    for p in range(NPOS):
        own = (p % 2 == 1)
        C.load_xT(None)
        if p + 1 < NPOS:
            C.load_x_dma(xseq[(p + 1) * G:(p + 2) * G, :])
        C.layernorm(0, 0)
        for s in range(2):
            wview, wk = C.slab(C.wcols(wb_qkv, D + s * 512, 512), KC, 512)
            for o in range(4):
                hh = s * 4 + o
                qk_head(hh, big[:, hh, :], ("big", hh), kg, ["small"], wview, wk, o, kmean_pos=p)
        S.add("sp", lambda e, p=p: e.dma_start(out=Kt.rearrange("h d t -> d h t")[:, :, p * G:(p + 1) * G],
                                               in_=big[:, 0:8, :]),
              reads=[("big", s_) for s_ in range(8)], writes=[("Kt", p)], dma=True)
        for hf in range(2):
            wview, wk = C.slab(C.wcols(wb_qkv, 2 * D + hf * 512, 512), KC, 512)
            for t in range(4):
                bank = C.bank()
                for kc in range(KC):
                    S.add("pe", lambda e, kc=kc, t=t, bank=bank, wview=wview: e.matmul(
                        C.ps[bank][:], lhsT=C.hT[:, kc, t * 128:(t + 1) * 128], rhs=wview[:, kc, :],
                        start=(kc == 0), stop=(kc == KC - 1)), reads=[wk, ("hT", kc)], writes=[("ps", bank)])
                slot = 16 + 2 * t + hf
                if t % 2 == 0:
                    S.add("act", lambda e, bank=bank, slot=slot: e.copy(big[:, slot, :], C.ps[bank][:]),
                          reads=[("ps", bank)], writes=[("big", slot)])
                else:
                    S.add("dve", lambda e, bank=bank, slot=slot: e.tensor_copy(big[:, slot, :], C.ps[bank][:]),
                          reads=[("ps", bank)], writes=[("big", slot)])
        for t in range(4):
            S.add("sp", lambda e, p=p, t=t: e.dma_start(
                out=Vs[:, :, p * 4 + t, :].rearrange("(hf h4) k d -> k hf h4 d", hf=2),
                in_=big[:, 16 + 2 * t:18 + 2 * t, :].rearrange("p hf (h4 d) -> p hf h4 d", h4=4)),
                reads=[("big", 16 + 2 * t), ("big", 17 + 2 * t)], writes=[("Vs", p, t)], dma=True)
        if own:
            i = p // 2
            for s in range(2):
                wview, wk = C.slab(C.wcols(wb_qkv, s * 512, 512), KC, 512)
                for o in range(4):
                    hh = s * 4 + o
                    qk_head(hh, big[:, 8 + hh, :], ("big", 8 + hh), C.qgs[:, 0:1], ["qgs"], wview, wk, o)
            S.add("sp", lambda e, i=i: e.dma_start(out=Qt.rearrange("h d t -> d h t")[:, :, i * G:(i + 1) * G],
                                                   in_=big[:, 8:16, :]),
                  reads=[("big", s_) for s_ in range(8, 16)], writes=[("Qt", i)], dma=True)
    C.load_xT(xhalo, ntile=1)
    C.layernorm(0, 0, N=16)
    for s in range(2):
        wview, wk = C.slab(C.wcols(wb_qkv, s * 512, 512), KC, 512)
        for o in range(4):
            hh = s * 4 + o
            qk_head(hh, QhaloT[:, hh, :], ("QhaloT", hh), C.qgs[:, 0:1], ["qgs"], wview, wk, o, N=16)
    S.add("dve", lambda e: e.tensor_copy(kmean_bf[:], kmean_f[:]), reads=["kmean_f"], writes=["kmean_bf"])

    rb = tabs[:, 0:1024].rearrange("p (i q n) -> p i q n", i=NOWN, q=4)
    npast = tabs[:, 1024:2048].rearrange("p (i q n) -> p i q n", i=NOWN, q=4)
    rb_h = tabs[:, 2048:2080]
    np_h = tabs[:, 2080:2112]
    patg = pat[:, 0:8 * G].rearrange("p (a b) -> p a b", a=8)
    hpat = pat[:, 8 * G:8 * G + 1024].rearrange("p (a b) -> p a b", a=64)
    RB = 6

    def desc_group(hh, i):
        return dict(N=G, nq=4, qrows=128, q_ap=Qh[:, i * G:(i + 1) * G], q_keys=[("Qh",)],
                    rb=rb[:, i, :, :], npst=npast[:, i, :, :], nkt=8 * i + 8,
                    pat_of=(lambda kt: patg[:, kt - 8 * i, :] if kt >= 8 * i else None), i=i, hh=hh)

    def desc_halo(hh):
        return dict(N=16, nq=1, qrows=16, q_ap=QhaloT[:, hh, :], q_keys=[("QhaloT", hh)],
                    rb=rb_h[0:16, :].rearrange("p (q n) -> p q n", q=1), npst=np_h[0:16, :].rearrange("p (q n) -> p q n", q=1),
                    nkt=64, pat_of=(lambda kt: hpat[:, kt, :]), i=None, hh=hh)

    def route(dsc, par):
        hh, N, nq, qr = dsc["hh"], dsc["N"], dsc["nq"], dsc["qrows"]
        qw = min(N, 128)
        for qt in range(nq):
            S.add("pe", lambda e, qt=qt: e.matmul(C.ps[RB][0:qr, qt * NBLK:(qt + 1) * NBLK],
                                                  lhsT=dsc["q_ap"][:, qt * qw:(qt + 1) * qw],
                                                  rhs=kmean_bf[:, hh, :], start=True, stop=True),
                  reads=dsc["q_keys"] + ["kmean_bf"], writes=[("ps", RB)])
        S.add("dve", lambda e: e.tensor_tensor(out=Rsb[0:qr, 0:nq, :],
                                               in0=C.ps[RB][0:qr, 0:nq * NBLK].rearrange("p (q n) -> p q n", q=nq),
                                               in1=dsc["rb"], op=ALU.add),
              reads=[("ps", RB), "tabs"], writes=["Rsb"])
        for qt in range(nq):
            S.add("dve", lambda e, qt=qt: e.max(out=max8[0:qr, qt, :], in_=Rsb[0:qr, qt, :]), reads=["Rsb"],
                  writes=[("max8", qt)])
            S.add("dve", lambda e, qt=qt: e.scalar_tensor_tensor(out=mb[0:qr, qt, :], in0=Rsb[0:qr, qt, :],
                                                                 scalar=max8[0:qr, qt, 2:3], in1=dsc["npst"][:, qt, :],
                                                                 op0=ALU.is_lt, op1=ALU.mult),
                  reads=["Rsb", ("max8", qt), "tabs"], writes=[("mb", qt)])
        for qt in range(nq):
            S.add("pe", lambda e, qt=qt: e.transpose(C.psb[0:32, qt * 128:qt * 128 + qr], mb[0:qr, qt, :],
                                                     C.ident_bf[0:qr, 0:qr]),
                  reads=[("mb", qt), "identbf"], writes=["psb"])
        S.add("act", lambda e: e.copy(mbT[par][0:32, 0:N], C.psb[0:32, 0:N]), reads=["psb"], writes=[("mbT", par)])

    def make_attn(dsc, par):
        hh, N, nkt = dsc["hh"], dsc["N"], dsc["nkt"]
        ob = 4 + par
        db = 4 + (1 - par)
        npair = nkt // 2

        def qkpair(pr):
            X = pr % 2
            for sub in range(2):
                kt = 2 * pr + sub
                sb = 2 * X + sub
                pt = dsc["pat_of"](kt)
                S.add("pe", lambda e, kt=kt, sb=sb: e.matmul(C.ps[sb][:, :N], lhsT=Kh[:, kt * 128:(kt + 1) * 128],
                                                             rhs=dsc["q_ap"], start=True, stop=False),
                      reads=[("Kh",)] + dsc["q_keys"], writes=[("ps", sb)])
                S.add("pe", lambda e, sb=sb, pt=pt: e.matmul(C.ps[sb][:, :N], lhsT=sel[:, pr, :], rhs=mbT[par][:, 0:N],
                                                             start=False, stop=(pt is None)),
                      reads=["sel", ("mbT", par)], writes=[("ps", sb)])
                if pt is not None:
                    S.add("pe", lambda e, sb=sb, pt=pt: e.matmul(C.ps[sb][:, :N], lhsT=C.ident_bf[:], rhs=pt, start=False,
                                                                 stop=True),
                          reads=["identbf", "pat"], writes=[("ps", sb)])
            scv = C.pspair[X][:, :].rearrange("p (b n) -> p b n", b=2)[:, :, 0:N]
            S.add("act", lambda e: e.activation(out=pTp[X][:, :, 0:N], in_=scv, func=AF.Exp),
                  reads=[("ps", 2 * X), ("ps", 2 * X + 1)], writes=[("pTp", X)])

        def pvpair(pr):
            X = pr % 2
            for sub in range(2):
                kt = 2 * pr + sub
                S.add("pe", lambda e, kt=kt, sub=sub: e.matmul(C.ps[ob][:, :N], lhsT=Vh[:, kt, :], rhs=pTp[X][:, sub, 0:N],
                                                               start=(kt == 0), stop=(kt == nkt - 1)),
                      reads=[("Vh",), ("pTp", X)], writes=[("ps", ob)])
            if pr == 0:
                S.add("dve", lambda e: e.tensor_copy(acc[:, :, 0:N], pTp[X][:, :, 0:N]), reads=[("pTp", X)], writes=["acc"])
            else:
                S.add("dve", lambda e: e.tensor_tensor(out=acc[:, :, 0:N], in0=acc[:, :, 0:N], in1=pTp[X][:, :, 0:N],
                                                       op=ALU.add), reads=[("pTp", X), "acc"], writes=["acc"])

        def epilogue():
            for ai in range(2):
                S.add("pe", lambda e, ai=ai: e.matmul(C.ps[db][:, :N], lhsT=ones_f[:], rhs=acc[:, ai, 0:N],
                                                      start=(ai == 0), stop=(ai == 1)),
                      reads=["ones_f", "acc"], writes=[("ps", db)])
            S.add("act", lambda e: e.activation(out=C.rstd[:, :N], in_=C.ps[db][:, :N], func=AF.Ln),
                  reads=[("ps", db)], writes=["rstd"])
            S.add("act", lambda e: e.activation(out=recip[:, :N], in_=C.rstd[:, :N], func=AF.Exp, scale=-1.0),
                  reads=["rstd"], writes=["std"])
            if dsc["i"] is None:
                S.add("dve", lambda e: e.tensor_tensor(out=oTh[:, hh, :], in0=C.ps[ob][:, :N], in1=recip[:, :N],
                                                       op=ALU.mult), reads=[("ps", ob), "std"], writes=[("oTh", hh)])
            else:
                i = dsc["i"]
                S.add("dve", lambda e: e.tensor_tensor(out=oTsb[par][:], in0=C.ps[ob][:], in1=recip[:], op=ALU.mult),
                      reads=[("ps", ob), "std"], writes=[("oTsb", par)])
                S.add("sp", lambda e: e.dma_start(out=Ot[i, :, hh, :], in_=oTsb[par][:]), reads=[("oTsb", par)],
                      writes=[("Ot", i, hh)], dma=True)

        return dict(qkpair=qkpair, pvpair=pvpair, epilogue=epilogue, npair=npair)

    cnt = 0
    bg_adaln = C.adaln_gen(w_ada[1], 1)
    wb = {}
    for hh in range(H):
        S.add("sp", lambda e, hh=hh: e.dma_start(out=Kh[:], in_=Kt[hh]), reads=[("Kt", p) for p in range(NPOS)],
              writes=[("Kh",)], dma=True)
        S.add("sp", lambda e, hh=hh: e.dma_start(out=Qh[:], in_=Qt[hh]), reads=[("Qt", i) for i in range(NOWN)],
              writes=[("Qh",)], dma=True)
        S.add("sp", lambda e, hh=hh: e.dma_start(out=Vh[:], in_=Vs[hh]),
              reads=[("Vs", p, t) for p in range(NPOS) for t in range(4)], writes=[("Vh",)], dma=True)
        seq = [desc_halo(hh)] + [desc_group(hh, i) for i in range(NOWN)]
        route(seq[0], cnt % 2)
        cur = make_attn(seq[0], cnt % 2)
        cur["qkpair"](0)
        for n_, dsc in enumerate(seq):
            nxt = None
            if n_ + 1 < len(seq):
                route(seq[n_ + 1], (cnt + 1) % 2)
                nxt = make_attn(seq[n_ + 1], (cnt + 1) % 2)
            if hh == 0:
                for _ in range(2):
                    next(bg_adaln, None)
            npair = cur["npair"]
            for pr in range(npair):
                if pr + 1 < npair:
                    cur["qkpair"](pr + 1)
                elif nxt is not None:
                    nxt["qkpair"](0)
                cur["pvpair"](pr)
            cur["epilogue"]()
            cur = nxt
            cnt += 1
        if hh == 0:
            for _ in bg_adaln:
                pass
            wb["o"] = C.precast("o", w_o, 512)
            wb["gu0"] = C.precast("gu0", w_gu[0], 256)
            wb["dn0"] = C.precast("dn0", w_dn[0], 1408)
            wb["in"] = C.precast("in", w_in, 512)
            wb["out"] = C.precast("out", w_out, 512)
            wb["gu1"] = C.precast("gu1", w_gu[1], 256)
            wb["dn1"] = C.precast("dn1", w_dn[1], 1408)

    C.load_xT(xhalo, ntile=1)
    C.proj_fm(wb["o"], 0, 8, lambda fc, bank: C.residual_add(0, 0, fc, bank, N=16),
              rhs_of=lambda kc: oTh[:, kc, :], rhs_keys=lambda kc: ("oTh", kc), N=16)
    C.layernorm(0, 1, N=16)
    C.ffn(0, wb["gu0"], wb["dn0"], N=16)
    layer1_halo_u(C, 1, uh, ctmp, wb["in"])
    oTs = Qh[:, :].rearrange("p (h t) -> p h t", h=H)
    for i in range(NOWN):
        p = 2 * i + 1
        C.load_xT(xseq[p * G:(p + 1) * G, :])
        S.add("sp", lambda e, i=i: e.dma_start(out=oTs, in_=Ot[i]), reads=[("Ot", i, hh) for hh in range(H)],
              writes=[("Qh",)], dma=True)
        C.proj_fm(wb["o"], 0, 8, lambda fc, bank: C.residual_add(0, 0, fc, bank),
                  rhs_of=lambda kc: oTs[:, kc, :], rhs_keys=lambda kc: ("Qh",))
        C.layernorm(0, 1)
        C.ffn(0, wb["gu0"], wb["dn0"])
        layer1_group(C, 1, i, None, uh, ubuf, cg, cv, wb["in"], wb["out"], wb["gu1"], wb["dn1"], out[i * G:(i + 1) * G, :])
    S.emit()
    return nc


def layer1_group(C, l, i, x_rows, uh, ubuf, cg, cv, w_in, w_out, w_gu, w_dn, out_rows):
    S = C.S
    big = C.big
    cw = C.small[:, SM_CONV:SM_CONV + 24].rearrange("p (j k) -> p j k", j=3)
    if x_rows is not None:
        C.load_xT(x_rows)
    C.layernorm(l, 0)
    for fc in range(0):
        S.add("dve", lambda e, fc=fc: e.tensor_copy(ubuf[:, fc, 0:2], uh[:, fc, 2 * i:2 * i + 2]),
              reads=[("uh", fc)], writes=[("ubuf", ub)])
    for s in range(2):
        views = []
        for part in range(3):
            views.append(C.slab(C.wcols(w_in, part * D + s * 512, 512), KC, 512))
        for o in range(4):
            fc = s * 4 + o
            banks = [C.bank(), C.bank(), C.bank()]
            for part in (1, 2, 0):
                view, wk = views[part]
                bk = banks[part]
                for kc in range(KC):
                    S.add("pe", lambda e, kc=kc, o=o, bk=bk, view=view: e.matmul(
                        C.ps[bk][:], lhsT=view[:, kc, o * 128:(o + 1) * 128], rhs=C.hT[:, kc, :],
                        start=(kc == 0), stop=(kc == KC - 1)), reads=[wk, ("hT", kc)], writes=[("ps", bk)])
            bb, bc, bu = banks
            ub = fc % 2
            S.add("act", lambda e, fc=fc, ub=ub: e.copy(ubuf[:, ub, 0:2], uh[:, fc, 2 * i:2 * i + 2]),
                  reads=[("uh", fc)], writes=[("ubuf", ub)])
            S.add("act", lambda e, bc=bc: e.copy(cg[:], C.ps[bc][:]), reads=[("ps", bc)], writes=["cg"])
            S.add("dve", lambda e, bu=bu, ub=ub: e.tensor_tensor(out=ubuf[:, ub, 2:2 + G], in0=C.ps[bu][:], in1=cg[:],
                                                                 op=ALU.mult),
                  reads=[("ps", bu), "cg"], writes=[("ubuf", ub)])
            cb = fc % 2
            S.add("dve", lambda e, fc=fc, cb=cb, ub=ub: e.tensor_scalar(out=cv[cb][:], in0=ubuf[:, ub, 0:G],
                                                                  scalar1=cw[:, 0, fc:fc + 1], scalar2=None,
                                                                  op0=ALU.mult),
                  reads=[("ubuf", ub), "small"], writes=[("cv", cb)])
            S.add("dve", lambda e, fc=fc, cb=cb, ub=ub: e.scalar_tensor_tensor(out=cv[cb][:], in0=ubuf[:, ub, 1:1 + G],
                                                                         scalar=cw[:, 1, fc:fc + 1], in1=cv[cb][:],
                                                                         op0=ALU.mult, op1=ALU.add),
                  reads=[("ubuf", ub), "small", ("cv", cb)], writes=[("cv", cb)])
            S.add("dve", lambda e, fc=fc, cb=cb, ub=ub: e.scalar_tensor_tensor(out=cv[cb][:], in0=ubuf[:, ub, 2:2 + G],
                                                                         scalar=cw[:, 2, fc:fc + 1], in1=cv[cb][:],
                                                                         op0=ALU.mult, op1=ALU.add),
                  reads=[("ubuf", ub), "small", ("cv", cb)], writes=[("cv", cb)])
            S.add("dve", lambda e, fc=fc, cb=cb, bb=bb: e.tensor_tensor(out=big[:, fc, :], in0=C.ps[bb][:], in1=cv[cb][:],
                                                                         op=ALU.mult),
                  reads=[("ps", bb), ("cv", cb)], writes=[("big", fc)])
    C.proj_fm(w_out, 0, 8, lambda fc, bank: C.residual_add(l, 0, fc, bank),
              rhs_of=lambda kc: big[:, kc, :], rhs_keys=lambda kc: ("big", kc))
    C.layernorm(l, 1)
    C.ffn(l, w_gu, w_dn)
    C.store_xT(out_rows)


def layer1_halo_u(C, l, uh, ctmp, w_in):
    S = C.S
    hval = C.small[:, SM_HVALID:SM_HVALID + 16]
    C.layernorm(l, 0, N=16)
    for part in range(2):
        def consume(oc, bank, part=part):
            if part == 0:
                S.add("act", lambda e: e.copy(uh[:, oc, :], C.ps[bank][:, 0:16]), reads=[("ps", bank)],
                      writes=[("uh", oc)])
            else:
                S.add("dve", lambda e: e.tensor_tensor(out=ctmp[:], in0=C.ps[bank][:, 0:16], in1=uh[:, oc, :],
                                                       op=ALU.mult), reads=[("ps", bank), ("uh", oc)], writes=["ctmp"])
                S.add("dve", lambda e: e.tensor_tensor(out=uh[:, oc, :], in0=ctmp[:], in1=hval, op=ALU.mult),
                      reads=["ctmp", "small"], writes=[("uh", oc)])
        C.proj_fm(w_in, D + part * D, 8, consume, N=16)


def _g_of_pos(p, half):
    return p if half == 1 else (p ^ 1)


def _tables(half):
    rb = np.zeros((NOWN, 4, NBLK), np.float32)
    npst = np.zeros((NOWN, 4, NBLK), np.float32)
    for i in range(NOWN):
        for qt in range(4):
            nbq = 2 * (2 * i + half) + qt // 2
            for pb in range(NBLK):
                gb = 2 * _g_of_pos(pb // 2, half) + pb % 2
                if gb < nbq:
                    npst[i, qt, pb] = NEG
                else:
                    rb[i, qt, pb] = -1e30
    tabs = np.concatenate([rb.reshape(-1), npst.reshape(-1)])[None, :].repeat(128, 0).astype(np.float32)
    rbh = np.zeros((128, NBLK), np.float32)
    nph = np.zeros((128, NBLK), np.float32)
    hpat = np.zeros((128, 64, 16), np.float32)
    kk = np.arange(128)
    for i in range(NOWN):
        gh = 2 * i + half - 1
        for t in range(2):
            col = 2 * i + t
            if gh < 0:
                continue
            gbq = 2 * gh + 1
            for pb in range(NBLK):
                gb = 2 * _g_of_pos(pb // 2, half) + pb % 2
                if gb < gbq:
                    nph[col, pb] = NEG
                else:
                    rbh[col, pb] = -1e30
                for sub in range(2):
                    kt = pb * 2 + sub
                    if gb < gbq:
                        hpat[:, kt, col] = 0.0
                    elif gb > gbq:
                        hpat[:, kt, col] = NEG
                    else:
                        hpat[:, kt, col] = np.where(sub * 128 + kk <= 254 + t, 0.0, NEG)
    tabs = np.concatenate([tabs, rbh, nph], axis=1).astype(np.float32)
    pat = np.zeros((128, 8, G), np.float32)
    k = np.arange(128)[:, None]
    q = np.arange(G)[None, :]
    qt = q // 128
    for ktw in range(8):
        if ktw < 4:
            pat[:, ktw, :] = 0.0 if half == 1 else NEG
        else:
            kt_ = ktw - 4
            kb = kt_ // 2
            qb = qt // 2
            kpos = (kt_ % 2) * 128 + k
            qpos = (qt % 2) * 128 + (q % 128)
            m = np.where(kb < qb, 0.0, np.where(kb > qb, NEG, np.where(kpos <= qpos, 0.0, NEG)))
            pat[:, ktw, :] = m
    sel = np.zeros((128, 32, 128), np.float32)
    for pb in range(32):
        sel[pb, pb, :] = 1.0
    import ml_dtypes
    bf = ml_dtypes.bfloat16
    patall = np.concatenate([pat.reshape(128, 8 * G), hpat.reshape(128, 1024)], axis=1)
    return tabs, patall.astype(bf), sel.reshape(128, 32 * 128).astype(bf)


def _small(c_b, b_ada_ls, nmix_ls, nffn_ls, q_gain, k_gain, conv_w, half):
    sm = np.zeros((128, NSM), np.float32)
    sm[:, SM_C:SM_C + 8] = c_b.reshape(8, 128).T
    for l, ba in enumerate(b_ada_ls):
        sm[:, SM_BADA + 48 * l:SM_BADA + 48 * (l + 1)] = ba.reshape(48, 128).T
    for l, v in enumerate(nmix_ls):
        sm[:, SM_NMIX + 8 * l:SM_NMIX + 8 * (l + 1)] = v.reshape(8, 128).T
    for l, v in enumerate(nffn_ls):
        sm[:, SM_NFFN + 8 * l:SM_NFFN + 8 * (l + 1)] = v.reshape(8, 128).T
    sm[:, SM_QG] = q_gain
    sm[:, SM_KG] = k_gain
    sm[:, SM_CONV:SM_CONV + 24] = conv_w.reshape(3, 8, 128).transpose(2, 0, 1).reshape(128, 24)
    hv = np.ones(16, np.float32)
    if half == 0:
        hv[0:2] = 0.0
    sm[:, SM_HVALID:SM_HVALID + 16] = hv[None, :]
    sm[:, SM_IDENT:SM_IDENT + 128] = np.eye(128, dtype=np.float32)
    return sm


_NC_CACHE = {}


def _get(name, fn):
    if name not in _NC_CACHE:
        _NC_CACHE[name] = fn()
    return _NC_CACHE[name]


def make_in_maps(x, c, w_ada, b_ada, norm_mix, norm_ffn, w_qkv, w_o, q_gain, k_gain, w_in, conv_w, w_out,
                 w_gate_up, w_down):
    in_maps = []
    for core in range(8):
        b, half = core // 2, core % 2
        perm = [_g_of_pos(p, half) for p in range(NPOS)]
        xseq = np.ascontiguousarray(x[b].reshape(NPOS, G, D)[perm].reshape(SEQ, D))
        xh = np.zeros((128, D), np.float32)
        for i in range(NOWN):
            g = 2 * i + half
            if g > 0:
                xh[2 * i:2 * i + 2] = x[b, g * G - 2:g * G]
            else:
                xh[2 * i:2 * i + 2] = x[b, 0:2]
        tabs, pat, sel = _tables(half)
        sm = _small(c[b], [b_ada[0], b_ada[1]], [norm_mix[0], norm_mix[1]], [norm_ffn[0], norm_ffn[1]],
                    q_gain[0], k_gain[0], conv_w[0], half)
        in_maps.append(dict(xseq=xseq, xhalo=xh, small=sm, tabs=tabs, pat=pat, sel=sel, w_ada=w_ada, w_qkv=w_qkv[0],
                            w_o=w_o[0], w_in=w_in[0], w_out=w_out[0], w_gu=w_gate_up, w_dn=w_down))
    return in_maps


def kernel(x, c, w_ada, b_ada, norm_mix, norm_ffn, w_qkv, w_o, q_gain, k_gain, w_in, conv_w, w_out,
           w_gate_up, w_down):
    f = lambda a: np.ascontiguousarray(np.asarray(a, dtype=np.float32))
    args = list(map(f, (x, c, w_ada, b_ada, norm_mix, norm_ffn, w_qkv, w_o, q_gain, k_gain, w_in, conv_w, w_out,
                        w_gate_up, w_down)))
    x = args[0]
    in_maps = make_in_maps(*args)
    nc = _get("F", build_fused)
    res = run_bass_kernel_spmd(nc, in_maps, core_ids=list(range(8)))
    out = np.zeros_like(x)
    for core in range(8):
        b, half = core // 2, core % 2
        o = np.asarray(res.results[core]["out"]).reshape(NOWN, G, D)
        for i in range(NOWN):
            g = 2 * i + half
            out[b, g * G:(g + 1) * G] = o[i]
    return out
```

```python
import contextlib
import numpy as np
import concourse.bass as bass
import concourse.mybir as mybir
from concourse.bass_utils import run_bass_kernel_spmd

F32 = mybir.dt.float32
BF16 = mybir.dt.bfloat16
ALU = mybir.AluOpType
AF = mybir.ActivationFunctionType

D = 1024
KC = 8
G = 512
H = 8
DH = 128
DFF = 2816
NJ = 22
NPOS = 16
NOWN = 8
NBLK = 32
SEQ = 8192
NEG = -30000.0
EPS = 1e-6
NW = 4
ENGS = ["pe", "act", "dve", "pool", "sp"]
INORDER = ("pe", "act", "dve")


class Sched:
    def __init__(self, nc, n_dma_sems=6):
        self.nc = nc
        self.ops = []
        self.last_write = {}
        self.readers = {}
        self.n_dma_sems = n_dma_sems

    def add(self, eng, fn, reads=(), writes=(), dma=False):
        oid = len(self.ops)
        deps = set()
        for b in reads:
            if b in self.last_write:
                deps.add(self.last_write[b])
        for b in writes:
            if b in self.last_write:
                deps.add(self.last_write[b])
            for r in self.readers.get(b, {}).values():
                deps.update(r)
        deps.discard(oid)
        self.ops.append(dict(id=oid, eng=eng, fn=fn, deps=deps, dma=dma, signal=False))
        for b in reads:
            rd = self.readers.setdefault(b, {})
            if dma or eng not in INORDER:
                rd.setdefault((eng, "dma"), []).append(oid)
            else:
                rd[eng] = [oid]
        for b in writes:
            self.last_write[b] = oid
            self.readers[b] = {}
        return oid

    def emit(self, final_wait_eng="sp"):
        nc = self.nc
        ops = self.ops

        def needs_sync(p, ceng):
            if p["dma"]:
                return True
            if p["eng"] != ceng:
                return True
            return p["eng"] in ("act", "dve", "pool")

        for op in ops:
            for d in op["deps"]:
                p = ops[d]
                if needs_sync(p, op["eng"]):
                    p["signal"] = True
            if op["dma"]:
                op["signal"] = True
        eng_count = {e: 0 for e in ENGS}
        dma_count = {}
        dma_rr = {e: 0 for e in ENGS}
        sem_keys = {}
        for op in ops:
            if not op["signal"]:
                continue
            if op["dma"]:
                k = dma_rr[op["eng"]] % self.n_dma_sems
                dma_rr[op["eng"]] += 1
                key = ("dma", op["eng"], k)
                prev = dma_count.get(key, 0)
                op["prev_on_sem"] = (key, prev)
                dma_count[key] = prev + 16
                op["sig"] = (key, prev + 16)
            else:
                key = ("eng", op["eng"])
                eng_count[op["eng"]] += 1
                op["sig"] = (key, eng_count[op["eng"]])
            sem_keys[key] = None
        with contextlib.ExitStack() as st:
            sems = {}
            for key in sem_keys:
                sems[key] = st.enter_context(nc.semaphore("s_" + "_".join(str(x) for x in key)))
            block = st.enter_context(nc.Block())
            per_eng = {e: [o for o in ops if o["eng"] == e] for e in ENGS}

            def body(ename):
                def run(eng):
                    waited = {}

                    def wait(key, val):
                        if waited.get(key, 0) >= val:
                            return
                        eng.wait_ge(sems[key], val)
                        waited[key] = val

                    for op in per_eng[ename]:
                        need = {}
                        for d in op["deps"]:
                            p = ops[d]
                            if not needs_sync(p, ename):
                                continue
                            key, val = p["sig"]
                            need[key] = max(need.get(key, 0), val)
                        if op["dma"]:
                            key, prev = op["prev_on_sem"]
                            if prev > 0:
                                need[key] = max(need.get(key, 0), prev)
                        for key, val in need.items():
                            wait(key, val)
                        ins = op["fn"](eng)
                        if op["signal"]:
                            key, val = op["sig"]
                            ins.then_inc(sems[key], 16 if op["dma"] else 1)
                    if ename == final_wait_eng:
                        for key, val in dma_count.items():
                            wait(key, val)
                        for e in ENGS:
                            if eng_count[e] > 0 and e != ename:
                                wait(("eng", e), eng_count[e])
                return run

            block.tensor(body("pe"))
            block.scalar(body("act"))
            block.vector(body("dve"))
            block.gpsimd(body("pool"))
            block.sync(body("sp"))


SM_C = 0
SM_BADA = 8
SM_NMIX = 104
SM_NFFN = 120
SM_QG = 136
SM_KG = 137
SM_CONV = 138
SM_HVALID = 162
SM_IDENT = 178
NSM = 306


class Ctx:
    def __init__(self, nc):
        self.nc = nc
        self.S = Sched(nc)
        self.wslot = 0
        self.bankrr = 0
        a = lambda n_, sh_, d_: nc.alloc_sbuf_tensor('sb_' + n_, sh_, d_)
        self.small = a("small", [128, NSM], F32)
        self.ident_bf = a("ident_bf", [128, 128], BF16)
        self.ones_bf = a("ones_bf", [128, 128], BF16)
        self.eps_t = a("eps_t", [128, 1], F32)
        self.wring = [a(f"wring{i}", [128, 4096], BF16) for i in range(NW)]
        self.xin = a("xin", [128, 4, D], F32)
        self.xT = a("xT", [128, KC, G], F32)
        self.sq = a("sq", [128, KC, G], BF16)
        self.hT = a("hT", [128, KC, G], BF16)
        self.tmp = [a(f"tmp{i}", [128, G], F32) for i in range(2)]
        self.std = a("std", [128, G], F32)
        self.rstd = a("rstd", [128, G], F32)
        self.big = a("big", [128, 24, G], BF16)
        self.mod = a("mod", [128, 2, 48], F32)
        self.modG = a("modG", [128, 2, 2, KC], F32)
        self.scbf = a("scbf", [128, KC], BF16)
        self.qgs = a("qgs", [128, 1], F32)
        self.pspair = [nc.alloc_psum_tensor(f"psp{i}", [128, 1024], F32) for i in range(3)]
        self.ps = []
        for i in range(3):
            self.ps.append(self.pspair[i][:, 0:512])
            self.ps.append(self.pspair[i][:, 512:1024])
        self.ps.append(nc.alloc_psum_tensor("ps6", [128, 512], F32)[:, :])
        self.psb = nc.alloc_psum_tensor("psb", [128, 1024], BF16)

    @property
    def ident(self):
        return self.small[:, SM_IDENT:SM_IDENT + 128]

    def bank(self, lo=0, hi=7):
        b = lo + self.bankrr % (hi - lo)
        self.bankrr += 1
        return b

    def slab(self, src3d, kparts, ncols):
        rk = []
        if isinstance(src3d, tuple):
            src3d, rk = src3d
        slot = self.wslot % NW
        self.wslot += 1
        view = self.wring[slot][:, 0:kparts * ncols].rearrange("p (k n) -> p k n", k=kparts)
        self.S.add("pool", lambda e: e.dma_start(out=view, in_=src3d), reads=rk, writes=[("w", slot)], dma=True)
        return view, ("w", slot)

    def wcols(self, w2d, c0, ncols, r0=0, kparts=KC):
        rk = []
        if isinstance(w2d, tuple):
            w2d, rk = w2d
        ap = w2d[r0:r0 + kparts * 128, c0:c0 + ncols].rearrange("(k p) n -> p k n", p=128)
        return (ap, rk) if rk else ap

    def precast(self, name, w2d, rows_per):
        R, Ncol = w2d.shape
        wb = self.nc.dram_tensor("Wb_" + name, [R, Ncol], BF16).ap()
        keys = []
        r = 0
        while r < R:
            r1 = min(R, r + rows_per)
            key = ("Wb", name, r)
            self.S.add("pool", lambda e, r=r, r1=r1: e.dma_start(out=wb[r:r1, :], in_=w2d[r:r1, :]),
                       writes=[key], dma=True)
            keys.append(key)
            r = r1
        return (wb, keys)

    def setup(self, small_ap):
        S = self.S
        S.add("sp", lambda e: e.dma_start(out=self.small[:], in_=small_ap), writes=["small"], dma=True)
        S.add("dve", lambda e: e.memset(self.ones_bf[:], 1.0), writes=["ones"])
        S.add("dve", lambda e: e.memset(self.eps_t[:], EPS), writes=["eps"])
        S.add("dve", lambda e: e.tensor_copy(self.ident_bf[:], self.ident), reads=["small"], writes=["identbf"])
        S.add("act", lambda e: e.activation(out=self.scbf[:], in_=self.small[:, SM_C:SM_C + 8], func=AF.Silu),
              reads=["small"], writes=["scbf"])
        S.add("act", lambda e: e.mul(self.qgs[:], self.small[:, SM_QG:SM_QG + 1], DH ** -0.5),
              reads=["small"], writes=["qgs"])

    def adaln(self, w_ada_l, l):
        for _ in self.adaln_gen(w_ada_l, l):
            pass

    def adaln_gen(self, w_ada_l, l):
        S = self.S
        bank = 6
        c0 = 256
        first = True
        for s in range(12):
            if s > 0:
                yield
            view, wk = self.slab(self.wcols(w_ada_l, s * 512, 512), KC, 512)
            for o in range(4):
                oc = s * 4 + o
                for kc in range(KC):
                    wr = ["ps2mod"] + ([("ps", bank)] if first else [])
                    first = False
                    S.add("pe", lambda e, oc=oc, kc=kc, o=o, view=view: e.matmul(
                        self.ps[bank][:, c0 + oc:c0 + oc + 1], lhsT=view[:, kc, o * 128:(o + 1) * 128],
                        rhs=self.scbf[:, kc:kc + 1], start=(kc == 0), stop=(kc == KC - 1)),
                        reads=[wk, "scbf"], writes=wr)
        mod = self.mod
        S.add("dve", lambda e: e.tensor_tensor(out=mod[:, l, :], in0=self.ps[bank][:, c0:c0 + 48],
                                               in1=self.small[:, SM_BADA + 48 * l:SM_BADA + 48 * (l + 1)], op=ALU.add),
              reads=["ps2mod", ("ps", bank), "small"], writes=[("mod", l)])
        S.add("dve", lambda e: e.scalar_tensor_tensor(out=self.modG[:, l, 0, :], in0=mod[:, l, 8:16], scalar=1.0,
                                                      in1=self.small[:, SM_NMIX + 8 * l:SM_NMIX + 8 * (l + 1)],
                                                      op0=ALU.add, op1=ALU.mult),
              reads=[("mod", l), "small"], writes=[("modG", l, 0)])
        S.add("dve", lambda e: e.scalar_tensor_tensor(out=self.modG[:, l, 1, :], in0=mod[:, l, 32:40], scalar=1.0,
                                                      in1=self.small[:, SM_NFFN + 8 * l:SM_NFFN + 8 * (l + 1)],
                                                      op0=ALU.add, op1=ALU.mult),
              reads=[("mod", l), "small"], writes=[("modG", l, 1)])

    def modcols(self, l, which):
        base = 0 if which == 0 else 24
        return (self.modG[:, l, which, :], self.mod[:, l, base:base + 8], self.mod[:, l, base + 16:base + 24],
                [("mod", l), ("modG", l, which)])

    def load_xT(self, x_rows, N=G, ntile=4):
        S = self.S
        xin = self.xin
        if x_rows is not None:
            self.load_x_dma(x_rows, ntile)
        for kc in range(KC):
            bank = self.bank()
            for t in range(ntile):
                S.add("pe", lambda e, kc=kc, t=t, bank=bank: e.transpose(
                    self.ps[bank][:, t * 128:(t + 1) * 128], xin[:, t, kc * 128:(kc + 1) * 128], self.ident),
                    reads=["xin", "small"], writes=[("ps", bank)])
            eng = "dve" if kc % 2 == 0 else "act"
            if eng == "dve":
                S.add("dve", lambda e, kc=kc, bank=bank: e.tensor_copy(self.xT[:, kc, 0:ntile * 128],
                                                                        self.ps[bank][:, 0:ntile * 128]),
                      reads=[("ps", bank)], writes=[("xT", kc)])
            else:
                S.add("act", lambda e, kc=kc, bank=bank: e.copy(self.xT[:, kc, 0:ntile * 128],
                                                                 self.ps[bank][:, 0:ntile * 128]),
                      reads=[("ps", bank)], writes=[("xT", kc)])

    def load_x_dma(self, x_rows, ntile=4):
        xin = self.xin
        self.S.add("sp", lambda e: e.dma_start(out=xin[:, 0:ntile, :], in_=x_rows.rearrange("(t p) f -> p t f", p=128)),
                   writes=["xin"], dma=True)

    def store_xT(self, out_rows, ntile=4):
        S = self.S
        xin = self.xin
        for t in range(ntile):
            for hf in range(2):
                bank = self.bank()
                for k4 in range(4):
                    kc = hf * 4 + k4
                    S.add("pe", lambda e, kc=kc, t=t, bank=bank, k4=k4: e.transpose(
                        self.ps[bank][:, k4 * 128:(k4 + 1) * 128], self.xT[:, kc, t * 128:(t + 1) * 128], self.ident),
                        reads=[("xT", kc), "small"], writes=[("ps", bank)])
                if hf == 0:
                    S.add("dve", lambda e, t=t, bank=bank, hf=hf: e.tensor_copy(xin[:, t, hf * 512:(hf + 1) * 512],
                                                                                 self.ps[bank][:]),
                          reads=[("ps", bank)], writes=["xin"])
                else:
                    S.add("act", lambda e, t=t, bank=bank, hf=hf: e.copy(xin[:, t, hf * 512:(hf + 1) * 512],
                                                                          self.ps[bank][:]),
                          reads=[("ps", bank)], writes=["xin"])
        S.add("sp", lambda e: e.dma_start(out=out_rows.rearrange("(t p) f -> p t f", p=128), in_=xin[:, 0:ntile, :]),
              reads=["xin"], writes=[("out", id(out_rows))], dma=True)

    def layernorm(self, l, which, N=G):
        S = self.S
        Gc, Sc, _, mkeys = self.modcols(l, which)
        for kc in range(KC):
            if kc % 2 == 0:
                S.add("act", lambda e, kc=kc: e.activation(out=self.sq[:, kc, :N], in_=self.xT[:, kc, :N],
                                                           func=AF.Square), reads=[("xT", kc)], writes=[("sq", kc)])
            else:
                S.add("dve", lambda e, kc=kc: e.tensor_tensor(out=self.sq[:, kc, :N], in0=self.xT[:, kc, :N],
                                                              in1=self.xT[:, kc, :N], op=ALU.mult),
                      reads=[("xT", kc)], writes=[("sq", kc)])
        bank = self.bank()
        for kc in range(KC):
            S.add("pe", lambda e, kc=kc: e.matmul(self.ps[bank][:, :N], lhsT=self.ones_bf[:], rhs=self.sq[:, kc, :N],
                                                  start=(kc == 0), stop=(kc == KC - 1)),
                  reads=[("sq", kc), "ones"], writes=[("ps", bank)])
        S.add("act", lambda e: e.activation(out=self.std[:, :N], in_=self.ps[bank][:, :N], func=AF.Ln,
                                            bias=self.eps_t[:, 0:1], scale=1.0 / D),
              reads=[("ps", bank), "eps"], writes=["std"])
        S.add("act", lambda e: e.activation(out=self.rstd[:, :N], in_=self.std[:, :N], func=AF.Exp, scale=-0.5),
              reads=["std"], writes=["rstd"])
        for kc in range(KC):
            tb = kc % 2
            S.add("dve", lambda e, kc=kc, tb=tb: e.tensor_tensor(out=self.tmp[tb][:, :N], in0=self.xT[:, kc, :N],
                                                                 in1=self.rstd[:, :N], op=ALU.mult),
                  reads=[("xT", kc), "rstd"], writes=[("tmp", tb)])
            S.add("act", lambda e, kc=kc, tb=tb: e.activation(out=self.hT[:, kc, :N], in_=self.tmp[tb][:, :N],
                                                              func=AF.Identity, scale=Gc[:, kc:kc + 1],
                                                              bias=Sc[:, kc:kc + 1]),
                  reads=[("tmp", tb)] + mkeys, writes=[("hT", kc)])

    def proj_fm(self, w2d, c0, nchunks, consume, rhs_of=None, rhs_keys=None, N=G, kparts=KC, r0=0):
        S = self.S
        if rhs_of is None:
            rhs_of = lambda kc: self.hT[:, kc, :N]
            rhs_keys = lambda kc: ("hT", kc)
        oc = 0
        while oc < nchunks:
            nch = min(4, nchunks - oc)
            view, wk = self.slab(self.wcols(w2d, c0 + oc * 128, nch * 128, r0=r0, kparts=kparts), kparts, nch * 128)
            for o in range(nch):
                bank = self.bank()
                for kc in range(kparts):
                    S.add("pe", lambda e, kc=kc, o=o, bank=bank, view=view: e.matmul(
                        self.ps[bank][:, :N], lhsT=view[:, kc, o * 128:(o + 1) * 128], rhs=rhs_of(kc),
                        start=(kc == 0), stop=(kc == kparts - 1)),
                        reads=[wk, rhs_keys(kc)], writes=[("ps", bank)])
                consume(oc + o, bank)
            oc += nch

    def residual_add(self, l, which, fc, bank, N=G):
        _, _, gate, mkeys = self.modcols(l, which)
        self.S.add("dve", lambda e: e.scalar_tensor_tensor(out=self.xT[:, fc, :N], in0=self.ps[bank][:, :N],
                                                           scalar=gate[:, fc:fc + 1], in1=self.xT[:, fc, :N],
                                                           op0=ALU.mult, op1=ALU.add),
                   reads=[("ps", bank), ("xT", fc)] + mkeys, writes=[("xT", fc)])

    def ffn(self, l, w_gu, w_dn, N=G):
        S = self.S
        big = self.big
        j = 0
        while j < NJ:
            nch = min(4, NJ - j)
            gview, gk = self.slab(self.wcols(w_gu, j * 128, nch * 128), KC, nch * 128)
            uview, uk = self.slab(self.wcols(w_gu, DFF + j * 128, nch * 128), KC, nch * 128)
            for o in range(nch):
                jj = j + o
                bg = self.bank()
                bu = self.bank()
                for kc in range(KC):
                    S.add("pe", lambda e, kc=kc, o=o, bg=bg, gview=gview: e.matmul(
                        self.ps[bg][:, :N], lhsT=gview[:, kc, o * 128:(o + 1) * 128], rhs=self.hT[:, kc, :N],
                        start=(kc == 0), stop=(kc == KC - 1)), reads=[gk, ("hT", kc)], writes=[("ps", bg)])
                for kc in range(KC):
                    S.add("pe", lambda e, kc=kc, o=o, bu=bu, uview=uview: e.matmul(
                        self.ps[bu][:, :N], lhsT=uview[:, kc, o * 128:(o + 1) * 128], rhs=self.hT[:, kc, :N],
                        start=(kc == 0), stop=(kc == KC - 1)), reads=[uk, ("hT", kc)], writes=[("ps", bu)])
                tb = jj % 2
                S.add("act", lambda e, bg=bg, tb=tb: e.activation(out=self.tmp[tb][:, :N], in_=self.ps[bg][:, :N],
                                                                  func=AF.Silu),
                      reads=[("ps", bg)], writes=[("tmp", tb)])
                S.add("dve", lambda e, bu=bu, tb=tb, jj=jj: e.tensor_tensor(out=big[:, jj, :N], in0=self.ps[bu][:, :N],
                                                                             in1=self.tmp[tb][:, :N], op=ALU.mult),
                      reads=[("ps", bu), ("tmp", tb)], writes=[("big", jj)])
            j += nch
        for fp in range(4):
            v0, k0 = self.slab(self.wcols(w_dn, fp * 256, 256, r0=0, kparts=11), 11, 256)
            v1, k1 = self.slab(self.wcols(w_dn, fp * 256, 256, r0=11 * 128, kparts=11), 11, 256)
            banks = [self.bank(), self.bank()]
            for jj in range(NJ):
                view, wk = (v0, k0) if jj < 11 else (v1, k1)
                for f2 in range(2):
                    S.add("pe", lambda e, jj=jj, f2=f2, view=view, b=banks[f2]: e.matmul(
                        self.ps[b][:, :N], lhsT=view[:, jj % 11, f2 * 128:(f2 + 1) * 128], rhs=big[:, jj, :N],
                        start=(jj == 0), stop=(jj == NJ - 1)), reads=[wk, ("big", jj)], writes=[("ps", banks[f2])])
            for f2 in range(2):
                self.residual_add(l, 1, fp * 2 + f2, banks[f2], N)


def build_fused():
    nc = bass.Bass("TRN2", target_bir_lowering=False)
    dt = nc.dram_tensor
    xseq = dt("xseq", [SEQ, D], F32, kind="ExternalInput").ap()
    xhalo = dt("xhalo", [128, D], F32, kind="ExternalInput").ap()
    small_ap = dt("small", [128, NSM], F32, kind="ExternalInput").ap()
    tabs_ap = dt("tabs", [128, 2048 + 64], F32, kind="ExternalInput").ap()
    pat_ap = dt("pat", [128, 8 * G + 64 * 16], BF16, kind="ExternalInput").ap()
    sel_ap = dt("sel", [128, 32 * 128], BF16, kind="ExternalInput").ap()
    w_ada = dt("w_ada", [2, D, 6 * D], F32, kind="ExternalInput").ap()
    w_qkv = dt("w_qkv", [D, 3 * D], F32, kind="ExternalInput").ap()
    w_o = dt("w_o", [D, D], F32, kind="ExternalInput").ap()
    w_in = dt("w_in", [D, 3 * D], F32, kind="ExternalInput").ap()
    w_out = dt("w_out", [D, D], F32, kind="ExternalInput").ap()
    w_gu = dt("w_gu", [2, D, 2 * DFF], F32, kind="ExternalInput").ap()
    w_dn = dt("w_dn", [2, DFF, D], F32, kind="ExternalInput").ap()
    out = dt("out", [NOWN * G, D], F32, kind="ExternalOutput").ap()
    Kt = dt("Kt", [H, DH, SEQ], BF16).ap()
    Vs = dt("Vs", [H, 128, 64, DH], BF16).ap()
    Qt = dt("Qt", [H, DH, NOWN * G], BF16).ap()
    Ot = dt("Ot", [NOWN, DH, H, G], BF16).ap()

    C = Ctx(nc)
    S = C.S
    a = lambda n_, sh_, d_: nc.alloc_sbuf_tensor('sb_' + n_, sh_, d_)
    tabs = a("tabs", [128, 2048 + 64], F32)
    pat = a("pat", [128, 8 * G + 64 * 16], BF16)
    sel = a("sel", [128, 32, 128], BF16)
    kmean_f = a("kmean_f", [128, H, NBLK], F32)
    kmean_bf = a("kmean_bf", [128, H, NBLK], BF16)
    Kh = a("Kh", [128, SEQ], BF16)
    Vh = a("Vh", [128, 64, DH], BF16)
    Qh = a("Qh", [128, NOWN * G], BF16)
    QhaloT = a("QhaloT", [128, H, 16], BF16)
    oTh = a("oTh", [128, H, 16], BF16)
    pTp = [a(f"pTp{i}", [128, 2, G], BF16) for i in range(2)]
    Rsb = a("Rsb", [128, 4, NBLK], F32)
    max8 = a("max8", [128, 4, 8], F32)
    mb = a("mb", [128, 4, NBLK], BF16)
    mbT = [a(f"mbT{i}", [128, G], BF16) for i in range(2)]
    oTsb = [a(f"oTsb{i}", [128, G], BF16) for i in range(2)]
    uh = a("uh", [128, KC, 16], F32)
    ctmp = a("ctmp", [128, 16], F32)
    cg = a("cg", [128, G], F32)
    ubuf = a("ubuf", [128, 2, 2 + G], F32)
    cv = [a(f"cv{i}", [128, G], F32) for i in range(2)]
    big = C.big
    recip = C.std

    C.setup(small_ap)
    S.add("sp", lambda e: e.dma_start(out=tabs[:], in_=tabs_ap), writes=["tabs"], dma=True)
    S.add("sp", lambda e: e.dma_start(out=pat[:], in_=pat_ap), writes=["pat"], dma=True)
    S.add("sp", lambda e: e.dma_start(out=sel[:].rearrange("p a b -> p (a b)"), in_=sel_ap), writes=["sel"], dma=True)
    S.add("dve", lambda e: e.memset(kmean_f[:], 0.0), writes=["kmean_f"])
    wb_qkv = C.precast("qkv", w_qkv, 256)
    C.adaln(w_ada[0], 0)
    kg = C.small[:, SM_KG:SM_KG + 1]
    acc = a("acc", [128, 2, G], F32)
    for i_ in range(2):
        S.add("dve", lambda e, i_=i_: e.memset(mbT[i_][:], 0.0), writes=[("mbT", i_)])
    ones_f = a("ones_f", [128, 128], F32)
    S.add("dve", lambda e: e.memset(ones_f[:], 1.0), writes=["ones_f"])

    def qk_head(hh, dst_ap, dst_key, gain_ap, gain_keys, wview, wk, o, kmean_pos=None, N=G):
        bank = C.bank()
        for kc in range(KC):
            S.add("pe", lambda e, kc=kc: e.matmul(C.ps[bank][:, :N], lhsT=wview[:, kc, o * 128:(o + 1) * 128],
                                                  rhs=C.hT[:, kc, :N], start=(kc == 0), stop=(kc == KC - 1)),
                  reads=[wk, ("hT", kc)], writes=[("ps", bank)])
        sqk = C.sq[:, hh, :N]
        S.add("act", lambda e: e.activation(out=sqk, in_=C.ps[bank][:, :N], func=AF.Square),
              reads=[("ps", bank)], writes=[("sq", hh)])
        b2 = C.bank()
        S.add("pe", lambda e: e.matmul(C.ps[b2][:, :N], lhsT=C.ones_bf[:], rhs=sqk, start=True, stop=True),
              reads=["ones", ("sq", hh)], writes=[("ps", b2)])
        if hh % 2 == 0:
            stdb, stdk, rstdb, rstdk = C.std, "std", C.rstd, "rstd"
        else:
            stdb, stdk, rstdb, rstdk = C.tmp[0], ("tmp", 0), C.tmp[1], ("tmp", 1)
        S.add("act", lambda e: e.activation(out=stdb[:, :N], in_=C.ps[b2][:, :N], func=AF.Ln, bias=C.eps_t[:, 0:1],
                                            scale=1.0 / DH), reads=[("ps", b2), "eps"], writes=[stdk])
        S.add("act", lambda e: e.activation(out=rstdb[:, :N], in_=stdb[:, :N], func=AF.Exp, scale=-0.5),
              reads=[stdk], writes=[rstdk])
        if kmean_pos is None:
            S.add("dve", lambda e: e.scalar_tensor_tensor(out=dst_ap, in0=C.ps[bank][:, :N], scalar=gain_ap,
                                                          in1=rstdb[:, :N], op0=ALU.mult, op1=ALU.mult),
                  reads=[("ps", bank), rstdk] + gain_keys, writes=[dst_key])
        else:
            for bb in range(2):
                pb = kmean_pos * 2 + bb
                S.add("dve", lambda e, bb=bb, pb=pb: e.scalar_tensor_tensor(
                    out=dst_ap[:, bb * 256:(bb + 1) * 256], in0=C.ps[bank][:, bb * 256:(bb + 1) * 256],
                    scalar=gain_ap, in1=rstdb[:, bb * 256:(bb + 1) * 256], op0=ALU.mult, op1=ALU.mult,
                    accum_out=kmean_f[:, hh, pb:pb + 1]),
                    reads=[("ps", bank), rstdk, "kmean_f"] + gain_keys, writes=[dst_key, "kmean_f"])

    C.load_x_dma(xseq[0:G, :])
    for p in range(NPOS):
        own = (p % 2 == 1)
        C.load_xT(None)
        if p + 1 < NPOS:
            C.load_x_dma(xseq[(p + 1) * G:(p + 2) * G, :])
        C.layernorm(0, 0)
        for s in range(2):
            wview, wk = C.slab(C.wcols(wb_qkv, D + s * 512, 512), KC, 512)
            for o in range(4):
                hh = s * 4 + o
                qk_head(hh, big[:, hh, :], ("big", hh), kg, ["small"], wview, wk, o, kmean_pos=p)
        S.add("sp", lambda e, p=p: e.dma_start(out=Kt.rearrange("h d t -> d h t")[:, :, p * G:(p + 1) * G],
                                               in_=big[:, 0:8, :]),
              reads=[("big", s_) for s_ in range(8)], writes=[("Kt", p)], dma=True)
        for hf in range(2):
            wview, wk = C.slab(C.wcols(wb_qkv, 2 * D + hf * 512, 512), KC, 512)
            for t in range(4):
                bank = C.bank()
                for kc in range(KC):
                    S.add("pe", lambda e, kc=kc, t=t, bank=bank, wview=wview: e.matmul(
                        C.ps[bank][:], lhsT=C.hT[:, kc, t * 128:(t + 1) * 128], rhs=wview[:, kc, :],
                        start=(kc == 0), stop=(kc == KC - 1)), reads=[wk, ("hT", kc)], writes=[("ps", bank)])
                slot = 16 + 2 * t + hf
                if t % 2 == 0:
                    S.add("act", lambda e, bank=bank, slot=slot: e.copy(big[:, slot, :], C.ps[bank][:]),
                          reads=[("ps", bank)], writes=[("big", slot)])
                else:
                    S.add("dve", lambda e, bank=bank, slot=slot: e.tensor_copy(big[:, slot, :], C.ps[bank][:]),
                          reads=[("ps", bank)], writes=[("big", slot)])
        for t in range(4):
            S.add("sp", lambda e, p=p, t=t: e.dma_start(
                out=Vs[:, :, p * 4 + t, :].rearrange("(hf h4) k d -> k hf h4 d", hf=2),
                in_=big[:, 16 + 2 * t:18 + 2 * t, :].rearrange("p hf (h4 d) -> p hf h4 d", h4=4)),
                reads=[("big", 16 + 2 * t), ("big", 17 + 2 * t)], writes=[("Vs", p, t)], dma=True)
        if own:
            i = p // 2
            for s in range(2):
                wview, wk = C.slab(C.wcols(wb_qkv, s * 512, 512), KC, 512)
                for o in range(4):
                    hh = s * 4 + o
                    qk_head(hh, big[:, 8 + hh, :], ("big", 8 + hh), C.qgs[:, 0:1], ["qgs"], wview, wk, o)
            S.add("sp", lambda e, i=i: e.dma_start(out=Qt.rearrange("h d t -> d h t")[:, :, i * G:(i + 1) * G],
                                                   in_=big[:, 8:16, :]),
                  reads=[("big", s_) for s_ in range(8, 16)], writes=[("Qt", i)], dma=True)
    C.load_xT(xhalo, ntile=1)
    C.layernorm(0, 0, N=16)
    for s in range(2):
        wview, wk = C.slab(C.wcols(wb_qkv, s * 512, 512), KC, 512)
        for o in range(4):
            hh = s * 4 + o
            qk_head(hh, QhaloT[:, hh, :], ("QhaloT", hh), C.qgs[:, 0:1], ["qgs"], wview, wk, o, N=16)
    S.add("dve", lambda e: e.tensor_copy(kmean_bf[:], kmean_f[:]), reads=["kmean_f"], writes=["kmean_bf"])

    rb = tabs[:, 0:1024].rearrange("p (i q n) -> p i q n", i=NOWN, q=4)
    npast = tabs[:, 1024:2048].rearrange("p (i q n) -> p i q n", i=NOWN, q=4)
    rb_h = tabs[:, 2048:2080]
    np_h = tabs[:, 2080:2112]
    patg = pat[:, 0:8 * G].rearrange("p (a b) -> p a b", a=8)
    hpat = pat[:, 8 * G:8 * G + 1024].rearrange("p (a b) -> p a b", a=64)
    RB = 6

    def desc_group(hh, i):
        return dict(N=G, nq=4, qrows=128, q_ap=Qh[:, i * G:(i + 1) * G], q_keys=[("Qh", i // 4)],
                    rb=rb[:, i, :, :], npst=npast[:, i, :, :], nkt=8 * i + 8,
                    pat_of=(lambda kt: patg[:, kt - 8 * i, :] if kt >= 8 * i else None), i=i, hh=hh)

    def desc_halo(hh):
        return dict(N=16, nq=1, qrows=16, q_ap=QhaloT[:, hh, :], q_keys=[("QhaloT", hh)],
                    rb=rb_h[0:16, :].rearrange("p (q n) -> p q n", q=1), npst=np_h[0:16, :].rearrange("p (q n) -> p q n", q=1),
                    nkt=64, pat_of=(lambda kt: hpat[:, kt, :]), i=None, hh=hh)

    def route(dsc, par):
        hh, N, nq, qr = dsc["hh"], dsc["N"], dsc["nq"], dsc["qrows"]
        qw = min(N, 128)
        for qt in range(nq):
            S.add("pe", lambda e, qt=qt: e.matmul(C.ps[RB][0:qr, qt * NBLK:(qt + 1) * NBLK],
                                                  lhsT=dsc["q_ap"][:, qt * qw:(qt + 1) * qw],
                                                  rhs=kmean_bf[:, hh, :], start=True, stop=True),
                  reads=dsc["q_keys"] + ["kmean_bf"], writes=[("ps", RB)])
        S.add("dve", lambda e: e.tensor_tensor(out=Rsb[0:qr, 0:nq, :],
                                               in0=C.ps[RB][0:qr, 0:nq * NBLK].rearrange("p (q n) -> p q n", q=nq),
                                               in1=dsc["rb"], op=ALU.add),
              reads=[("ps", RB), "tabs"], writes=["Rsb"])
        for qt in range(nq):
            S.add("dve", lambda e, qt=qt: e.max(out=max8[0:qr, qt, :], in_=Rsb[0:qr, qt, :]), reads=["Rsb"],
                  writes=[("max8", qt)])
            S.add("dve", lambda e, qt=qt: e.scalar_tensor_tensor(out=mb[0:qr, qt, :], in0=Rsb[0:qr, qt, :],
                                                                 scalar=max8[0:qr, qt, 2:3], in1=dsc["npst"][:, qt, :],
                                                                 op0=ALU.is_lt, op1=ALU.mult),
                  reads=["Rsb", ("max8", qt), "tabs"], writes=[("mb", qt)])

    def route_b(dsc, par):
        N, nq, qr = dsc["N"], dsc["nq"], dsc["qrows"]
        for qt in range(nq):
            S.add("pe", lambda e, qt=qt: e.transpose(C.psb[0:32, qt * 128:qt * 128 + qr], mb[0:qr, qt, :],
                                                     C.ident_bf[0:qr, 0:qr]),
                  reads=[("mb", qt), "identbf"], writes=["psb"])
        S.add("act", lambda e: e.copy(mbT[par][0:32, 0:N], C.psb[0:32, 0:N]), reads=["psb"], writes=[("mbT", par)])

    def make_attn(dsc, par):
        hh, N, nkt = dsc["hh"], dsc["N"], dsc["nkt"]
        ob = 4 + par
        db = 4 + (1 - par)
        npair = nkt // 2

        def qkpair(pr):
            X = pr % 2
            for sub in range(2):
                kt = 2 * pr + sub
                sb = 2 * X + sub
                pt = dsc["pat_of"](kt)
                S.add("pe", lambda e, kt=kt, sb=sb: e.matmul(C.ps[sb][:, :N], lhsT=Kh[:, kt * 128:(kt + 1) * 128],
                                                             rhs=dsc["q_ap"], start=True, stop=False),
                      reads=[("Kh", kt // 16)] + dsc["q_keys"], writes=[("ps", sb)])
                S.add("pe", lambda e, sb=sb, pt=pt: e.matmul(C.ps[sb][:, :N], lhsT=sel[:, pr, :], rhs=mbT[par][:, 0:N],
                                                             start=False, stop=(pt is None)),
                      reads=["sel", ("mbT", par)], writes=[("ps", sb)])
                if pt is not None:
                    S.add("pe", lambda e, sb=sb, pt=pt: e.matmul(C.ps[sb][:, :N], lhsT=C.ident_bf[:], rhs=pt, start=False,
                                                                 stop=True),
                          reads=["identbf", "pat"], writes=[("ps", sb)])
            scv = C.pspair[X][:, :].rearrange("p (b n) -> p b n", b=2)[:, :, 0:N]
            S.add("act", lambda e: e.activation(out=pTp[X][:, :, 0:N], in_=scv, func=AF.Exp),
                  reads=[("ps", 2 * X), ("ps", 2 * X + 1)], writes=[("pTp", X)])

        def pvpair(pr):
            X = pr % 2
            for sub in range(2):
                kt = 2 * pr + sub
                S.add("pe", lambda e, kt=kt, sub=sub: e.matmul(C.ps[ob][:, :N], lhsT=Vh[:, kt, :], rhs=pTp[X][:, sub, 0:N],
                                                               start=(kt == 0), stop=(kt == nkt - 1)),
                      reads=[("Vh", kt // 16), ("pTp", X)], writes=[("ps", ob)])
            if pr == 0:
                S.add("dve", lambda e: e.tensor_copy(acc[:, :, 0:N], pTp[X][:, :, 0:N]), reads=[("pTp", X)], writes=["acc"])
            else:
                S.add("dve", lambda e: e.tensor_tensor(out=acc[:, :, 0:N], in0=acc[:, :, 0:N], in1=pTp[X][:, :, 0:N],
                                                       op=ALU.add), reads=[("pTp", X), "acc"], writes=["acc"])

        def epilogue():
            for ai in range(2):
                S.add("pe", lambda e, ai=ai: e.matmul(C.ps[db][:, :N], lhsT=ones_f[:], rhs=acc[:, ai, 0:N],
                                                      start=(ai == 0), stop=(ai == 1)),
                      reads=["ones_f", "acc"], writes=[("ps", db)])
            S.add("act", lambda e: e.activation(out=C.rstd[:, :N], in_=C.ps[db][:, :N], func=AF.Ln),
                  reads=[("ps", db)], writes=["rstd"])
            S.add("act", lambda e: e.activation(out=recip[:, :N], in_=C.rstd[:, :N], func=AF.Exp, scale=-1.0),
                  reads=["rstd"], writes=["std"])
            if dsc["i"] is None:
                S.add("dve", lambda e: e.tensor_tensor(out=oTh[:, hh, :], in0=C.ps[ob][:, :N], in1=recip[:, :N],
                                                       op=ALU.mult), reads=[("ps", ob), "std"], writes=[("oTh", hh)])
            else:
                i = dsc["i"]
                S.add("dve", lambda e: e.tensor_tensor(out=oTsb[par][:], in0=C.ps[ob][:], in1=recip[:], op=ALU.mult),
                      reads=[("ps", ob), "std"], writes=[("oTsb", par)])
                S.add("sp", lambda e: e.dma_start(out=Ot[i, :, hh, :], in_=oTsb[par][:]), reads=[("oTsb", par)],
                      writes=[("Ot", i, hh)], dma=True)

        return dict(qkpair=qkpair, pvpair=pvpair, epilogue=epilogue, npair=npair)

    cnt = 0
    bg_adaln = C.adaln_gen(w_ada[1], 1)
    wb = {}
    for hh in range(H):
        for c4 in range(4):
            if c4 < 2:
                S.add("sp", lambda e, hh=hh, c4=c4: e.dma_start(out=Qh[:, c4 * 2048:(c4 + 1) * 2048],
                                                                in_=Qt[hh, :, c4 * 2048:(c4 + 1) * 2048]),
                      reads=[("Qt", i) for i in range(NOWN)], writes=[("Qh", c4)], dma=True)
            S.add("sp", lambda e, hh=hh, c4=c4: e.dma_start(out=Kh[:, c4 * 2048:(c4 + 1) * 2048],
                                                            in_=Kt[hh, :, c4 * 2048:(c4 + 1) * 2048]),
                  reads=[("Kt", p) for p in range(NPOS)], writes=[("Kh", c4)], dma=True)
            S.add("sp", lambda e, hh=hh, c4=c4: e.dma_start(out=Vh[:, c4 * 16:(c4 + 1) * 16, :],
                                                            in_=Vs[hh, :, c4 * 16:(c4 + 1) * 16, :]),
                  reads=[("Vs", p, t) for p in range(NPOS) for t in range(4)], writes=[("Vh", c4)], dma=True)
        seq = [desc_group(hh, i) for i in range(NOWN)] + [desc_halo(hh)]
        route(seq[0], cnt % 2)
        route_b(seq[0], cnt % 2)
        cur = make_attn(seq[0], cnt % 2)
        cur["qkpair"](0)
        for n_, dsc in enumerate(seq):
            nxt = None
            if n_ + 1 < len(seq):
                route(seq[n_ + 1], (cnt + 1) % 2)
                nxt = make_attn(seq[n_ + 1], (cnt + 1) % 2)
            if hh == 0:
                for _ in range(2):
                    next(bg_adaln, None)
            npair = cur["npair"]
            for pr in range(npair):
                if pr + 1 < npair:
                    cur["qkpair"](pr + 1)
                elif nxt is not None:
                    nxt["qkpair"](0)
                cur["pvpair"](pr)
                if pr == npair - 2 and nxt is not None:
                    route_b(seq[n_ + 1], (cnt + 1) % 2)
            cur["epilogue"]()
            cur = nxt
            cnt += 1
        if hh == 0:
            for _ in bg_adaln:
                pass
            wb["o"] = C.precast("o", w_o, 512)
            wb["gu0"] = C.precast("gu0", w_gu[0], 256)
            wb["dn0"] = C.precast("dn0", w_dn[0], 1408)
            wb["in"] = C.precast("in", w_in, 512)
            wb["out"] = C.precast("out", w_out, 512)
            wb["gu1"] = C.precast("gu1", w_gu[1], 256)
            wb["dn1"] = C.precast("dn1", w_dn[1], 1408)

    C.load_xT(xhalo, ntile=1)
    C.proj_fm(wb["o"], 0, 8, lambda fc, bank: C.residual_add(0, 0, fc, bank, N=16),
              rhs_of=lambda kc: oTh[:, kc, :], rhs_keys=lambda kc: ("oTh", kc), N=16)
    C.layernorm(0, 1, N=16)
    C.ffn(0, wb["gu0"], wb["dn0"], N=16)
    layer1_halo_u(C, 1, uh, ctmp, wb["in"])
    oTs = Qh[:, :].rearrange("p (h t) -> p h t", h=H)
    for i in range(NOWN):
        p = 2 * i + 1
        C.load_xT(xseq[p * G:(p + 1) * G, :])
        S.add("sp", lambda e, i=i: e.dma_start(out=oTs, in_=Ot[i]), reads=[("Ot", i, hh) for hh in range(H)],
              writes=[("Qh", 0), ("Qh", 1)], dma=True)
        C.proj_fm(wb["o"], 0, 8, lambda fc, bank: C.residual_add(0, 0, fc, bank),
                  rhs_of=lambda kc: oTs[:, kc, :], rhs_keys=lambda kc: ("Qh", kc // 4))
        C.layernorm(0, 1)
        C.ffn(0, wb["gu0"], wb["dn0"])
        layer1_group(C, 1, i, None, uh, ubuf, cg, cv, wb["in"], wb["out"], wb["gu1"], wb["dn1"], out[i * G:(i + 1) * G, :])
    S.emit()
    return nc


def layer1_group(C, l, i, x_rows, uh, ubuf, cg, cv, w_in, w_out, w_gu, w_dn, out_rows):
    S = C.S
    big = C.big
    cw = C.small[:, SM_CONV:SM_CONV + 24].rearrange("p (j k) -> p j k", j=3)
    if x_rows is not None:
        C.load_xT(x_rows)
    C.layernorm(l, 0)
    for fc in range(0):
        S.add("dve", lambda e, fc=fc: e.tensor_copy(ubuf[:, fc, 0:2], uh[:, fc, 2 * i:2 * i + 2]),
              reads=[("uh", fc)], writes=[("ubuf", ub)])
    for s in range(2):
        views = []
        for part in range(3):
            views.append(C.slab(C.wcols(w_in, part * D + s * 512, 512), KC, 512))
        for o in range(4):
            fc = s * 4 + o
            banks = [C.bank(), C.bank(), C.bank()]
            for part in (1, 2, 0):
                view, wk = views[part]
                bk = banks[part]
                for kc in range(KC):
                    S.add("pe", lambda e, kc=kc, o=o, bk=bk, view=view: e.matmul(
                        C.ps[bk][:], lhsT=view[:, kc, o * 128:(o + 1) * 128], rhs=C.hT[:, kc, :],
                        start=(kc == 0), stop=(kc == KC - 1)), reads=[wk, ("hT", kc)], writes=[("ps", bk)])
            bb, bc, bu = banks
            ub = fc % 2
            S.add("act", lambda e, fc=fc, ub=ub: e.copy(ubuf[:, ub, 0:2], uh[:, fc, 2 * i:2 * i + 2]),
                  reads=[("uh", fc)], writes=[("ubuf", ub)])
            S.add("act", lambda e, bc=bc: e.copy(cg[:], C.ps[bc][:]), reads=[("ps", bc)], writes=["cg"])
            S.add("dve", lambda e, bu=bu, ub=ub: e.tensor_tensor(out=ubuf[:, ub, 2:2 + G], in0=C.ps[bu][:], in1=cg[:],
                                                                 op=ALU.mult),
                  reads=[("ps", bu), "cg"], writes=[("ubuf", ub)])
            cb = fc % 2
            S.add("dve", lambda e, fc=fc, cb=cb, ub=ub: e.tensor_scalar(out=cv[cb][:], in0=ubuf[:, ub, 0:G],
                                                                  scalar1=cw[:, 0, fc:fc + 1], scalar2=None,
                                                                  op0=ALU.mult),
                  reads=[("ubuf", ub), "small"], writes=[("cv", cb)])
            S.add("dve", lambda e, fc=fc, cb=cb, ub=ub: e.scalar_tensor_tensor(out=cv[cb][:], in0=ubuf[:, ub, 1:1 + G],
                                                                         scalar=cw[:, 1, fc:fc + 1], in1=cv[cb][:],
                                                                         op0=ALU.mult, op1=ALU.add),
                  reads=[("ubuf", ub), "small", ("cv", cb)], writes=[("cv", cb)])
            S.add("dve", lambda e, fc=fc, cb=cb, ub=ub: e.scalar_tensor_tensor(out=cv[cb][:], in0=ubuf[:, ub, 2:2 + G],
                                                                         scalar=cw[:, 2, fc:fc + 1], in1=cv[cb][:],
                                                                         op0=ALU.mult, op1=ALU.add),
                  reads=[("ubuf", ub), "small", ("cv", cb)], writes=[("cv", cb)])
            S.add("dve", lambda e, fc=fc, cb=cb, bb=bb: e.tensor_tensor(out=big[:, fc, :], in0=C.ps[bb][:], in1=cv[cb][:],
                                                                         op=ALU.mult),
                  reads=[("ps", bb), ("cv", cb)], writes=[("big", fc)])
    C.proj_fm(w_out, 0, 8, lambda fc, bank: C.residual_add(l, 0, fc, bank),
              rhs_of=lambda kc: big[:, kc, :], rhs_keys=lambda kc: ("big", kc))
    C.layernorm(l, 1)
    C.ffn(l, w_gu, w_dn)
    C.store_xT(out_rows)


def layer1_halo_u(C, l, uh, ctmp, w_in):
    S = C.S
    hval = C.small[:, SM_HVALID:SM_HVALID + 16]
    C.layernorm(l, 0, N=16)
    for part in range(2):
        def consume(oc, bank, part=part):
            if part == 0:
                S.add("act", lambda e: e.copy(uh[:, oc, :], C.ps[bank][:, 0:16]), reads=[("ps", bank)],
                      writes=[("uh", oc)])
            else:
                S.add("dve", lambda e: e.tensor_tensor(out=ctmp[:], in0=C.ps[bank][:, 0:16], in1=uh[:, oc, :],
                                                       op=ALU.mult), reads=[("ps", bank), ("uh", oc)], writes=["ctmp"])
                S.add("dve", lambda e: e.tensor_tensor(out=uh[:, oc, :], in0=ctmp[:], in1=hval, op=ALU.mult),
                      reads=["ctmp", "small"], writes=[("uh", oc)])
        C.proj_fm(w_in, D + part * D, 8, consume, N=16)


def _g_of_pos(p, half):
    return p if half == 1 else (p ^ 1)


def _tables(half):
    rb = np.zeros((NOWN, 4, NBLK), np.float32)
    npst = np.zeros((NOWN, 4, NBLK), np.float32)
    for i in range(NOWN):
        for qt in range(4):
            nbq = 2 * (2 * i + half) + qt // 2
            for pb in range(NBLK):
                gb = 2 * _g_of_pos(pb // 2, half) + pb % 2
                if gb < nbq:
                    npst[i, qt, pb] = NEG
                else:
                    rb[i, qt, pb] = -1e30
    tabs = np.concatenate([rb.reshape(-1), npst.reshape(-1)])[None, :].repeat(128, 0).astype(np.float32)
    rbh = np.zeros((128, NBLK), np.float32)
    nph = np.zeros((128, NBLK), np.float32)
    hpat = np.zeros((128, 64, 16), np.float32)
    kk = np.arange(128)
    for i in range(NOWN):
        gh = 2 * i + half - 1
        for t in range(2):
            col = 2 * i + t
            if gh < 0:
                continue
            gbq = 2 * gh + 1
            for pb in range(NBLK):
                gb = 2 * _g_of_pos(pb // 2, half) + pb % 2
                if gb < gbq:
                    nph[col, pb] = NEG
                else:
                    rbh[col, pb] = -1e30
                for sub in range(2):
                    kt = pb * 2 + sub
                    if gb < gbq:
                        hpat[:, kt, col] = 0.0
                    elif gb > gbq:
                        hpat[:, kt, col] = NEG
                    else:
                        hpat[:, kt, col] = np.where(sub * 128 + kk <= 254 + t, 0.0, NEG)
    tabs = np.concatenate([tabs, rbh, nph], axis=1).astype(np.float32)
    pat = np.zeros((128, 8, G), np.float32)
    k = np.arange(128)[:, None]
    q = np.arange(G)[None, :]
    qt = q // 128
    for ktw in range(8):
        if ktw < 4:
            pat[:, ktw, :] = 0.0 if half == 1 else NEG
        else:
            kt_ = ktw - 4
            kb = kt_ // 2
            qb = qt // 2
            kpos = (kt_ % 2) * 128 + k
            qpos = (qt % 2) * 128 + (q % 128)
            m = np.where(kb < qb, 0.0, np.where(kb > qb, NEG, np.where(kpos <= qpos, 0.0, NEG)))
            pat[:, ktw, :] = m
    sel = np.zeros((128, 32, 128), np.float32)
    for pb in range(32):
        sel[pb, pb, :] = 1.0
    import ml_dtypes
    bf = ml_dtypes.bfloat16
    patall = np.concatenate([pat.reshape(128, 8 * G), hpat.reshape(128, 1024)], axis=1)
    return tabs, patall.astype(bf), sel.reshape(128, 32 * 128).astype(bf)


def _small(c_b, b_ada_ls, nmix_ls, nffn_ls, q_gain, k_gain, conv_w, half):
    sm = np.zeros((128, NSM), np.float32)
    sm[:, SM_C:SM_C + 8] = c_b.reshape(8, 128).T
    for l, ba in enumerate(b_ada_ls):
        sm[:, SM_BADA + 48 * l:SM_BADA + 48 * (l + 1)] = ba.reshape(48, 128).T
    for l, v in enumerate(nmix_ls):
        sm[:, SM_NMIX + 8 * l:SM_NMIX + 8 * (l + 1)] = v.reshape(8, 128).T
    for l, v in enumerate(nffn_ls):
        sm[:, SM_NFFN + 8 * l:SM_NFFN + 8 * (l + 1)] = v.reshape(8, 128).T
    sm[:, SM_QG] = q_gain
    sm[:, SM_KG] = k_gain
    sm[:, SM_CONV:SM_CONV + 24] = conv_w.reshape(3, 8, 128).transpose(2, 0, 1).reshape(128, 24)
    hv = np.ones(16, np.float32)
    if half == 0:
        hv[0:2] = 0.0
    sm[:, SM_HVALID:SM_HVALID + 16] = hv[None, :]
    sm[:, SM_IDENT:SM_IDENT + 128] = np.eye(128, dtype=np.float32)
    return sm


_NC_CACHE = {}


def _get(name, fn):
    if name not in _NC_CACHE:
        _NC_CACHE[name] = fn()
    return _NC_CACHE[name]


def make_in_maps(x, c, w_ada, b_ada, norm_mix, norm_ffn, w_qkv, w_o, q_gain, k_gain, w_in, conv_w, w_out,
                 w_gate_up, w_down):
    in_maps = []
    for core in range(8):
        b, half = core // 2, core % 2
        perm = [_g_of_pos(p, half) for p in range(NPOS)]
        xseq = np.ascontiguousarray(x[b].reshape(NPOS, G, D)[perm].reshape(SEQ, D))
        xh = np.zeros((128, D), np.float32)
        for i in range(NOWN):
            g = 2 * i + half
            if g > 0:
                xh[2 * i:2 * i + 2] = x[b, g * G - 2:g * G]
            else:
                xh[2 * i:2 * i + 2] = x[b, 0:2]
        tabs, pat, sel = _tables(half)
        sm = _small(c[b], [b_ada[0], b_ada[1]], [norm_mix[0], norm_mix[1]], [norm_ffn[0], norm_ffn[1]],
                    q_gain[0], k_gain[0], conv_w[0], half)
        in_maps.append(dict(xseq=xseq, xhalo=xh, small=sm, tabs=tabs, pat=pat, sel=sel, w_ada=w_ada, w_qkv=w_qkv[0],
                            w_o=w_o[0], w_in=w_in[0], w_out=w_out[0], w_gu=w_gate_up, w_dn=w_down))
    return in_maps


def kernel(x, c, w_ada, b_ada, norm_mix, norm_ffn, w_qkv, w_o, q_gain, k_gain, w_in, conv_w, w_out,
           w_gate_up, w_down):
    f = lambda a: np.ascontiguousarray(np.asarray(a, dtype=np.float32))
    args = list(map(f, (x, c, w_ada, b_ada, norm_mix, norm_ffn, w_qkv, w_o, q_gain, k_gain, w_in, conv_w, w_out,
                        w_gate_up, w_down)))
    x = args[0]
    in_maps = make_in_maps(*args)
    nc = _get("F", build_fused)
    res = run_bass_kernel_spmd(nc, in_maps, core_ids=list(range(8)))
    out = np.zeros_like(x)
    for core in range(8):
        b, half = core // 2, core % 2
        o = np.asarray(res.results[core]["out"]).reshape(NOWN, G, D)
        for i in range(NOWN):
            g = 2 * i + half
            out[b, g * G:(g + 1) * G] = o[i]
    return out
```

```python
import contextlib
import numpy as np
import concourse.bass as bass
import concourse.mybir as mybir
from concourse.bass_utils import run_bass_kernel_spmd

F32 = mybir.dt.float32
BF16 = mybir.dt.bfloat16
ALU = mybir.AluOpType
AF = mybir.ActivationFunctionType

D = 1024
KC = 8
G = 512
H = 8
DH = 128
DFF = 2816
NJ = 22
NPOS = 16
NOWN = 8
NBLK = 32
SEQ = 8192
NEG = -30000.0
EPS = 1e-6
NW = 4
ENGS = ["pe", "act", "dve", "pool", "sp"]
INORDER = ("pe", "act", "dve")


class Sched:
    def __init__(self, nc, n_dma_sems=6):
        self.nc = nc
        self.ops = []
        self.last_write = {}
        self.readers = {}
        self.n_dma_sems = n_dma_sems

    def add(self, eng, fn, reads=(), writes=(), dma=False):
        oid = len(self.ops)
        deps = set()
        for b in reads:
            if b in self.last_write:
                deps.add(self.last_write[b])
        for b in writes:
            if b in self.last_write:
                deps.add(self.last_write[b])
            for r in self.readers.get(b, {}).values():
                deps.update(r)
        deps.discard(oid)
        self.ops.append(dict(id=oid, eng=eng, fn=fn, deps=deps, dma=dma, signal=False))
        for b in reads:
            rd = self.readers.setdefault(b, {})
            if dma or eng not in INORDER:
                rd.setdefault((eng, "dma"), []).append(oid)
            else:
                rd[eng] = [oid]
        for b in writes:
            self.last_write[b] = oid
            self.readers[b] = {}
        return oid

    def emit(self, final_wait_eng="sp"):
        nc = self.nc
        ops = self.ops

        def needs_sync(p, ceng):
            if p["dma"]:
                return True
            if p["eng"] != ceng:
                return True
            return p["eng"] in ("act", "dve", "pool")

        for op in ops:
            for d in op["deps"]:
                p = ops[d]
                if needs_sync(p, op["eng"]):
                    p["signal"] = True
            if op["dma"]:
                op["signal"] = True
        eng_count = {e: 0 for e in ENGS}
        dma_count = {}
        dma_rr = {e: 0 for e in ENGS}
        sem_keys = {}
        for op in ops:
            if not op["signal"]:
                continue
            if op["dma"]:
                k = dma_rr[op["eng"]] % self.n_dma_sems
                dma_rr[op["eng"]] += 1
                key = ("dma", op["eng"], k)
                prev = dma_count.get(key, 0)
                op["prev_on_sem"] = (key, prev)
                dma_count[key] = prev + 16
                op["sig"] = (key, prev + 16)
            else:
                key = ("eng", op["eng"])
                eng_count[op["eng"]] += 1
                op["sig"] = (key, eng_count[op["eng"]])
            sem_keys[key] = None
        with contextlib.ExitStack() as st:
            sems = {}
            for key in sem_keys:
                sems[key] = st.enter_context(nc.semaphore("s_" + "_".join(str(x) for x in key)))
            block = st.enter_context(nc.Block())
            per_eng = {e: [o for o in ops if o["eng"] == e] for e in ENGS}

            def body(ename):
                def run(eng):
                    waited = {}

                    def wait(key, val):
                        if waited.get(key, 0) >= val:
                            return
                        eng.wait_ge(sems[key], val)
                        waited[key] = val

                    for op in per_eng[ename]:
                        need = {}
                        for d in op["deps"]:
                            p = ops[d]
                            if not needs_sync(p, ename):
                                continue
                            key, val = p["sig"]
                            need[key] = max(need.get(key, 0), val)
                        if op["dma"]:
                            key, prev = op["prev_on_sem"]
                            if prev > 0:
                                need[key] = max(need.get(key, 0), prev)
                        for key, val in need.items():
                            wait(key, val)
                        ins = op["fn"](eng)
                        if op["signal"]:
                            key, val = op["sig"]
                            ins.then_inc(sems[key], 16 if op["dma"] else 1)
                    if ename == final_wait_eng:
                        for key, val in dma_count.items():
                            wait(key, val)
                        for e in ENGS:
                            if eng_count[e] > 0 and e != ename:
                                wait(("eng", e), eng_count[e])
                return run

            block.tensor(body("pe"))
            block.scalar(body("act"))
            block.vector(body("dve"))
            block.gpsimd(body("pool"))
            block.sync(body("sp"))


SM_C = 0
SM_BADA = 8
SM_NMIX = 104
SM_NFFN = 120
SM_QG = 136
SM_KG = 137
SM_CONV = 138
SM_HVALID = 162
SM_IDENT = 178
NSM = 306


class Ctx:
    def __init__(self, nc):
        self.nc = nc
        self.S = Sched(nc)
        self.wslot = 0
        self.bankrr = 0
        a = lambda n_, sh_, d_: nc.alloc_sbuf_tensor('sb_' + n_, sh_, d_)
        self.small = a("small", [128, NSM], F32)
        self.ident_bf = a("ident_bf", [128, 128], BF16)
        self.ones_bf = a("ones_bf", [128, 128], BF16)
        self.eps_t = a("eps_t", [128, 1], F32)
        self.wring = [a(f"wring{i}", [128, 4096], BF16) for i in range(NW)]
        self.xin = a("xin", [128, 4, D], F32)
        self.xT = a("xT", [128, KC, G], F32)
        self.sq = a("sq", [128, KC, G], BF16)
        self.hT = a("hT", [128, KC, G], BF16)
        self.tmp = [a(f"tmp{i}", [128, G], F32) for i in range(2)]
        self.std = a("std", [128, G], F32)
        self.rstd = a("rstd", [128, G], F32)
        self.big = a("big", [128, 24, G], BF16)
        self.mod = a("mod", [128, 2, 48], F32)
        self.modG = a("modG", [128, 2, 2, KC], F32)
        self.scbf = a("scbf", [128, KC], BF16)
        self.qgs = a("qgs", [128, 1], F32)
        self.pspair = [nc.alloc_psum_tensor(f"psp{i}", [128, 1024], F32) for i in range(3)]
        self.ps = []
        for i in range(3):
            self.ps.append(self.pspair[i][:, 0:512])
            self.ps.append(self.pspair[i][:, 512:1024])
        self.ps.append(nc.alloc_psum_tensor("ps6", [128, 512], F32)[:, :])
        self.psb = nc.alloc_psum_tensor("psb", [128, 1024], BF16)

    @property
    def ident(self):
        return self.small[:, SM_IDENT:SM_IDENT + 128]

    def bank(self, lo=0, hi=7):
        b = lo + self.bankrr % (hi - lo)
        self.bankrr += 1
        return b

    def slab(self, src3d, kparts, ncols):
        rk = []
        if isinstance(src3d, tuple):
            src3d, rk = src3d
        slot = self.wslot % NW
        self.wslot += 1
        view = self.wring[slot][:, 0:kparts * ncols].rearrange("p (k n) -> p k n", k=kparts)
        self.S.add("pool", lambda e: e.dma_start(out=view, in_=src3d), reads=rk, writes=[("w", slot)], dma=True)
        return view, ("w", slot)

    def wcols(self, w2d, c0, ncols, r0=0, kparts=KC):
        rk = []
        if isinstance(w2d, tuple):
            w2d, rk = w2d
        ap = w2d[r0:r0 + kparts * 128, c0:c0 + ncols].rearrange("(k p) n -> p k n", p=128)
        return (ap, rk) if rk else ap

    def precast(self, name, w2d, rows_per):
        R, Ncol = w2d.shape
        wb = self.nc.dram_tensor("Wb_" + name, [R, Ncol], BF16).ap()
        keys = []
        r = 0
        while r < R:
            r1 = min(R, r + rows_per)
            key = ("Wb", name, r)
            self.S.add("pool", lambda e, r=r, r1=r1: e.dma_start(out=wb[r:r1, :], in_=w2d[r:r1, :]),
                       writes=[key], dma=True)
            keys.append(key)
            r = r1
        return (wb, keys)

    def setup(self, small_ap):
        S = self.S
        S.add("sp", lambda e: e.dma_start(out=self.small[:], in_=small_ap), writes=["small"], dma=True)
        S.add("dve", lambda e: e.memset(self.ones_bf[:], 1.0), writes=["ones"])
        S.add("dve", lambda e: e.memset(self.eps_t[:], EPS), writes=["eps"])
        S.add("dve", lambda e: e.tensor_copy(self.ident_bf[:], self.ident), reads=["small"], writes=["identbf"])
        S.add("act", lambda e: e.activation(out=self.scbf[:], in_=self.small[:, SM_C:SM_C + 8], func=AF.Silu),
              reads=["small"], writes=["scbf"])
        S.add("act", lambda e: e.mul(self.qgs[:], self.small[:, SM_QG:SM_QG + 1], DH ** -0.5),
              reads=["small"], writes=["qgs"])

    def adaln(self, w_ada_l, l):
        for _ in self.adaln_gen(w_ada_l, l):
            pass

    def adaln_gen(self, w_ada_l, l):
        S = self.S
        bank = 6
        c0 = 256
        first = True
        for s in range(12):
            if s > 0:
                yield
            view, wk = self.slab(self.wcols(w_ada_l, s * 512, 512), KC, 512)
            for o in range(4):
                oc = s * 4 + o
                for kc in range(KC):
                    wr = [("ps", bank)]
                    S.add("pe", lambda e, oc=oc, kc=kc, o=o, view=view: e.matmul(
                        self.ps[bank][:, c0 + oc:c0 + oc + 1], lhsT=view[:, kc, o * 128:(o + 1) * 128],
                        rhs=self.scbf[:, kc:kc + 1], start=(kc == 0), stop=(kc == KC - 1)),
                        reads=[wk, "scbf"], writes=wr)
        mod = self.mod
        S.add("dve", lambda e: e.tensor_tensor(out=mod[:, l, :], in0=self.ps[bank][:, c0:c0 + 48],
                                               in1=self.small[:, SM_BADA + 48 * l:SM_BADA + 48 * (l + 1)], op=ALU.add),
              reads=[("ps", bank), "small"], writes=[("mod", l)])
        S.add("dve", lambda e: e.scalar_tensor_tensor(out=self.modG[:, l, 0, :], in0=mod[:, l, 8:16], scalar=1.0,
                                                      in1=self.small[:, SM_NMIX + 8 * l:SM_NMIX + 8 * (l + 1)],
                                                      op0=ALU.add, op1=ALU.mult),
              reads=[("mod", l), "small"], writes=[("modG", l, 0)])
        S.add("dve", lambda e: e.scalar_tensor_tensor(out=self.modG[:, l, 1, :], in0=mod[:, l, 32:40], scalar=1.0,
                                                      in1=self.small[:, SM_NFFN + 8 * l:SM_NFFN + 8 * (l + 1)],
                                                      op0=ALU.add, op1=ALU.mult),
              reads=[("mod", l), "small"], writes=[("modG", l, 1)])

    def modcols(self, l, which):
        base = 0 if which == 0 else 24
        return (self.modG[:, l, which, :], self.mod[:, l, base:base + 8], self.mod[:, l, base + 16:base + 24],
                [("mod", l), ("modG", l, which)])

    def load_xT(self, x_rows, N=G, ntile=4):
        S = self.S
        xin = self.xin
        if x_rows is not None:
            self.load_x_dma(x_rows, ntile)
        for kc in range(KC):
            bank = self.bank()
            for t in range(ntile):
                S.add("pe", lambda e, kc=kc, t=t, bank=bank: e.transpose(
                    self.ps[bank][:, t * 128:(t + 1) * 128], xin[:, t, kc * 128:(kc + 1) * 128], self.ident),
                    reads=["xin", "small"], writes=[("ps", bank)])
            eng = "dve" if kc % 2 == 0 else "act"
            if eng == "dve":
                S.add("dve", lambda e, kc=kc, bank=bank: e.tensor_copy(self.xT[:, kc, 0:ntile * 128],
                                                                        self.ps[bank][:, 0:ntile * 128]),
                      reads=[("ps", bank)], writes=[("xT", kc)])
            else:
                S.add("act", lambda e, kc=kc, bank=bank: e.copy(self.xT[:, kc, 0:ntile * 128],
                                                                 self.ps[bank][:, 0:ntile * 128]),
                      reads=[("ps", bank)], writes=[("xT", kc)])

    def load_x_dma(self, x_rows, ntile=4):
        xin = self.xin
        self.S.add("sp", lambda e: e.dma_start(out=xin[:, 0:ntile, :], in_=x_rows.rearrange("(t p) f -> p t f", p=128)),
                   writes=["xin"], dma=True)

    def store_xT(self, out_rows, ntile=4, staging=None):
        S = self.S
        xin = self.xin
        if staging is not None:
            for t in range(ntile):
                for hf in range(2):
                    st, stk = staging[(t % 2) * 2 + hf]
                    bank = self.bank()
                    for k4 in range(4):
                        kc = hf * 4 + k4
                        S.add("pe", lambda e, kc=kc, t=t, bank=bank, k4=k4: e.transpose(
                            self.ps[bank][:, k4 * 128:(k4 + 1) * 128], self.xT[:, kc, t * 128:(t + 1) * 128],
                            self.ident), reads=[("xT", kc), "small"], writes=[("ps", bank)])
                    if hf == 0:
                        S.add("dve", lambda e, bank=bank, st=st: e.tensor_copy(st, self.ps[bank][:]),
                              reads=[("ps", bank)], writes=[stk])
                    else:
                        S.add("act", lambda e, bank=bank, st=st: e.copy(st, self.ps[bank][:]),
                              reads=[("ps", bank)], writes=[stk])
                    S.add("sp", lambda e, t=t, hf=hf, st=st: e.dma_start(
                        out=out_rows[t * 128:(t + 1) * 128, hf * 512:(hf + 1) * 512], in_=st),
                        reads=[stk], writes=[("out", id(out_rows), t, hf)], dma=True)
            return
        for t in range(ntile):
            for hf in range(2):
                bank = self.bank()
                for k4 in range(4):
                    kc = hf * 4 + k4
                    S.add("pe", lambda e, kc=kc, t=t, bank=bank, k4=k4: e.transpose(
                        self.ps[bank][:, k4 * 128:(k4 + 1) * 128], self.xT[:, kc, t * 128:(t + 1) * 128], self.ident),
                        reads=[("xT", kc), "small"], writes=[("ps", bank)])
                if hf == 0:
                    S.add("dve", lambda e, t=t, bank=bank, hf=hf: e.tensor_copy(xin[:, t, hf * 512:(hf + 1) * 512],
                                                                                 self.ps[bank][:]),
                          reads=[("ps", bank)], writes=["xin"])
                else:
                    S.add("act", lambda e, t=t, bank=bank, hf=hf: e.copy(xin[:, t, hf * 512:(hf + 1) * 512],
                                                                          self.ps[bank][:]),
                          reads=[("ps", bank)], writes=["xin"])
        S.add("sp", lambda e: e.dma_start(out=out_rows.rearrange("(t p) f -> p t f", p=128), in_=xin[:, 0:ntile, :]),
              reads=["xin"], writes=[("out", id(out_rows))], dma=True)

    def layernorm(self, l, which, N=G):
        S = self.S
        Gc, Sc, _, mkeys = self.modcols(l, which)
        for kc in range(KC):
            if kc % 2 == 0:
                S.add("act", lambda e, kc=kc: e.activation(out=self.sq[:, kc, :N], in_=self.xT[:, kc, :N],
                                                           func=AF.Square), reads=[("xT", kc)], writes=[("sq", kc)])
            else:
                S.add("dve", lambda e, kc=kc: e.tensor_tensor(out=self.sq[:, kc, :N], in0=self.xT[:, kc, :N],
                                                              in1=self.xT[:, kc, :N], op=ALU.mult),
                      reads=[("xT", kc)], writes=[("sq", kc)])
        bank = self.bank()
        for kc in range(KC):
            S.add("pe", lambda e, kc=kc: e.matmul(self.ps[bank][:, :N], lhsT=self.ones_bf[:], rhs=self.sq[:, kc, :N],
                                                  start=(kc == 0), stop=(kc == KC - 1)),
                  reads=[("sq", kc), "ones"], writes=[("ps", bank)])
        S.add("act", lambda e: e.activation(out=self.std[:, :N], in_=self.ps[bank][:, :N], func=AF.Ln,
                                            bias=self.eps_t[:, 0:1], scale=1.0 / D),
              reads=[("ps", bank), "eps"], writes=["std"])
        S.add("act", lambda e: e.activation(out=self.rstd[:, :N], in_=self.std[:, :N], func=AF.Exp, scale=-0.5),
              reads=["std"], writes=["rstd"])
        for kc in range(KC):
            tb = kc % 2
            S.add("dve", lambda e, kc=kc, tb=tb: e.tensor_tensor(out=self.tmp[tb][:, :N], in0=self.xT[:, kc, :N],
                                                                 in1=self.rstd[:, :N], op=ALU.mult),
                  reads=[("xT", kc), "rstd"], writes=[("tmp", tb)])
            S.add("act", lambda e, kc=kc, tb=tb: e.activation(out=self.hT[:, kc, :N], in_=self.tmp[tb][:, :N],
                                                              func=AF.Identity, scale=Gc[:, kc:kc + 1],
                                                              bias=Sc[:, kc:kc + 1]),
                  reads=[("tmp", tb)] + mkeys, writes=[("hT", kc)])

    def proj_fm(self, w2d, c0, nchunks, consume, rhs_of=None, rhs_keys=None, N=G, kparts=KC, r0=0):
        S = self.S
        if rhs_of is None:
            rhs_of = lambda kc: self.hT[:, kc, :N]
            rhs_keys = lambda kc: ("hT", kc)
        oc = 0
        while oc < nchunks:
            nch = min(4, nchunks - oc)
            view, wk = self.slab(self.wcols(w2d, c0 + oc * 128, nch * 128, r0=r0, kparts=kparts), kparts, nch * 128)
            for o in range(nch):
                bank = self.bank()
                for kc in range(kparts):
                    S.add("pe", lambda e, kc=kc, o=o, bank=bank, view=view: e.matmul(
                        self.ps[bank][:, :N], lhsT=view[:, kc, o * 128:(o + 1) * 128], rhs=rhs_of(kc),
                        start=(kc == 0), stop=(kc == kparts - 1)),
                        reads=[wk, rhs_keys(kc)], writes=[("ps", bank)])
                consume(oc + o, bank)
            oc += nch

    def residual_add(self, l, which, fc, bank, N=G):
        _, _, gate, mkeys = self.modcols(l, which)
        self.S.add("dve", lambda e: e.scalar_tensor_tensor(out=self.xT[:, fc, :N], in0=self.ps[bank][:, :N],
                                                           scalar=gate[:, fc:fc + 1], in1=self.xT[:, fc, :N],
                                                           op0=ALU.mult, op1=ALU.add),
                   reads=[("ps", bank), ("xT", fc)] + mkeys, writes=[("xT", fc)])

    def ffn(self, l, w_gu, w_dn, N=G):
        S = self.S
        big = self.big
        j = 0
        while j < NJ:
            nch = min(4, NJ - j)
            gview, gk = self.slab(self.wcols(w_gu, j * 128, nch * 128), KC, nch * 128)
            uview, uk = self.slab(self.wcols(w_gu, DFF + j * 128, nch * 128), KC, nch * 128)
            for o in range(nch):
                jj = j + o
                bg = self.bank()
                bu = self.bank()
                for kc in range(KC):
                    S.add("pe", lambda e, kc=kc, o=o, bg=bg, gview=gview: e.matmul(
                        self.ps[bg][:, :N], lhsT=gview[:, kc, o * 128:(o + 1) * 128], rhs=self.hT[:, kc, :N],
                        start=(kc == 0), stop=(kc == KC - 1)), reads=[gk, ("hT", kc)], writes=[("ps", bg)])
                for kc in range(KC):
                    S.add("pe", lambda e, kc=kc, o=o, bu=bu, uview=uview: e.matmul(
                        self.ps[bu][:, :N], lhsT=uview[:, kc, o * 128:(o + 1) * 128], rhs=self.hT[:, kc, :N],
                        start=(kc == 0), stop=(kc == KC - 1)), reads=[uk, ("hT", kc)], writes=[("ps", bu)])
                tb = jj % 2
                S.add("act", lambda e, bg=bg, tb=tb: e.activation(out=self.tmp[tb][:, :N], in_=self.ps[bg][:, :N],
                                                                  func=AF.Silu),
                      reads=[("ps", bg)], writes=[("tmp", tb)])
                S.add("dve", lambda e, bu=bu, tb=tb, jj=jj: e.tensor_tensor(out=big[:, jj, :N], in0=self.ps[bu][:, :N],
                                                                             in1=self.tmp[tb][:, :N], op=ALU.mult),
                      reads=[("ps", bu), ("tmp", tb)], writes=[("big", jj)])
            j += nch
        for fp in range(4):
            v0, k0 = self.slab(self.wcols(w_dn, fp * 256, 256, r0=0, kparts=11), 11, 256)
            v1, k1 = self.slab(self.wcols(w_dn, fp * 256, 256, r0=11 * 128, kparts=11), 11, 256)
            banks = [self.bank(), self.bank()]
            for jj in range(NJ):
                view, wk = (v0, k0) if jj < 11 else (v1, k1)
                for f2 in range(2):
                    S.add("pe", lambda e, jj=jj, f2=f2, view=view, b=banks[f2]: e.matmul(
                        self.ps[b][:, :N], lhsT=view[:, jj % 11, f2 * 128:(f2 + 1) * 128], rhs=big[:, jj, :N],
                        start=(jj == 0), stop=(jj == NJ - 1)), reads=[wk, ("big", jj)], writes=[("ps", banks[f2])])
            for f2 in range(2):
                self.residual_add(l, 1, fp * 2 + f2, banks[f2], N)


def build_fused():
    nc = bass.Bass("TRN2", target_bir_lowering=False)
    dt = nc.dram_tensor
    xseq = dt("xseq", [SEQ, D], F32, kind="ExternalInput").ap()
    xhalo = dt("xhalo", [128, D], F32, kind="ExternalInput").ap()
    small_ap = dt("small", [128, NSM], F32, kind="ExternalInput").ap()
    tabs_ap = dt("tabs", [128, 2048 + 64], F32, kind="ExternalInput").ap()
    pat_ap = dt("pat", [128, 8 * G + 64 * 16], BF16, kind="ExternalInput").ap()
    sel_ap = dt("sel", [128, 32 * 128], BF16, kind="ExternalInput").ap()
    w_ada = dt("w_ada", [2, D, 6 * D], F32, kind="ExternalInput").ap()
    w_qkv = dt("w_qkv", [D, 3 * D], F32, kind="ExternalInput").ap()
    w_o = dt("w_o", [D, D], F32, kind="ExternalInput").ap()
    w_in = dt("w_in", [D, 3 * D], F32, kind="ExternalInput").ap()
    w_out = dt("w_out", [D, D], F32, kind="ExternalInput").ap()
    w_gu = dt("w_gu", [2, D, 2 * DFF], F32, kind="ExternalInput").ap()
    w_dn = dt("w_dn", [2, DFF, D], F32, kind="ExternalInput").ap()
    out = dt("out", [NOWN * G, D], F32, kind="ExternalOutput").ap()
    Kt = dt("Kt", [H, DH, SEQ], BF16).ap()
    Vs = dt("Vs", [H, 128, 64, DH], BF16).ap()
    Qt = dt("Qt", [H, DH, NOWN * G], BF16).ap()
    Ot = dt("Ot", [NOWN, DH, H, G], BF16).ap()

    C = Ctx(nc)
    S = C.S
    a = lambda n_, sh_, d_: nc.alloc_sbuf_tensor('sb_' + n_, sh_, d_)
    tabs = a("tabs", [128, 2048 + 64], F32)
    pat = a("pat", [128, 8 * G + 64 * 16], BF16)
    sel = a("sel", [128, 32, 128], BF16)
    kmean_f = a("kmean_f", [128, H, NBLK], F32)
    kmean_bf = a("kmean_bf", [128, H, NBLK], BF16)
    Kh = a("Kh", [128, SEQ], BF16)
    Vh = a("Vh", [128, 64, DH], BF16)
    Qh = a("Qh", [128, NOWN * G], BF16)
    QhaloT = a("QhaloT", [128, H, 16], BF16)
    oTh = a("oTh", [128, H, 16], BF16)
    pTp = [a(f"pTp{i}", [128, 2, G], BF16) for i in range(2)]
    Rsb = a("Rsb", [128, 4, NBLK], F32)
    max8 = a("max8", [128, 4, 8], F32)
    mb = a("mb", [128, 4, NBLK], BF16)
    mbT = [a(f"mbT{i}", [128, G], BF16) for i in range(2)]
    oTsb = [a(f"oTsb{i}", [128, G], BF16) for i in range(2)]
    uh = a("uh", [128, KC, 16], F32)
    ctmp = a("ctmp", [128, 16], F32)
    cg = a("cg", [128, G], F32)
    ubuf = a("ubuf", [128, 2, 2 + G], F32)
    cv = [a(f"cv{i}", [128, G], F32) for i in range(2)]
    big = C.big
    recip = C.std

    C.setup(small_ap)
    S.add("sp", lambda e: e.dma_start(out=tabs[:], in_=tabs_ap), writes=["tabs"], dma=True)
    S.add("sp", lambda e: e.dma_start(out=pat[:], in_=pat_ap), writes=["pat"], dma=True)
    S.add("sp", lambda e: e.dma_start(out=sel[:].rearrange("p a b -> p (a b)"), in_=sel_ap), writes=["sel"], dma=True)
    S.add("dve", lambda e: e.memset(kmean_f[:], 0.0), writes=["kmean_f"])
    wb_qkv = C.precast("qkv", w_qkv, 256)
    C.adaln(w_ada[0], 0)
    kg = C.small[:, SM_KG:SM_KG + 1]
    acc = a("acc", [128, 2, G], F32)
    for i_ in range(2):
        S.add("dve", lambda e, i_=i_: e.memset(mbT[i_][:], 0.0), writes=[("mbT", i_)])
    ones_f = a("ones_f", [128, 128], F32)
    S.add("dve", lambda e: e.memset(ones_f[:], 1.0), writes=["ones_f"])

    def qk_head(hh, dst_ap, dst_key, gain_ap, gain_keys, wview, wk, o, kmean_pos=None, N=G):
        bank = C.bank()
        for kc in range(KC):
            S.add("pe", lambda e, kc=kc: e.matmul(C.ps[bank][:, :N], lhsT=wview[:, kc, o * 128:(o + 1) * 128],
                                                  rhs=C.hT[:, kc, :N], start=(kc == 0), stop=(kc == KC - 1)),
                  reads=[wk, ("hT", kc)], writes=[("ps", bank)])
        sqk = C.sq[:, hh, :N]
        S.add("act", lambda e: e.activation(out=sqk, in_=C.ps[bank][:, :N], func=AF.Square),
              reads=[("ps", bank)], writes=[("sq", hh)])
        b2 = C.bank()
        S.add("pe", lambda e: e.matmul(C.ps[b2][:, :N], lhsT=C.ones_bf[:], rhs=sqk, start=True, stop=True),
              reads=["ones", ("sq", hh)], writes=[("ps", b2)])
        if hh % 2 == 0:
            stdb, stdk, rstdb, rstdk = C.std, "std", C.rstd, "rstd"
        else:
            stdb, stdk, rstdb, rstdk = C.tmp[0], ("tmp", 0), C.tmp[1], ("tmp", 1)
        S.add("act", lambda e: e.activation(out=stdb[:, :N], in_=C.ps[b2][:, :N], func=AF.Ln, bias=C.eps_t[:, 0:1],
                                            scale=1.0 / DH), reads=[("ps", b2), "eps"], writes=[stdk])
        S.add("act", lambda e: e.activation(out=rstdb[:, :N], in_=stdb[:, :N], func=AF.Exp, scale=-0.5),
              reads=[stdk], writes=[rstdk])
        if kmean_pos is None:
            S.add("dve", lambda e: e.scalar_tensor_tensor(out=dst_ap, in0=C.ps[bank][:, :N], scalar=gain_ap,
                                                          in1=rstdb[:, :N], op0=ALU.mult, op1=ALU.mult),
                  reads=[("ps", bank), rstdk] + gain_keys, writes=[dst_key])
        else:
            for bb in range(2):
                pb = kmean_pos * 2 + bb
                S.add("dve", lambda e, bb=bb, pb=pb: e.scalar_tensor_tensor(
                    out=dst_ap[:, bb * 256:(bb + 1) * 256], in0=C.ps[bank][:, bb * 256:(bb + 1) * 256],
                    scalar=gain_ap, in1=rstdb[:, bb * 256:(bb + 1) * 256], op0=ALU.mult, op1=ALU.mult,
                    accum_out=kmean_f[:, hh, pb:pb + 1]),
                    reads=[("ps", bank), rstdk, "kmean_f"] + gain_keys, writes=[dst_key, "kmean_f"])

    C.load_x_dma(xseq[0:G, :])
    for p in range(NPOS):
        own = (p % 2 == 1)
        C.load_xT(None)
        if p + 1 < NPOS:
            C.load_x_dma(xseq[(p + 1) * G:(p + 2) * G, :])
        C.layernorm(0, 0)
        for s in range(2):
            wview, wk = C.slab(C.wcols(wb_qkv, D + s * 512, 512), KC, 512)
            for o in range(4):
                hh = s * 4 + o
                qk_head(hh, big[:, hh, :], ("big", hh), kg, ["small"], wview, wk, o, kmean_pos=p)
        S.add("sp", lambda e, p=p: e.dma_start(out=Kt.rearrange("h d t -> d h t")[:, :, p * G:(p + 1) * G],
                                               in_=big[:, 0:8, :]),
              reads=[("big", s_) for s_ in range(8)], writes=[("Kt", p)], dma=True)
        for hf in range(2):
            wview, wk = C.slab(C.wcols(wb_qkv, 2 * D + hf * 512, 512), KC, 512)
            for t in range(4):
                bank = C.bank()
                for kc in range(KC):
                    S.add("pe", lambda e, kc=kc, t=t, bank=bank, wview=wview: e.matmul(
                        C.ps[bank][:], lhsT=C.hT[:, kc, t * 128:(t + 1) * 128], rhs=wview[:, kc, :],
                        start=(kc == 0), stop=(kc == KC - 1)), reads=[wk, ("hT", kc)], writes=[("ps", bank)])
                slot = 16 + 2 * t + hf
                if t % 2 == 0:
                    S.add("act", lambda e, bank=bank, slot=slot: e.copy(big[:, slot, :], C.ps[bank][:]),
                          reads=[("ps", bank)], writes=[("big", slot)])
                else:
                    S.add("dve", lambda e, bank=bank, slot=slot: e.tensor_copy(big[:, slot, :], C.ps[bank][:]),
                          reads=[("ps", bank)], writes=[("big", slot)])
        for t in range(4):
            S.add("sp", lambda e, p=p, t=t: e.dma_start(
                out=Vs[:, :, p * 4 + t, :].rearrange("(hf h4) k d -> k hf h4 d", hf=2),
                in_=big[:, 16 + 2 * t:18 + 2 * t, :].rearrange("p hf (h4 d) -> p hf h4 d", h4=4)),
                reads=[("big", 16 + 2 * t), ("big", 17 + 2 * t)], writes=[("Vs", p, t)], dma=True)
        if own:
            i = p // 2
            for s in range(2):
                wview, wk = C.slab(C.wcols(wb_qkv, s * 512, 512), KC, 512)
                for o in range(4):
                    hh = s * 4 + o
                    qk_head(hh, big[:, 8 + hh, :], ("big", 8 + hh), C.qgs[:, 0:1], ["qgs"], wview, wk, o)
            S.add("sp", lambda e, i=i: e.dma_start(out=Qt.rearrange("h d t -> d h t")[:, :, i * G:(i + 1) * G],
                                                   in_=big[:, 8:16, :]),
                  reads=[("big", s_) for s_ in range(8, 16)], writes=[("Qt", i)], dma=True)
    C.load_xT(xhalo, ntile=1)
    C.layernorm(0, 0, N=16)
    for s in range(2):
        wview, wk = C.slab(C.wcols(wb_qkv, s * 512, 512), KC, 512)
        for o in range(4):
            hh = s * 4 + o
            qk_head(hh, QhaloT[:, hh, :], ("QhaloT", hh), C.qgs[:, 0:1], ["qgs"], wview, wk, o, N=16)
    S.add("dve", lambda e: e.tensor_copy(kmean_bf[:], kmean_f[:]), reads=["kmean_f"], writes=["kmean_bf"])

    rb = tabs[:, 0:1024].rearrange("p (i q n) -> p i q n", i=NOWN, q=4)
    npast = tabs[:, 1024:2048].rearrange("p (i q n) -> p i q n", i=NOWN, q=4)
    rb_h = tabs[:, 2048:2080]
    np_h = tabs[:, 2080:2112]
    patg = pat[:, 0:8 * G].rearrange("p (a b) -> p a b", a=8)
    hpat = pat[:, 8 * G:8 * G + 1024].rearrange("p (a b) -> p a b", a=64)
    RB = 6

    def desc_group(hh, i):
        return dict(N=G, nq=4, qrows=128, q_ap=Qh[:, i * G:(i + 1) * G], q_keys=[("Qh", i // 4)],
                    rb=rb[:, i, :, :], npst=npast[:, i, :, :], nkt=8 * i + 8,
                    pat_of=(lambda kt: patg[:, kt - 8 * i, :] if kt >= 8 * i else None), i=i, hh=hh)

    def desc_halo(hh):
        return dict(N=16, nq=1, qrows=16, q_ap=QhaloT[:, hh, :], q_keys=[("QhaloT", hh)],
                    rb=rb_h[0:16, :].rearrange("p (q n) -> p q n", q=1), npst=np_h[0:16, :].rearrange("p (q n) -> p q n", q=1),
                    nkt=64, pat_of=(lambda kt: hpat[:, kt, :]), i=None, hh=hh)

    def route(dsc, par):
        hh, N, nq, qr = dsc["hh"], dsc["N"], dsc["nq"], dsc["qrows"]
        qw = min(N, 128)
        for qt in range(nq):
            S.add("pe", lambda e, qt=qt: e.matmul(C.ps[RB][0:qr, qt * NBLK:(qt + 1) * NBLK],
                                                  lhsT=dsc["q_ap"][:, qt * qw:(qt + 1) * qw],
                                                  rhs=kmean_bf[:, hh, :], start=True, stop=True),
                  reads=dsc["q_keys"] + ["kmean_bf"], writes=[("ps", RB)])
        S.add("dve", lambda e: e.tensor_tensor(out=Rsb[0:qr, 0:nq, :],
                                               in0=C.ps[RB][0:qr, 0:nq * NBLK].rearrange("p (q n) -> p q n", q=nq),
                                               in1=dsc["rb"], op=ALU.add),
              reads=[("ps", RB), "tabs"], writes=["Rsb"])
        for qt in range(nq):
            S.add("dve", lambda e, qt=qt: e.max(out=max8[0:qr, qt, :], in_=Rsb[0:qr, qt, :]), reads=["Rsb"],
                  writes=[("max8", qt)])
            S.add("dve", lambda e, qt=qt: e.scalar_tensor_tensor(out=mb[0:qr, qt, :], in0=Rsb[0:qr, qt, :],
                                                                 scalar=max8[0:qr, qt, 2:3], in1=dsc["npst"][:, qt, :],
                                                                 op0=ALU.is_lt, op1=ALU.mult),
                  reads=["Rsb", ("max8", qt), "tabs"], writes=[("mb", qt)])

    def route_b(dsc, par):
        N, nq, qr = dsc["N"], dsc["nq"], dsc["qrows"]
        for qt in range(nq):
            S.add("pe", lambda e, qt=qt: e.transpose(C.psb[0:32, qt * 128:qt * 128 + qr], mb[0:qr, qt, :],
                                                     C.ident_bf[0:qr, 0:qr]),
                  reads=[("mb", qt), "identbf"], writes=["psb"])
        S.add("act", lambda e: e.copy(mbT[par][0:32, 0:N], C.psb[0:32, 0:N]), reads=["psb"], writes=[("mbT", par)])

    def make_attn(dsc, par):
        hh, N, nkt = dsc["hh"], dsc["N"], dsc["nkt"]
        ob = 4 + par
        db = 4 + (1 - par)
        npair = nkt // 2

        def qkpair(pr):
            X = pr % 2
            for sub in range(2):
                kt = 2 * pr + sub
                sb = 2 * X + sub
                pt = dsc["pat_of"](kt)
                S.add("pe", lambda e, kt=kt, sb=sb: e.matmul(C.ps[sb][:, :N], lhsT=Kh[:, kt * 128:(kt + 1) * 128],
                                                             rhs=dsc["q_ap"], start=True, stop=False),
                      reads=[("Kh", kt // 16)] + dsc["q_keys"], writes=[("ps", sb)])
                S.add("pe", lambda e, sb=sb, pt=pt: e.matmul(C.ps[sb][:, :N], lhsT=sel[:, pr, :], rhs=mbT[par][:, 0:N],
                                                             start=False, stop=(pt is None)),
                      reads=["sel", ("mbT", par)], writes=[("ps", sb)])
                if pt is not None:
                    S.add("pe", lambda e, sb=sb, pt=pt: e.matmul(C.ps[sb][:, :N], lhsT=C.ident_bf[:], rhs=pt, start=False,
                                                                 stop=True),
                          reads=["identbf", "pat"], writes=[("ps", sb)])
            scv = C.pspair[X][:, :].rearrange("p (b n) -> p b n", b=2)[:, :, 0:N]
            S.add("act", lambda e: e.activation(out=pTp[X][:, :, 0:N], in_=scv, func=AF.Exp),
                  reads=[("ps", 2 * X), ("ps", 2 * X + 1)], writes=[("pTp", X)])

        def pvpair(pr):
            X = pr % 2
            for sub in range(2):
                kt = 2 * pr + sub
                S.add("pe", lambda e, kt=kt, sub=sub: e.matmul(C.ps[ob][:, :N], lhsT=Vh[:, kt, :], rhs=pTp[X][:, sub, 0:N],
                                                               start=(kt == 0), stop=(kt == nkt - 1)),
                      reads=[("Vh", kt // 16), ("pTp", X)], writes=[("ps", ob)])
            if pr == 0:
                S.add("dve", lambda e: e.tensor_copy(acc[:, :, 0:N], pTp[X][:, :, 0:N]), reads=[("pTp", X)], writes=["acc"])
            else:
                S.add("dve", lambda e: e.tensor_tensor(out=acc[:, :, 0:N], in0=acc[:, :, 0:N], in1=pTp[X][:, :, 0:N],
                                                       op=ALU.add), reads=[("pTp", X), "acc"], writes=["acc"])

        def epilogue():
            for ai in range(2):
                S.add("pe", lambda e, ai=ai: e.matmul(C.ps[db][:, :N], lhsT=ones_f[:], rhs=acc[:, ai, 0:N],
                                                      start=(ai == 0), stop=(ai == 1)),
                      reads=["ones_f", "acc"], writes=[("ps", db)])
            S.add("act", lambda e: e.activation(out=C.rstd[:, :N], in_=C.ps[db][:, :N], func=AF.Ln),
                  reads=[("ps", db)], writes=["rstd"])
            S.add("act", lambda e: e.activation(out=recip[:, :N], in_=C.rstd[:, :N], func=AF.Exp, scale=-1.0),
                  reads=["rstd"], writes=["std"])
            if dsc["i"] is None:
                S.add("dve", lambda e: e.tensor_tensor(out=oTh[:, hh, :], in0=C.ps[ob][:, :N], in1=recip[:, :N],
                                                       op=ALU.mult), reads=[("ps", ob), "std"], writes=[("oTh", hh)])
            else:
                i = dsc["i"]
                S.add("dve", lambda e: e.tensor_tensor(out=oTsb[par][:], in0=C.ps[ob][:], in1=recip[:], op=ALU.mult),
                      reads=[("ps", ob), "std"], writes=[("oTsb", par)])
                S.add("sp", lambda e: e.dma_start(out=Ot[i, :, hh, :], in_=oTsb[par][:]), reads=[("oTsb", par)],
                      writes=[("Ot", i, hh)], dma=True)

        return dict(qkpair=qkpair, pvpair=pvpair, epilogue=epilogue, npair=npair)

    cnt = 0
    bg_adaln = C.adaln_gen(w_ada[1], 1)
    wb = {}
    for hh in range(H):
        for c4 in range(4):
            if c4 < 2:
                S.add("sp", lambda e, hh=hh, c4=c4: e.dma_start(out=Qh[:, c4 * 2048:(c4 + 1) * 2048],
                                                                in_=Qt[hh, :, c4 * 2048:(c4 + 1) * 2048]),
                      reads=[("Qt", i) for i in range(NOWN)], writes=[("Qh", c4)], dma=True)
            S.add("sp", lambda e, hh=hh, c4=c4: e.dma_start(out=Kh[:, c4 * 2048:(c4 + 1) * 2048],
                                                            in_=Kt[hh, :, c4 * 2048:(c4 + 1) * 2048]),
                  reads=[("Kt", p) for p in range(NPOS)], writes=[("Kh", c4)], dma=True)
            S.add("sp", lambda e, hh=hh, c4=c4: e.dma_start(out=Vh[:, c4 * 16:(c4 + 1) * 16, :],
                                                            in_=Vs[hh, :, c4 * 16:(c4 + 1) * 16, :]),
                  reads=[("Vs", p, t) for p in range(NPOS) for t in range(4)], writes=[("Vh", c4)], dma=True)
        seq = [desc_group(hh, i) for i in range(NOWN)] + [desc_halo(hh)]
        route(seq[0], cnt % 2)
        route_b(seq[0], cnt % 2)
        cur = make_attn(seq[0], cnt % 2)
        cur["qkpair"](0)
        for n_, dsc in enumerate(seq):
            nxt = None
            if n_ + 1 < len(seq):
                route(seq[n_ + 1], (cnt + 1) % 2)
                nxt = make_attn(seq[n_ + 1], (cnt + 1) % 2)
            if hh == 0:
                for _ in range(2):
                    next(bg_adaln, None)
            npair = cur["npair"]
            for pr in range(npair):
                if pr + 1 < npair:
                    cur["qkpair"](pr + 1)
                elif nxt is not None:
                    nxt["qkpair"](0)
                cur["pvpair"](pr)
                if pr == npair - 2 and nxt is not None:
                    route_b(seq[n_ + 1], (cnt + 1) % 2)
            cur["epilogue"]()
            cur = nxt
            cnt += 1
        if hh == 0:
            for _ in bg_adaln:
                pass
            wb["o"] = C.precast("o", w_o, 512)
            wb["gu0"] = C.precast("gu0", w_gu[0], 256)
            wb["dn0"] = C.precast("dn0", w_dn[0], 1408)
            wb["in"] = C.precast("in", w_in, 512)
            wb["out"] = C.precast("out", w_out, 512)
            wb["gu1"] = C.precast("gu1", w_gu[1], 256)
            wb["dn1"] = C.precast("dn1", w_dn[1], 1408)

    C.load_xT(xhalo, ntile=1)
    C.load_x_dma(xseq[G:2 * G, :])
    C.proj_fm(wb["o"], 0, 8, lambda fc, bank: C.residual_add(0, 0, fc, bank, N=16),
              rhs_of=lambda kc: oTh[:, kc, :], rhs_keys=lambda kc: ("oTh", kc), N=16)
    C.layernorm(0, 1, N=16)
    C.ffn(0, wb["gu0"], wb["dn0"], N=16)
    layer1_halo_u(C, 1, uh, ctmp, wb["in"])
    oTs = Qh[:, :].rearrange("p (h t) -> p h t", h=H)
    S.add("sp", lambda e: e.dma_start(out=oTs, in_=Ot[0]), reads=[("Ot", 0, hh) for hh in range(H)],
          writes=[("Qh", 0), ("Qh", 1)], dma=True)
    for i in range(NOWN):
        p = 2 * i + 1
        C.load_xT(None)
        if i + 1 < NOWN:
            C.load_x_dma(xseq[(p + 2) * G:(p + 3) * G, :])
        C.proj_fm(wb["o"], 0, 8, lambda fc, bank: C.residual_add(0, 0, fc, bank),
                  rhs_of=lambda kc: oTs[:, kc, :], rhs_keys=lambda kc: ("Qh", kc // 4))
        if i + 1 < NOWN:
            S.add("sp", lambda e, i=i: e.dma_start(out=oTs, in_=Ot[i + 1]), reads=[("Ot", i + 1, hh) for hh in range(H)],
                  writes=[("Qh", 0), ("Qh", 1)], dma=True)
        C.layernorm(0, 1)
        C.ffn(0, wb["gu0"], wb["dn0"])
        layer1_group(C, 1, i, None, uh, ubuf, cg, cv, wb["in"], wb["out"], wb["gu1"], wb["dn1"], out[i * G:(i + 1) * G, :])
    S.emit()
    return nc


def layer1_group(C, l, i, x_rows, uh, ubuf, cg, cv, w_in, w_out, w_gu, w_dn, out_rows):
    S = C.S
    big = C.big
    cw = C.small[:, SM_CONV:SM_CONV + 24].rearrange("p (j k) -> p j k", j=3)
    if x_rows is not None:
        C.load_xT(x_rows)
    C.layernorm(l, 0)
    for fc in range(0):
        S.add("dve", lambda e, fc=fc: e.tensor_copy(ubuf[:, fc, 0:2], uh[:, fc, 2 * i:2 * i + 2]),
              reads=[("uh", fc)], writes=[("ubuf", ub)])
    for s in range(2):
        views = []
        for part in range(3):
            views.append(C.slab(C.wcols(w_in, part * D + s * 512, 512), KC, 512))
        for o in range(4):
            fc = s * 4 + o
            banks = [C.bank(), C.bank(), C.bank()]
            for part in (1, 2, 0):
                view, wk = views[part]
                bk = banks[part]
                for kc in range(KC):
                    S.add("pe", lambda e, kc=kc, o=o, bk=bk, view=view: e.matmul(
                        C.ps[bk][:], lhsT=view[:, kc, o * 128:(o + 1) * 128], rhs=C.hT[:, kc, :],
                        start=(kc == 0), stop=(kc == KC - 1)), reads=[wk, ("hT", kc)], writes=[("ps", bk)])
            bb, bc, bu = banks
            ub = fc % 2
            S.add("act", lambda e, fc=fc, ub=ub: e.copy(ubuf[:, ub, 0:2], uh[:, fc, 2 * i:2 * i + 2]),
                  reads=[("uh", fc)], writes=[("ubuf", ub)])
            S.add("act", lambda e, bc=bc: e.copy(cg[:], C.ps[bc][:]), reads=[("ps", bc)], writes=["cg"])
            S.add("dve", lambda e, bu=bu, ub=ub: e.tensor_tensor(out=ubuf[:, ub, 2:2 + G], in0=C.ps[bu][:], in1=cg[:],
                                                                 op=ALU.mult),
                  reads=[("ps", bu), "cg"], writes=[("ubuf", ub)])
            cb = fc % 2
            S.add("dve", lambda e, fc=fc, cb=cb, ub=ub: e.tensor_scalar(out=cv[cb][:], in0=ubuf[:, ub, 0:G],
                                                                  scalar1=cw[:, 0, fc:fc + 1], scalar2=None,
                                                                  op0=ALU.mult),
                  reads=[("ubuf", ub), "small"], writes=[("cv", cb)])
            S.add("dve", lambda e, fc=fc, cb=cb, ub=ub: e.scalar_tensor_tensor(out=cv[cb][:], in0=ubuf[:, ub, 1:1 + G],
                                                                         scalar=cw[:, 1, fc:fc + 1], in1=cv[cb][:],
                                                                         op0=ALU.mult, op1=ALU.add),
                  reads=[("ubuf", ub), "small", ("cv", cb)], writes=[("cv", cb)])
            S.add("dve", lambda e, fc=fc, cb=cb, ub=ub: e.scalar_tensor_tensor(out=cv[cb][:], in0=ubuf[:, ub, 2:2 + G],
                                                                         scalar=cw[:, 2, fc:fc + 1], in1=cv[cb][:],
                                                                         op0=ALU.mult, op1=ALU.add),
                  reads=[("ubuf", ub), "small", ("cv", cb)], writes=[("cv", cb)])
            S.add("dve", lambda e, fc=fc, cb=cb, bb=bb: e.tensor_tensor(out=big[:, fc, :], in0=C.ps[bb][:], in1=cv[cb][:],
                                                                         op=ALU.mult),
                  reads=[("ps", bb), ("cv", cb)], writes=[("big", fc)])
    C.proj_fm(w_out, 0, 8, lambda fc, bank: C.residual_add(l, 0, fc, bank),
              rhs_of=lambda kc: big[:, kc, :], rhs_keys=lambda kc: ("big", kc))
    C.layernorm(l, 1)
    C.ffn(l, w_gu, w_dn)
    C.store_xT(out_rows, staging=[(cv[0][:, :], ("cv", 0)), (cv[1][:, :], ("cv", 1)),
                                  (ubuf[:, 0, 0:G], ("ubuf", 0)), (ubuf[:, 1, 0:G], ("ubuf", 1))])


def layer1_halo_u(C, l, uh, ctmp, w_in):
    S = C.S
    hval = C.small[:, SM_HVALID:SM_HVALID + 16]
    C.layernorm(l, 0, N=16)
    for part in range(2):
        def consume(oc, bank, part=part):
            if part == 0:
                S.add("act", lambda e: e.copy(uh[:, oc, :], C.ps[bank][:, 0:16]), reads=[("ps", bank)],
                      writes=[("uh", oc)])
            else:
                S.add("dve", lambda e: e.tensor_tensor(out=ctmp[:], in0=C.ps[bank][:, 0:16], in1=uh[:, oc, :],
                                                       op=ALU.mult), reads=[("ps", bank), ("uh", oc)], writes=["ctmp"])
                S.add("dve", lambda e: e.tensor_tensor(out=uh[:, oc, :], in0=ctmp[:], in1=hval, op=ALU.mult),
                      reads=["ctmp", "small"], writes=[("uh", oc)])
        C.proj_fm(w_in, D + part * D, 8, consume, N=16)


def _g_of_pos(p, half):
    return p if half == 1 else (p ^ 1)


def _tables(half):
    rb = np.zeros((NOWN, 4, NBLK), np.float32)
    npst = np.zeros((NOWN, 4, NBLK), np.float32)
    for i in range(NOWN):
        for qt in range(4):
            nbq = 2 * (2 * i + half) + qt // 2
            for pb in range(NBLK):
                gb = 2 * _g_of_pos(pb // 2, half) + pb % 2
                if gb < nbq:
                    npst[i, qt, pb] = NEG
                else:
                    rb[i, qt, pb] = -1e30
    tabs = np.concatenate([rb.reshape(-1), npst.reshape(-1)])[None, :].repeat(128, 0).astype(np.float32)
    rbh = np.zeros((128, NBLK), np.float32)
    nph = np.zeros((128, NBLK), np.float32)
    hpat = np.zeros((128, 64, 16), np.float32)
    kk = np.arange(128)
    for i in range(NOWN):
        gh = 2 * i + half - 1
        for t in range(2):
            col = 2 * i + t
            if gh < 0:
                continue
            gbq = 2 * gh + 1
            for pb in range(NBLK):
                gb = 2 * _g_of_pos(pb // 2, half) + pb % 2
                if gb < gbq:
                    nph[col, pb] = NEG
                else:
                    rbh[col, pb] = -1e30
                for sub in range(2):
                    kt = pb * 2 + sub
                    if gb < gbq:
                        hpat[:, kt, col] = 0.0
                    elif gb > gbq:
                        hpat[:, kt, col] = NEG
                    else:
                        hpat[:, kt, col] = np.where(sub * 128 + kk <= 254 + t, 0.0, NEG)
    tabs = np.concatenate([tabs, rbh, nph], axis=1).astype(np.float32)
    pat = np.zeros((128, 8, G), np.float32)
    k = np.arange(128)[:, None]
    q = np.arange(G)[None, :]
    qt = q // 128
    for ktw in range(8):
        if ktw < 4:
            pat[:, ktw, :] = 0.0 if half == 1 else NEG
        else:
            kt_ = ktw - 4
            kb = kt_ // 2
            qb = qt // 2
            kpos = (kt_ % 2) * 128 + k
            qpos = (qt % 2) * 128 + (q % 128)
            m = np.where(kb < qb, 0.0, np.where(kb > qb, NEG, np.where(kpos <= qpos, 0.0, NEG)))
            pat[:, ktw, :] = m
    sel = np.zeros((128, 32, 128), np.float32)
    for pb in range(32):
        sel[pb, pb, :] = 1.0
    import ml_dtypes
    bf = ml_dtypes.bfloat16
    patall = np.concatenate([pat.reshape(128, 8 * G), hpat.reshape(128, 1024)], axis=1)
    return tabs, patall.astype(bf), sel.reshape(128, 32 * 128).astype(bf)


def _small(c_b, b_ada_ls, nmix_ls, nffn_ls, q_gain, k_gain, conv_w, half):
    sm = np.zeros((128, NSM), np.float32)
    sm[:, SM_C:SM_C + 8] = c_b.reshape(8, 128).T
    for l, ba in enumerate(b_ada_ls):
        sm[:, SM_BADA + 48 * l:SM_BADA + 48 * (l + 1)] = ba.reshape(48, 128).T
    for l, v in enumerate(nmix_ls):
        sm[:, SM_NMIX + 8 * l:SM_NMIX + 8 * (l + 1)] = v.reshape(8, 128).T
    for l, v in enumerate(nffn_ls):
        sm[:, SM_NFFN + 8 * l:SM_NFFN + 8 * (l + 1)] = v.reshape(8, 128).T
    sm[:, SM_QG] = q_gain
    sm[:, SM_KG] = k_gain
    sm[:, SM_CONV:SM_CONV + 24] = conv_w.reshape(3, 8, 128).transpose(2, 0, 1).reshape(128, 24)
    hv = np.ones(16, np.float32)
    if half == 0:
        hv[0:2] = 0.0
    sm[:, SM_HVALID:SM_HVALID + 16] = hv[None, :]
    sm[:, SM_IDENT:SM_IDENT + 128] = np.eye(128, dtype=np.float32)
    return sm


_NC_CACHE = {}


def _get(name, fn):
    if name not in _NC_CACHE:
        _NC_CACHE[name] = fn()
    return _NC_CACHE[name]


def make_in_maps(x, c, w_ada, b_ada, norm_mix, norm_ffn, w_qkv, w_o, q_gain, k_gain, w_in, conv_w, w_out,
                 w_gate_up, w_down):
    in_maps = []
    for core in range(8):
        b, half = core // 2, core % 2
        perm = [_g_of_pos(p, half) for p in range(NPOS)]
        xseq = np.ascontiguousarray(x[b].reshape(NPOS, G, D)[perm].reshape(SEQ, D))
        xh = np.zeros((128, D), np.float32)
        for i in range(NOWN):
            g = 2 * i + half
            if g > 0:
                xh[2 * i:2 * i + 2] = x[b, g * G - 2:g * G]
            else:
                xh[2 * i:2 * i + 2] = x[b, 0:2]
        tabs, pat, sel = _tables(half)
        sm = _small(c[b], [b_ada[0], b_ada[1]], [norm_mix[0], norm_mix[1]], [norm_ffn[0], norm_ffn[1]],
                    q_gain[0], k_gain[0], conv_w[0], half)
        in_maps.append(dict(xseq=xseq, xhalo=xh, small=sm, tabs=tabs, pat=pat, sel=sel, w_ada=w_ada, w_qkv=w_qkv[0],
                            w_o=w_o[0], w_in=w_in[0], w_out=w_out[0], w_gu=w_gate_up, w_dn=w_down))
    return in_maps


def kernel(x, c, w_ada, b_ada, norm_mix, norm_ffn, w_qkv, w_o, q_gain, k_gain, w_in, conv_w, w_out,
           w_gate_up, w_down):
    f = lambda a: np.ascontiguousarray(np.asarray(a, dtype=np.float32))
    args = list(map(f, (x, c, w_ada, b_ada, norm_mix, norm_ffn, w_qkv, w_o, q_gain, k_gain, w_in, conv_w, w_out,
                        w_gate_up, w_down)))
    x = args[0]
    in_maps = make_in_maps(*args)
    nc = _get("F", build_fused)
    res = run_bass_kernel_spmd(nc, in_maps, core_ids=list(range(8)))
    out = np.zeros_like(x)
    for core in range(8):
        b, half = core // 2, core % 2
        o = np.asarray(res.results[core]["out"]).reshape(NOWN, G, D)
        for i in range(NOWN):
            g = 2 * i + half
            out[b, g * G:(g + 1) * G] = o[i]
    return out
```

```python
import contextlib
import numpy as np
import concourse.bass as bass
import concourse.mybir as mybir
from concourse.bass_utils import run_bass_kernel_spmd

F32 = mybir.dt.float32
BF16 = mybir.dt.bfloat16
ALU = mybir.AluOpType
AF = mybir.ActivationFunctionType

D = 1024
KC = 8
G = 512
H = 8
DH = 128
DFF = 2816
NJ = 22
NPOS = 16
NOWN = 8
NBLK = 32
SEQ = 8192
NEG = -30000.0
EPS = 1e-6
NW = 4
ENGS = ["pe", "act", "dve", "pool", "sp"]
INORDER = ("pe", "act", "dve")


class Sched:
    def __init__(self, nc, n_dma_sems=6):
        self.nc = nc
        self.ops = []
        self.last_write = {}
        self.readers = {}
        self.n_dma_sems = n_dma_sems

    def add(self, eng, fn, reads=(), writes=(), dma=False):
        oid = len(self.ops)
        deps = set()
        for b in reads:
            if b in self.last_write:
                deps.add(self.last_write[b])
        for b in writes:
            if b in self.last_write:
                deps.add(self.last_write[b])
            for r in self.readers.get(b, {}).values():
                deps.update(r)
        deps.discard(oid)
        self.ops.append(dict(id=oid, eng=eng, fn=fn, deps=deps, dma=dma, signal=False))
        for b in reads:
            rd = self.readers.setdefault(b, {})
            if dma or eng not in INORDER:
                rd.setdefault((eng, "dma"), []).append(oid)
            else:
                rd[eng] = [oid]
        for b in writes:
            self.last_write[b] = oid
            self.readers[b] = {}
        return oid

    def emit(self, final_wait_eng="sp"):
        nc = self.nc
        ops = self.ops

        def needs_sync(p, ceng):
            if p["dma"]:
                return True
            if p["eng"] != ceng:
                return True
            return p["eng"] in ("act", "dve", "pool")

        for op in ops:
            for d in op["deps"]:
                p = ops[d]
                if needs_sync(p, op["eng"]):
                    p["signal"] = True
            if op["dma"]:
                op["signal"] = True
        eng_count = {e: 0 for e in ENGS}
        dma_count = {}
        dma_rr = {e: 0 for e in ENGS}
        sem_keys = {}
        for op in ops:
            if not op["signal"]:
                continue
            if op["dma"]:
                k = dma_rr[op["eng"]] % self.n_dma_sems
                dma_rr[op["eng"]] += 1
                key = ("dma", op["eng"], k)
                prev = dma_count.get(key, 0)
                op["prev_on_sem"] = (key, prev)
                dma_count[key] = prev + 16
                op["sig"] = (key, prev + 16)
            else:
                key = ("eng", op["eng"])
                eng_count[op["eng"]] += 1
                op["sig"] = (key, eng_count[op["eng"]])
            sem_keys[key] = None
        with contextlib.ExitStack() as st:
            sems = {}
            for key in sem_keys:
                sems[key] = st.enter_context(nc.semaphore("s_" + "_".join(str(x) for x in key)))
            block = st.enter_context(nc.Block())
            per_eng = {e: [o for o in ops if o["eng"] == e] for e in ENGS}

            def body(ename):
                def run(eng):
                    waited = {}

                    def wait(key, val):
                        if waited.get(key, 0) >= val:
                            return
                        eng.wait_ge(sems[key], val)
                        waited[key] = val

                    for op in per_eng[ename]:
                        need = {}
                        for d in op["deps"]:
                            p = ops[d]
                            if not needs_sync(p, ename):
                                continue
                            key, val = p["sig"]
                            need[key] = max(need.get(key, 0), val)
                        if op["dma"]:
                            key, prev = op["prev_on_sem"]
                            if prev > 0:
                                need[key] = max(need.get(key, 0), prev)
                        for key, val in need.items():
                            wait(key, val)
                        ins = op["fn"](eng)
                        if op["signal"]:
                            key, val = op["sig"]
                            ins.then_inc(sems[key], 16 if op["dma"] else 1)
                    if ename == final_wait_eng:
                        for key, val in dma_count.items():
                            wait(key, val)
                        for e in ENGS:
                            if eng_count[e] > 0 and e != ename:
                                wait(("eng", e), eng_count[e])
                return run

            block.tensor(body("pe"))
            block.scalar(body("act"))
            block.vector(body("dve"))
            block.gpsimd(body("pool"))
            block.sync(body("sp"))


SM_C = 0
SM_BADA = 8
SM_NMIX = 104
SM_NFFN = 120
SM_QG = 136
SM_KG = 137
SM_CONV = 138
SM_HVALID = 162
SM_IDENT = 178
NSM = 306


class Ctx:
    def __init__(self, nc):
        self.nc = nc
        self.S = Sched(nc)
        self.wslot = 0
        self.bankrr = 0
        a = lambda n_, sh_, d_: nc.alloc_sbuf_tensor('sb_' + n_, sh_, d_)
        self.small = a("small", [128, NSM], F32)
        self.ident_bf = a("ident_bf", [128, 128], BF16)
        self.ones_bf = a("ones_bf", [128, 128], BF16)
        self.eps_t = a("eps_t", [128, 1], F32)
        self.wring = [a(f"wring{i}", [128, 4096], BF16) for i in range(NW)]
        self.xin = a("xin", [128, 4, D], F32)
        self.xT = a("xT", [128, KC, G], F32)
        self.sq = a("sq", [128, KC, G], BF16)
        self.hT = a("hT", [128, KC, G], BF16)
        self.tmp = [a(f"tmp{i}", [128, G], F32) for i in range(2)]
        self.std = a("std", [128, G], F32)
        self.rstd = a("rstd", [128, G], F32)
        self.big = a("big", [128, 24, G], BF16)
        self.mod = a("mod", [128, 2, 48], F32)
        self.modG = a("modG", [128, 2, 2, KC], F32)
        self.scbf = a("scbf", [128, KC], BF16)
        self.qgs = a("qgs", [128, 1], F32)
        self.pspair = [nc.alloc_psum_tensor(f"psp{i}", [128, 1024], F32) for i in range(3)]
        self.ps = []
        for i in range(3):
            self.ps.append(self.pspair[i][:, 0:512])
            self.ps.append(self.pspair[i][:, 512:1024])
        self.ps.append(nc.alloc_psum_tensor("ps6", [128, 512], F32)[:, :])
        self.psb = nc.alloc_psum_tensor("psb", [128, 1024], BF16)

    @property
    def ident(self):
        return self.small[:, SM_IDENT:SM_IDENT + 128]

    def bank(self, lo=0, hi=7):
        b = lo + self.bankrr % (hi - lo)
        self.bankrr += 1
        return b

    def slab(self, src3d, kparts, ncols):
        rk = []
        if isinstance(src3d, tuple):
            src3d, rk = src3d
        slot = self.wslot % NW
        self.wslot += 1
        view = self.wring[slot][:, 0:kparts * ncols].rearrange("p (k n) -> p k n", k=kparts)
        self.S.add("pool", lambda e: e.dma_start(out=view, in_=src3d), reads=rk, writes=[("w", slot)], dma=True)
        return view, ("w", slot)

    def wcols(self, w2d, c0, ncols, r0=0, kparts=KC):
        rk = []
        if isinstance(w2d, tuple):
            w2d, rk = w2d
        ap = w2d[r0:r0 + kparts * 128, c0:c0 + ncols].rearrange("(k p) n -> p k n", p=128)
        return (ap, rk) if rk else ap

    def precast(self, name, w2d, rows_per):
        R, Ncol = w2d.shape
        wb = self.nc.dram_tensor("Wb_" + name, [R, Ncol], BF16).ap()
        keys = []
        r = 0
        while r < R:
            r1 = min(R, r + rows_per)
            key = ("Wb", name, r)
            self.S.add("pool", lambda e, r=r, r1=r1: e.dma_start(out=wb[r:r1, :], in_=w2d[r:r1, :]),
                       writes=[key], dma=True)
            keys.append(key)
            r = r1
        return (wb, keys)

    def setup(self, small_ap):
        S = self.S
        S.add("sp", lambda e: e.dma_start(out=self.small[:], in_=small_ap), writes=["small"], dma=True)
        S.add("dve", lambda e: e.memset(self.ones_bf[:], 1.0), writes=["ones"])
        S.add("dve", lambda e: e.memset(self.eps_t[:], EPS), writes=["eps"])
        S.add("dve", lambda e: e.tensor_copy(self.ident_bf[:], self.ident), reads=["small"], writes=["identbf"])
        S.add("act", lambda e: e.activation(out=self.scbf[:], in_=self.small[:, SM_C:SM_C + 8], func=AF.Silu),
              reads=["small"], writes=["scbf"])
        S.add("act", lambda e: e.mul(self.qgs[:], self.small[:, SM_QG:SM_QG + 1], DH ** -0.5),
              reads=["small"], writes=["qgs"])

    def adaln(self, w_ada_l, l):
        for _ in self.adaln_gen(w_ada_l, l):
            pass

    def adaln_gen(self, w_ada_l, l):
        S = self.S
        bank = 6
        c0 = 256
        first = True
        for s in range(12):
            if s > 0:
                yield
            view, wk = self.slab(self.wcols(w_ada_l, s * 512, 512), KC, 512)
            for o in range(4):
                oc = s * 4 + o
                for kc in range(KC):
                    wr = [("ps", bank)]
                    S.add("pe", lambda e, oc=oc, kc=kc, o=o, view=view: e.matmul(
                        self.ps[bank][:, c0 + oc:c0 + oc + 1], lhsT=view[:, kc, o * 128:(o + 1) * 128],
                        rhs=self.scbf[:, kc:kc + 1], start=(kc == 0), stop=(kc == KC - 1)),
                        reads=[wk, "scbf"], writes=wr)
        mod = self.mod
        S.add("dve", lambda e: e.tensor_tensor(out=mod[:, l, :], in0=self.ps[bank][:, c0:c0 + 48],
                                               in1=self.small[:, SM_BADA + 48 * l:SM_BADA + 48 * (l + 1)], op=ALU.add),
              reads=[("ps", bank), "small"], writes=[("mod", l)])
        S.add("dve", lambda e: e.scalar_tensor_tensor(out=self.modG[:, l, 0, :], in0=mod[:, l, 8:16], scalar=1.0,
                                                      in1=self.small[:, SM_NMIX + 8 * l:SM_NMIX + 8 * (l + 1)],
                                                      op0=ALU.add, op1=ALU.mult),
              reads=[("mod", l), "small"], writes=[("modG", l, 0)])
        S.add("dve", lambda e: e.scalar_tensor_tensor(out=self.modG[:, l, 1, :], in0=mod[:, l, 32:40], scalar=1.0,
                                                      in1=self.small[:, SM_NFFN + 8 * l:SM_NFFN + 8 * (l + 1)],
                                                      op0=ALU.add, op1=ALU.mult),
              reads=[("mod", l), "small"], writes=[("modG", l, 1)])

    def modcols(self, l, which):
        base = 0 if which == 0 else 24
        return (self.modG[:, l, which, :], self.mod[:, l, base:base + 8], self.mod[:, l, base + 16:base + 24],
                [("mod", l), ("modG", l, which)])

    def load_xT(self, x_rows, N=G, ntile=4):
        S = self.S
        xin = self.xin
        if x_rows is not None:
            self.load_x_dma(x_rows, ntile)
        for kc in range(KC):
            bank = self.bank()
            for t in range(ntile):
                S.add("pe", lambda e, kc=kc, t=t, bank=bank: e.transpose(
                    self.ps[bank][:, t * 128:(t + 1) * 128], xin[:, t, kc * 128:(kc + 1) * 128], self.ident),
                    reads=["xin", "small"], writes=[("ps", bank)])
            eng = "dve" if kc % 2 == 0 else "act"
            if eng == "dve":
                S.add("dve", lambda e, kc=kc, bank=bank: e.tensor_copy(self.xT[:, kc, 0:ntile * 128],
                                                                        self.ps[bank][:, 0:ntile * 128]),
                      reads=[("ps", bank)], writes=[("xT", kc)])
            else:
                S.add("act", lambda e, kc=kc, bank=bank: e.copy(self.xT[:, kc, 0:ntile * 128],
                                                                 self.ps[bank][:, 0:ntile * 128]),
                      reads=[("ps", bank)], writes=[("xT", kc)])

    def load_x_dma(self, x_rows, ntile=4):
        xin = self.xin
        self.S.add("sp", lambda e: e.dma_start(out=xin[:, 0:ntile, :], in_=x_rows.rearrange("(t p) f -> p t f", p=128)),
                   writes=["xin"], dma=True)

    def store_xT(self, out_rows, ntile=4, staging=None):
        S = self.S
        xin = self.xin
        if staging is not None:
            for t in range(ntile):
                for hf in range(2):
                    st, stk = staging[(t % 2) * 2 + hf]
                    bank = self.bank()
                    for k4 in range(4):
                        kc = hf * 4 + k4
                        S.add("pe", lambda e, kc=kc, t=t, bank=bank, k4=k4: e.transpose(
                            self.ps[bank][:, k4 * 128:(k4 + 1) * 128], self.xT[:, kc, t * 128:(t + 1) * 128],
                            self.ident), reads=[("xT", kc), "small"], writes=[("ps", bank)])
                    if hf == 0:
                        S.add("dve", lambda e, bank=bank, st=st: e.tensor_copy(st, self.ps[bank][:]),
                              reads=[("ps", bank)], writes=[stk])
                    else:
                        S.add("act", lambda e, bank=bank, st=st: e.copy(st, self.ps[bank][:]),
                              reads=[("ps", bank)], writes=[stk])
                    S.add("sp", lambda e, t=t, hf=hf, st=st: e.dma_start(
                        out=out_rows[t * 128:(t + 1) * 128, hf * 512:(hf + 1) * 512], in_=st),
                        reads=[stk], writes=[("out", id(out_rows), t, hf)], dma=True)
            return
        for t in range(ntile):
            for hf in range(2):
                bank = self.bank()
                for k4 in range(4):
                    kc = hf * 4 + k4
                    S.add("pe", lambda e, kc=kc, t=t, bank=bank, k4=k4: e.transpose(
                        self.ps[bank][:, k4 * 128:(k4 + 1) * 128], self.xT[:, kc, t * 128:(t + 1) * 128], self.ident),
                        reads=[("xT", kc), "small"], writes=[("ps", bank)])
                if hf == 0:
                    S.add("dve", lambda e, t=t, bank=bank, hf=hf: e.tensor_copy(xin[:, t, hf * 512:(hf + 1) * 512],
                                                                                 self.ps[bank][:]),
                          reads=[("ps", bank)], writes=["xin"])
                else:
                    S.add("act", lambda e, t=t, bank=bank, hf=hf: e.copy(xin[:, t, hf * 512:(hf + 1) * 512],
                                                                          self.ps[bank][:]),
                          reads=[("ps", bank)], writes=["xin"])
        S.add("sp", lambda e: e.dma_start(out=out_rows.rearrange("(t p) f -> p t f", p=128), in_=xin[:, 0:ntile, :]),
              reads=["xin"], writes=[("out", id(out_rows))], dma=True)

    def layernorm(self, l, which, N=G):
        S = self.S
        Gc, Sc, _, mkeys = self.modcols(l, which)
        for kc in range(KC):
            if kc % 2 == 0:
                S.add("act", lambda e, kc=kc: e.activation(out=self.sq[:, kc, :N], in_=self.xT[:, kc, :N],
                                                           func=AF.Square), reads=[("xT", kc)], writes=[("sq", kc)])
            else:
                S.add("dve", lambda e, kc=kc: e.tensor_tensor(out=self.sq[:, kc, :N], in0=self.xT[:, kc, :N],
                                                              in1=self.xT[:, kc, :N], op=ALU.mult),
                      reads=[("xT", kc)], writes=[("sq", kc)])
        bank = self.bank()
        for kc in range(KC):
            S.add("pe", lambda e, kc=kc: e.matmul(self.ps[bank][:, :N], lhsT=self.ones_bf[:], rhs=self.sq[:, kc, :N],
                                                  start=(kc == 0), stop=(kc == KC - 1)),
                  reads=[("sq", kc), "ones"], writes=[("ps", bank)])
        S.add("act", lambda e: e.activation(out=self.std[:, :N], in_=self.ps[bank][:, :N], func=AF.Ln,
                                            bias=self.eps_t[:, 0:1], scale=1.0 / D),
              reads=[("ps", bank), "eps"], writes=["std"])
        S.add("act", lambda e: e.activation(out=self.rstd[:, :N], in_=self.std[:, :N], func=AF.Exp, scale=-0.5),
              reads=["std"], writes=["rstd"])
        for kc in range(KC):
            tb = kc % 2
            S.add("dve", lambda e, kc=kc, tb=tb: e.tensor_tensor(out=self.tmp[tb][:, :N], in0=self.xT[:, kc, :N],
                                                                 in1=self.rstd[:, :N], op=ALU.mult),
                  reads=[("xT", kc), "rstd"], writes=[("tmp", tb)])
            S.add("act", lambda e, kc=kc, tb=tb: e.activation(out=self.hT[:, kc, :N], in_=self.tmp[tb][:, :N],
                                                              func=AF.Identity, scale=Gc[:, kc:kc + 1],
                                                              bias=Sc[:, kc:kc + 1]),
                  reads=[("tmp", tb)] + mkeys, writes=[("hT", kc)])

    def proj_fm(self, w2d, c0, nchunks, consume, rhs_of=None, rhs_keys=None, N=G, kparts=KC, r0=0):
        S = self.S
        if rhs_of is None:
            rhs_of = lambda kc: self.hT[:, kc, :N]
            rhs_keys = lambda kc: ("hT", kc)
        oc = 0
        while oc < nchunks:
            nch = min(4, nchunks - oc)
            view, wk = self.slab(self.wcols(w2d, c0 + oc * 128, nch * 128, r0=r0, kparts=kparts), kparts, nch * 128)
            for o in range(nch):
                bank = self.bank()
                for kc in range(kparts):
                    S.add("pe", lambda e, kc=kc, o=o, bank=bank, view=view: e.matmul(
                        self.ps[bank][:, :N], lhsT=view[:, kc, o * 128:(o + 1) * 128], rhs=rhs_of(kc),
                        start=(kc == 0), stop=(kc == kparts - 1)),
                        reads=[wk, rhs_keys(kc)], writes=[("ps", bank)])
                consume(oc + o, bank)
            oc += nch

    def residual_add(self, l, which, fc, bank, N=G):
        _, _, gate, mkeys = self.modcols(l, which)
        self.S.add("dve", lambda e: e.scalar_tensor_tensor(out=self.xT[:, fc, :N], in0=self.ps[bank][:, :N],
                                                           scalar=gate[:, fc:fc + 1], in1=self.xT[:, fc, :N],
                                                           op0=ALU.mult, op1=ALU.add),
                   reads=[("ps", bank), ("xT", fc)] + mkeys, writes=[("xT", fc)])

    def ffn(self, l, w_gu, w_dn, N=G):
        S = self.S
        big = self.big
        j = 0
        while j < NJ:
            nch = min(4, NJ - j)
            gview, gk = self.slab(self.wcols(w_gu, j * 128, nch * 128), KC, nch * 128)
            uview, uk = self.slab(self.wcols(w_gu, DFF + j * 128, nch * 128), KC, nch * 128)
            for o in range(nch):
                jj = j + o
                bg = self.bank()
                bu = self.bank()
                for kc in range(KC):
                    S.add("pe", lambda e, kc=kc, o=o, bg=bg, gview=gview: e.matmul(
                        self.ps[bg][:, :N], lhsT=gview[:, kc, o * 128:(o + 1) * 128], rhs=self.hT[:, kc, :N],
                        start=(kc == 0), stop=(kc == KC - 1)), reads=[gk, ("hT", kc)], writes=[("ps", bg)])
                for kc in range(KC):
                    S.add("pe", lambda e, kc=kc, o=o, bu=bu, uview=uview: e.matmul(
                        self.ps[bu][:, :N], lhsT=uview[:, kc, o * 128:(o + 1) * 128], rhs=self.hT[:, kc, :N],
                        start=(kc == 0), stop=(kc == KC - 1)), reads=[uk, ("hT", kc)], writes=[("ps", bu)])
                tb = jj % 2
                S.add("act", lambda e, bg=bg, tb=tb: e.activation(out=self.tmp[tb][:, :N], in_=self.ps[bg][:, :N],
                                                                  func=AF.Silu),
                      reads=[("ps", bg)], writes=[("tmp", tb)])
                S.add("dve", lambda e, bu=bu, tb=tb, jj=jj: e.tensor_tensor(out=big[:, jj, :N], in0=self.ps[bu][:, :N],
                                                                             in1=self.tmp[tb][:, :N], op=ALU.mult),
                      reads=[("ps", bu), ("tmp", tb)], writes=[("big", jj)])
            j += nch
        for fp in range(4):
            v0, k0 = self.slab(self.wcols(w_dn, fp * 256, 256, r0=0, kparts=11), 11, 256)
            v1, k1 = self.slab(self.wcols(w_dn, fp * 256, 256, r0=11 * 128, kparts=11), 11, 256)
            banks = [self.bank(), self.bank()]
            for jj in range(NJ):
                view, wk = (v0, k0) if jj < 11 else (v1, k1)
                for f2 in range(2):
                    S.add("pe", lambda e, jj=jj, f2=f2, view=view, b=banks[f2]: e.matmul(
                        self.ps[b][:, :N], lhsT=view[:, jj % 11, f2 * 128:(f2 + 1) * 128], rhs=big[:, jj, :N],
                        start=(jj == 0), stop=(jj == NJ - 1)), reads=[wk, ("big", jj)], writes=[("ps", banks[f2])])
            for f2 in range(2):
                self.residual_add(l, 1, fp * 2 + f2, banks[f2], N)


def build_fused():
    nc = bass.Bass("TRN2", target_bir_lowering=False)
    dt = nc.dram_tensor
    xseq = dt("xseq", [SEQ, D], F32, kind="ExternalInput").ap()
    xhalo = dt("xhalo", [128, D], F32, kind="ExternalInput").ap()
    small_ap = dt("small", [128, NSM], F32, kind="ExternalInput").ap()
    tabs_ap = dt("tabs", [128, 2048 + 64], F32, kind="ExternalInput").ap()
    pat_ap = dt("pat", [128, 8 * G + 64 * 16], BF16, kind="ExternalInput").ap()
    sel_ap = dt("sel", [128, 32 * 128], BF16, kind="ExternalInput").ap()
    w_ada = dt("w_ada", [2, D, 6 * D], F32, kind="ExternalInput").ap()
    w_qkv = dt("w_qkv", [D, 3 * D], F32, kind="ExternalInput").ap()
    w_o = dt("w_o", [D, D], F32, kind="ExternalInput").ap()
    w_in = dt("w_in", [D, 3 * D], F32, kind="ExternalInput").ap()
    w_out = dt("w_out", [D, D], F32, kind="ExternalInput").ap()
    w_gu = dt("w_gu", [2, D, 2 * DFF], F32, kind="ExternalInput").ap()
    w_dn = dt("w_dn", [2, DFF, D], F32, kind="ExternalInput").ap()
    out = dt("out", [NOWN * G, D], F32, kind="ExternalOutput").ap()
    Kt = dt("Kt", [H, DH, SEQ], BF16).ap()
    Vs = dt("Vs", [H, 128, 64, DH], BF16).ap()
    Qt = dt("Qt", [H, DH, NOWN * G], BF16).ap()
    Ot = dt("Ot", [NOWN, DH, H, G], BF16).ap()

    C = Ctx(nc)
    S = C.S
    a = lambda n_, sh_, d_: nc.alloc_sbuf_tensor('sb_' + n_, sh_, d_)
    tabs = a("tabs", [128, 2048 + 64], F32)
    pat = a("pat", [128, 8 * G + 64 * 16], BF16)
    sel = a("sel", [128, 32, 128], BF16)
    kmean_f = a("kmean_f", [128, H, NBLK], F32)
    kmean_bf = a("kmean_bf", [128, H, NBLK], BF16)
    Kh = a("Kh", [128, SEQ], BF16)
    Vh = a("Vh", [128, 64, DH], BF16)
    Qh = a("Qh", [128, NOWN * G], BF16)
    QhaloT = a("QhaloT", [128, H, 16], BF16)
    oTh = a("oTh", [128, H, 16], BF16)
    pTp = [a(f"pTp{i}", [128, 2, G], BF16) for i in range(2)]
    Rsb = a("Rsb", [128, 4, NBLK], F32)
    max8 = a("max8", [128, 4, 8], F32)
    mb = a("mb", [128, 4, NBLK], BF16)
    mbT = [a(f"mbT{i}", [128, G], BF16) for i in range(2)]
    oTsb = [a(f"oTsb{i}", [128, G], BF16) for i in range(2)]
    uh = a("uh", [128, KC, 16], F32)
    ctmp = a("ctmp", [128, 16], F32)
    cg = a("cg", [128, G], F32)
    ubuf = a("ubuf", [128, 2, 2 + G], F32)
    cv = [a(f"cv{i}", [128, G], F32) for i in range(2)]
    big = C.big
    recip = C.std

    C.setup(small_ap)
    S.add("sp", lambda e: e.dma_start(out=tabs[:], in_=tabs_ap), writes=["tabs"], dma=True)
    S.add("sp", lambda e: e.dma_start(out=pat[:], in_=pat_ap), writes=["pat"], dma=True)
    S.add("sp", lambda e: e.dma_start(out=sel[:].rearrange("p a b -> p (a b)"), in_=sel_ap), writes=["sel"], dma=True)
    S.add("dve", lambda e: e.memset(kmean_f[:], 0.0), writes=["kmean_f"])
    wb_qkv = C.precast("qkv", w_qkv, 256)
    C.adaln(w_ada[0], 0)
    kg = C.small[:, SM_KG:SM_KG + 1]
    acc = a("acc", [128, 2, G], F32)
    for i_ in range(2):
        S.add("dve", lambda e, i_=i_: e.memset(mbT[i_][:], 0.0), writes=[("mbT", i_)])
    ones_f = a("ones_f", [128, 128], F32)
    S.add("dve", lambda e: e.memset(ones_f[:], 1.0), writes=["ones_f"])

    def qk_head(hh, dst_ap, dst_key, gain_ap, gain_keys, wview, wk, o, kmean_pos=None, N=G):
        bank = C.bank()
        for kc in range(KC):
            S.add("pe", lambda e, kc=kc: e.matmul(C.ps[bank][:, :N], lhsT=wview[:, kc, o * 128:(o + 1) * 128],
                                                  rhs=C.hT[:, kc, :N], start=(kc == 0), stop=(kc == KC - 1)),
                  reads=[wk, ("hT", kc)], writes=[("ps", bank)])
        sqk = C.sq[:, hh, :N]
        S.add("act", lambda e: e.activation(out=sqk, in_=C.ps[bank][:, :N], func=AF.Square),
              reads=[("ps", bank)], writes=[("sq", hh)])
        b2 = C.bank()
        S.add("pe", lambda e: e.matmul(C.ps[b2][:, :N], lhsT=C.ones_bf[:], rhs=sqk, start=True, stop=True),
              reads=["ones", ("sq", hh)], writes=[("ps", b2)])
        if hh % 2 == 0:
            stdb, stdk, rstdb, rstdk = C.std, "std", C.rstd, "rstd"
        else:
            stdb, stdk, rstdb, rstdk = C.tmp[0], ("tmp", 0), C.tmp[1], ("tmp", 1)
        S.add("act", lambda e: e.activation(out=stdb[:, :N], in_=C.ps[b2][:, :N], func=AF.Ln, bias=C.eps_t[:, 0:1],
                                            scale=1.0 / DH), reads=[("ps", b2), "eps"], writes=[stdk])
        S.add("act", lambda e: e.activation(out=rstdb[:, :N], in_=stdb[:, :N], func=AF.Exp, scale=-0.5),
              reads=[stdk], writes=[rstdk])
        if kmean_pos is None:
            S.add("dve", lambda e: e.scalar_tensor_tensor(out=dst_ap, in0=C.ps[bank][:, :N], scalar=gain_ap,
                                                          in1=rstdb[:, :N], op0=ALU.mult, op1=ALU.mult),
                  reads=[("ps", bank), rstdk] + gain_keys, writes=[dst_key])
        else:
            for bb in range(2):
                pb = kmean_pos * 2 + bb
                S.add("dve", lambda e, bb=bb, pb=pb: e.scalar_tensor_tensor(
                    out=dst_ap[:, bb * 256:(bb + 1) * 256], in0=C.ps[bank][:, bb * 256:(bb + 1) * 256],
                    scalar=gain_ap, in1=rstdb[:, bb * 256:(bb + 1) * 256], op0=ALU.mult, op1=ALU.mult,
                    accum_out=kmean_f[:, hh, pb:pb + 1]),
                    reads=[("ps", bank), rstdk, "kmean_f"] + gain_keys, writes=[dst_key, "kmean_f"])

    C.load_x_dma(xseq[0:G, :])
    for p in range(NPOS):
        own = (p % 2 == 1)
        C.load_xT(None)
        if p + 1 < NPOS:
            C.load_x_dma(xseq[(p + 1) * G:(p + 2) * G, :])
        C.layernorm(0, 0)
        for s in range(2):
            wview, wk = C.slab(C.wcols(wb_qkv, D + s * 512, 512), KC, 512)
            for o in range(4):
                hh = s * 4 + o
                qk_head(hh, big[:, hh, :], ("big", hh), kg, ["small"], wview, wk, o, kmean_pos=p)
        S.add("sp", lambda e, p=p: e.dma_start(out=Kt.rearrange("h d t -> d h t")[:, :, p * G:(p + 1) * G],
                                               in_=big[:, 0:8, :]),
              reads=[("big", s_) for s_ in range(8)], writes=[("Kt", p)], dma=True)
        for hf in range(2):
            wview, wk = C.slab(C.wcols(wb_qkv, 2 * D + hf * 512, 512), KC, 512)
            for t in range(4):
                bank = C.bank()
                for kc in range(KC):
                    S.add("pe", lambda e, kc=kc, t=t, bank=bank, wview=wview: e.matmul(
                        C.ps[bank][:], lhsT=C.hT[:, kc, t * 128:(t + 1) * 128], rhs=wview[:, kc, :],
                        start=(kc == 0), stop=(kc == KC - 1)), reads=[wk, ("hT", kc)], writes=[("ps", bank)])
                slot = 16 + 2 * t + hf
                if t % 2 == 0:
                    S.add("act", lambda e, bank=bank, slot=slot: e.copy(big[:, slot, :], C.ps[bank][:]),
                          reads=[("ps", bank)], writes=[("big", slot)])
                else:
                    S.add("dve", lambda e, bank=bank, slot=slot: e.tensor_copy(big[:, slot, :], C.ps[bank][:]),
                          reads=[("ps", bank)], writes=[("big", slot)])
        for t in range(4):
            S.add("sp", lambda e, p=p, t=t: e.dma_start(
                out=Vs[:, :, p * 4 + t, :].rearrange("(hf h4) k d -> k hf h4 d", hf=2),
                in_=big[:, 16 + 2 * t:18 + 2 * t, :].rearrange("p hf (h4 d) -> p hf h4 d", h4=4)),
                reads=[("big", 16 + 2 * t), ("big", 17 + 2 * t)], writes=[("Vs", p, t)], dma=True)
        if own:
            i = p // 2
            for s in range(2):
                wview, wk = C.slab(C.wcols(wb_qkv, s * 512, 512), KC, 512)
                for o in range(4):
                    hh = s * 4 + o
                    qk_head(hh, big[:, 8 + hh, :], ("big", 8 + hh), C.qgs[:, 0:1], ["qgs"], wview, wk, o)
            S.add("sp", lambda e, i=i: e.dma_start(out=Qt.rearrange("h d t -> d h t")[:, :, i * G:(i + 1) * G],
                                                   in_=big[:, 8:16, :]),
                  reads=[("big", s_) for s_ in range(8, 16)], writes=[("Qt", i)], dma=True)
    C.load_xT(xhalo, ntile=1)
    C.layernorm(0, 0, N=16)
    for s in range(2):
        wview, wk = C.slab(C.wcols(wb_qkv, s * 512, 512), KC, 512)
        for o in range(4):
            hh = s * 4 + o
            qk_head(hh, QhaloT[:, hh, :], ("QhaloT", hh), C.qgs[:, 0:1], ["qgs"], wview, wk, o, N=16)
    S.add("dve", lambda e: e.tensor_copy(kmean_bf[:], kmean_f[:]), reads=["kmean_f"], writes=["kmean_bf"])

    rb = tabs[:, 0:1024].rearrange("p (i q n) -> p i q n", i=NOWN, q=4)
    npast = tabs[:, 1024:2048].rearrange("p (i q n) -> p i q n", i=NOWN, q=4)
    rb_h = tabs[:, 2048:2080]
    np_h = tabs[:, 2080:2112]
    patg = pat[:, 0:8 * G].rearrange("p (a b) -> p a b", a=8)
    hpat = pat[:, 8 * G:8 * G + 1024].rearrange("p (a b) -> p a b", a=64)
    RB = 6

    def desc_group(hh, i):
        return dict(N=G, nq=4, qrows=128, q_ap=Qh[:, i * G:(i + 1) * G], q_keys=[("Qh", i // 4)],
                    rb=rb[:, i, :, :], npst=npast[:, i, :, :], nkt=8 * i + 8,
                    pat_of=(lambda kt: patg[:, kt - 8 * i, :] if kt >= 8 * i else None), i=i, hh=hh)

    def desc_halo(hh):
        return dict(N=16, nq=1, qrows=16, q_ap=QhaloT[:, hh, :], q_keys=[("QhaloT", hh)],
                    rb=rb_h[0:16, :].rearrange("p (q n) -> p q n", q=1), npst=np_h[0:16, :].rearrange("p (q n) -> p q n", q=1),
                    nkt=64, pat_of=(lambda kt: hpat[:, kt, :]), i=None, hh=hh)

    def route(dsc, par):
        hh, N, nq, qr = dsc["hh"], dsc["N"], dsc["nq"], dsc["qrows"]
        qw = min(N, 128)
        for qt in range(nq):
            S.add("pe", lambda e, qt=qt: e.matmul(C.ps[RB][0:qr, qt * NBLK:(qt + 1) * NBLK],
                                                  lhsT=dsc["q_ap"][:, qt * qw:(qt + 1) * qw],
                                                  rhs=kmean_bf[:, hh, :], start=True, stop=True),
                  reads=dsc["q_keys"] + ["kmean_bf"], writes=[("ps", RB)])
        S.add("dve", lambda e: e.tensor_tensor(out=Rsb[0:qr, 0:nq, :],
                                               in0=C.ps[RB][0:qr, 0:nq * NBLK].rearrange("p (q n) -> p q n", q=nq),
                                               in1=dsc["rb"], op=ALU.add),
              reads=[("ps", RB), "tabs"], writes=["Rsb"])
        for qt in range(nq):
            S.add("dve", lambda e, qt=qt: e.max(out=max8[0:qr, qt, :], in_=Rsb[0:qr, qt, :]), reads=["Rsb"],
                  writes=[("max8", qt)])
            S.add("dve", lambda e, qt=qt: e.scalar_tensor_tensor(out=mb[0:qr, qt, :], in0=Rsb[0:qr, qt, :],
                                                                 scalar=max8[0:qr, qt, 2:3], in1=dsc["npst"][:, qt, :],
                                                                 op0=ALU.is_lt, op1=ALU.mult),
                  reads=["Rsb", ("max8", qt), "tabs"], writes=[("mb", qt)])

    def route_b(dsc, par):
        N, nq, qr = dsc["N"], dsc["nq"], dsc["qrows"]
        for qt in range(nq):
            S.add("pe", lambda e, qt=qt: e.transpose(C.psb[0:32, qt * 128:qt * 128 + qr], mb[0:qr, qt, :],
                                                     C.ident_bf[0:qr, 0:qr]),
                  reads=[("mb", qt), "identbf"], writes=["psb"])
        S.add("act", lambda e: e.copy(mbT[par][0:32, 0:N], C.psb[0:32, 0:N]), reads=["psb"], writes=[("mbT", par)])

    def make_attn(dsc, par):
        hh, N, nkt = dsc["hh"], dsc["N"], dsc["nkt"]
        ob = 4 + par
        db = 4 + (1 - par)
        npair = nkt // 2

        def qkpair(pr):
            X = pr % 2
            for sub in range(2):
                kt = 2 * pr + sub
                sb = 2 * X + sub
                pt = dsc["pat_of"](kt)
                S.add("pe", lambda e, kt=kt, sb=sb: e.matmul(C.ps[sb][:, :N], lhsT=Kh[:, kt * 128:(kt + 1) * 128],
                                                             rhs=dsc["q_ap"], start=True, stop=False),
                      reads=[("Kh", kt // 16)] + dsc["q_keys"], writes=[("ps", sb)])
                S.add("pe", lambda e, sb=sb, pt=pt: e.matmul(C.ps[sb][:, :N], lhsT=sel[:, pr, :], rhs=mbT[par][:, 0:N],
                                                             start=False, stop=(pt is None)),
                      reads=["sel", ("mbT", par)], writes=[("ps", sb)])
                if pt is not None:
                    S.add("pe", lambda e, sb=sb, pt=pt: e.matmul(C.ps[sb][:, :N], lhsT=C.ident_bf[:], rhs=pt, start=False,
                                                                 stop=True),
                          reads=["identbf", "pat"], writes=[("ps", sb)])
            scv = C.pspair[X][:, :].rearrange("p (b n) -> p b n", b=2)[:, :, 0:N]
            S.add("act", lambda e: e.activation(out=pTp[X][:, :, 0:N], in_=scv, func=AF.Exp),
                  reads=[("ps", 2 * X), ("ps", 2 * X + 1)], writes=[("pTp", X)])

        def pvpair(pr):
            X = pr % 2
            for sub in range(2):
                kt = 2 * pr + sub
                S.add("pe", lambda e, kt=kt, sub=sub: e.matmul(C.ps[ob][:, :N], lhsT=Vh[:, kt, :], rhs=pTp[X][:, sub, 0:N],
                                                               start=(kt == 0), stop=(kt == nkt - 1)),
                      reads=[("Vh", kt // 16), ("pTp", X)], writes=[("ps", ob)])
            if pr == 0:
                S.add("dve", lambda e: e.tensor_copy(acc[:, :, 0:N], pTp[X][:, :, 0:N]), reads=[("pTp", X)], writes=["acc"])
            else:
                S.add("dve", lambda e: e.tensor_tensor(out=acc[:, :, 0:N], in0=acc[:, :, 0:N], in1=pTp[X][:, :, 0:N],
                                                       op=ALU.add), reads=[("pTp", X), "acc"], writes=["acc"])

        def epilogue():
            for ai in range(2):
                S.add("pe", lambda e, ai=ai: e.matmul(C.ps[db][:, :N], lhsT=ones_f[:], rhs=acc[:, ai, 0:N],
                                                      start=(ai == 0), stop=(ai == 1)),
                      reads=["ones_f", "acc"], writes=[("ps", db)])
            S.add("act", lambda e: e.activation(out=C.rstd[:, :N], in_=C.ps[db][:, :N], func=AF.Ln),
                  reads=[("ps", db)], writes=["rstd"])
            S.add("act", lambda e: e.activation(out=recip[:, :N], in_=C.rstd[:, :N], func=AF.Exp, scale=-1.0),
                  reads=["rstd"], writes=["std"])
            if dsc["i"] is None:
                S.add("dve", lambda e: e.tensor_tensor(out=oTh[:, hh, :], in0=C.ps[ob][:, :N], in1=recip[:, :N],
                                                       op=ALU.mult), reads=[("ps", ob), "std"], writes=[("oTh", hh)])
            else:
                i = dsc["i"]
                S.add("dve", lambda e: e.tensor_tensor(out=oTsb[par][:], in0=C.ps[ob][:], in1=recip[:], op=ALU.mult),
                      reads=[("ps", ob), "std"], writes=[("oTsb", par)])
                S.add("sp", lambda e: e.dma_start(out=Ot[i, :, hh, :], in_=oTsb[par][:]), reads=[("oTsb", par)],
                      writes=[("Ot", i, hh)], dma=True)

        return dict(qkpair=qkpair, pvpair=pvpair, epilogue=epilogue, npair=npair)

    cnt = 0
    bg_adaln = C.adaln_gen(w_ada[1], 1)
    wb = {}
    for hh in range(H):
        for c4 in range(4):
            if c4 < 2:
                S.add("sp", lambda e, hh=hh, c4=c4: e.dma_start(out=Qh[:, c4 * 2048:(c4 + 1) * 2048],
                                                                in_=Qt[hh, :, c4 * 2048:(c4 + 1) * 2048]),
                      reads=[("Qt", i) for i in range(NOWN)], writes=[("Qh", c4)], dma=True)
            S.add("sp", lambda e, hh=hh, c4=c4: e.dma_start(out=Kh[:, c4 * 2048:(c4 + 1) * 2048],
                                                            in_=Kt[hh, :, c4 * 2048:(c4 + 1) * 2048]),
                  reads=[("Kt", p) for p in range(NPOS)], writes=[("Kh", c4)], dma=True)
            S.add("sp", lambda e, hh=hh, c4=c4: e.dma_start(out=Vh[:, c4 * 16:(c4 + 1) * 16, :],
                                                            in_=Vs[hh, :, c4 * 16:(c4 + 1) * 16, :]),
                  reads=[("Vs", p, t) for p in range(NPOS) for t in range(4)], writes=[("Vh", c4)], dma=True)
        seq = [desc_group(hh, i) for i in range(NOWN)] + [desc_halo(hh)]
        route(seq[0], cnt % 2)
        route_b(seq[0], cnt % 2)
        cur = make_attn(seq[0], cnt % 2)
        cur["qkpair"](0)
        for n_, dsc in enumerate(seq):
            nxt = None
            if n_ + 1 < len(seq):
                route(seq[n_ + 1], (cnt + 1) % 2)
                nxt = make_attn(seq[n_ + 1], (cnt + 1) % 2)
            npair = cur["npair"]
            for pr in range(npair):
                if pr + 1 < npair:
                    cur["qkpair"](pr + 1)
                elif nxt is not None:
                    nxt["qkpair"](0)
                cur["pvpair"](pr)
                if hh == 0 and pr == npair // 2:
                    for _ in range(2):
                        next(bg_adaln, None)
                if pr == npair - 2 and nxt is not None:
                    route_b(seq[n_ + 1], (cnt + 1) % 2)
            cur["epilogue"]()
            cur = nxt
            cnt += 1
        if hh == 0:
            for _ in bg_adaln:
                pass
            wb["o"] = C.precast("o", w_o, 512)
            wb["gu0"] = C.precast("gu0", w_gu[0], 256)
            wb["dn0"] = C.precast("dn0", w_dn[0], 1408)
            wb["in"] = C.precast("in", w_in, 512)
            wb["out"] = C.precast("out", w_out, 512)
            wb["gu1"] = C.precast("gu1", w_gu[1], 256)
            wb["dn1"] = C.precast("dn1", w_dn[1], 1408)

    C.load_xT(xhalo, ntile=1)
    C.load_x_dma(xseq[G:2 * G, :])
    C.proj_fm(wb["o"], 0, 8, lambda fc, bank: C.residual_add(0, 0, fc, bank, N=16),
              rhs_of=lambda kc: oTh[:, kc, :], rhs_keys=lambda kc: ("oTh", kc), N=16)
    C.layernorm(0, 1, N=16)
    C.ffn(0, wb["gu0"], wb["dn0"], N=16)
    layer1_halo_u(C, 1, uh, ctmp, wb["in"])
    oTs = Qh[:, :].rearrange("p (h t) -> p h t", h=H)
    S.add("sp", lambda e: e.dma_start(out=oTs, in_=Ot[0]), reads=[("Ot", 0, hh) for hh in range(H)],
          writes=[("Qh", 0), ("Qh", 1)], dma=True)
    for i in range(NOWN):
        p = 2 * i + 1
        C.load_xT(None)
        if i + 1 < NOWN:
            C.load_x_dma(xseq[(p + 2) * G:(p + 3) * G, :])
        C.proj_fm(wb["o"], 0, 8, lambda fc, bank: C.residual_add(0, 0, fc, bank),
                  rhs_of=lambda kc: oTs[:, kc, :], rhs_keys=lambda kc: ("Qh", kc // 4))
        if i + 1 < NOWN:
            S.add("sp", lambda e, i=i: e.dma_start(out=oTs, in_=Ot[i + 1]), reads=[("Ot", i + 1, hh) for hh in range(H)],
                  writes=[("Qh", 0), ("Qh", 1)], dma=True)
        C.layernorm(0, 1)
        C.ffn(0, wb["gu0"], wb["dn0"])
        layer1_group(C, 1, i, None, uh, ubuf, cg, cv, wb["in"], wb["out"], wb["gu1"], wb["dn1"], out[i * G:(i + 1) * G, :])
    S.emit()
    return nc


def layer1_group(C, l, i, x_rows, uh, ubuf, cg, cv, w_in, w_out, w_gu, w_dn, out_rows):
    S = C.S
    big = C.big
    cw = C.small[:, SM_CONV:SM_CONV + 24].rearrange("p (j k) -> p j k", j=3)
    if x_rows is not None:
        C.load_xT(x_rows)
    C.layernorm(l, 0)
    for fc in range(0):
        S.add("dve", lambda e, fc=fc: e.tensor_copy(ubuf[:, fc, 0:2], uh[:, fc, 2 * i:2 * i + 2]),
              reads=[("uh", fc)], writes=[("ubuf", ub)])
    for s in range(2):
        views = []
        for part in range(3):
            views.append(C.slab(C.wcols(w_in, part * D + s * 512, 512), KC, 512))
        for o in range(4):
            fc = s * 4 + o
            banks = [C.bank(), C.bank(), C.bank()]
            for part in (1, 2, 0):
                view, wk = views[part]
                bk = banks[part]
                for kc in range(KC):
                    S.add("pe", lambda e, kc=kc, o=o, bk=bk, view=view: e.matmul(
                        C.ps[bk][:], lhsT=view[:, kc, o * 128:(o + 1) * 128], rhs=C.hT[:, kc, :],
                        start=(kc == 0), stop=(kc == KC - 1)), reads=[wk, ("hT", kc)], writes=[("ps", bk)])
            bb, bc, bu = banks
            ub = fc % 2
            S.add("act", lambda e, fc=fc, ub=ub: e.copy(ubuf[:, ub, 0:2], uh[:, fc, 2 * i:2 * i + 2]),
                  reads=[("uh", fc)], writes=[("ubuf", ub)])
            S.add("act", lambda e, bc=bc: e.copy(cg[:], C.ps[bc][:]), reads=[("ps", bc)], writes=["cg"])
            S.add("dve", lambda e, bu=bu, ub=ub: e.tensor_tensor(out=ubuf[:, ub, 2:2 + G], in0=C.ps[bu][:], in1=cg[:],
                                                                 op=ALU.mult),
                  reads=[("ps", bu), "cg"], writes=[("ubuf", ub)])
            cb = fc % 2
            S.add("dve", lambda e, fc=fc, cb=cb, ub=ub: e.tensor_scalar(out=cv[cb][:], in0=ubuf[:, ub, 0:G],
                                                                  scalar1=cw[:, 0, fc:fc + 1], scalar2=None,
                                                                  op0=ALU.mult),
                  reads=[("ubuf", ub), "small"], writes=[("cv", cb)])
            S.add("dve", lambda e, fc=fc, cb=cb, ub=ub: e.scalar_tensor_tensor(out=cv[cb][:], in0=ubuf[:, ub, 1:1 + G],
                                                                         scalar=cw[:, 1, fc:fc + 1], in1=cv[cb][:],
                                                                         op0=ALU.mult, op1=ALU.add),
                  reads=[("ubuf", ub), "small", ("cv", cb)], writes=[("cv", cb)])
            S.add("dve", lambda e, fc=fc, cb=cb, ub=ub: e.scalar_tensor_tensor(out=cv[cb][:], in0=ubuf[:, ub, 2:2 + G],
                                                                         scalar=cw[:, 2, fc:fc + 1], in1=cv[cb][:],
                                                                         op0=ALU.mult, op1=ALU.add),
                  reads=[("ubuf", ub), "small", ("cv", cb)], writes=[("cv", cb)])
            S.add("dve", lambda e, fc=fc, cb=cb, bb=bb: e.tensor_tensor(out=big[:, fc, :], in0=C.ps[bb][:], in1=cv[cb][:],
                                                                         op=ALU.mult),
                  reads=[("ps", bb), ("cv", cb)], writes=[("big", fc)])
    C.proj_fm(w_out, 0, 8, lambda fc, bank: C.residual_add(l, 0, fc, bank),
              rhs_of=lambda kc: big[:, kc, :], rhs_keys=lambda kc: ("big", kc))
    C.layernorm(l, 1)
    C.ffn(l, w_gu, w_dn)
    C.store_xT(out_rows, staging=[(cv[0][:, :], ("cv", 0)), (cv[1][:, :], ("cv", 1)),
                                  (ubuf[:, 0, 0:G], ("ubuf", 0)), (ubuf[:, 1, 0:G], ("ubuf", 1))])


def layer1_halo_u(C, l, uh, ctmp, w_in):
    S = C.S
    hval = C.small[:, SM_HVALID:SM_HVALID + 16]
    C.layernorm(l, 0, N=16)
    for part in range(2):
        def consume(oc, bank, part=part):
            if part == 0:
                S.add("act", lambda e: e.copy(uh[:, oc, :], C.ps[bank][:, 0:16]), reads=[("ps", bank)],
                      writes=[("uh", oc)])
            else:
                S.add("dve", lambda e: e.tensor_tensor(out=ctmp[:], in0=C.ps[bank][:, 0:16], in1=uh[:, oc, :],
                                                       op=ALU.mult), reads=[("ps", bank), ("uh", oc)], writes=["ctmp"])
                S.add("dve", lambda e: e.tensor_tensor(out=uh[:, oc, :], in0=ctmp[:], in1=hval, op=ALU.mult),
                      reads=["ctmp", "small"], writes=[("uh", oc)])
        C.proj_fm(w_in, D + part * D, 8, consume, N=16)


def _g_of_pos(p, half):
    return p if half == 1 else (p ^ 1)


def _tables(half):
    rb = np.zeros((NOWN, 4, NBLK), np.float32)
    npst = np.zeros((NOWN, 4, NBLK), np.float32)
    for i in range(NOWN):
        for qt in range(4):
            nbq = 2 * (2 * i + half) + qt // 2
            for pb in range(NBLK):
                gb = 2 * _g_of_pos(pb // 2, half) + pb % 2
                if gb < nbq:
                    npst[i, qt, pb] = NEG
                else:
                    rb[i, qt, pb] = -1e30
    tabs = np.concatenate([rb.reshape(-1), npst.reshape(-1)])[None, :].repeat(128, 0).astype(np.float32)
    rbh = np.zeros((128, NBLK), np.float32)
    nph = np.zeros((128, NBLK), np.float32)
    hpat = np.zeros((128, 64, 16), np.float32)
    kk = np.arange(128)
    for i in range(NOWN):
        gh = 2 * i + half - 1
        for t in range(2):
            col = 2 * i + t
            if gh < 0:
                continue
            gbq = 2 * gh + 1
            for pb in range(NBLK):
                gb = 2 * _g_of_pos(pb // 2, half) + pb % 2
                if gb < gbq:
                    nph[col, pb] = NEG
                else:
                    rbh[col, pb] = -1e30
                for sub in range(2):
                    kt = pb * 2 + sub
                    if gb < gbq:
                        hpat[:, kt, col] = 0.0
                    elif gb > gbq:
                        hpat[:, kt, col] = NEG
                    else:
                        hpat[:, kt, col] = np.where(sub * 128 + kk <= 254 + t, 0.0, NEG)
    tabs = np.concatenate([tabs, rbh, nph], axis=1).astype(np.float32)
    pat = np.zeros((128, 8, G), np.float32)
    k = np.arange(128)[:, None]
    q = np.arange(G)[None, :]
    qt = q // 128
    for ktw in range(8):
        if ktw < 4:
            pat[:, ktw, :] = 0.0 if half == 1 else NEG
        else:
            kt_ = ktw - 4
            kb = kt_ // 2
            qb = qt // 2
            kpos = (kt_ % 2) * 128 + k
            qpos = (qt % 2) * 128 + (q % 128)
            m = np.where(kb < qb, 0.0, np.where(kb > qb, NEG, np.where(kpos <= qpos, 0.0, NEG)))
            pat[:, ktw, :] = m
    sel = np.zeros((128, 32, 128), np.float32)
    for pb in range(32):
        sel[pb, pb, :] = 1.0
    import ml_dtypes
    bf = ml_dtypes.bfloat16
    patall = np.concatenate([pat.reshape(128, 8 * G), hpat.reshape(128, 1024)], axis=1)
    return tabs, patall.astype(bf), sel.reshape(128, 32 * 128).astype(bf)


def _small(c_b, b_ada_ls, nmix_ls, nffn_ls, q_gain, k_gain, conv_w, half):
    sm = np.zeros((128, NSM), np.float32)
    sm[:, SM_C:SM_C + 8] = c_b.reshape(8, 128).T
    for l, ba in enumerate(b_ada_ls):
        sm[:, SM_BADA + 48 * l:SM_BADA + 48 * (l + 1)] = ba.reshape(48, 128).T
    for l, v in enumerate(nmix_ls):
        sm[:, SM_NMIX + 8 * l:SM_NMIX + 8 * (l + 1)] = v.reshape(8, 128).T
    for l, v in enumerate(nffn_ls):
        sm[:, SM_NFFN + 8 * l:SM_NFFN + 8 * (l + 1)] = v.reshape(8, 128).T
    sm[:, SM_QG] = q_gain
    sm[:, SM_KG] = k_gain
    sm[:, SM_CONV:SM_CONV + 24] = conv_w.reshape(3, 8, 128).transpose(2, 0, 1).reshape(128, 24)
    hv = np.ones(16, np.float32)
    if half == 0:
        hv[0:2] = 0.0
    sm[:, SM_HVALID:SM_HVALID + 16] = hv[None, :]
    sm[:, SM_IDENT:SM_IDENT + 128] = np.eye(128, dtype=np.float32)
    return sm


_NC_CACHE = {}


def _get(name, fn):
    if name not in _NC_CACHE:
        _NC_CACHE[name] = fn()
    return _NC_CACHE[name]


def make_in_maps(x, c, w_ada, b_ada, norm_mix, norm_ffn, w_qkv, w_o, q_gain, k_gain, w_in, conv_w, w_out,
                 w_gate_up, w_down):
    in_maps = []
    for core in range(8):
        b, half = core // 2, core % 2
        perm = [_g_of_pos(p, half) for p in range(NPOS)]
        xseq = np.ascontiguousarray(x[b].reshape(NPOS, G, D)[perm].reshape(SEQ, D))
        xh = np.zeros((128, D), np.float32)
        for i in range(NOWN):
            g = 2 * i + half
            if g > 0:
                xh[2 * i:2 * i + 2] = x[b, g * G - 2:g * G]
            else:
                xh[2 * i:2 * i + 2] = x[b, 0:2]
        tabs, pat, sel = _tables(half)
        sm = _small(c[b], [b_ada[0], b_ada[1]], [norm_mix[0], norm_mix[1]], [norm_ffn[0], norm_ffn[1]],
                    q_gain[0], k_gain[0], conv_w[0], half)
        in_maps.append(dict(xseq=xseq, xhalo=xh, small=sm, tabs=tabs, pat=pat, sel=sel, w_ada=w_ada, w_qkv=w_qkv[0],
                            w_o=w_o[0], w_in=w_in[0], w_out=w_out[0], w_gu=w_gate_up, w_dn=w_down))
    return in_maps


def kernel(x, c, w_ada, b_ada, norm_mix, norm_ffn, w_qkv, w_o, q_gain, k_gain, w_in, conv_w, w_out,
           w_gate_up, w_down):
    f = lambda a: np.ascontiguousarray(np.asarray(a, dtype=np.float32))
    args = list(map(f, (x, c, w_ada, b_ada, norm_mix, norm_ffn, w_qkv, w_o, q_gain, k_gain, w_in, conv_w, w_out,
                        w_gate_up, w_down)))
    x = args[0]
    in_maps = make_in_maps(*args)
    nc = _get("F", build_fused)
    res = run_bass_kernel_spmd(nc, in_maps, core_ids=list(range(8)))
    out = np.zeros_like(x)
    for core in range(8):
        b, half = core // 2, core % 2
        o = np.asarray(res.results[core]["out"]).reshape(NOWN, G, D)
        for i in range(NOWN):
            g = 2 * i + half
            out[b, g * G:(g + 1) * G] = o[i]
    return out
```

```python
import contextlib
import numpy as np
import concourse.bass as bass
import concourse.mybir as mybir
from concourse.bass_utils import run_bass_kernel_spmd

F32 = mybir.dt.float32
BF16 = mybir.dt.bfloat16
ALU = mybir.AluOpType
AF = mybir.ActivationFunctionType

D = 1024
KC = 8
G = 512
H = 8
DH = 128
DFF = 2816
NJ = 22
NPOS = 16
NOWN = 8
NBLK = 32
SEQ = 8192
NEG = -30000.0
EPS = 1e-6
NW = 4
ENGS = ["pe", "act", "dve", "pool", "sp"]
INORDER = ("pe", "act", "dve")


class Sched:
    def __init__(self, nc, n_dma_sems=6):
        self.nc = nc
        self.ops = []
        self.last_write = {}
        self.readers = {}
        self.n_dma_sems = n_dma_sems

    def add(self, eng, fn, reads=(), writes=(), dma=False):
        oid = len(self.ops)
        deps = set()
        for b in reads:
            if b in self.last_write:
                deps.add(self.last_write[b])
        for b in writes:
            if b in self.last_write:
                deps.add(self.last_write[b])
            for r in self.readers.get(b, {}).values():
                deps.update(r)
        deps.discard(oid)
        self.ops.append(dict(id=oid, eng=eng, fn=fn, deps=deps, dma=dma, signal=False))
        for b in reads:
            rd = self.readers.setdefault(b, {})
            if dma or eng not in INORDER:
                rd.setdefault((eng, "dma"), []).append(oid)
            else:
                rd[eng] = [oid]
        for b in writes:
            self.last_write[b] = oid
            self.readers[b] = {}
        return oid

    def emit(self, final_wait_eng="sp"):
        nc = self.nc
        ops = self.ops

        def needs_sync(p, ceng):
            if p["dma"]:
                return True
            if p["eng"] != ceng:
                return True
            return p["eng"] in ("act", "dve", "pool")

        for op in ops:
            for d in op["deps"]:
                p = ops[d]
                if needs_sync(p, op["eng"]):
                    p["signal"] = True
            if op["dma"]:
                op["signal"] = True
        eng_count = {e: 0 for e in ENGS}
        dma_count = {}
        dma_rr = {e: 0 for e in ENGS}
        sem_keys = {}
        for op in ops:
            if not op["signal"]:
                continue
            if op["dma"]:
                k = dma_rr[op["eng"]] % self.n_dma_sems
                dma_rr[op["eng"]] += 1
                key = ("dma", op["eng"], k)
                prev = dma_count.get(key, 0)
                op["prev_on_sem"] = (key, prev)
                dma_count[key] = prev + 16
                op["sig"] = (key, prev + 16)
            else:
                key = ("eng", op["eng"])
                eng_count[op["eng"]] += 1
                op["sig"] = (key, eng_count[op["eng"]])
            sem_keys[key] = None
        with contextlib.ExitStack() as st:
            sems = {}
            for key in sem_keys:
                sems[key] = st.enter_context(nc.semaphore("s_" + "_".join(str(x) for x in key)))
            block = st.enter_context(nc.Block())
            per_eng = {e: [o for o in ops if o["eng"] == e] for e in ENGS}

            def body(ename):
                def run(eng):
                    waited = {}

                    def wait(key, val):
                        if waited.get(key, 0) >= val:
                            return
                        eng.wait_ge(sems[key], val)
                        waited[key] = val

                    for op in per_eng[ename]:
                        need = {}
                        for d in op["deps"]:
                            p = ops[d]
                            if not needs_sync(p, ename):
                                continue
                            key, val = p["sig"]
                            need[key] = max(need.get(key, 0), val)
                        if op["dma"]:
                            key, prev = op["prev_on_sem"]
                            if prev > 0:
                                need[key] = max(need.get(key, 0), prev)
                        for key, val in need.items():
                            wait(key, val)
                        ins = op["fn"](eng)
                        if op["signal"]:
                            key, val = op["sig"]
                            ins.then_inc(sems[key], 16 if op["dma"] else 1)
                    if ename == final_wait_eng:
                        for key, val in dma_count.items():
                            wait(key, val)
                        for e in ENGS:
                            if eng_count[e] > 0 and e != ename:
                                wait(("eng", e), eng_count[e])
                return run

            block.tensor(body("pe"))
            block.scalar(body("act"))
            block.vector(body("dve"))
            block.gpsimd(body("pool"))
            block.sync(body("sp"))


SM_C = 0
SM_BADA = 8
SM_NMIX = 104
SM_NFFN = 120
SM_QG = 136
SM_KG = 137
SM_CONV = 138
SM_HVALID = 162
SM_IDENT = 178
NSM = 306


class Ctx:
    def __init__(self, nc):
        self.nc = nc
        self.S = Sched(nc)
        self.wslot = 0
        self.bankrr = 0
        a = lambda n_, sh_, d_: nc.alloc_sbuf_tensor('sb_' + n_, sh_, d_)
        self.small = a("small", [128, NSM], F32)
        self.ident_bf = a("ident_bf", [128, 128], BF16)
        self.ones_bf = a("ones_bf", [128, 128], BF16)
        self.eps_t = a("eps_t", [128, 1], F32)
        self.wring = [a(f"wring{i}", [128, 4096], BF16) for i in range(NW)]
        self.xin = a("xin", [128, 4, D], F32)
        self.xT = a("xT", [128, KC, G], F32)
        self.sq = a("sq", [128, KC, G], BF16)
        self.hT = a("hT", [128, KC, G], BF16)
        self.tmp = [a(f"tmp{i}", [128, G], F32) for i in range(2)]
        self.std = a("std", [128, G], F32)
        self.rstd = a("rstd", [128, G], F32)
        self.big = a("big", [128, 24, G], BF16)
        self.mod = a("mod", [128, 2, 48], F32)
        self.modG = a("modG", [128, 2, 2, KC], F32)
        self.scbf = a("scbf", [128, KC], BF16)
        self.qgs = a("qgs", [128, 1], F32)
        self.pspair = [nc.alloc_psum_tensor(f"psp{i}", [128, 1024], F32) for i in range(3)]
        self.ps = []
        for i in range(3):
            self.ps.append(self.pspair[i][:, 0:512])
            self.ps.append(self.pspair[i][:, 512:1024])
        self.ps.append(nc.alloc_psum_tensor("ps6", [128, 512], F32)[:, :])
        self.psb = nc.alloc_psum_tensor("psb", [128, 1024], BF16)

    @property
    def ident(self):
        return self.small[:, SM_IDENT:SM_IDENT + 128]

    def bank(self, lo=0, hi=7):
        b = lo + self.bankrr % (hi - lo)
        self.bankrr += 1
        return b

    def slab(self, src3d, kparts, ncols):
        rk = []
        if isinstance(src3d, tuple):
            src3d, rk = src3d
        slot = self.wslot % NW
        self.wslot += 1
        view = self.wring[slot][:, 0:kparts * ncols].rearrange("p (k n) -> p k n", k=kparts)
        self.S.add("pool", lambda e: e.dma_start(out=view, in_=src3d), reads=rk, writes=[("w", slot)], dma=True)
        return view, ("w", slot)

    def wcols(self, w2d, c0, ncols, r0=0, kparts=KC):
        rk = []
        if isinstance(w2d, tuple):
            w2d, rk = w2d
        ap = w2d[r0:r0 + kparts * 128, c0:c0 + ncols].rearrange("(k p) n -> p k n", p=128)
        return (ap, rk) if rk else ap

    def precast(self, name, w2d, rows_per, defer=None):
        R, Ncol = w2d.shape
        wb = self.nc.dram_tensor("Wb_" + name, [R, Ncol], BF16).ap()
        keys = []
        r = 0
        while r < R:
            r1 = min(R, r + rows_per)
            key = ("Wb", name, r)

            def emit(extra_reads=(), r=r, r1=r1, key=key):
                self.S.add("pool", lambda e: e.dma_start(out=wb[r:r1, :], in_=w2d[r:r1, :]),
                           reads=list(extra_reads), writes=[key], dma=True)
            if defer is None:
                emit()
            else:
                defer.append(emit)
            keys.append(key)
            r = r1
        return (wb, keys)

    def setup(self, small_ap):
        S = self.S
        S.add("sp", lambda e: e.dma_start(out=self.small[:], in_=small_ap), writes=["small"], dma=True)
        S.add("dve", lambda e: e.memset(self.ones_bf[:], 1.0), writes=["ones"])
        S.add("dve", lambda e: e.memset(self.eps_t[:], EPS), writes=["eps"])
        S.add("dve", lambda e: e.tensor_copy(self.ident_bf[:], self.ident), reads=["small"], writes=["identbf"])
        S.add("act", lambda e: e.activation(out=self.scbf[:], in_=self.small[:, SM_C:SM_C + 8], func=AF.Silu),
              reads=["small"], writes=["scbf"])
        S.add("act", lambda e: e.mul(self.qgs[:], self.small[:, SM_QG:SM_QG + 1], DH ** -0.5),
              reads=["small"], writes=["qgs"])

    def adaln(self, w_ada_l, l):
        for _ in self.adaln_gen(w_ada_l, l):
            pass

    def adaln_gen(self, w_ada_l, l):
        S = self.S
        bank = 6
        c0 = 256
        first = True
        for s in range(12):
            if s > 0:
                yield
            view, wk = self.slab(self.wcols(w_ada_l, s * 512, 512), KC, 512)
            for o in range(4):
                oc = s * 4 + o
                for kc in range(KC):
                    wr = [("ps", bank)]
                    S.add("pe", lambda e, oc=oc, kc=kc, o=o, view=view: e.matmul(
                        self.ps[bank][:, c0 + oc:c0 + oc + 1], lhsT=view[:, kc, o * 128:(o + 1) * 128],
                        rhs=self.scbf[:, kc:kc + 1], start=(kc == 0), stop=(kc == KC - 1)),
                        reads=[wk, "scbf"], writes=wr)
        mod = self.mod
        S.add("dve", lambda e: e.tensor_tensor(out=mod[:, l, :], in0=self.ps[bank][:, c0:c0 + 48],
                                               in1=self.small[:, SM_BADA + 48 * l:SM_BADA + 48 * (l + 1)], op=ALU.add),
              reads=[("ps", bank), "small"], writes=[("mod", l)])
        S.add("dve", lambda e: e.scalar_tensor_tensor(out=self.modG[:, l, 0, :], in0=mod[:, l, 8:16], scalar=1.0,
                                                      in1=self.small[:, SM_NMIX + 8 * l:SM_NMIX + 8 * (l + 1)],
                                                      op0=ALU.add, op1=ALU.mult),
              reads=[("mod", l), "small"], writes=[("modG", l, 0)])
        S.add("dve", lambda e: e.scalar_tensor_tensor(out=self.modG[:, l, 1, :], in0=mod[:, l, 32:40], scalar=1.0,
                                                      in1=self.small[:, SM_NFFN + 8 * l:SM_NFFN + 8 * (l + 1)],
                                                      op0=ALU.add, op1=ALU.mult),
              reads=[("mod", l), "small"], writes=[("modG", l, 1)])

    def modcols(self, l, which):
        base = 0 if which == 0 else 24
        return (self.modG[:, l, which, :], self.mod[:, l, base:base + 8], self.mod[:, l, base + 16:base + 24],
                [("mod", l), ("modG", l, which)])

    def load_xT(self, x_rows, N=G, ntile=4):
        S = self.S
        xin = self.xin
        if x_rows is not None:
            self.load_x_dma(x_rows, ntile)
        for kc in range(KC):
            bank = self.bank()
            for t in range(ntile):
                S.add("pe", lambda e, kc=kc, t=t, bank=bank: e.transpose(
                    self.ps[bank][:, t * 128:(t + 1) * 128], xin[:, t, kc * 128:(kc + 1) * 128], self.ident),
                    reads=["xin", "small"], writes=[("ps", bank)])
            eng = "dve" if kc % 2 == 0 else "act"
            if eng == "dve":
                S.add("dve", lambda e, kc=kc, bank=bank: e.tensor_copy(self.xT[:, kc, 0:ntile * 128],
                                                                        self.ps[bank][:, 0:ntile * 128]),
                      reads=[("ps", bank)], writes=[("xT", kc)])
            else:
                S.add("act", lambda e, kc=kc, bank=bank: e.copy(self.xT[:, kc, 0:ntile * 128],
                                                                 self.ps[bank][:, 0:ntile * 128]),
                      reads=[("ps", bank)], writes=[("xT", kc)])

    def load_x_dma(self, x_rows, ntile=4):
        xin = self.xin
        self.S.add("sp", lambda e: e.dma_start(out=xin[:, 0:ntile, :], in_=x_rows.rearrange("(t p) f -> p t f", p=128)),
                   writes=["xin"], dma=True)

    def store_xT(self, out_rows, ntile=4, staging=None):
        S = self.S
        xin = self.xin
        if staging is not None:
            for t in range(ntile):
                for hf in range(2):
                    st, stk = staging[(t % 2) * 2 + hf]
                    bank = self.bank()
                    for k4 in range(4):
                        kc = hf * 4 + k4
                        S.add("pe", lambda e, kc=kc, t=t, bank=bank, k4=k4: e.transpose(
                            self.ps[bank][:, k4 * 128:(k4 + 1) * 128], self.xT[:, kc, t * 128:(t + 1) * 128],
                            self.ident), reads=[("xT", kc), "small"], writes=[("ps", bank)])
                    if hf == 0:
                        S.add("dve", lambda e, bank=bank, st=st: e.tensor_copy(st, self.ps[bank][:]),
                              reads=[("ps", bank)], writes=[stk])
                    else:
                        S.add("act", lambda e, bank=bank, st=st: e.copy(st, self.ps[bank][:]),
                              reads=[("ps", bank)], writes=[stk])
                    S.add("sp", lambda e, t=t, hf=hf, st=st: e.dma_start(
                        out=out_rows[t * 128:(t + 1) * 128, hf * 512:(hf + 1) * 512], in_=st),
                        reads=[stk], writes=[("out", id(out_rows), t, hf)], dma=True)
            return
        for t in range(ntile):
            for hf in range(2):
                bank = self.bank()
                for k4 in range(4):
                    kc = hf * 4 + k4
                    S.add("pe", lambda e, kc=kc, t=t, bank=bank, k4=k4: e.transpose(
                        self.ps[bank][:, k4 * 128:(k4 + 1) * 128], self.xT[:, kc, t * 128:(t + 1) * 128], self.ident),
                        reads=[("xT", kc), "small"], writes=[("ps", bank)])
                if hf == 0:
                    S.add("dve", lambda e, t=t, bank=bank, hf=hf: e.tensor_copy(xin[:, t, hf * 512:(hf + 1) * 512],
                                                                                 self.ps[bank][:]),
                          reads=[("ps", bank)], writes=["xin"])
                else:
                    S.add("act", lambda e, t=t, bank=bank, hf=hf: e.copy(xin[:, t, hf * 512:(hf + 1) * 512],
                                                                          self.ps[bank][:]),
                          reads=[("ps", bank)], writes=["xin"])
        S.add("sp", lambda e: e.dma_start(out=out_rows.rearrange("(t p) f -> p t f", p=128), in_=xin[:, 0:ntile, :]),
              reads=["xin"], writes=[("out", id(out_rows))], dma=True)

    def layernorm(self, l, which, N=G):
        S = self.S
        Gc, Sc, _, mkeys = self.modcols(l, which)
        for kc in range(KC):
            if kc % 2 == 0:
                S.add("act", lambda e, kc=kc: e.activation(out=self.sq[:, kc, :N], in_=self.xT[:, kc, :N],
                                                           func=AF.Square), reads=[("xT", kc)], writes=[("sq", kc)])
            else:
                S.add("dve", lambda e, kc=kc: e.tensor_tensor(out=self.sq[:, kc, :N], in0=self.xT[:, kc, :N],
                                                              in1=self.xT[:, kc, :N], op=ALU.mult),
                      reads=[("xT", kc)], writes=[("sq", kc)])
        bank = self.bank()
        for kc in range(KC):
            S.add("pe", lambda e, kc=kc: e.matmul(self.ps[bank][:, :N], lhsT=self.ones_bf[:], rhs=self.sq[:, kc, :N],
                                                  start=(kc == 0), stop=(kc == KC - 1)),
                  reads=[("sq", kc), "ones"], writes=[("ps", bank)])
        S.add("act", lambda e: e.activation(out=self.std[:, :N], in_=self.ps[bank][:, :N], func=AF.Ln,
                                            bias=self.eps_t[:, 0:1], scale=1.0 / D),
              reads=[("ps", bank), "eps"], writes=["std"])
        S.add("act", lambda e: e.activation(out=self.rstd[:, :N], in_=self.std[:, :N], func=AF.Exp, scale=-0.5),
              reads=["std"], writes=["rstd"])
        for kc in range(KC):
            tb = kc % 2
            S.add("dve", lambda e, kc=kc, tb=tb: e.tensor_tensor(out=self.tmp[tb][:, :N], in0=self.xT[:, kc, :N],
                                                                 in1=self.rstd[:, :N], op=ALU.mult),
                  reads=[("xT", kc), "rstd"], writes=[("tmp", tb)])
            S.add("act", lambda e, kc=kc, tb=tb: e.activation(out=self.hT[:, kc, :N], in_=self.tmp[tb][:, :N],
                                                              func=AF.Identity, scale=Gc[:, kc:kc + 1],
                                                              bias=Sc[:, kc:kc + 1]),
                  reads=[("tmp", tb)] + mkeys, writes=[("hT", kc)])

    def proj_fm(self, w2d, c0, nchunks, consume, rhs_of=None, rhs_keys=None, N=G, kparts=KC, r0=0):
        S = self.S
        if rhs_of is None:
            rhs_of = lambda kc: self.hT[:, kc, :N]
            rhs_keys = lambda kc: ("hT", kc)
        oc = 0
        while oc < nchunks:
            nch = min(4, nchunks - oc)
            view, wk = self.slab(self.wcols(w2d, c0 + oc * 128, nch * 128, r0=r0, kparts=kparts), kparts, nch * 128)
            for o in range(nch):
                bank = self.bank()
                for kc in range(kparts):
                    S.add("pe", lambda e, kc=kc, o=o, bank=bank, view=view: e.matmul(
                        self.ps[bank][:, :N], lhsT=view[:, kc, o * 128:(o + 1) * 128], rhs=rhs_of(kc),
                        start=(kc == 0), stop=(kc == kparts - 1)),
                        reads=[wk, rhs_keys(kc)], writes=[("ps", bank)])
                consume(oc + o, bank)
            oc += nch

    def residual_add(self, l, which, fc, bank, N=G):
        _, _, gate, mkeys = self.modcols(l, which)
        self.S.add("dve", lambda e: e.scalar_tensor_tensor(out=self.xT[:, fc, :N], in0=self.ps[bank][:, :N],
                                                           scalar=gate[:, fc:fc + 1], in1=self.xT[:, fc, :N],
                                                           op0=ALU.mult, op1=ALU.add),
                   reads=[("ps", bank), ("xT", fc)] + mkeys, writes=[("xT", fc)])

    def ffn(self, l, w_gu, w_dn, N=G):
        S = self.S
        big = self.big
        j = 0
        while j < NJ:
            nch = min(4, NJ - j)
            gview, gk = self.slab(self.wcols(w_gu, j * 128, nch * 128), KC, nch * 128)
            uview, uk = self.slab(self.wcols(w_gu, DFF + j * 128, nch * 128), KC, nch * 128)
            for o in range(nch):
                jj = j + o
                bg = self.bank()
                bu = self.bank()
                for kc in range(KC):
                    S.add("pe", lambda e, kc=kc, o=o, bg=bg, gview=gview: e.matmul(
                        self.ps[bg][:, :N], lhsT=gview[:, kc, o * 128:(o + 1) * 128], rhs=self.hT[:, kc, :N],
                        start=(kc == 0), stop=(kc == KC - 1)), reads=[gk, ("hT", kc)], writes=[("ps", bg)])
                for kc in range(KC):
                    S.add("pe", lambda e, kc=kc, o=o, bu=bu, uview=uview: e.matmul(
                        self.ps[bu][:, :N], lhsT=uview[:, kc, o * 128:(o + 1) * 128], rhs=self.hT[:, kc, :N],
                        start=(kc == 0), stop=(kc == KC - 1)), reads=[uk, ("hT", kc)], writes=[("ps", bu)])
                tb = jj % 2
                S.add("act", lambda e, bg=bg, tb=tb: e.activation(out=self.tmp[tb][:, :N], in_=self.ps[bg][:, :N],
                                                                  func=AF.Silu),
                      reads=[("ps", bg)], writes=[("tmp", tb)])
                S.add("dve", lambda e, bu=bu, tb=tb, jj=jj: e.tensor_tensor(out=big[:, jj, :N], in0=self.ps[bu][:, :N],
                                                                             in1=self.tmp[tb][:, :N], op=ALU.mult),
                      reads=[("ps", bu), ("tmp", tb)], writes=[("big", jj)])
            j += nch
        for fp in range(4):
            v0, k0 = self.slab(self.wcols(w_dn, fp * 256, 256, r0=0, kparts=11), 11, 256)
            v1, k1 = self.slab(self.wcols(w_dn, fp * 256, 256, r0=11 * 128, kparts=11), 11, 256)
            banks = [self.bank(), self.bank()]
            for jj in range(NJ):
                view, wk = (v0, k0) if jj < 11 else (v1, k1)
                for f2 in range(2):
                    S.add("pe", lambda e, jj=jj, f2=f2, view=view, b=banks[f2]: e.matmul(
                        self.ps[b][:, :N], lhsT=view[:, jj % 11, f2 * 128:(f2 + 1) * 128], rhs=big[:, jj, :N],
                        start=(jj == 0), stop=(jj == NJ - 1)), reads=[wk, ("big", jj)], writes=[("ps", banks[f2])])
            for f2 in range(2):
                self.residual_add(l, 1, fp * 2 + f2, banks[f2], N)


def build_fused():
    nc = bass.Bass("TRN2", target_bir_lowering=False)
    dt = nc.dram_tensor
    xseq = dt("xseq", [SEQ, D], F32, kind="ExternalInput").ap()
    xhalo = dt("xhalo", [128, D], F32, kind="ExternalInput").ap()
    small_ap = dt("small", [128, NSM], F32, kind="ExternalInput").ap()
    tabs_ap = dt("tabs", [128, 2048 + 64], F32, kind="ExternalInput").ap()
    pat_ap = dt("pat", [128, 8 * G + 64 * 16], BF16, kind="ExternalInput").ap()
    sel_ap = dt("sel", [128, 32 * 128], BF16, kind="ExternalInput").ap()
    w_ada = dt("w_ada", [2, D, 6 * D], F32, kind="ExternalInput").ap()
    w_qkv = dt("w_qkv", [D, 3 * D], F32, kind="ExternalInput").ap()
    w_o = dt("w_o", [D, D], F32, kind="ExternalInput").ap()
    w_in = dt("w_in", [D, 3 * D], F32, kind="ExternalInput").ap()
    w_out = dt("w_out", [D, D], F32, kind="ExternalInput").ap()
    w_gu = dt("w_gu", [2, D, 2 * DFF], F32, kind="ExternalInput").ap()
    w_dn = dt("w_dn", [2, DFF, D], F32, kind="ExternalInput").ap()
    out = dt("out", [NOWN * G, D], F32, kind="ExternalOutput").ap()
    Kt = dt("Kt", [H, DH, SEQ], BF16).ap()
    Vs = dt("Vs", [H, 128, 64, DH], BF16).ap()
    Qt = dt("Qt", [H, DH, NOWN * G], BF16).ap()
    Ot = dt("Ot", [NOWN, DH, H, G], BF16).ap()

    C = Ctx(nc)
    S = C.S
    a = lambda n_, sh_, d_: nc.alloc_sbuf_tensor('sb_' + n_, sh_, d_)
    tabs = a("tabs", [128, 2048 + 64], F32)
    pat = a("pat", [128, 8 * G + 64 * 16], BF16)
    sel = a("sel", [128, 32, 128], BF16)
    kmean_f = a("kmean_f", [128, H, NBLK], F32)
    kmean_bf = a("kmean_bf", [128, H, NBLK], BF16)
    Kh = a("Kh", [128, SEQ], BF16)
    Vh = a("Vh", [128, 64, DH], BF16)
    Qh = a("Qh", [128, NOWN * G], BF16)
    QhaloT = a("QhaloT", [128, H, 16], BF16)
    oTh = a("oTh", [128, H, 16], BF16)
    pTp = [a(f"pTp{i}", [128, 2, G], BF16) for i in range(2)]
    Rsb = a("Rsb", [128, 4, NBLK], F32)
    max8 = a("max8", [128, 4, 8], F32)
    mb = a("mb", [128, 4, NBLK], BF16)
    mbT = [a(f"mbT{i}", [128, G], BF16) for i in range(2)]
    oTsb = [a(f"oTsb{i}", [128, G], BF16) for i in range(2)]
    uh = a("uh", [128, KC, 16], F32)
    ctmp = a("ctmp", [128, 16], F32)
    cg = a("cg", [128, G], F32)
    ubuf = a("ubuf", [128, 2, 2 + G], F32)
    cv = [a(f"cv{i}", [128, G], F32) for i in range(2)]
    big = C.big
    recip = C.std

    C.setup(small_ap)
    S.add("sp", lambda e: e.dma_start(out=tabs[:], in_=tabs_ap), writes=["tabs"], dma=True)
    S.add("sp", lambda e: e.dma_start(out=pat[:], in_=pat_ap), writes=["pat"], dma=True)
    S.add("sp", lambda e: e.dma_start(out=sel[:].rearrange("p a b -> p (a b)"), in_=sel_ap), writes=["sel"], dma=True)
    S.add("dve", lambda e: e.memset(kmean_f[:], 0.0), writes=["kmean_f"])
    wb_qkv = C.precast("qkv", w_qkv, 256)
    C.adaln(w_ada[0], 0)
    kg = C.small[:, SM_KG:SM_KG + 1]
    acc = a("acc", [128, 2, G], F32)
    for i_ in range(2):
        S.add("dve", lambda e, i_=i_: e.memset(mbT[i_][:], 0.0), writes=[("mbT", i_)])
    ones_f = a("ones_f", [128, 128], F32)
    S.add("dve", lambda e: e.memset(ones_f[:], 1.0), writes=["ones_f"])

    def qk_head(hh, dst_ap, dst_key, gain_ap, gain_keys, wview, wk, o, kmean_pos=None, N=G):
        bank = C.bank()
        for kc in range(KC):
            S.add("pe", lambda e, kc=kc: e.matmul(C.ps[bank][:, :N], lhsT=wview[:, kc, o * 128:(o + 1) * 128],
                                                  rhs=C.hT[:, kc, :N], start=(kc == 0), stop=(kc == KC - 1)),
                  reads=[wk, ("hT", kc)], writes=[("ps", bank)])
        sqk = C.sq[:, hh, :N]
        S.add("act", lambda e: e.activation(out=sqk, in_=C.ps[bank][:, :N], func=AF.Square),
              reads=[("ps", bank)], writes=[("sq", hh)])
        b2 = C.bank()
        S.add("pe", lambda e: e.matmul(C.ps[b2][:, :N], lhsT=C.ones_bf[:], rhs=sqk, start=True, stop=True),
              reads=["ones", ("sq", hh)], writes=[("ps", b2)])
        if hh % 2 == 0:
            stdb, stdk, rstdb, rstdk = C.std, "std", C.rstd, "rstd"
        else:
            stdb, stdk, rstdb, rstdk = C.tmp[0], ("tmp", 0), C.tmp[1], ("tmp", 1)
        S.add("act", lambda e: e.activation(out=stdb[:, :N], in_=C.ps[b2][:, :N], func=AF.Ln, bias=C.eps_t[:, 0:1],
                                            scale=1.0 / DH), reads=[("ps", b2), "eps"], writes=[stdk])
        S.add("act", lambda e: e.activation(out=rstdb[:, :N], in_=stdb[:, :N], func=AF.Exp, scale=-0.5),
              reads=[stdk], writes=[rstdk])
        if kmean_pos is None:
            S.add("dve", lambda e: e.scalar_tensor_tensor(out=dst_ap, in0=C.ps[bank][:, :N], scalar=gain_ap,
                                                          in1=rstdb[:, :N], op0=ALU.mult, op1=ALU.mult),
                  reads=[("ps", bank), rstdk] + gain_keys, writes=[dst_key])
        else:
            for bb in range(2):
                pb = kmean_pos * 2 + bb
                S.add("dve", lambda e, bb=bb, pb=pb: e.scalar_tensor_tensor(
                    out=dst_ap[:, bb * 256:(bb + 1) * 256], in0=C.ps[bank][:, bb * 256:(bb + 1) * 256],
                    scalar=gain_ap, in1=rstdb[:, bb * 256:(bb + 1) * 256], op0=ALU.mult, op1=ALU.mult,
                    accum_out=kmean_f[:, hh, pb:pb + 1]),
                    reads=[("ps", bank), rstdk, "kmean_f"] + gain_keys, writes=[dst_key, "kmean_f"])

    C.load_x_dma(xseq[0:G, :])
    for p in range(NPOS):
        own = (p % 2 == 1)
        C.load_xT(None)
        if p + 1 < NPOS:
            C.load_x_dma(xseq[(p + 1) * G:(p + 2) * G, :])
        C.layernorm(0, 0)
        for s in range(2):
            wview, wk = C.slab(C.wcols(wb_qkv, D + s * 512, 512), KC, 512)
            for o in range(4):
                hh = s * 4 + o
                qk_head(hh, big[:, hh, :], ("big", hh), kg, ["small"], wview, wk, o, kmean_pos=p)
        S.add("sp", lambda e, p=p: e.dma_start(out=Kt.rearrange("h d t -> d h t")[:, :, p * G:(p + 1) * G],
                                               in_=big[:, 0:8, :]),
              reads=[("big", s_) for s_ in range(8)], writes=[("Kt", p)], dma=True)
        for hf in range(2):
            wview, wk = C.slab(C.wcols(wb_qkv, 2 * D + hf * 512, 512), KC, 512)
            for t in range(4):
                bank = C.bank()
                for kc in range(KC):
                    S.add("pe", lambda e, kc=kc, t=t, bank=bank, wview=wview: e.matmul(
                        C.ps[bank][:], lhsT=C.hT[:, kc, t * 128:(t + 1) * 128], rhs=wview[:, kc, :],
                        start=(kc == 0), stop=(kc == KC - 1)), reads=[wk, ("hT", kc)], writes=[("ps", bank)])
                slot = 16 + 2 * t + hf
                if t % 2 == 0:
                    S.add("act", lambda e, bank=bank, slot=slot: e.copy(big[:, slot, :], C.ps[bank][:]),
                          reads=[("ps", bank)], writes=[("big", slot)])
                else:
                    S.add("dve", lambda e, bank=bank, slot=slot: e.tensor_copy(big[:, slot, :], C.ps[bank][:]),
                          reads=[("ps", bank)], writes=[("big", slot)])
        for t in range(4):
            S.add("sp", lambda e, p=p, t=t: e.dma_start(
                out=Vs[:, :, p * 4 + t, :].rearrange("(hf h4) k d -> k hf h4 d", hf=2),
                in_=big[:, 16 + 2 * t:18 + 2 * t, :].rearrange("p hf (h4 d) -> p hf h4 d", h4=4)),
                reads=[("big", 16 + 2 * t), ("big", 17 + 2 * t)], writes=[("Vs", p, t)], dma=True)
        if own:
            i = p // 2
            for s in range(2):
                wview, wk = C.slab(C.wcols(wb_qkv, s * 512, 512), KC, 512)
                for o in range(4):
                    hh = s * 4 + o
                    qk_head(hh, big[:, 8 + hh, :], ("big", 8 + hh), C.qgs[:, 0:1], ["qgs"], wview, wk, o)
            S.add("sp", lambda e, i=i: e.dma_start(out=Qt.rearrange("h d t -> d h t")[:, :, i * G:(i + 1) * G],
                                                   in_=big[:, 8:16, :]),
                  reads=[("big", s_) for s_ in range(8, 16)], writes=[("Qt", i)], dma=True)
    C.load_xT(xhalo, ntile=1)
    C.layernorm(0, 0, N=16)
    for s in range(2):
        wview, wk = C.slab(C.wcols(wb_qkv, s * 512, 512), KC, 512)
        for o in range(4):
            hh = s * 4 + o
            qk_head(hh, QhaloT[:, hh, :], ("QhaloT", hh), C.qgs[:, 0:1], ["qgs"], wview, wk, o, N=16)
    S.add("dve", lambda e: e.tensor_copy(kmean_bf[:], kmean_f[:]), reads=["kmean_f"], writes=["kmean_bf"])

    rb = tabs[:, 0:1024].rearrange("p (i q n) -> p i q n", i=NOWN, q=4)
    npast = tabs[:, 1024:2048].rearrange("p (i q n) -> p i q n", i=NOWN, q=4)
    rb_h = tabs[:, 2048:2080]
    np_h = tabs[:, 2080:2112]
    patg = pat[:, 0:8 * G].rearrange("p (a b) -> p a b", a=8)
    hpat = pat[:, 8 * G:8 * G + 1024].rearrange("p (a b) -> p a b", a=64)
    RB = 6

    def desc_group(hh, i):
        return dict(N=G, nq=4, qrows=128, q_ap=Qh[:, i * G:(i + 1) * G], q_keys=[("Qh", i // 4)],
                    rb=rb[:, i, :, :], npst=npast[:, i, :, :], nkt=8 * i + 8,
                    pat_of=(lambda kt: patg[:, kt - 8 * i, :] if kt >= 8 * i else None), i=i, hh=hh)

    def desc_halo(hh):
        return dict(N=16, nq=1, qrows=16, q_ap=QhaloT[:, hh, :], q_keys=[("QhaloT", hh)],
                    rb=rb_h[0:16, :].rearrange("p (q n) -> p q n", q=1), npst=np_h[0:16, :].rearrange("p (q n) -> p q n", q=1),
                    nkt=64, pat_of=(lambda kt: hpat[:, kt, :]), i=None, hh=hh)

    def route(dsc, par):
        hh, N, nq, qr = dsc["hh"], dsc["N"], dsc["nq"], dsc["qrows"]
        qw = min(N, 128)
        for qt in range(nq):
            S.add("pe", lambda e, qt=qt: e.matmul(C.ps[RB][0:qr, qt * NBLK:(qt + 1) * NBLK],
                                                  lhsT=dsc["q_ap"][:, qt * qw:(qt + 1) * qw],
                                                  rhs=kmean_bf[:, hh, :], start=True, stop=True),
                  reads=dsc["q_keys"] + ["kmean_bf"], writes=[("ps", RB)])
        S.add("dve", lambda e: e.tensor_tensor(out=Rsb[0:qr, 0:nq, :],
                                               in0=C.ps[RB][0:qr, 0:nq * NBLK].rearrange("p (q n) -> p q n", q=nq),
                                               in1=dsc["rb"], op=ALU.add),
              reads=[("ps", RB), "tabs"], writes=["Rsb"])
        for qt in range(nq):
            S.add("dve", lambda e, qt=qt: e.max(out=max8[0:qr, qt, :], in_=Rsb[0:qr, qt, :]), reads=["Rsb"],
                  writes=[("max8", qt)])
            S.add("dve", lambda e, qt=qt: e.scalar_tensor_tensor(out=mb[0:qr, qt, :], in0=Rsb[0:qr, qt, :],
                                                                 scalar=max8[0:qr, qt, 2:3], in1=dsc["npst"][:, qt, :],
                                                                 op0=ALU.is_lt, op1=ALU.mult),
                  reads=["Rsb", ("max8", qt), "tabs"], writes=[("mb", qt)])

    def route_b(dsc, par):
        N, nq, qr = dsc["N"], dsc["nq"], dsc["qrows"]
        for qt in range(nq):
            S.add("pe", lambda e, qt=qt: e.transpose(C.psb[0:32, qt * 128:qt * 128 + qr], mb[0:qr, qt, :],
                                                     C.ident_bf[0:qr, 0:qr]),
                  reads=[("mb", qt), "identbf"], writes=["psb"])
        S.add("act", lambda e: e.copy(mbT[par][0:32, 0:N], C.psb[0:32, 0:N]), reads=["psb"], writes=[("mbT", par)])

    def make_attn(dsc, par):
        hh, N, nkt = dsc["hh"], dsc["N"], dsc["nkt"]
        ob = 4 + par
        db = 4 + (1 - par)
        npair = nkt // 2

        def qkpair(pr):
            X = pr % 2
            for sub in range(2):
                kt = 2 * pr + sub
                sb = 2 * X + sub
                pt = dsc["pat_of"](kt)
                S.add("pe", lambda e, kt=kt, sb=sb: e.matmul(C.ps[sb][:, :N], lhsT=Kh[:, kt * 128:(kt + 1) * 128],
                                                             rhs=dsc["q_ap"], start=True, stop=False),
                      reads=[("Kh", kt // 16)] + dsc["q_keys"], writes=[("ps", sb)])
                S.add("pe", lambda e, sb=sb, pt=pt: e.matmul(C.ps[sb][:, :N], lhsT=sel[:, pr, :], rhs=mbT[par][:, 0:N],
                                                             start=False, stop=(pt is None)),
                      reads=["sel", ("mbT", par)], writes=[("ps", sb)])
                if pt is not None:
                    S.add("pe", lambda e, sb=sb, pt=pt: e.matmul(C.ps[sb][:, :N], lhsT=C.ident_bf[:], rhs=pt, start=False,
                                                                 stop=True),
                          reads=["identbf", "pat"], writes=[("ps", sb)])
            scv = C.pspair[X][:, :].rearrange("p (b n) -> p b n", b=2)[:, :, 0:N]
            S.add("act", lambda e: e.activation(out=pTp[X][:, :, 0:N], in_=scv, func=AF.Exp),
                  reads=[("ps", 2 * X), ("ps", 2 * X + 1)], writes=[("pTp", X)])

        def pvpair(pr):
            X = pr % 2
            for sub in range(2):
                kt = 2 * pr + sub
                S.add("pe", lambda e, kt=kt, sub=sub: e.matmul(C.ps[ob][:, :N], lhsT=Vh[:, kt, :], rhs=pTp[X][:, sub, 0:N],
                                                               start=(kt == 0), stop=(kt == nkt - 1)),
                      reads=[("Vh", kt // 16), ("pTp", X)], writes=[("ps", ob)])
            if pr == 0:
                S.add("dve", lambda e: e.tensor_copy(acc[:, :, 0:N], pTp[X][:, :, 0:N]), reads=[("pTp", X)], writes=["acc"])
            else:
                S.add("dve", lambda e: e.tensor_tensor(out=acc[:, :, 0:N], in0=acc[:, :, 0:N], in1=pTp[X][:, :, 0:N],
                                                       op=ALU.add), reads=[("pTp", X), "acc"], writes=["acc"])

        def epilogue():
            for ai in range(2):
                S.add("pe", lambda e, ai=ai: e.matmul(C.ps[db][:, :N], lhsT=ones_f[:], rhs=acc[:, ai, 0:N],
                                                      start=(ai == 0), stop=(ai == 1)),
                      reads=["ones_f", "acc"], writes=[("ps", db)])
            S.add("act", lambda e: e.activation(out=C.rstd[:, :N], in_=C.ps[db][:, :N], func=AF.Ln),
                  reads=[("ps", db)], writes=["rstd"])
            S.add("act", lambda e: e.activation(out=recip[:, :N], in_=C.rstd[:, :N], func=AF.Exp, scale=-1.0),
                  reads=["rstd"], writes=["std"])
            if dsc["i"] is None:
                S.add("dve", lambda e: e.tensor_tensor(out=oTh[:, hh, :], in0=C.ps[ob][:, :N], in1=recip[:, :N],
                                                       op=ALU.mult), reads=[("ps", ob), "std"], writes=[("oTh", hh)])
            else:
                i = dsc["i"]
                S.add("dve", lambda e: e.tensor_tensor(out=oTsb[par][:], in0=C.ps[ob][:], in1=recip[:], op=ALU.mult),
                      reads=[("ps", ob), "std"], writes=[("oTsb", par)])
                S.add("sp", lambda e: e.dma_start(out=Ot[i, :, hh, :], in_=oTsb[par][:]), reads=[("oTsb", par)],
                      writes=[("Ot", i, hh)], dma=True)

        return dict(qkpair=qkpair, pvpair=pvpair, epilogue=epilogue, npair=npair)

    cnt = 0
    bg_adaln = C.adaln_gen(w_ada[1], 1)
    wb = {}
    casts = []
    wb["o"] = C.precast("o", w_o, 512, defer=casts)
    wb["gu0"] = C.precast("gu0", w_gu[0], 256, defer=casts)
    wb["dn0"] = C.precast("dn0", w_dn[0], 1408, defer=casts)
    wb["in"] = C.precast("in", w_in, 512, defer=casts)
    wb["out"] = C.precast("out", w_out, 512, defer=casts)
    wb["gu1"] = C.precast("gu1", w_gu[1], 256, defer=casts)
    wb["dn1"] = C.precast("dn1", w_dn[1], 1408, defer=casts)
    for hh in range(H):
        for c4 in range(4):
            if c4 < 2:
                S.add("sp", lambda e, hh=hh, c4=c4: e.dma_start(out=Qh[:, c4 * 2048:(c4 + 1) * 2048],
                                                                in_=Qt[hh, :, c4 * 2048:(c4 + 1) * 2048]),
                      reads=[("Qt", i) for i in range(NOWN)], writes=[("Qh", c4)], dma=True)
            S.add("sp", lambda e, hh=hh, c4=c4: e.dma_start(out=Kh[:, c4 * 2048:(c4 + 1) * 2048],
                                                            in_=Kt[hh, :, c4 * 2048:(c4 + 1) * 2048]),
                  reads=[("Kt", p) for p in range(NPOS)], writes=[("Kh", c4)], dma=True)
            S.add("sp", lambda e, hh=hh, c4=c4: e.dma_start(out=Vh[:, c4 * 16:(c4 + 1) * 16, :],
                                                            in_=Vs[hh, :, c4 * 16:(c4 + 1) * 16, :]),
                  reads=[("Vs", p, t) for p in range(NPOS) for t in range(4)], writes=[("Vh", c4)], dma=True)
        if hh >= 1:
            for _ in range(4):
                if casts:
                    casts.pop(0)(extra_reads=[("Kh", 3), ("Vh", 3)])
        if hh == H - 1:
            while casts:
                casts.pop(0)(extra_reads=[("Kh", 3), ("Vh", 3)])
        seq = [desc_group(hh, i) for i in range(NOWN)] + [desc_halo(hh)]
        route(seq[0], cnt % 2)
        route_b(seq[0], cnt % 2)
        cur = make_attn(seq[0], cnt % 2)
        cur["qkpair"](0)
        for n_, dsc in enumerate(seq):
            nxt = None
            if n_ + 1 < len(seq):
                route(seq[n_ + 1], (cnt + 1) % 2)
                nxt = make_attn(seq[n_ + 1], (cnt + 1) % 2)
            npair = cur["npair"]
            for pr in range(npair):
                if pr + 1 < npair:
                    cur["qkpair"](pr + 1)
                elif nxt is not None:
                    nxt["qkpair"](0)
                cur["pvpair"](pr)
                if hh == 0 and pr == npair // 2:
                    for _ in range(2):
                        next(bg_adaln, None)
                if pr == npair - 2 and nxt is not None:
                    route_b(seq[n_ + 1], (cnt + 1) % 2)
            cur["epilogue"]()
            cur = nxt
            cnt += 1
        if hh == 0:
            for _ in bg_adaln:
                pass

    C.load_xT(xhalo, ntile=1)
    C.load_x_dma(xseq[G:2 * G, :])
    C.proj_fm(wb["o"], 0, 8, lambda fc, bank: C.residual_add(0, 0, fc, bank, N=16),
              rhs_of=lambda kc: oTh[:, kc, :], rhs_keys=lambda kc: ("oTh", kc), N=16)
    C.layernorm(0, 1, N=16)
    C.ffn(0, wb["gu0"], wb["dn0"], N=16)
    layer1_halo_u(C, 1, uh, ctmp, wb["in"])
    oTs = Qh[:, :].rearrange("p (h t) -> p h t", h=H)
    S.add("sp", lambda e: e.dma_start(out=oTs, in_=Ot[0]), reads=[("Ot", 0, hh) for hh in range(H)],
          writes=[("Qh", 0), ("Qh", 1)], dma=True)
    for i in range(NOWN):
        p = 2 * i + 1
        C.load_xT(None)
        if i + 1 < NOWN:
            C.load_x_dma(xseq[(p + 2) * G:(p + 3) * G, :])
        C.proj_fm(wb["o"], 0, 8, lambda fc, bank: C.residual_add(0, 0, fc, bank),
                  rhs_of=lambda kc: oTs[:, kc, :], rhs_keys=lambda kc: ("Qh", kc // 4))
        if i + 1 < NOWN:
            S.add("sp", lambda e, i=i: e.dma_start(out=oTs, in_=Ot[i + 1]), reads=[("Ot", i + 1, hh) for hh in range(H)],
                  writes=[("Qh", 0), ("Qh", 1)], dma=True)
        C.layernorm(0, 1)
        C.ffn(0, wb["gu0"], wb["dn0"])
        layer1_group(C, 1, i, None, uh, ubuf, cg, cv, wb["in"], wb["out"], wb["gu1"], wb["dn1"], out[i * G:(i + 1) * G, :])
    S.emit()
    return nc


def layer1_group(C, l, i, x_rows, uh, ubuf, cg, cv, w_in, w_out, w_gu, w_dn, out_rows):
    S = C.S
    big = C.big
    cw = C.small[:, SM_CONV:SM_CONV + 24].rearrange("p (j k) -> p j k", j=3)
    if x_rows is not None:
        C.load_xT(x_rows)
    C.layernorm(l, 0)
    for fc in range(0):
        S.add("dve", lambda e, fc=fc: e.tensor_copy(ubuf[:, fc, 0:2], uh[:, fc, 2 * i:2 * i + 2]),
              reads=[("uh", fc)], writes=[("ubuf", ub)])
    for s in range(2):
        views = []
        for part in range(3):
            views.append(C.slab(C.wcols(w_in, part * D + s * 512, 512), KC, 512))
        for o in range(4):
            fc = s * 4 + o
            banks = [C.bank(), C.bank(), C.bank()]
            for part in (1, 2, 0):
                view, wk = views[part]
                bk = banks[part]
                for kc in range(KC):
                    S.add("pe", lambda e, kc=kc, o=o, bk=bk, view=view: e.matmul(
                        C.ps[bk][:], lhsT=view[:, kc, o * 128:(o + 1) * 128], rhs=C.hT[:, kc, :],
                        start=(kc == 0), stop=(kc == KC - 1)), reads=[wk, ("hT", kc)], writes=[("ps", bk)])
            bb, bc, bu = banks
            ub = fc % 2
            S.add("act", lambda e, fc=fc, ub=ub: e.copy(ubuf[:, ub, 0:2], uh[:, fc, 2 * i:2 * i + 2]),
                  reads=[("uh", fc)], writes=[("ubuf", ub)])
            S.add("act", lambda e, bc=bc: e.copy(cg[:], C.ps[bc][:]), reads=[("ps", bc)], writes=["cg"])
            S.add("dve", lambda e, bu=bu, ub=ub: e.tensor_tensor(out=ubuf[:, ub, 2:2 + G], in0=C.ps[bu][:], in1=cg[:],
                                                                 op=ALU.mult),
                  reads=[("ps", bu), "cg"], writes=[("ubuf", ub)])
            cb = fc % 2
            S.add("dve", lambda e, fc=fc, cb=cb, ub=ub: e.tensor_scalar(out=cv[cb][:], in0=ubuf[:, ub, 0:G],
                                                                  scalar1=cw[:, 0, fc:fc + 1], scalar2=None,
                                                                  op0=ALU.mult),
                  reads=[("ubuf", ub), "small"], writes=[("cv", cb)])
            S.add("dve", lambda e, fc=fc, cb=cb, ub=ub: e.scalar_tensor_tensor(out=cv[cb][:], in0=ubuf[:, ub, 1:1 + G],
                                                                         scalar=cw[:, 1, fc:fc + 1], in1=cv[cb][:],
                                                                         op0=ALU.mult, op1=ALU.add),
                  reads=[("ubuf", ub), "small", ("cv", cb)], writes=[("cv", cb)])
            S.add("dve", lambda e, fc=fc, cb=cb, ub=ub: e.scalar_tensor_tensor(out=cv[cb][:], in0=ubuf[:, ub, 2:2 + G],
                                                                         scalar=cw[:, 2, fc:fc + 1], in1=cv[cb][:],
                                                                         op0=ALU.mult, op1=ALU.add),
                  reads=[("ubuf", ub), "small", ("cv", cb)], writes=[("cv", cb)])
            S.add("dve", lambda e, fc=fc, cb=cb, bb=bb: e.tensor_tensor(out=big[:, fc, :], in0=C.ps[bb][:], in1=cv[cb][:],
                                                                         op=ALU.mult),
                  reads=[("ps", bb), ("cv", cb)], writes=[("big", fc)])
    C.proj_fm(w_out, 0, 8, lambda fc, bank: C.residual_add(l, 0, fc, bank),
              rhs_of=lambda kc: big[:, kc, :], rhs_keys=lambda kc: ("big", kc))
    C.layernorm(l, 1)
    C.ffn(l, w_gu, w_dn)
    C.store_xT(out_rows, staging=[(cv[0][:, :], ("cv", 0)), (cv[1][:, :], ("cv", 1)),
                                  (ubuf[:, 0, 0:G], ("ubuf", 0)), (ubuf[:, 1, 0:G], ("ubuf", 1))])


def layer1_halo_u(C, l, uh, ctmp, w_in):
    S = C.S
    hval = C.small[:, SM_HVALID:SM_HVALID + 16]
    C.layernorm(l, 0, N=16)
    for part in range(2):
        def consume(oc, bank, part=part):
            if part == 0:
                S.add("act", lambda e: e.copy(uh[:, oc, :], C.ps[bank][:, 0:16]), reads=[("ps", bank)],
                      writes=[("uh", oc)])
            else:
                S.add("dve", lambda e: e.tensor_tensor(out=ctmp[:], in0=C.ps[bank][:, 0:16], in1=uh[:, oc, :],
                                                       op=ALU.mult), reads=[("ps", bank), ("uh", oc)], writes=["ctmp"])
                S.add("dve", lambda e: e.tensor_tensor(out=uh[:, oc, :], in0=ctmp[:], in1=hval, op=ALU.mult),
                      reads=["ctmp", "small"], writes=[("uh", oc)])
        C.proj_fm(w_in, D + part * D, 8, consume, N=16)


def _g_of_pos(p, half):
    return p if half == 1 else (p ^ 1)


def _tables(half):
    rb = np.zeros((NOWN, 4, NBLK), np.float32)
    npst = np.zeros((NOWN, 4, NBLK), np.float32)
    for i in range(NOWN):
        for qt in range(4):
            nbq = 2 * (2 * i + half) + qt // 2
            for pb in range(NBLK):
                gb = 2 * _g_of_pos(pb // 2, half) + pb % 2
                if gb < nbq:
                    npst[i, qt, pb] = NEG
                else:
                    rb[i, qt, pb] = -1e30
    tabs = np.concatenate([rb.reshape(-1), npst.reshape(-1)])[None, :].repeat(128, 0).astype(np.float32)
    rbh = np.zeros((128, NBLK), np.float32)
    nph = np.zeros((128, NBLK), np.float32)
    hpat = np.zeros((128, 64, 16), np.float32)
    kk = np.arange(128)
    for i in range(NOWN):
        gh = 2 * i + half - 1
        for t in range(2):
            col = 2 * i + t
            if gh < 0:
                continue
            gbq = 2 * gh + 1
            for pb in range(NBLK):
                gb = 2 * _g_of_pos(pb // 2, half) + pb % 2
                if gb < gbq:
                    nph[col, pb] = NEG
                else:
                    rbh[col, pb] = -1e30
                for sub in range(2):
                    kt = pb * 2 + sub
                    if gb < gbq:
                        hpat[:, kt, col] = 0.0
                    elif gb > gbq:
                        hpat[:, kt, col] = NEG
                    else:
                        hpat[:, kt, col] = np.where(sub * 128 + kk <= 254 + t, 0.0, NEG)
    tabs = np.concatenate([tabs, rbh, nph], axis=1).astype(np.float32)
    pat = np.zeros((128, 8, G), np.float32)
    k = np.arange(128)[:, None]
    q = np.arange(G)[None, :]
    qt = q // 128
    for ktw in range(8):
        if ktw < 4:
            pat[:, ktw, :] = 0.0 if half == 1 else NEG
        else:
            kt_ = ktw - 4
            kb = kt_ // 2
            qb = qt // 2
            kpos = (kt_ % 2) * 128 + k
            qpos = (qt % 2) * 128 + (q % 128)
            m = np.where(kb < qb, 0.0, np.where(kb > qb, NEG, np.where(kpos <= qpos, 0.0, NEG)))
            pat[:, ktw, :] = m
    sel = np.zeros((128, 32, 128), np.float32)
    for pb in range(32):
        sel[pb, pb, :] = 1.0
    import ml_dtypes
    bf = ml_dtypes.bfloat16
    patall = np.concatenate([pat.reshape(128, 8 * G), hpat.reshape(128, 1024)], axis=1)
    return tabs, patall.astype(bf), sel.reshape(128, 32 * 128).astype(bf)


def _small(c_b, b_ada_ls, nmix_ls, nffn_ls, q_gain, k_gain, conv_w, half):
    sm = np.zeros((128, NSM), np.float32)
    sm[:, SM_C:SM_C + 8] = c_b.reshape(8, 128).T
    for l, ba in enumerate(b_ada_ls):
        sm[:, SM_BADA + 48 * l:SM_BADA + 48 * (l + 1)] = ba.reshape(48, 128).T
    for l, v in enumerate(nmix_ls):
        sm[:, SM_NMIX + 8 * l:SM_NMIX + 8 * (l + 1)] = v.reshape(8, 128).T
    for l, v in enumerate(nffn_ls):
        sm[:, SM_NFFN + 8 * l:SM_NFFN + 8 * (l + 1)] = v.reshape(8, 128).T
    sm[:, SM_QG] = q_gain
    sm[:, SM_KG] = k_gain
    sm[:, SM_CONV:SM_CONV + 24] = conv_w.reshape(3, 8, 128).transpose(2, 0, 1).reshape(128, 24)
    hv = np.ones(16, np.float32)
    if half == 0:
        hv[0:2] = 0.0
    sm[:, SM_HVALID:SM_HVALID + 16] = hv[None, :]
    sm[:, SM_IDENT:SM_IDENT + 128] = np.eye(128, dtype=np.float32)
    return sm


_NC_CACHE = {}


def _get(name, fn):
    if name not in _NC_CACHE:
        _NC_CACHE[name] = fn()
    return _NC_CACHE[name]


def make_in_maps(x, c, w_ada, b_ada, norm_mix, norm_ffn, w_qkv, w_o, q_gain, k_gain, w_in, conv_w, w_out,
                 w_gate_up, w_down):
    in_maps = []
    for core in range(8):
        b, half = core // 2, core % 2
        perm = [_g_of_pos(p, half) for p in range(NPOS)]
        xseq = np.ascontiguousarray(x[b].reshape(NPOS, G, D)[perm].reshape(SEQ, D))
        xh = np.zeros((128, D), np.float32)
        for i in range(NOWN):
            g = 2 * i + half
            if g > 0:
                xh[2 * i:2 * i + 2] = x[b, g * G - 2:g * G]
            else:
                xh[2 * i:2 * i + 2] = x[b, 0:2]
        tabs, pat, sel = _tables(half)
        sm = _small(c[b], [b_ada[0], b_ada[1]], [norm_mix[0], norm_mix[1]], [norm_ffn[0], norm_ffn[1]],
                    q_gain[0], k_gain[0], conv_w[0], half)
        in_maps.append(dict(xseq=xseq, xhalo=xh, small=sm, tabs=tabs, pat=pat, sel=sel, w_ada=w_ada, w_qkv=w_qkv[0],
                            w_o=w_o[0], w_in=w_in[0], w_out=w_out[0], w_gu=w_gate_up, w_dn=w_down))
    return in_maps


def kernel(x, c, w_ada, b_ada, norm_mix, norm_ffn, w_qkv, w_o, q_gain, k_gain, w_in, conv_w, w_out,
           w_gate_up, w_down):
    f = lambda a: np.ascontiguousarray(np.asarray(a, dtype=np.float32))
    args = list(map(f, (x, c, w_ada, b_ada, norm_mix, norm_ffn, w_qkv, w_o, q_gain, k_gain, w_in, conv_w, w_out,
                        w_gate_up, w_down)))
    x = args[0]
    in_maps = make_in_maps(*args)
    nc = _get("F", build_fused)
    res = run_bass_kernel_spmd(nc, in_maps, core_ids=list(range(8)))
    out = np.zeros_like(x)
    for core in range(8):
        b, half = core // 2, core % 2
        o = np.asarray(res.results[core]["out"]).reshape(NOWN, G, D)
        for i in range(NOWN):
            g = 2 * i + half
            out[b, g * G:(g + 1) * G] = o[i]
    return out
```
